# Optimizing a Trainium2 kernel written in Bass

```python
import math
import jax, jax.numpy as jnp
from jax import lax
import numpy as np

D_MODEL = 1024
BATCH = 4
SEQ = 8192
DEPTH = 2

ROPE_THETA = 500000.0
RMS_EPS = 1e-6
Q_BLOCK = 128
CONV_WIDTH = 3
D_CONV = D_MODEL // 2
MLA_HEADS = 8
MLA_NOPE = 64
MLA_ROPE = 32
MLA_V = 64
MLA_Q_LORA = 3 * D_MODEL // 8
MLA_KV_LORA = D_MODEL // 4
NSA_HEADS = 8
NSA_GROUPS = 2
NSA_HPG = NSA_HEADS // NSA_GROUPS
NSA_DIM = 64
NSA_ROT = NSA_DIM // 4
CMP_LEN = 32
CMP_STRIDE = 16
SEL_LEN = 64
N_SEL = 16
WINDOW = 512
FORCED_SCORE = 1e6
D_FF = 2816
IN_SPLITS = (D_MODEL, D_MODEL, D_MODEL,
             D_CONV, D_CONV, D_CONV,
             MLA_Q_LORA, MLA_KV_LORA, MLA_ROPE,
             NSA_HEADS * NSA_DIM,
             NSA_GROUPS * NSA_DIM, NSA_GROUPS * NSA_DIM,
             NSA_GROUPS * NSA_DIM, NSA_GROUPS * NSA_DIM,
             NSA_GROUPS * NSA_DIM, NSA_GROUPS * NSA_DIM,
             3 * NSA_HEADS)
N_IN = sum(IN_SPLITS)
SPLIT_POINTS = tuple(int(v) for v in np.cumsum(IN_SPLITS)[:-1])

kernel_name = 'hybrid_conv_mla_nsa_block'


def rms_norm(x, g):
    x32 = x.astype(jnp.float32)
    y = x32 * lax.rsqrt(jnp.mean(x32 * x32, axis=-1, keepdims=True) + RMS_EPS)
    return (y * g.astype(jnp.float32)).astype(x.dtype)


def masked_softmax(s, mask):
    s = jnp.where(mask, s.astype(jnp.float32), -1e30)
    return jnp.where(mask, jax.nn.softmax(s, axis=-1), 0.0)


def rope(x, positions, rot_dim):
    half = rot_dim // 2
    inv_freq = ROPE_THETA ** (-jnp.arange(half, dtype=jnp.float32) / half)
    ang = positions.astype(jnp.float32)[..., None] * inv_freq
    cos = jnp.cos(ang)[:, :, None, :]
    sin = jnp.sin(ang)[:, :, None, :]
    x1 = x[..., :half].astype(jnp.float32)
    x2 = x[..., half:rot_dim].astype(jnp.float32)
    rot = jnp.concatenate([x1 * cos - x2 * sin, x2 * cos + x1 * sin], axis=-1).astype(x.dtype)
    return jnp.concatenate([rot, x[..., rot_dim:]], axis=-1)


def causal_dwconv(u, w, b=None):
    S = u.shape[1]
    up = jnp.pad(u, ((0, 0), (CONV_WIDTH - 1, 0), (0, 0)))
    y = up[:, 0:S] * w[0]
    for k in range(1, CONV_WIDTH):
        y = y + up[:, k:k + S] * w[k]
    return y if b is None else y + b


def short_conv_mixer(b_gate, c_gate, x_in, conv_w):
    return b_gate * causal_dwconv(c_gate * x_in, conv_w)


def mla_mixer(c_q, c_kv, k_r, positions, q_norm, w_uq, kv_norm, w_ukv):
    B, S, _ = c_q.shape
    q = (rms_norm(c_q, q_norm) @ w_uq).reshape(B, S, MLA_HEADS, MLA_NOPE + MLA_ROPE)
    q_nope = q[..., :MLA_NOPE]
    q_rope = rope(q[..., MLA_NOPE:], positions, MLA_ROPE)
    kv = (rms_norm(c_kv, kv_norm) @ w_ukv).reshape(B, S, MLA_HEADS, MLA_NOPE + MLA_V)
    k_nope = kv[..., :MLA_NOPE]
    v = kv[..., MLA_NOPE:]
    k_rope = rope(k_r[:, :, None, :], positions, MLA_ROPE)[:, :, 0, :]
    scale = (MLA_NOPE + MLA_ROPE) ** -0.5
    kpos = jnp.arange(S)

    def block(i):
        s0 = i * Q_BLOCK
        qn = lax.dynamic_slice_in_dim(q_nope, s0, Q_BLOCK, axis=1)
        qr = lax.dynamic_slice_in_dim(q_rope, s0, Q_BLOCK, axis=1)
        s = (jnp.einsum('bqhd,bkhd->bhqk', qn, k_nope)
             + jnp.einsum('bqhd,bkd->bhqk', qr, k_rope)) * scale
        t = s0 + jnp.arange(Q_BLOCK)
        p = masked_softmax(s, kpos[None, :] <= t[:, None])
        return jnp.einsum('bhqk,bkhd->bqhd', p.astype(v.dtype), v)

    o = lax.map(block, jnp.arange(S // Q_BLOCK))
    return o.transpose(1, 0, 2, 3, 4).reshape(B, S, MLA_HEADS * MLA_V)


def nsa_mixer(q, k_cmp, v_cmp, k_slc, v_slc, k_win, v_win, gate_logits, positions,
              pos_k, pos_v, w_ck, w_cv):
    B, S, _ = q.shape
    G, J, dk = NSA_GROUPS, NSA_HPG, NSA_DIM
    q = rope(q.reshape(B, S, NSA_HEADS, dk), positions, NSA_ROT).reshape(B, S, G, J, dk)

    def kv_heads(t):
        return t.reshape(B, S, G, dk)

    k_cmp = rope(kv_heads(k_cmp), positions, NSA_ROT)
    k_slc = rope(kv_heads(k_slc), positions, NSA_ROT)
    k_win = rope(kv_heads(k_win), positions, NSA_ROT)
    v_cmp, v_slc, v_win = kv_heads(v_cmp), kv_heads(v_slc), kv_heads(v_win)

    n_cmp = (S - CMP_LEN) // CMP_STRIDE + 1
    idx = jnp.arange(n_cmp)[:, None] * CMP_STRIDE + jnp.arange(CMP_LEN)[None, :]

    def compress(t, pos, w):
        blocks = t[:, idx] + pos[None, None, :, None, :]
        return jnp.einsum('bnrgd,rde->bnge', blocks, w.reshape(CMP_LEN, dk, dk))

    kc = compress(k_cmp, pos_k, w_ck)
    vc = compress(v_cmp, pos_v, w_cv)
    cmp_end = jnp.arange(n_cmp) * CMP_STRIDE + CMP_LEN - 1

    n_sel = S // SEL_LEN
    n_top = min(N_SEL, n_sel)
    kb = k_slc.reshape(B, n_sel, SEL_LEN, G, dk).transpose(0, 3, 1, 2, 4)
    vb = v_slc.reshape(B, n_sel, SEL_LEN, G, dk).transpose(0, 3, 1, 2, 4)
    cmp_start = jnp.arange(n_cmp)[:, None] * CMP_STRIDE
    sel_start = jnp.arange(n_sel)[None, :] * SEL_LEN
    overlap = ((cmp_start <= sel_start + SEL_LEN - 1)
               & (cmp_start + CMP_LEN - 1 >= sel_start)).astype(jnp.float32)
    sel_id = jnp.arange(n_sel)
    bi = jnp.arange(B)[:, None, None, None]
    gi = jnp.arange(G)[None, :, None, None]

    pad = ((0, 0), (WINDOW, 0), (0, 0), (0, 0))
    kw = jnp.pad(k_win, pad)
    vw = jnp.pad(v_win, pad)

    gates = jax.nn.sigmoid(gate_logits.reshape(B, S, G, J, 3))
    scale = dk ** -0.5

    def block(i):
        s0 = i * Q_BLOCK
        qb = lax.dynamic_slice_in_dim(q, s0, Q_BLOCK, axis=1)
        gb = lax.dynamic_slice_in_dim(gates, s0, Q_BLOCK, axis=1)
        t = s0 + jnp.arange(Q_BLOCK)
        sc = jnp.einsum('bqgjd,bngd->bgjqn', qb, kc) * scale
        pc = masked_softmax(sc, cmp_end[None, :] <= t[:, None])
        o_cmp = jnp.einsum('bgjqn,bngd->bqgjd', pc.astype(vc.dtype), vc)
        imp = jnp.einsum('bgjqn,nm->bgqm', pc, overlap)
        cur = t // SEL_LEN
        forced = ((sel_id[None, :] == 0) | (sel_id[None, :] == cur[:, None])
                  | (sel_id[None, :] == cur[:, None] - 1))
        causal = sel_id[None, :] * SEL_LEN <= t[:, None]
        imp = jnp.where(forced, FORCED_SCORE, jnp.where(causal, imp, -1.0))
        top_val, top_idx = lax.top_k(imp, n_top)
        ks = kb[bi, gi, top_idx]
        vs = vb[bi, gi, top_idx]
        kpos = top_idx[..., None] * SEL_LEN + jnp.arange(SEL_LEN)
        smask = (top_val >= 0)[..., None] & (kpos <= t[None, None, :, None, None])
        ss = jnp.einsum('bqgjd,bgqnrd->bgjqnr', qb, ks) * scale
        ps = masked_softmax(ss.reshape(B, G, J, Q_BLOCK, n_top * SEL_LEN),
                            smask.reshape(B, G, 1, Q_BLOCK, n_top * SEL_LEN)).reshape(ss.shape)
        o_slc = jnp.einsum('bgjqnr,bgqnrd->bqgjd', ps.astype(vs.dtype), vs)
        kwb = lax.dynamic_slice_in_dim(kw, s0, Q_BLOCK + WINDOW, axis=1)
        vwb = lax.dynamic_slice_in_dim(vw, s0, Q_BLOCK + WINDOW, axis=1)
        kp = s0 - WINDOW + jnp.arange(Q_BLOCK + WINDOW)
        wmask = ((kp[None, :] >= 0) & (kp[None, :] <= t[:, None])
                 & (kp[None, :] > t[:, None] - WINDOW))
        sw = jnp.einsum('bqgjd,bkgd->bgjqk', qb, kwb) * scale
        pw = masked_softmax(sw, wmask)
        o_win = jnp.einsum('bgjqk,bkgd->bqgjd', pw.astype(vwb.dtype), vwb)
        o = gb[..., 0:1] * o_cmp + gb[..., 1:2] * o_slc + gb[..., 2:3] * o_win
        return o.reshape(B, Q_BLOCK, NSA_HEADS * dk)

    o = lax.map(block, jnp.arange(S // Q_BLOCK))
    return o.transpose(1, 0, 2, 3).reshape(B, S, NSA_HEADS * dk)


def gated_conv_ffn(h, w_up, conv_w, conv_b, w_down):
    a, b = jnp.split(h @ w_up, 2, axis=-1)
    a = causal_dwconv(a, conv_w, conv_b)
    return (jax.nn.gelu(a) * b) @ w_down


def setup_inputs(seed: int = 0) -> dict:
    key = jax.random.key(seed)
    keys = jax.random.split(key, 24)

    def dense(k, shape, fan_in):
        return jax.random.normal(k, shape, jnp.float32) * fan_in ** -0.5

    def gain(k, n):
        return 1.0 + 0.05 * jax.random.normal(k, (DEPTH, n), jnp.float32)

    x = jax.random.normal(keys[0], (BATCH, SEQ, D_MODEL), jnp.float32)
    positions = jnp.broadcast_to(jnp.arange(SEQ, dtype=jnp.int32)[None, :], (BATCH, SEQ))
    return {
        'x': x,
        'positions': positions,
        'norm_mix_pre': gain(keys[1], D_MODEL),
        'norm_mix_post': gain(keys[2], D_MODEL),
        'w_in': dense(keys[3], (DEPTH, D_MODEL, N_IN), D_MODEL),
        'conv_w': dense(keys[4], (DEPTH, CONV_WIDTH, D_CONV), CONV_WIDTH),
        'mla_q_norm': gain(keys[5], MLA_Q_LORA),
        'mla_w_uq': dense(keys[6], (DEPTH, MLA_Q_LORA, MLA_HEADS * (MLA_NOPE + MLA_ROPE)), MLA_Q_LORA),
        'mla_kv_norm': gain(keys[7], MLA_KV_LORA),
        'mla_w_ukv': dense(keys[8], (DEPTH, MLA_KV_LORA, MLA_HEADS * (MLA_NOPE + MLA_V)), MLA_KV_LORA),
        'nsa_cmp_pos_k': 0.1 * jax.random.normal(keys[9], (DEPTH, CMP_LEN, NSA_DIM), jnp.float32),
        'nsa_cmp_pos_v': 0.1 * jax.random.normal(keys[10], (DEPTH, CMP_LEN, NSA_DIM), jnp.float32),
        'nsa_cmp_w_k': dense(keys[11], (DEPTH, CMP_LEN * NSA_DIM, NSA_DIM), CMP_LEN * NSA_DIM),
        'nsa_cmp_w_v': dense(keys[12], (DEPTH, CMP_LEN * NSA_DIM, NSA_DIM), CMP_LEN * NSA_DIM),
        'w_branch_conv': dense(keys[13], (DEPTH, D_CONV, D_MODEL), D_CONV),
        'w_branch_mla': dense(keys[14], (DEPTH, MLA_HEADS * MLA_V, D_MODEL), MLA_HEADS * MLA_V),
        'w_branch_nsa': dense(keys[15], (DEPTH, NSA_HEADS * NSA_DIM, D_MODEL), NSA_HEADS * NSA_DIM),
        'w_out': dense(keys[16], (DEPTH, D_MODEL, D_MODEL), D_MODEL),
        'norm_ffn_pre': gain(keys[17], D_MODEL),
        'norm_ffn_post': gain(keys[18], D_MODEL),
        'ffn_w_up': dense(keys[19], (DEPTH, D_MODEL, 2 * D_FF), D_MODEL),
        'ffn_conv_w': dense(keys[20], (DEPTH, CONV_WIDTH, D_FF), CONV_WIDTH),
        'ffn_conv_b': 0.01 * jax.random.normal(keys[21], (DEPTH, D_FF), jnp.float32),
        'ffn_w_down': dense(keys[22], (DEPTH, D_FF, D_MODEL), D_FF),
    }


def reference(x, positions, norm_mix_pre, norm_mix_post, w_in, conv_w, mla_q_norm, mla_w_uq,
              mla_kv_norm, mla_w_ukv, nsa_cmp_pos_k, nsa_cmp_pos_v, nsa_cmp_w_k, nsa_cmp_w_v,
              w_branch_conv, w_branch_mla, w_branch_nsa, w_out, norm_ffn_pre, norm_ffn_post,
              ffn_w_up, ffn_conv_w, ffn_conv_b, ffn_w_down):
    for l in range(DEPTH):
        h = rms_norm(x, norm_mix_pre[l])
        (g_conv, g_mla, g_nsa, c_b, c_c, c_x, mla_cq, mla_ckv, mla_kr,
         nsa_q, nsa_kc, nsa_vc, nsa_ks, nsa_vs, nsa_kw, nsa_vw, nsa_g) = jnp.split(
            h @ w_in[l], SPLIT_POINTS, axis=-1)
        y_conv = short_conv_mixer(c_b, c_c, c_x, conv_w[l])
        y_mla = mla_mixer(mla_cq, mla_ckv, mla_kr, positions, mla_q_norm[l], mla_w_uq[l],
                          mla_kv_norm[l], mla_w_ukv[l])
        y_nsa = nsa_mixer(nsa_q, nsa_kc, nsa_vc, nsa_ks, nsa_vs, nsa_kw, nsa_vw, nsa_g, positions,
                          nsa_cmp_pos_k[l], nsa_cmp_pos_v[l], nsa_cmp_w_k[l], nsa_cmp_w_v[l])
        merged = (jax.nn.sigmoid(g_conv) * (y_conv @ w_branch_conv[l])
                  + jax.nn.sigmoid(g_mla) * (y_mla @ w_branch_mla[l])
                  + jax.nn.sigmoid(g_nsa) * (y_nsa @ w_branch_nsa[l]))
        x = x + rms_norm(merged @ w_out[l], norm_mix_post[l])
        h = rms_norm(x, norm_ffn_pre[l])
        x = x + rms_norm(gated_conv_ffn(h, ffn_w_up[l], ffn_conv_w[l], ffn_conv_b[l], ffn_w_down[l]),
                         norm_ffn_post[l])
    return x
```

```python
from contextlib import ExitStack
import numpy as np
import concourse.bass as bass
import concourse.mybir as mybir
from concourse.bass_utils import run_bass_kernel_spmd

F32 = mybir.dt.float32
BF16 = mybir.dt.bfloat16
I32 = mybir.dt.int32
AF = mybir.ActivationFunctionType
ALU = mybir.AluOpType

D = 1024
DEPTH = 2
THETA = 500000.0
EPS = 1e-6
DFF = 2816
NIN = 6584
NEG = -30000.0


class Buf:
    __slots__ = ("name", "last_w", "readers", "excl")

    def __init__(self, name, excl=False):
        self.name = name
        self.last_w = None
        self.readers = []
        self.excl = excl


class Ctx:
    def __init__(self, nc, stack, n_dma_ring=8):
        self.nc = nc
        self.engs = ("pe", "act", "dve", "pool", "sp")
        self.sems = {}
        self.count = {}
        self.waited = {k: {} for k in self.engs}
        for k in ("pe", "act", "dve", "pool"):
            self.sems[k] = stack.enter_context(nc.semaphore("s_" + k))
            self.count[k] = 0
        self.ring = {}
        self.ring_n = {}
        for q in ("sp", "pool", "act"):
            self.ring[q] = []
            for i in range(n_dma_ring):
                s = stack.enter_context(nc.semaphore("d_%s%d" % (q, i)))
                self.ring[q].append(s)
                self.sems[("d", q, i)] = s
            self.ring_n[q] = 0
        self.prog = {k: [] for k in self.engs}
        self.n_inst = 0
        self.n_wait = 0

    def _wait(self, e, tok):
        key, val = tok
        w = self.waited[e]
        if w.get(key, 0) >= val:
            return
        w[key] = val
        sem = self.sems[key]
        self.prog[e].append(lambda eng, sem=sem, val=val: eng.wait_ge(sem, val))
        self.n_wait += 1

    def _deps(self, e, reads, writes, same_ok):
        for b in reads:
            if b.last_w is not None and not (same_ok and b.last_w[0] == e):
                self._wait(e, b.last_w)
            if b.excl:
                for t in b.readers:
                    if t[0] != e:
                        self._wait(e, t)
        for b in writes:
            if b.last_w is not None and not (same_ok and b.last_w[0] == e):
                self._wait(e, b.last_w)
            for t in b.readers:
                if not (same_ok and t[0] == e):
                    self._wait(e, t)

    def _record(self, tok, reads, writes):
        for b in reads:
            b.readers.append(tok)
            if len(b.readers) > 24:
                d = {}
                for k, v in b.readers:
                    if d.get(k, 0) < v:
                        d[k] = v
                b.readers = list(d.items())
        for b in writes:
            b.last_w = tok
            b.readers = []

    def op(self, e, fn, reads=(), writes=()):
        self._deps(e, reads, writes, same_ok=(e == "pe"))
        self.count[e] += 1
        sem = self.sems[e]
        self.prog[e].append(lambda eng, fn=fn, sem=sem: fn(eng).then_inc(sem, 1))
        tok = (e, self.count[e])
        self._record(tok, reads, writes)
        self.n_inst += 1
        return tok

    def dma(self, q, out, in_, reads=(), writes=(), **kw):
        n = self.ring_n[q]
        R = len(self.ring[q])
        key = ("d", q, n % R)
        prev = n // R
        if prev > 0:
            self._wait(q, (key, 16 * prev))
        self._deps(q, reads, writes, same_ok=False)
        sem = self.sems[key]
        self.prog[q].append(lambda eng, out=out, in_=in_, kw=kw, sem=sem:
                            eng.dma_start(out=out, in_=in_, **kw).then_inc(sem, 16))
        self.ring_n[q] = n + 1
        tok = (key, 16 * (prev + 1))
        self._record(tok, reads, writes)
        self.n_inst += 1
        return tok

    def barrier(self):
        toks = [(k, self.count[k]) for k in ("pe", "act", "dve", "pool") if self.count[k] > 0]
        for q in self.ring:
            n = self.ring_n[q]
            R = len(self.ring[q])
            for slot in range(R):
                cnt = (n - slot + R - 1) // R
                if cnt > 0:
                    toks.append((("d", q, slot), 16 * cnt))
        for e in self.engs:
            for t in toks:
                if t[0] != e:
                    self._wait(e, t)

    def emit(self):
        prog = self.prog
        with self.nc.Block() as block:
            @block.sync
            def _(eng):
                for f in prog["sp"]:
                    f(eng)

            @block.tensor
            def _(eng):
                for f in prog["pe"]:
                    f(eng)

            @block.scalar
            def _(eng):
                for f in prog["act"]:
                    f(eng)

            @block.vector
            def _(eng):
                for f in prog["dve"]:
                    f(eng)

            @block.gpsimd
            def _(eng):
                for f in prog["pool"]:
                    f(eng)
        self.prog = {k: [] for k in self.engs}


def bcast_mid(ap, n):
    a = ap.ap
    return bass.AP(ap.tensor, ap.offset, [list(a[0]), [0, n]] + [list(x) for x in a[1:]])


def bcast_last(ap, n):
    a = ap.ap
    return bass.AP(ap.tensor, ap.offset, [list(x) for x in a] + [[0, n]])


def host_consts(S):
    NT = S // 128
    NSEL = S // 64
    k = np.arange(128)[:, None]
    q = np.arange(128)[None, :]
    c = {}
    c["ident"] = np.eye(128, dtype=np.float32)
    c["tri"] = (k <= q).astype(np.float32)
    c["wlow"] = (k > q).astype(np.float32)
    mc = np.zeros((128, 16, 128), np.float32)
    for o in range(16):
        mc[:, o, :] = (16 * (k - 8 * o) + 31 <= q)
    c["maskc"] = mc
    n = np.arange(512)[:, None]
    m = np.arange(128)[None, :]
    ov = ((n >= 4 * m - 1) & (n <= 4 * m + 3) & (m < NSEL)).astype(np.float32)
    c["ovl"] = ov.reshape(4, 128, 128).transpose(1, 0, 2).copy()
    A = np.zeros((NT, 128, 128), np.float32)
    for i in range(NT):
        t = 128 * i + np.arange(128)[:, None]
        cur = t // 64
        mm = np.arange(128)[None, :]
        forced = (mm == 0) | (mm == cur) | (mm == cur - 1)
        causal = 64 * mm <= t
        A[i] = np.where(forced, 1e6, np.where(causal, 0.0, -1.0))
        A[i][:, NSEL:] = -1.0
    c["amask"] = A
    E = np.zeros((128, NT, 128), np.float32)
    for j in range(NT):
        for kk in range(128):
            mrow = 2 * j + (kk >= 64)
            if mrow < 128:
                E[mrow, j, kk] = 1.0
    c["emat"] = E
    f32 = np.float32
    inv_m = (f32(THETA) ** (-(np.arange(16, dtype=f32) / f32(16)))).astype(f32)
    inv_n = (f32(THETA) ** (-(np.arange(8, dtype=f32) / f32(8)))).astype(f32)
    fm = np.zeros((128, 1), f32)
    fm[0:16, 0] = inv_m
    fm[16:32, 0] = inv_m
    fn = np.zeros((128, 1), f32)
    for base in (0, 64):
        fn[base:base + 8, 0] = inv_n
        fn[base + 8:base + 16, 0] = inv_n
    c["invf"] = np.concatenate([fm, fn], axis=1)
    return c


def build(S, depth=DEPTH, stop_after=None):
    assert S % 512 == 0
    NT = S // 128
    NS = S // 512
    NSEL = S // 64
    NCMP = S // 16 - 1
    NCT = (NCMP + 127) // 128
    CW = 65 + 128
    nc = bass.Bass("TRN2", target_bir_lowering=False)

    def din(name, shape, dt=F32):
        return nc.dram_tensor(name, list(shape), dt, kind="ExternalInput").ap()

    def dscr(name, shape, dt=BF16):
        return nc.dram_tensor(name, list(shape), dt, kind="Internal").ap()

    x_in = din("x", [S, D])
    pos_in = din("positions", [1, S], I32)
    W = {}
    W["norm_mix_pre"] = din("norm_mix_pre", [depth, D])
    W["norm_mix_post"] = din("norm_mix_post", [depth, D])
    W["w_in"] = din("w_in", [depth, D, NIN])
    W["conv_w"] = din("conv_w", [depth, 3, 512])
    W["mla_q_norm"] = din("mla_q_norm", [depth, 384])
    W["mla_w_uq"] = din("mla_w_uq", [depth, 384, 768])
    W["mla_kv_norm"] = din("mla_kv_norm", [depth, 256])
    W["mla_w_ukv"] = din("mla_w_ukv", [depth, 256, 1024])
    W["nsa_cmp_pos_k"] = din("nsa_cmp_pos_k", [depth, 32, 64])
    W["nsa_cmp_pos_v"] = din("nsa_cmp_pos_v", [depth, 32, 64])
    W["nsa_cmp_w_k"] = din("nsa_cmp_w_k", [depth, 2048, 64])
    W["nsa_cmp_w_v"] = din("nsa_cmp_w_v", [depth, 2048, 64])
    W["w_branch_conv"] = din("w_branch_conv", [depth, 512, D])
    W["w_branch_mla"] = din("w_branch_mla", [depth, 512, D])
    W["w_branch_nsa"] = din("w_branch_nsa", [depth, 512, D])
    W["w_out"] = din("w_out", [depth, D, D])
    W["norm_ffn_pre"] = din("norm_ffn_pre", [depth, D])
    W["norm_ffn_post"] = din("norm_ffn_post", [depth, D])
    W["ffn_w_up"] = din("ffn_w_up", [depth, D, 2 * DFF])
    W["ffn_conv_w"] = din("ffn_conv_w", [depth, 3, DFF])
    W["ffn_conv_b"] = din("ffn_conv_b", [depth, DFF])
    W["ffn_w_down"] = din("ffn_w_down", [depth, DFF, D])
    c_ident = din("c_ident", [128, 128])
    c_tri = din("c_tri", [128, 128])
    c_wlow = din("c_wlow", [128, 128])
    c_maskc = din("c_maskc", [128, 16, 128])
    c_ovl = din("c_ovl", [128, 4, 128])
    c_amask = din("c_amask", [NT, 128, 128])
    c_emat = din("c_emat", [128, NT, 128])
    c_invf = din("c_invf", [128, 2])
    y_out = nc.dram_tensor("y", [S, D], F32, kind="ExternalOutput").ap()

    xs = dscr("xs", [S, D], F32)
    qmT = dscr("qmT", [8, 96, S])
    kmT = dscr("kmT", [8, 96, S])
    vm = dscr("vm", [S, 8, 65])
    qnT = dscr("qnT", [2, 64, 4, S])
    kcmpT = dscr("kcmpT", [2, 64, S])
    vcmpT = dscr("vcmpT", [2, 64, S])
    kslcT = dscr("kslcT", [2, 64, S])
    kwinT = dscr("kwinT", [2, 64, S])
    vslc = dscr("vslc", [S, 2, 65])
    vwin = dscr("vwin", [S, 2, 65])
    gat = dscr("gat", [S, 24], F32)
    ymla = dscr("ymla", [S, 512])
    ynsa = dscr("ynsa", [S, 512])
    wgin_b = dscr("wgin_b", [D, 4608])
    wup_b = dscr("wup_b", [D, 2 * DFF])
    wdn_b = dscr("wdn_b", [DFF, D])

    with ExitStack() as top:
        cx = Ctx(nc, top)
        banks = [top.enter_context(nc.psum_tensor("bank%d" % i, [128, 512], F32)) for i in range(8)]
        bb = [Buf("bank%d" % i, excl=True) for i in range(8)]

        uid = [0]

        def sbt(st, name, shape, dt):
            uid[0] += 1
            nm = "%s_u%d" % (name, uid[0])
            return st.enter_context(nc.sbuf_tensor(nm, list(shape), dt)), Buf(nm)

        def ACT(out, in_, func, reads, writes, **kw):
            cx.op("act", lambda e: e.activation(out=out, in_=in_, func=func, **kw), reads, writes)

        def TT(out, in0, in1, op, reads, writes, eng="dve"):
            cx.op(eng, lambda e: e.tensor_tensor(out=out, in0=in0, in1=in1, op=op), reads, writes)

        def TS(out, in0, s1, s2, op0, op1, reads, writes, eng="dve", **kw):
            if s2 is None:
                cx.op(eng, lambda e: e.tensor_scalar(out=out, in0=in0, scalar1=s1, scalar2=None, op0=op0, **kw),
                      reads, writes)
            else:
                cx.op(eng, lambda e: e.tensor_scalar(out=out, in0=in0, scalar1=s1, scalar2=s2, op0=op0, op1=op1, **kw),
                      reads, writes)

        def STT(out, in0, scalar, in1, op0, op1, reads, writes, eng="dve"):
            cx.op(eng, lambda e: e.scalar_tensor_tensor(out=out, in0=in0, scalar=scalar, in1=in1, op0=op0, op1=op1),
                  reads, writes)

        def CP(out, in_, reads, writes, eng="dve"):
            cx.op(eng, lambda e: e.tensor_copy(out=out, in_=in_), reads, writes)

        def MS(ap, val, writes, eng="pool"):
            cx.op(eng, lambda e: e.memset(ap, val), (), writes)

        def MM(out, lhsT, rhs, start, stop, reads, writes):
            cx.op("pe", lambda e: e.matmul(out, lhsT=lhsT, rhs=rhs, start=start, stop=stop, skip_group_check=True),
                  reads, writes)

        def TR(out, in_, ident, reads, writes):
            cx.op("pe", lambda e: e.transpose(out, in_, ident), reads, writes)

        LD = lambda out, in_, writes, reads=(), **kw: cx.dma("sp", out, in_, reads=reads, writes=writes, **kw)
        ST = lambda out, in_, reads, writes=(), **kw: cx.dma("pool", out, in_, reads=reads, writes=writes, **kw)

        B = {n: Buf(n) for n in ("xs", "qmT", "kmT", "vm", "qnT", "kcmpT", "vcmpT", "kslcT", "kwinT", "vslc",
                                 "vwin", "gat", "ymla", "ynsa", "wgin_b", "wup_b", "wdn_b", "y")}

        identf, b_identf = sbt(top, "identf", [128, 128], F32)
        identb, b_identb = sbt(top, "identb", [128, 128], BF16)
        ones_b, b_ones = sbt(top, "ones_b", [128, 128], BF16)
        invf, b_invf = sbt(top, "invf", [128, 2], F32)
        LD(identf[:], c_ident[:, :], [b_identf])
        LD(invf[:], c_invf[:, :], [b_invf])
        CP(identb[:], identf[:], [b_identf], [b_identb])
        MS(ones_b[:], 1.0, [b_ones])
        CONST = [b_identf, b_identb, b_ones, b_invf]

        epst, b_epst = sbt(top, "epst", [128, 1], F32)
        MS(epst[:], EPS, [b_epst])

        def rsqrt_ip(ap, b_ap, inv_n, src=None, b_src=None):
            p = ap.shape[0]
            if src is None:
                src, b_src = ap, b_ap
            ACT(ap, src, AF.Sqrt, [b_src, b_epst], [b_ap], bias=epst[0:p, 0:1], scale=float(inv_n))
            cx.op("dve", lambda e: e.reciprocal(out=ap, in_=ap), [b_ap], [b_ap])

        def vec_col(st, name, src_row, n):
            c = n // 128
            t, b = sbt(st, name, [128, c], F32)
            for k in range(c):
                LD(t[:, k:k + 1], src_row[k * 128:(k + 1) * 128].rearrange("(p o) -> p o", o=1), [b])
            return t, b

        def rms_tile(xt, b_xt, rs, b_rs, col, junk, b_junk, n):
            ACT(junk, xt, AF.Square, [b_xt], [b_junk, b_rs], accum_out=rs[:, col:col + 1])
            rsqrt_ip(rs[:, col:col + 1], b_rs, 1.0 / n)

        def norm_transpose(src, s0, xt4, b_xt4, hT, b_hT, rs, b_rs, hn, b_hn, junk, b_junk, b_src, bank, b_bank):
            for t in range(4):
                LD(xt4[:, t, :], src[s0 + t * 128:s0 + (t + 1) * 128, :], [b_xt4], reads=[b_src])
            for t in range(4):
                MS(rs[:, t:t + 1], 0.0, [b_rs], eng="dve")
                rms_tile(xt4[:, t, :], b_xt4, rs, b_rs, t, junk[:], b_junk, D)
                TS(hn[:], xt4[:, t, :], rs[:, t:t + 1], None, ALU.mult, None, [b_xt4, b_rs], [b_hn])
                pb = bank[:, :].bitcast(BF16)
                for k in range(8):
                    TR(pb[:, k * 128:(k + 1) * 128], hn[:, k * 128:(k + 1) * 128], identb[:],
                       [b_hn, b_identb], [b_bank])
                CP(hT[:, :, t * 128:(t + 1) * 128], pb[:, 0:1024].rearrange("p (k t) -> p k t", k=8),
                   [b_bank], [b_hT], eng="act" if False else "dve")

        def sincos(st_name, pos_f, b_pos, col, rows, shift, out, b_out, wk, b_wk, wki, b_wki):
            r = slice(0, rows)
            TS(wk[r, 0, :], pos_f[r, :], invf[r, col:col + 1], shift, ALU.mult, ALU.add, [b_pos, b_invf], [b_wk])
            TS(wk[r, 1, :], wk[r, 0, :], float(1.0 / (2 * np.pi)), None, ALU.mult, None, [b_wk], [b_wk])
            CP(wki[r, :], wk[r, 1, :], [b_wk], [b_wki])
            CP(wk[r, 1, :], wki[r, :], [b_wki], [b_wk])
            STT(wk[r, 0, :], wk[r, 1, :], float(-2 * np.pi), wk[r, 0, :], ALU.mult, ALU.add, [b_wk], [b_wk])
            TS(wk[r, 1, :], wk[r, 0, :], float(np.pi), None, ALU.is_gt, None, [b_wk], [b_wk])
            STT(wk[r, 0, :], wk[r, 1, :], float(-2 * np.pi), wk[r, 0, :], ALU.mult, ALU.add, [b_wk], [b_wk])
            TS(wk[r, 1, :], wk[r, 0, :], float(-np.pi), None, ALU.is_lt, None, [b_wk], [b_wk])
            STT(wk[r, 0, :], wk[r, 1, :], float(2 * np.pi), wk[r, 0, :], ALU.mult, ALU.add, [b_wk], [b_wk])
            ACT(out[r, :], wk[r, 0, :], AF.Sin, [b_wk], [b_out])

        PH = [0]
        for l in range(depth):
            x_src, b_xsrc = (x_in, Buf("x_in")) if l == 0 else (xs, B["xs"])
            x_dst, b_xdst = (y_out, B["y"]) if l == depth - 1 else (xs, B["xs"])
            w_in = W["w_in"][l]

            with ExitStack() as st:
                NA = 2904
                WA, b_WA = sbt(st, "WA", [128, 8, NA], BF16)
                WQ, b_WQ = sbt(st, "WQ", [128, 3, 1024], BF16)
                WKV, b_WKV = sbt(st, "WKV", [128, 2, 1024], BF16)
                stage, b_stage = sbt(st, "stage", [128, 1976], F32)
                gpre, b_gpre = vec_col(st, "gpre", W["norm_mix_pre"][l], D)
                gq, b_gq = vec_col(st, "gq", W["mla_q_norm"][l], 384)
                gkv, b_gkv = vec_col(st, "gkv", W["mla_kv_norm"][l], 256)
                ngpre, b_ngpre = sbt(st, "ngpre", [128, 8], F32)
                ngq, b_ngq = sbt(st, "ngq", [128, 3], F32)
                TS(ngpre[:], gpre[:], -1.0, None, ALU.mult, None, [b_gpre], [b_ngpre])
                TS(ngq[:], gq[:], -1.0, None, ALU.mult, None, [b_gq], [b_ngq])
                MS(WA[:, :, 1216:1728], 0.0, [b_WA])
                for c in (1856, 2112, 2368):
                    MS(WA[:, :, c:c + 128], 0.0, [b_WA])

                def SC(out, in_, sc, rd, wr):
                    TS(out, in_, sc, None, ALU.mult, None, rd, wr)

                for kc in range(8):
                    LD(stage[:], w_in[kc * 128:(kc + 1) * 128, 4608:6584], [b_stage])
                    g = gpre[:, kc:kc + 1]
                    ng = ngpre[:, kc:kc + 1]
                    rd = [b_stage, b_gpre, b_ngpre]
                    SC(WA[:, kc, 0:672], stage[:, 0:672], g, rd, [b_WA])
                    SC(WA[:, kc, 672:688], stage[:, 656:672], ng, rd, [b_WA])
                    SC(WA[:, kc, 688:704], stage[:, 640:656], g, rd, [b_WA])
                    SC(WA[:, kc, 704:1216], stage[:, 672:1184], g, rd, [b_WA])
                    sq = stage[:, 672:1184].rearrange("p (h d) -> p h d", d=64)
                    dq = WA[:, kc, 1216:1728].rearrange("p (h d) -> p h d", d=64)
                    SC(dq[:, :, 0:8], sq[:, :, 8:16], ng, rd, [b_WA])
                    SC(dq[:, :, 8:16], sq[:, :, 0:8], g, rd, [b_WA])
                    for (src_o, dst_o) in ((1184, 1728), (1440, 1984), (1696, 2240)):
                        SC(WA[:, kc, dst_o:dst_o + 128], stage[:, src_o:src_o + 128], g, rd, [b_WA])
                        sk = stage[:, src_o:src_o + 128].rearrange("p (h d) -> p h d", d=64)
                        dk = WA[:, kc, dst_o + 128:dst_o + 256].rearrange("p (h d) -> p h d", d=64)
                        SC(dk[:, :, 0:8], sk[:, :, 8:16], ng, rd, [b_WA])
                        SC(dk[:, :, 8:16], sk[:, :, 0:8], g, rd, [b_WA])
                    SC(WA[:, kc, 2496:2624], stage[:, 1312:1440], g, rd, [b_WA])
                    SC(WA[:, kc, 2624:2752], stage[:, 1568:1696], g, rd, [b_WA])
                    SC(WA[:, kc, 2752:2880], stage[:, 1824:1952], g, rd, [b_WA])
                    SC(WA[:, kc, 2880:2904], stage[:, 1952:1976], g, rd, [b_WA])
                for kc in range(3):
                    LD(stage[:, 0:768], W["mla_w_uq"][l][kc * 128:(kc + 1) * 128, :], [b_stage])
                    g = gq[:, kc:kc + 1]
                    ng = ngq[:, kc:kc + 1]
                    rd = [b_stage, b_gq, b_ngq]
                    s3 = stage[:, 0:768].rearrange("p (h d) -> p h d", d=96)
                    d3 = WQ[:, kc, 0:768].rearrange("p (h d) -> p h d", d=96)
                    SC(d3[:, :, 0:32], s3[:, :, 64:96], g, rd, [b_WQ])
                    SC(d3[:, :, 32:96], s3[:, :, 0:64], g, rd, [b_WQ])
                    r3 = WQ[:, kc, 768:1024].rearrange("p (h d) -> p h d", d=32)
                    SC(r3[:, :, 0:16], s3[:, :, 80:96], ng, rd, [b_WQ])
                    SC(r3[:, :, 16:32], s3[:, :, 64:80], g, rd, [b_WQ])
                for kc in range(2):
                    LD(stage[:, 0:1024], W["mla_w_ukv"][l][kc * 128:(kc + 1) * 128, :], [b_stage])
                    g = gkv[:, kc:kc + 1]
                    rd = [b_stage, b_gkv]
                    s3 = stage[:, 0:1024].rearrange("p (h d) -> p h d", d=128)
                    SC(WKV[:, kc, 0:512].rearrange("p (h d) -> p h d", d=64), s3[:, :, 0:64], g, rd, [b_WKV])
                    SC(WKV[:, kc, 512:1024].rearrange("p (h d) -> p h d", d=64), s3[:, :, 64:128], g, rd, [b_WKV])

                gf, b_gf = vec_col(st, "gf", W["norm_ffn_pre"][l], D)
                wst, b_wst = sbt(st, "wst", [128, 5632], F32)
                wsb, b_wsb = sbt(st, "wsb", [128, 5632], BF16)
                for kc in range(8):
                    LD(wst[:, 0:4608], w_in[kc * 128:(kc + 1) * 128, 0:4608], [b_wst])
                    SC(wsb[:, 0:4608], wst[:, 0:4608], gpre[:, kc:kc + 1], [b_wst, b_gpre], [b_wsb])
                    ST(wgin_b[kc * 128:(kc + 1) * 128, :], wsb[:, 0:4608], [b_wsb], [B["wgin_b"]])
                for kc in range(8):
                    LD(wst[:, :], W["ffn_w_up"][l][kc * 128:(kc + 1) * 128, :], [b_wst])
                    SC(wsb[:, :], wst[:, :], gf[:, kc:kc + 1], [b_wst, b_gf], [b_wsb])
                    ST(wup_b[kc * 128:(kc + 1) * 128, :], wsb[:, :], [b_wsb], [B["wup_b"]])
                for kc in range(22):
                    LD(wst[:, 0:1024], W["ffn_w_down"][l][kc * 128:(kc + 1) * 128, :], [b_wst])
                    CP(wsb[:, 0:1024], wst[:, 0:1024], [b_wst], [b_wsb])
                    ST(wdn_b[kc * 128:(kc + 1) * 128, :], wsb[:, 0:1024], [b_wsb], [B["wdn_b"]])

                xt4, b_xt4 = sbt(st, "xt4", [128, 4, D], F32)
                hT, b_hT = sbt(st, "hT", [128, 8, 512], BF16)
                hn, b_hn = sbt(st, "hn", [128, D], BF16)
                junk, b_junk = sbt(st, "junk", [128, D], BF16)
                rs, b_rs = sbt(st, "rs", [128, 8], F32)
                posi, b_posi = sbt(st, "posi", [128, 512], I32)
                posf, b_posf = sbt(st, "posf", [128, 512], F32)
                wk, b_wk = sbt(st, "wk", [128, 2, 512], F32)
                wki, b_wki = sbt(st, "wki", [128, 512], I32)
                Cm, b_Cm = sbt(st, "Cm", [128, 512], F32)
                Sm, b_Sm = sbt(st, "Sm", [128, 512], F32)
                Cn, b_Cn = sbt(st, "Cn", [128, 512], F32)
                Sn, b_Sn = sbt(st, "Sn", [128, 512], F32)
                cqT, b_cqT = sbt(st, "cqT", [128, 3, 512], BF16)
                ckvT, b_ckvT = sbt(st, "ckvT", [128, 2, 512], BF16)
                sqb, b_sqb = sbt(st, "sqb", [128, 3, 512], BF16)
                rq, b_rq = sbt(st, "rq", [128, 512], F32)
                rkv, b_rkv = sbt(st, "rkv", [128, 512], F32)
                rkt, b_rkt = sbt(st, "rkt", [128, 4], F32)
                CR, b_CR = sbt(st, "CR", [128, 512], F32)
                SR, b_SR = sbt(st, "SR", [128, 512], F32)
                t1, b_t1 = sbt(st, "t1", [128, 512], F32)
                t2, b_t2 = sbt(st, "t2", [128, 512], F32)
                ob = [sbt(st, "ob%d" % i, [128, 512], BF16) for i in range(3)]
                vt = [sbt(st, "vt%d" % i, [128, 8, 65], BF16) for i in range(2)]
                vs2 = [sbt(st, "vs2_%d" % i, [128, 2, 2, 65], BF16) for i in range(2)]
                gt = [sbt(st, "gt%d" % i, [128, 24], F32) for i in range(2)]
                for (t_, b_) in vt:
                    MS(t_[:], 1.0, [b_])
                for (t_, b_) in vs2:
                    MS(t_[:], 1.0, [b_])
                obi = [0]

                def next_ob():
                    obi[0] += 1
                    return ob[obi[0] % 3]

                for s in range(NS):
                    s0 = s * 512
                    norm_transpose(x_src, s0, xt4, b_xt4, hT, b_hT, rs, b_rs, hn, b_hn, junk, b_junk, b_xsrc,
                                   banks[7], bb[7])
                    LD(posi[:], pos_in[0:1, s0:s0 + 512].partition_broadcast(128), [b_posi])
                    CP(posf[:], posi[:], [b_posi], [b_posf])
                    sincos("cm", posf, b_posf, 0, 128, float(np.pi / 2), Cm, b_Cm, wk, b_wk, wki, b_wki)
                    sincos("sm", posf, b_posf, 0, 128, 0.0, Sm, b_Sm, wk, b_wk, wki, b_wki)
                    sincos("cn", posf, b_posf, 1, 128, float(np.pi / 2), Cn, b_Cn, wk, b_wk, wki, b_wki)
                    sincos("sn", posf, b_posf, 1, 128, 0.0, Sn, b_Sn, wk, b_wk, wki, b_wki)

                    def proj(bank_i, c0, m, rd_extra=()):
                        for kc in range(8):
                            MM(banks[bank_i][0:m, :], WA[:, kc, c0:c0 + m], hT[:, kc, :], kc == 0, kc == 7,
                               [b_WA, b_hT], [bb[bank_i]])

                    for c in range(3):
                        proj(c % 2, c * 128, 128)
                        CP(cqT[:, c, :], banks[c % 2][:, :], [bb[c % 2]], [b_cqT], eng="act" if False else "dve")
                        ACT(sqb[:, c, :], banks[c % 2][:, :], AF.Square, [bb[c % 2]], [b_sqb])
                    for c in range(3):
                        MM(banks[2][:, :], ones_b[:, :], sqb[:, c, :], c == 0, c == 2, [b_ones, b_sqb], [bb[2]])
                    rsqrt_ip(rq[:], b_rq, 1.0 / 384, src=banks[2][:, :], b_src=bb[2])
                    TT(CR[:], Cm[:], rq[:], ALU.mult, [b_Cm, b_rq], [b_CR])
                    TT(SR[:], Sm[:], rq[:], ALU.mult, [b_Sm, b_rq], [b_SR])
                    for h in range(8):
                        bq = 3 + (h % 2) * 2
                        for kc in range(3):
                            MM(banks[bq][0:96, :], WQ[:, kc, h * 96:(h + 1) * 96], cqT[:, kc, :], kc == 0, kc == 2,
                               [b_WQ, b_cqT], [bb[bq]])
                        for kc in range(3):
                            MM(banks[bq + 1][0:32, :], WQ[:, kc, 768 + h * 32:768 + (h + 1) * 32], cqT[:, kc, :],
                               kc == 0, kc == 2, [b_WQ, b_cqT], [bb[bq + 1]])
                        o_t, o_b = next_ob()
                        TT(t1[0:96, :], banks[bq][0:96, :], CR[0:96, :], ALU.mult, [bb[bq], b_CR], [b_t1])
                        TT(t2[0:32, :], banks[bq + 1][0:32, :], SR[0:32, :], ALU.mult, [bb[bq + 1], b_SR], [b_t2])
                        TT(o_t[0:32, :], t1[0:32, :], t2[0:32, :], ALU.add, [b_t1, b_t2], [o_b], eng="pool")
                        CP(o_t[32:64, :], t1[32:64, :], [b_t1], [o_b], eng="pool")
                        CP(o_t[64:96, :], t1[64:96, :], [b_t1], [o_b], eng="pool")
                        ST(qmT[h, :, s0:s0 + 512], o_t[0:96, :], [o_b], [B["qmT"]])
                    for c in range(2):
                        proj(c, 384 + c * 128, 128)
                        CP(ckvT[:, c, :], banks[c][:, :], [bb[c]], [b_ckvT])
                        ACT(sqb[:, c, :], banks[c][:, :], AF.Square, [bb[c]], [b_sqb])
                    for c in range(2):
                        MM(banks[2][:, :], ones_b[:, :], sqb[:, c, :], c == 0, c == 1, [b_ones, b_sqb], [bb[2]])
                    rsqrt_ip(rkv[:], b_rkv, 1.0 / 256, src=banks[2][:, :], b_src=bb[2])
                    for t in range(4):
                        for c in range(2):
                            MM(banks[2][:, t:t + 1], sqb[:, c, t * 128:(t + 1) * 128], ones_b[:, 0:1],
                               (t == 0 and c == 0), c == 1, [b_ones, b_sqb], [bb[2]])
                    rsqrt_ip(rkt[:], b_rkt, 1.0 / 256, src=banks[2][:, 0:4], b_src=bb[2])
                    for g2 in range(4):
                        bk = 3 + (g2 % 2)
                        for kc in range(2):
                            MM(banks[bk][:, :], WKV[:, kc, g2 * 128:(g2 + 1) * 128], ckvT[:, kc, :], kc == 0, kc == 1,
                               [b_WKV, b_ckvT], [bb[bk]])
                        o_t, o_b = next_ob()
                        TT(o_t[:, :], banks[bk][:, :], rkv[:, :], ALU.mult, [bb[bk], b_rkv], [o_b])
                        ST(kmT[2 * g2, 32:96, s0:s0 + 512], o_t[0:64, :], [o_b], [B["kmT"]])
                        ST(kmT[2 * g2 + 1, 32:96, s0:s0 + 512], o_t[64:128, :], [o_b], [B["kmT"]])
                    for t in range(4):
                        bv = 5 + (t % 2)
                        for kc in range(2):
                            MM(banks[bv][:, :], ckvT[:, kc, t * 128:(t + 1) * 128], WKV[:, kc, 512:1024], kc == 0,
                               kc == 1, [b_WKV, b_ckvT], [bb[bv]])
                        v_t, v_b = vt[t % 2]
                        TS(v_t[:, :, 0:64], banks[bv][:, :].rearrange("p (h d) -> p h d", d=64), rkt[:, t:t + 1], None,
                           ALU.mult, None, [bb[bv], b_rkt], [v_b])
                        ST(vm[s0 + t * 128:s0 + (t + 1) * 128, :, :], v_t[:], [v_b], [B["vm"]])
                    proj(0, 640, 32)
                    proj(1, 672, 32)
                    o_t, o_b = next_ob()
                    TT(t1[0:32, :], banks[0][0:32, :], Cm[0:32, :], ALU.mult, [bb[0], b_Cm], [b_t1])
                    TT(t2[0:32, :], banks[1][0:32, :], Sm[0:32, :], ALU.mult, [bb[1], b_Sm], [b_t2])
                    TT(o_t[0:32, :], t1[0:32, :], t2[0:32, :], ALU.add, [b_t1, b_t2], [o_b], eng="pool")
                    for h in range(8):
                        ST(kmT[h, 0:32, s0:s0 + 512], o_t[0:32, :], [o_b], [B["kmT"]])
                    def rope_group(c0, r0, dests):
                        bq = 3
                        proj(bq, c0, 128)
                        proj(bq + 1, r0, 128)
                        o_t, o_b = next_ob()
                        TT(t1[:, :], banks[bq][:, :], Cn[:, :], ALU.mult, [bb[bq], b_Cn], [b_t1])
                        TT(t2[:, :], banks[bq + 1][:, :], Sn[:, :], ALU.mult, [bb[bq + 1], b_Sn], [b_t2])
                        TT(o_t[:, :], t1[:, :], t2[:, :], ALU.add, [b_t1, b_t2], [o_b], eng="pool")
                        for (dst, bname, lo) in dests:
                            ST(dst, o_t[lo:lo + 64, :], [o_b], [B[bname]])
                    for pr in range(4):
                        h0 = 2 * pr
                        rope_group(704 + pr * 128, 1216 + pr * 128,
                                   [(qnT[h0 // 4, :, h0 % 4, s0:s0 + 512], "qnT", 0),
                                    (qnT[(h0 + 1) // 4, :, (h0 + 1) % 4, s0:s0 + 512], "qnT", 64)])
                    for (c0, dst, nm) in ((1728, kcmpT, "kcmpT"), (1984, kslcT, "kslcT"), (2240, kwinT, "kwinT")):
                        rope_group(c0, c0 + 128, [(dst[0, :, s0:s0 + 512], nm, 0), (dst[1, :, s0:s0 + 512], nm, 64)])
                    proj(5, 2496, 128)
                    o_t, o_b = next_ob()
                    CP(o_t[:, :], banks[5][:, :], [bb[5]], [o_b], eng="act" if False else "dve")
                    ST(vcmpT[0, :, s0:s0 + 512], o_t[0:64, :], [o_b], [B["vcmpT"]])
                    ST(vcmpT[1, :, s0:s0 + 512], o_t[64:128, :], [o_b], [B["vcmpT"]])
                    for t in range(4):
                        bv = 5 + (t % 2)
                        for kc in range(8):
                            MM(banks[bv][:, 0:280], hT[:, kc, t * 128:(t + 1) * 128], WA[:, kc, 2624:2904], kc == 0,
                               kc == 7, [b_WA, b_hT], [bb[bv]])
                        v_t, v_b = vs2[t % 2]
                        CP(v_t[:, :, :, 0:64], banks[bv][:, 0:256].rearrange("p (a g d) -> p a g d", a=2, g=2),
                           [bb[bv]], [v_b])
                        g_t, g_b = gt[t % 2]
                        ACT(g_t[:, :], banks[bv][:, 256:280], AF.Sigmoid, [bb[bv]], [g_b])
                        r0, r1 = s0 + t * 128, s0 + (t + 1) * 128
                        ST(vslc[r0:r1, :, :], v_t[:, 0, :, :], [v_b], [B["vslc"]])
                        ST(vwin[r0:r1, :, :], v_t[:, 1, :, :], [v_b], [B["vwin"]])
                        ST(gat[r0:r1, :], g_t[:, :], [g_b], [B["gat"]])
                cx.barrier()
                cx.emit()
            PH[0] += 1
            if stop_after is not None and PH[0] >= stop_after:
                break

            with ExitStack() as st:
                tri, b_tri = sbt(st, "tri", [128, 128], BF16)
                trif, b_trif = sbt(st, "trif", [128, 128], F32)
                LD(trif[:], c_tri[:, :], [b_trif])
                CP(tri[:], trif[:], [b_trif], [b_tri])
                KT, b_KT = sbt(st, "KT", [96, 4, S], BF16)
                VV, b_VV = sbt(st, "VV", [128, NT, 4, 65], BF16)
                QT = [sbt(st, "QT%d" % i, [96, 4, 128], BF16) for i in range(2)]
                PT = [sbt(st, "PT%d" % i, [128, 512], BF16) for i in range(3)]
                rec, b_rec = sbt(st, "rec", [128, 4], F32)
                yo = [sbt(st, "yo%d" % i, [128, 4, 64], BF16) for i in range(2)]
                scale_m = float(96 ** -0.5)
                for hg in range(2):
                    for h in range(4):
                        LD(KT[:, h, :], kmT[hg * 4 + h, :, :], [b_KT], reads=[B["kmT"]])
                    for jt in range(NT):
                        LD(VV[:, jt, :, :], vm[jt * 128:(jt + 1) * 128, hg * 4:(hg + 1) * 4, :], [b_VV],
                           reads=[B["vm"]])
                    cnt = 0
                    for i in range(NT):
                        q_t, q_b = QT[i % 2]
                        for h in range(4):
                            LD(q_t[:, h, :], qmT[hg * 4 + h, :, i * 128:(i + 1) * 128], [q_b], reads=[B["qmT"]])
                        ab = 4 + (i % 2)
                        for j in range(i + 1):
                            sb_i = cnt % 3
                            p_t, p_b = PT[cnt % 3]
                            cnt += 1
                            for h in range(4):
                                MM(banks[sb_i][:, h * 128:(h + 1) * 128], KT[:, h, j * 128:(j + 1) * 128],
                                   q_t[:, h, :], h == 0, h == 3, [b_KT, q_b], [bb[sb_i]])
                            ACT(p_t[:, :], banks[sb_i][:, :], AF.Exp, [bb[sb_i]], [p_b], scale=scale_m)
                            if j == i:
                                TT(p_t[:, :].rearrange("p (h q) -> p h q", h=4),
                                   p_t[:, :].rearrange("p (h q) -> p h q", h=4), bcast_mid(tri[:, :], 4), ALU.mult,
                                   [p_b, b_tri], [p_b], eng="pool")
                            for h in range(4):
                                MM(banks[ab][:, h * 65:(h + 1) * 65], p_t[:, h * 128:(h + 1) * 128], VV[:, j, h, :],
                                   (j == 0 and h == 0), j == i, [p_b, b_VV], [bb[ab]])
                        acc = banks[ab][:, 0:260].rearrange("p (h c) -> p h c", c=65)
                        TS(rec[:, :], acc[:, :, 64], 1e-30, None, ALU.max, None, [bb[ab]], [b_rec])
                        cx.op("dve", lambda e: e.reciprocal(out=rec[:, :], in_=rec[:, :]), [b_rec], [b_rec])
                        y_t, y_b = yo[i % 2]
                        TT(y_t[:, :, :], acc[:, :, 0:64], bcast_last(rec[:, :], 64), ALU.mult, [bb[ab], b_rec], [y_b])
                        ST(ymla[i * 128:(i + 1) * 128, hg * 256:(hg + 1) * 256],
                           y_t[:, :, :].rearrange("p h d -> p (h d)"), [y_b], [B["ymla"]])
                cx.barrier()
                cx.emit()
            PH[0] += 1
            if stop_after is not None and PH[0] >= stop_after:
                break

            with ExitStack() as st:
                cst, b_cst = sbt(st, "cst", [128, 16 * 128], F32)
                tri, b_tri = sbt(st, "tri", [128, 128], BF16)
                wlow, b_wlow = sbt(st, "wlow", [128, 128], BF16)
                maskc, b_maskc = sbt(st, "maskc", [128, 16, 128], BF16)
                emat, b_emat = sbt(st, "emat", [128, NT, 128], BF16)
                LD(cst[:, 0:128], c_tri[:, :], [b_cst])
                CP(tri[:], cst[:, 0:128], [b_cst], [b_tri])
                LD(cst[:, 0:128], c_wlow[:, :], [b_cst])
                CP(wlow[:], cst[:, 0:128], [b_cst], [b_wlow])
                LD(cst[:, :].rearrange("p (o q) -> p o q", o=16), c_maskc[:, :, :], [b_cst])
                CP(maskc[:], cst[:, :].rearrange("p (o q) -> p o q", o=16), [b_cst], [b_maskc])
                for j0 in range(0, NT, 16):
                    jn = min(16, NT - j0)
                    LD(cst[:, 0:jn * 128].rearrange("p (o q) -> p o q", o=jn), c_emat[:, j0:j0 + jn, :], [b_cst])
                    CP(emat[:, j0:j0 + jn, :], cst[:, 0:jn * 128].rearrange("p (o q) -> p o q", o=jn), [b_cst],
                       [b_emat])
                wck, b_wck = sbt(st, "wck", [64, 32, 64], BF16)
                wcv, b_wcv = sbt(st, "wcv", [64, 32, 64], BF16)
                wcs, b_wcs = sbt(st, "wcs", [64, 32, 64], F32)
                LD(wcs[:], W["nsa_cmp_w_k"][l].rearrange("(r d) e -> d r e", d=64), [b_wcs])
                CP(wck[:], wcs[:], [b_wcs], [b_wck])
                LD(wcs[:], W["nsa_cmp_w_v"][l].rearrange("(r d) e -> d r e", d=64), [b_wcs])
                CP(wcv[:], wcs[:], [b_wcs], [b_wcv])
                pkv, b_pkv = sbt(st, "pkv", [32, 2, 64], F32)
                LD(pkv[:, 0, :], W["nsa_cmp_pos_k"][l], [b_pkv])
                LD(pkv[:, 1, :], W["nsa_cmp_pos_v"][l], [b_pkv])
                posT, b_posT = sbt(st, "posT", [64, 2, 32], BF16)
                for a in range(2):
                    TR(banks[0][0:64, a * 32:(a + 1) * 32], pkv[:, a, :], identf[0:32, 0:32], [b_pkv, b_identf],
                       [bb[0]])
                CP(posT[:, :, :], banks[0][0:64, 0:64].rearrange("p (a r) -> p a r", a=2), [bb[0]], [b_posT])
                biask, b_biask = sbt(st, "biask", [64, 1], F32)
                biasv, b_biasv = sbt(st, "biasv", [1, 64], BF16)
                for r in range(32):
                    MM(banks[1][0:64, 0:1], wck[:, r, :], posT[:, 0, r:r + 1], r == 0, r == 31, [b_wck, b_posT],
                       [bb[1]])
                CP(biask[:, :], banks[1][0:64, 0:1], [bb[1]], [b_biask])
                for r in range(32):
                    MM(banks[2][0:1, 0:64], posT[:, 1, r:r + 1], wcv[:, r, :], r == 0, r == 31, [b_wcv, b_posT],
                       [bb[2]])
                CP(biasv[:, :], banks[2][0:1, 0:64], [bb[2]], [b_biasv])
                KS, b_KS = sbt(st, "KS", [64, 2, S], BF16)
                KW, b_KW = sbt(st, "KW", [64, 2, S], BF16)
                kcT, b_kcT = sbt(st, "kcT", [64, 2, 512], BF16)
                VC, b_VC = sbt(st, "VC", [128, 4, 2, CW], BF16)
                MS(kcT[:], 0.0, [b_kcT])
                MS(VC[:], 0.0, [b_VC])
                for g in range(2):
                    LD(KS[:, g, :], kcmpT[g, :, :], [b_KS], reads=[B["kcmpT"]])
                    LD(KW[:, g, :], vcmpT[g, :, :], [b_KW], reads=[B["vcmpT"]])
                LD(cst[:, 0:512].rearrange("p (t m) -> p t m", t=4), c_ovl[:, :, :], [b_cst])
                for g in range(2):
                    MS(VC[:, :, g, 64:65], 1.0, [b_VC])
                    CP(VC[:, :, g, 65:CW], cst[:, 0:512].rearrange("p (t m) -> p t m", t=4), [b_cst], [b_VC])
                for g in range(2):
                    n0 = 0
                    while n0 < NCMP:
                        nn = min(512, NCMP - n0)
                        for r in range(32):
                            src = KS[:, g, r + 16 * n0:r + 16 * n0 + 16 * (nn - 1) + 1]
                            rhs = bass.AP(src.tensor, src.offset, [list(src.ap[0]), [16, nn]])
                            MM(banks[3][0:64, 0:nn], wck[:, r, :], rhs, r == 0, r == 31, [b_wck, b_KS], [bb[3]])
                        ACT(kcT[:, g, n0:n0 + nn], banks[3][0:64, 0:nn], AF.Identity, [bb[3], b_biask], [b_kcT],
                            bias=biask[:, 0:1], scale=1.0)
                        n0 += nn
                    for jt in range(NCT):
                        nn = min(128, NCMP - jt * 128)
                        for r in range(32):
                            o = r + 16 * jt * 128
                            src = KW[:, g, o:o + 16 * (nn - 1) + 1]
                            lhsT = bass.AP(src.tensor, src.offset, [list(src.ap[0]), [16, nn]])
                            MM(banks[4][0:nn, 0:64], lhsT, wcv[:, r, :], r == 0, False, [b_wcv, b_KW], [bb[4]])
                        MM(banks[4][0:nn, 0:64], ones_b[0:1, 0:nn], biasv[0:1, :], False, True, [b_ones, b_biasv],
                           [bb[4]])
                        CP(VC[0:nn, jt, g, 0:64], banks[4][0:nn, 0:64], [bb[4]], [b_VC])
                VS, b_VS = sbt(st, "VS", [128, NT, 2, 65], BF16)
                VW, b_VW = sbt(st, "VW", [128, NT, 2, 65], BF16)
                for g in range(2):
                    LD(KS[:, g, :], kslcT[g, :, :], [b_KS], reads=[B["kslcT"]])
                    LD(KW[:, g, :], kwinT[g, :, :], [b_KW], reads=[B["kwinT"]])
                for jt in range(NT):
                    LD(VS[:, jt, :, :], vslc[jt * 128:(jt + 1) * 128, :, :], [b_VS], reads=[B["vslc"]])
                    LD(VW[:, jt, :, :], vwin[jt * 128:(jt + 1) * 128, :, :], [b_VW], reads=[B["vwin"]])
                QN = [sbt(st, "QN%d" % i, [64, 2, 4, 128], BF16) for i in range(2)]
                GA = [sbt(st, "GA%d" % i, [128, 24], F32) for i in range(2)]
                AM = [sbt(st, "AM%d" % i, [128, 128], F32) for i in range(2)]
                PT = [sbt(st, "PTn%d" % i, [128, 512], BF16) for i in range(3)]
                imp, b_imp = sbt(st, "imp", [128, 128], F32)
                impw, b_impw = sbt(st, "impw", [128, 128], F32)
                top16, b_top16 = sbt(st, "top16", [128, 16], F32)
                negm, b_negm = sbt(st, "negm", [128, 128], BF16)
                negT, b_negT = sbt(st, "negT", [128, 128], BF16)
                rec, b_rec = sbt(st, "recn", [128, 4], F32)
                wgt, b_wgt = sbt(st, "wgt", [128, 4], F32)
                yacc, b_yacc = sbt(st, "yacc", [128, 8, 64], F32)
                ytmp, b_ytmp = sbt(st, "ytmp", [128, 4, 64], F32)
                yo = [sbt(st, "yon%d" % i, [128, 512], BF16) for i in range(2)]
                scale_n = 0.125
                cnt = 0
                for i in range(NT):
                    q_t, q_b = QN[i % 2]
                    g_t, g_b = GA[i % 2]
                    a_t, a_b = AM[i % 2]
                    for g in range(2):
                        LD(q_t[:, g, :, :], qnT[g, :, :, i * 128:(i + 1) * 128], [q_b], reads=[B["qnT"]])
                    LD(g_t[:, :], gat[i * 128:(i + 1) * 128, :], [g_b], reads=[B["gat"]])
                    LD(a_t[:, :], c_amask[i, :, :], [a_b])
                    gv = g_t[:, :].rearrange("p (h b) -> p h b", b=3)
                    for g in range(2):
                        qrhs = q_t[:, g, :, :]

                        def branch_out(acc, br, first):
                            TS(rec[:, :], acc[:, :, 64], 1e-30, None, ALU.max, None, [acc_b], [b_rec])
                            cx.op("dve", lambda e: e.reciprocal(out=rec[:, :], in_=rec[:, :]), [b_rec], [b_rec])
                            TT(wgt[:, :], rec[:, :], gv[:, g * 4:(g + 1) * 4, br], ALU.mult, [b_rec, g_b], [b_wgt])
                            if first:
                                TT(yacc[:, g * 4:(g + 1) * 4, :], acc[:, :, 0:64], bcast_last(wgt[:, :], 64), ALU.mult,
                                   [acc_b, b_wgt], [b_yacc])
                            else:
                                TT(ytmp[:, :, :], acc[:, :, 0:64], bcast_last(wgt[:, :], 64), ALU.mult,
                                   [acc_b, b_wgt], [b_ytmp])
                                TT(yacc[:, g * 4:(g + 1) * 4, :], yacc[:, g * 4:(g + 1) * 4, :], ytmp[:, :, :], ALU.add,
                                   [b_yacc, b_ytmp], [b_yacc], eng="pool")

                        jl = min(i // 16, NCT - 1)
                        for jt in range(jl + 1):
                            sb_i = cnt % 3
                            p_t, p_b = PT[cnt % 3]
                            cnt += 1
                            MM(banks[sb_i][:, :].rearrange("p (h q) -> p h q", h=4), kcT[:, g, jt * 128:(jt + 1) * 128],
                               qrhs, True, True, [b_kcT, q_b], [bb[sb_i]])
                            ACT(p_t[:, :], banks[sb_i][:, :], AF.Exp, [bb[sb_i]], [p_b], scale=scale_n)
                            if jt == i // 16:
                                TT(p_t[:, :].rearrange("p (h q) -> p h q", h=4),
                                   p_t[:, :].rearrange("p (h q) -> p h q", h=4), bcast_mid(maskc[:, i % 16, :], 4),
                                   ALU.mult, [p_b, b_maskc], [p_b], eng="pool")
                            for h in range(4):
                                bk = 3 + h // 2
                                MM(banks[bk][:, (h % 2) * CW:(h % 2 + 1) * CW], p_t[:, h * 128:(h + 1) * 128],
                                   VC[:, jt, g, :], (jt == 0 and h % 2 == 0), jt == jl, [p_b, b_VC], [bb[bk]])
                        for h in range(4):
                            bk = 3 + h // 2
                            a0 = (h % 2) * CW
                            TS(rec[:, h:h + 1], banks[bk][:, a0 + 64:a0 + 65], 1e-30, None, ALU.max, None, [bb[bk]],
                               [b_rec])
                        cx.op("dve", lambda e: e.reciprocal(out=rec[:, :], in_=rec[:, :]), [b_rec], [b_rec])
                        for h in range(4):
                            bk = 3 + h // 2
                            a0 = (h % 2) * CW
                            if h == 0:
                                TS(imp[:, :], banks[bk][:, a0 + 65:a0 + CW], rec[:, 0:1], None, ALU.mult, None,
                                   [bb[bk], b_rec], [b_imp])
                            else:
                                STT(imp[:, :], banks[bk][:, a0 + 65:a0 + CW], rec[:, h:h + 1], imp[:, :], ALU.mult,
                                    ALU.add, [bb[bk], b_rec, b_imp], [b_imp])
                        TT(wgt[:, :], rec[:, :], gv[:, g * 4:(g + 1) * 4, 0], ALU.mult, [b_rec, g_b], [b_wgt])
                        for h in range(4):
                            bk = 3 + h // 2
                            a0 = (h % 2) * CW
                            TS(yacc[:, g * 4 + h, :], banks[bk][:, a0:a0 + 64], wgt[:, h:h + 1], None, ALU.mult, None,
                               [bb[bk], b_wgt], [b_yacc])
                        TT(imp[:, :], imp[:, :], a_t[:, :], ALU.add, [b_imp, a_b], [b_imp])
                        cx.op("dve", lambda e: e.max(out=top16[:, 0:8], in_=imp[:, :]), [b_imp], [b_top16])
                        cx.op("dve", lambda e: e.match_replace(out=impw[:, :], in_to_replace=top16[:, 0:8],
                                                               in_values=imp[:, :], imm_value=-3.0),
                              [b_imp, b_top16], [b_impw])
                        cx.op("dve", lambda e: e.max(out=top16[:, 8:16], in_=impw[:, :]), [b_impw], [b_top16])
                        TS(impw[:, :], imp[:, :], top16[:, 15:16], -1.0, ALU.is_ge, ALU.add, [b_imp, b_top16],
                           [b_impw])
                        TS(negm[:, :], impw[:, :], -NEG, None, ALU.mult, None, [b_impw], [b_negm])
                        pb = banks[7][:, :].bitcast(BF16)
                        TR(pb[:, 0:128], negm[:, :], identb[:, :], [b_negm, b_identb], [bb[7]])
                        CP(negT[:, :], pb[:, 0:128], [bb[7]], [b_negT])
                        acc_b = bb[5]
                        for j in range(i + 1):
                            sb_i = cnt % 3
                            p_t, p_b = PT[cnt % 3]
                            cnt += 1
                            MM(banks[sb_i][:, :].rearrange("p (h q) -> p h q", h=4), KS[:, g, j * 128:(j + 1) * 128],
                               qrhs, True, False, [b_KS, q_b], [bb[sb_i]])
                            MM(banks[sb_i][:, :].rearrange("p (h q) -> p h q", h=4), emat[:, j, :],
                               bcast_mid(negT[:, :], 4), False, True, [b_emat, b_negT], [bb[sb_i]])
                            ACT(p_t[:, :], banks[sb_i][:, :], AF.Exp, [bb[sb_i]], [p_b], scale=scale_n)
                            if j == i:
                                TT(p_t[:, :].rearrange("p (h q) -> p h q", h=4),
                                   p_t[:, :].rearrange("p (h q) -> p h q", h=4), bcast_mid(tri[:, :], 4), ALU.mult,
                                   [p_b, b_tri], [p_b], eng="pool")
                            for h in range(4):
                                MM(banks[5][:, h * 65:(h + 1) * 65], p_t[:, h * 128:(h + 1) * 128], VS[:, j, g, :],
                                   (j == 0 and h == 0), j == i, [p_b, b_VS], [bb[5]])
                        branch_out(banks[5][:, 0:260].rearrange("p (h c) -> p h c", c=65), 1, False)
                        acc_b = bb[6]
                        j_lo = max(0, i - 4)
                        for j in range(j_lo, i + 1):
                            sb_i = cnt % 3
                            p_t, p_b = PT[cnt % 3]
                            cnt += 1
                            MM(banks[sb_i][:, :].rearrange("p (h q) -> p h q", h=4), KW[:, g, j * 128:(j + 1) * 128],
                               qrhs, True, True, [b_KW, q_b], [bb[sb_i]])
                            ACT(p_t[:, :], banks[sb_i][:, :], AF.Exp, [bb[sb_i]], [p_b], scale=scale_n)
                            if j == i:
                                TT(p_t[:, :].rearrange("p (h q) -> p h q", h=4),
                                   p_t[:, :].rearrange("p (h q) -> p h q", h=4), bcast_mid(tri[:, :], 4), ALU.mult,
                                   [p_b, b_tri], [p_b], eng="pool")
                            if j == i - 4:
                                TT(p_t[:, :].rearrange("p (h q) -> p h q", h=4),
                                   p_t[:, :].rearrange("p (h q) -> p h q", h=4), bcast_mid(wlow[:, :], 4), ALU.mult,
                                   [p_b, b_wlow], [p_b], eng="pool")
                            for h in range(4):
                                MM(banks[6][:, h * 65:(h + 1) * 65], p_t[:, h * 128:(h + 1) * 128], VW[:, j, g, :],
                                   (j == j_lo and h == 0), j == i, [p_b, b_VW], [bb[6]])
                        branch_out(banks[6][:, 0:260].rearrange("p (h c) -> p h c", c=65), 2, False)
                    y_t, y_b = yo[i % 2]
                    CP(y_t[:, :], yacc[:, :, :].rearrange("p h d -> p (h d)"), [b_yacc], [y_b], eng="act" if False else "dve")
                    ST(ynsa[i * 128:(i + 1) * 128, :], y_t[:, :], [y_b], [B["ynsa"]])
                cx.barrier()
                cx.emit()
            PH[0] += 1
            if stop_after is not None and PH[0] >= stop_after:
                break

            with ExitStack() as st:
                WB, b_WB = sbt(st, "WB", [128, 12, D], BF16)
                WO, b_WO = sbt(st, "WO", [128, 8, D], BF16)
                wst, b_wst = sbt(st, "wstT", [128, D], F32)
                for bi, nm in enumerate(("w_branch_conv", "w_branch_mla", "w_branch_nsa")):
                    for kc in range(4):
                        LD(wst[:], W[nm][l][kc * 128:(kc + 1) * 128, :], [b_wst])
                        CP(WB[:, bi * 4 + kc, :], wst[:], [b_wst], [b_WB])
                for kc in range(8):
                    LD(wst[:], W["w_out"][l][kc * 128:(kc + 1) * 128, :], [b_wst])
                    CP(WO[:, kc, :], wst[:], [b_wst], [b_WO])
                gpost, b_gpost = sbt(st, "gpost", [128, D], F32)
                gfpost, b_gfpost = sbt(st, "gfpost", [128, D], F32)
                LD(gpost[:], W["norm_mix_post"][l:l + 1, :].partition_broadcast(128), [b_gpost])
                LD(gfpost[:], W["norm_ffn_post"][l:l + 1, :].partition_broadcast(128), [b_gfpost])
                cw, b_cw = sbt(st, "cw", [128, 4, 3], F32)
                fw, b_fw = sbt(st, "fw", [128, 22, 3], F32)
                fb, b_fb = sbt(st, "fb", [128, 22], F32)
                for k in range(3):
                    for c in range(4):
                        LD(cw[:, c, k:k + 1], W["conv_w"][l][k, c * 128:(c + 1) * 128].rearrange("(p o) -> p o", o=1),
                           [b_cw])
                    for c in range(22):
                        LD(fw[:, c, k:k + 1],
                           W["ffn_conv_w"][l][k, c * 128:(c + 1) * 128].rearrange("(p o) -> p o", o=1), [b_fw])
                for c in range(22):
                    LD(fb[:, c:c + 1], W["ffn_conv_b"][l][c * 128:(c + 1) * 128].rearrange("(p o) -> p o", o=1), [b_fb])
                xt4, b_xt4 = sbt(st, "xt4T", [128, 4, D], F32)
                hT, b_hT = sbt(st, "hTT", [128, 8, 512], BF16)
                hn, b_hn = sbt(st, "hnT", [128, D], BF16)
                junk, b_junk = sbt(st, "junkT", [128, D], BF16)
                rs, b_rs = sbt(st, "rsT", [128, 8], F32)
                WS = [sbt(st, "WS%d" % i, [128, 8, 256], BF16) for i in range(3)]
                WD = [sbt(st, "WD%d" % i, [128, D], BF16) for i in range(3)]
                cin, b_cin = sbt(st, "cin", [128, 12, 512], BF16)
                uu, b_uu = sbt(st, "uu", [128, 4, 514], F32)
                cacc, b_cacc = sbt(st, "cacc", [128, 512], F32)
                ycT, b_ycT = sbt(st, "ycT", [128, 4, 512], BF16)
                ymT, b_ymT = sbt(st, "ymT", [128, 4, 512], BF16)
                ynT, b_ynT = sbt(st, "ynT", [128, 4, 512], BF16)
                ytk, b_ytk = sbt(st, "ytk", [128, 512], BF16)
                gsb, b_gsb = sbt(st, "gsb", [128, 3, 512], F32)
                mrg, b_mrg = sbt(st, "mrg", [128, 512], F32)
                mT, b_mT = sbt(st, "mT", [128, 8, 512], BF16)
                aa, b_aa = sbt(st, "aa", [128, 514], F32)
                halo, b_halo = sbt(st, "halo", [128, 22, 2], F32)
                z1, b_z1 = sbt(st, "z1", [128, 512], F32)
                z2, b_z2 = sbt(st, "z2", [128, 512], F32)
                gT, b_gT = sbt(st, "gT", [128, 22, 512], BF16)
                xo, b_xo = sbt(st, "xo", [128, D], F32)
                MS(uu[:], 0.0, [b_uu])
                MS(halo[:], 0.0, [b_halo])
                wsi = [0]

                def load_ws(src, c0):
                    wsi[0] += 1
                    w_t, w_b = WS[wsi[0] % 3]
                    LD(w_t[:, :, :], src[:, c0:c0 + 256].rearrange("(k p) c -> p k c", p=128), [w_b],
                       reads=[B["wgin_b"], B["wup_b"]])
                    return w_t, w_b

                def post_norm_residual(t, gtile, b_gt, first_bank):
                    MS(rs[:, 4 + t:5 + t], 0.0, [b_rs], eng="dve")
                    MS(rs[:, 0:1], 0.0, [b_rs], eng="dve")
                    ACT(junk[:, 0:512], banks[first_bank][:, :], AF.Square, [bb[first_bank]], [b_junk, b_rs],
                        accum_out=rs[:, 4 + t:5 + t])
                    ACT(junk[:, 512:1024], banks[first_bank + 1][:, :], AF.Square, [bb[first_bank + 1]],
                        [b_junk, b_rs], accum_out=rs[:, 0:1])
                    TT(rs[:, 4 + t:5 + t], rs[:, 4 + t:5 + t], rs[:, 0:1], ALU.add, [b_rs], [b_rs])
                    rsqrt_ip(rs[:, 4 + t:5 + t], b_rs, 1.0 / D)
                    for hf in range(2):
                        STT(xo[:, hf * 512:(hf + 1) * 512], banks[first_bank + hf][:, :], rs[:, 4 + t:5 + t],
                            gtile[:, hf * 512:(hf + 1) * 512], ALU.mult, ALU.mult,
                            [bb[first_bank + hf], b_rs, b_gt], [b_xo])
                    TT(xt4[:, t, :], xt4[:, t, :], xo[:, :], ALU.add, [b_xt4, b_xo], [b_xt4], eng="pool")

                for s in range(NS):
                    s0 = s * 512
                    norm_transpose(x_src, s0, xt4, b_xt4, hT, b_hT, rs, b_rs, hn, b_hn, junk, b_junk, b_xsrc,
                                   banks[7], bb[7])
                    for (ysrc, bname, yT, b_yT) in ((ymla, "ymla", ymT, b_ymT), (ynsa, "ynsa", ynT, b_ynT)):
                        for t in range(4):
                            LD(ytk[:, :], ysrc[s0 + t * 128:s0 + (t + 1) * 128, :], [b_ytk], reads=[B[bname]])
                            pb = banks[7][:, :].bitcast(BF16)
                            for k in range(4):
                                TR(pb[:, k * 128:(k + 1) * 128], ytk[:, k * 128:(k + 1) * 128], identb[:],
                                   [b_ytk, b_identb], [bb[7]])
                            CP(yT[:, :, t * 128:(t + 1) * 128], pb[:, 0:512].rearrange("p (k t) -> p k t", k=4),
                               [bb[7]], [b_yT])
                    for c2 in range(6):
                        w_t, w_b = load_ws(wgin_b, 3072 + c2 * 256)
                        for c in range(2):
                            bk = c % 2
                            for kc in range(8):
                                MM(banks[bk][:, :], w_t[:, kc, c * 128:(c + 1) * 128], hT[:, kc, :], kc == 0, kc == 7,
                                   [w_b, b_hT], [bb[bk]])
                            CP(cin[:, c2 * 2 + c, :], banks[bk][:, :], [bb[bk]], [b_cin])
                    for c in range(4):
                        CP(uu[:, c, 0:2], uu[:, c, 512:514], [b_uu], [b_uu])
                    for c in range(4):
                        TT(uu[:, c, 2:514], cin[:, 4 + c, :], cin[:, 8 + c, :], ALU.mult, [b_cin], [b_uu])
                        TS(cacc[:, :], uu[:, c, 0:512], cw[:, c, 0:1], None, ALU.mult, None, [b_uu, b_cw], [b_cacc])
                        STT(cacc[:, :], uu[:, c, 1:513], cw[:, c, 1:2], cacc[:, :], ALU.mult, ALU.add,
                            [b_uu, b_cw, b_cacc], [b_cacc])
                        STT(cacc[:, :], uu[:, c, 2:514], cw[:, c, 2:3], cacc[:, :], ALU.mult, ALU.add,
                            [b_uu, b_cw, b_cacc], [b_cacc])
                        TT(ycT[:, c, :], cacc[:, :], cin[:, c, :], ALU.mult, [b_cacc, b_cin], [b_ycT])
                    for n2 in range(4):
                        wts = [load_ws(wgin_b, bi * 1024 + n2 * 256) for bi in range(3)]
                        for nn in range(2):
                            n = n2 * 2 + nn
                            for bi in range(3):
                                w_t, w_b = wts[bi]
                                for kc in range(8):
                                    MM(banks[bi][:, :], w_t[:, kc, nn * 128:(nn + 1) * 128], hT[:, kc, :], kc == 0,
                                       kc == 7, [w_b, b_hT], [bb[bi]])
                                ACT(gsb[:, bi, :], banks[bi][:, :], AF.Sigmoid, [bb[bi]], [b_gsb])
                            for bi, (yT, b_yT) in enumerate(((ycT, b_ycT), (ymT, b_ymT), (ynT, b_ynT))):
                                for kc in range(4):
                                    MM(banks[3 + bi][:, :], WB[:, bi * 4 + kc, n * 128:(n + 1) * 128], yT[:, kc, :],
                                       kc == 0, kc == 3, [b_WB, b_yT], [bb[3 + bi]])
                            TT(mrg[:, :], banks[3][:, :], gsb[:, 0, :], ALU.mult, [bb[3], b_gsb], [b_mrg])
                            TT(z1[:, :], banks[4][:, :], gsb[:, 1, :], ALU.mult, [bb[4], b_gsb], [b_z1])
                            TT(z2[:, :], banks[5][:, :], gsb[:, 2, :], ALU.mult, [bb[5], b_gsb], [b_z2])
                            TT(mrg[:, :], mrg[:, :], z1[:, :], ALU.add, [b_mrg, b_z1], [b_mrg], eng="pool")
                            TT(mT[:, n, :], mrg[:, :], z2[:, :], ALU.add, [b_mrg, b_z2], [b_mT], eng="pool")
                    for t in range(4):
                        for hf in range(2):
                            for kc in range(8):
                                MM(banks[hf][:, :], mT[:, kc, t * 128:(t + 1) * 128], WO[:, kc, hf * 512:(hf + 1) * 512],
                                   kc == 0, kc == 7, [b_mT, b_WO], [bb[hf]])
                        post_norm_residual(t, gpost, b_gpost, 0)
                    for t in range(4):
                        MS(rs[:, t:t + 1], 0.0, [b_rs], eng="dve")
                        rms_tile(xt4[:, t, :], b_xt4, rs, b_rs, t, junk[:], b_junk, D)
                        TS(hn[:], xt4[:, t, :], rs[:, t:t + 1], None, ALU.mult, None, [b_xt4, b_rs], [b_hn])
                        pb = banks[7][:, :].bitcast(BF16)
                        for k in range(8):
                            TR(pb[:, k * 128:(k + 1) * 128], hn[:, k * 128:(k + 1) * 128], identb[:],
                               [b_hn, b_identb], [bb[7]])
                        CP(hT[:, :, t * 128:(t + 1) * 128], pb[:, 0:1024].rearrange("p (k t) -> p k t", k=8),
                           [bb[7]], [b_hT])
                    for c2 in range(11):
                        wa_t, wa_b = load_ws(wup_b, c2 * 256)
                        wb_t, wb_b = load_ws(wup_b, DFF + c2 * 256)
                        for cc in range(2):
                            c = c2 * 2 + cc
                            for kc in range(8):
                                MM(banks[0][:, :], wa_t[:, kc, cc * 128:(cc + 1) * 128], hT[:, kc, :],
                                   kc == 0, kc == 7, [wa_b, b_hT], [bb[0]])
                            for kc in range(8):
                                MM(banks[1][:, :], wb_t[:, kc, cc * 128:(cc + 1) * 128], hT[:, kc, :],
                                   kc == 0, kc == 7, [wb_b, b_hT], [bb[1]])
                            CP(aa[:, 0:2], halo[:, c, :], [b_halo], [b_aa], eng="pool")
                            CP(aa[:, 2:514], banks[0][:, :], [bb[0]], [b_aa], eng="act" if False else "dve")
                            CP(halo[:, c, :], aa[:, 512:514], [b_aa], [b_halo], eng="pool")
                            TS(z1[:, :], aa[:, 0:512], fw[:, c, 0:1], fb[:, c:c + 1], ALU.mult, ALU.add,
                               [b_aa, b_fw, b_fb], [b_z1])
                            STT(z1[:, :], aa[:, 1:513], fw[:, c, 1:2], z1[:, :], ALU.mult, ALU.add,
                                [b_aa, b_fw, b_z1], [b_z1])
                            STT(z1[:, :], aa[:, 2:514], fw[:, c, 2:3], z1[:, :], ALU.mult, ALU.add,
                                [b_aa, b_fw, b_z1], [b_z1])
                            TT(z2[:, :], z1[:, :], z1[:, :], ALU.mult, [b_z1], [b_z2], eng="pool")
                            TS(z2[:, :], z2[:, :], 0.044715, 1.0, ALU.mult, ALU.add, [b_z2], [b_z2], eng="pool")
                            TT(z2[:, :], z2[:, :], z1[:, :], ALU.mult, [b_z2, b_z1], [b_z2], eng="pool")
                            ACT(z2[:, :], z2[:, :], AF.Sigmoid, [b_z2], [b_z2], scale=float(2.0 * np.sqrt(2.0 / np.pi)))
                            TT(z2[:, :], z2[:, :], z1[:, :], ALU.mult, [b_z2, b_z1], [b_z2], eng="pool")
                            TT(gT[:, c, :], z2[:, :], banks[1][:, :], ALU.mult, [b_z2, bb[1]], [b_gT])
                    wdi = 0
                    for tp in range(2):
                        for kc in range(22):
                            w_t, w_b = WD[wdi % 3]
                            wdi += 1
                            LD(w_t[:, :], wdn_b[kc * 128:(kc + 1) * 128, :], [w_b], reads=[B["wdn_b"]])
                            for tt in range(2):
                                t = tp * 2 + tt
                                for hf in range(2):
                                    bk = 2 + tt * 2 + hf
                                    MM(banks[bk][:, :], gT[:, kc, t * 128:(t + 1) * 128],
                                       w_t[:, hf * 512:(hf + 1) * 512], kc == 0, kc == 21, [b_gT, w_b], [bb[bk]])
                        for tt in range(2):
                            t = tp * 2 + tt
                            post_norm_residual(t, gfpost, b_gfpost, 2 + tt * 2)
                            ST(x_dst[s0 + t * 128:s0 + (t + 1) * 128, :], xt4[:, t, :], [b_xt4], [b_xdst])
                cx.barrier()
                cx.emit()
            PH[0] += 1
            if stop_after is not None and PH[0] >= stop_after:
                break
    return nc


_CACHE = {}


def kernel(_stop_after=None, **inputs):
    x = np.asarray(inputs["x"])
    Bn, S, _ = x.shape
    depth = np.asarray(inputs["w_in"]).shape[0]
    key = (S, depth, _stop_after)
    if key not in _CACHE:
        _CACHE[key] = build(S, depth, _stop_after)
    nc = _CACHE[key]
    consts = host_consts(S)
    shared = {}
    for k, v in inputs.items():
        if k in ("x", "positions"):
            continue
        shared[k] = np.ascontiguousarray(np.asarray(v, dtype=np.float32))
    for k, v in consts.items():
        shared["c_" + k] = np.ascontiguousarray(v.astype(np.float32))
    pos = np.asarray(inputs["positions"]).astype(np.int32)
    in_maps = []
    for c in range(8):
        b = c % Bn
        m = dict(shared)
        m["x"] = np.ascontiguousarray(x[b].astype(np.float32))
        m["positions"] = np.ascontiguousarray(pos[b][None, :])
        in_maps.append(m)
    res = run_bass_kernel_spmd(nc, in_maps, core_ids=list(range(8)))
    out = np.stack([np.asarray(res.results[b]["y"]) for b in range(Bn)], axis=0)
    return out.astype(np.float32)
```

```python
from contextlib import ExitStack
import numpy as np
import concourse.bass as bass
import concourse.mybir as mybir
from concourse.bass_utils import run_bass_kernel_spmd

F32 = mybir.dt.float32
BF16 = mybir.dt.bfloat16
I32 = mybir.dt.int32
AF = mybir.ActivationFunctionType
ALU = mybir.AluOpType

D = 1024
DEPTH = 2
THETA = 500000.0
EPS = 1e-6
DFF = 2816
NIN = 6584
NEG = -30000.0


class Buf:
    __slots__ = ("name", "last_w", "readers", "excl")

    def __init__(self, name, excl=False):
        self.name = name
        self.last_w = None
        self.readers = []
        self.excl = excl


class Ctx:
    def __init__(self, nc, stack, n_dma_ring=8):
        self.nc = nc
        self.engs = ("pe", "act", "dve", "pool", "sp")
        self.sems = {}
        self.count = {}
        self.waited = {k: {} for k in self.engs}
        for k in ("pe", "act", "dve", "pool"):
            self.sems[k] = stack.enter_context(nc.semaphore("s_" + k))
            self.count[k] = 0
        self.ring = {}
        self.ring_n = {}
        for q in ("sp", "pool", "act"):
            self.ring[q] = []
            for i in range(n_dma_ring):
                s = stack.enter_context(nc.semaphore("d_%s%d" % (q, i)))
                self.ring[q].append(s)
                self.sems[("d", q, i)] = s
            self.ring_n[q] = 0
        self.prog = {k: [] for k in self.engs}
        self.n_inst = 0
        self.n_wait = 0

    def _wait(self, e, tok):
        key, val = tok
        w = self.waited[e]
        if w.get(key, 0) >= val:
            return
        w[key] = val
        sem = self.sems[key]
        self.prog[e].append(lambda eng, sem=sem, val=val: eng.wait_ge(sem, val))
        self.n_wait += 1

    def _deps(self, e, reads, writes, same_ok):
        for b in reads:
            if b.last_w is not None and not (same_ok and b.last_w[0] == e):
                self._wait(e, b.last_w)
            if b.excl:
                for t in b.readers:
                    if t[0] != e:
                        self._wait(e, t)
        for b in writes:
            if b.last_w is not None and not (same_ok and b.last_w[0] == e):
                self._wait(e, b.last_w)
            for t in b.readers:
                if not (same_ok and t[0] == e):
                    self._wait(e, t)

    def _record(self, tok, reads, writes):
        for b in reads:
            b.readers.append(tok)
            if len(b.readers) > 24:
                d = {}
                for k, v in b.readers:
                    if d.get(k, 0) < v:
                        d[k] = v
                b.readers = list(d.items())
        for b in writes:
            b.last_w = tok
            b.readers = []

    def op(self, e, fn, reads=(), writes=()):
        self._deps(e, reads, writes, same_ok=(e == "pe"))
        self.count[e] += 1
        sem = self.sems[e]
        self.prog[e].append(lambda eng, fn=fn, sem=sem: fn(eng).then_inc(sem, 1))
        tok = (e, self.count[e])
        self._record(tok, reads, writes)
        self.n_inst += 1
        return tok

    def dma(self, q, out, in_, reads=(), writes=(), **kw):
        n = self.ring_n[q]
        R = len(self.ring[q])
        key = ("d", q, n % R)
        prev = n // R
        if prev > 0:
            self._wait(q, (key, 16 * prev))
        self._deps(q, reads, writes, same_ok=False)
        sem = self.sems[key]
        self.prog[q].append(lambda eng, out=out, in_=in_, kw=kw, sem=sem:
                            eng.dma_start(out=out, in_=in_, **kw).then_inc(sem, 16))
        self.ring_n[q] = n + 1
        tok = (key, 16 * (prev + 1))
        self._record(tok, reads, writes)
        self.n_inst += 1
        return tok

    def barrier(self):
        toks = [(k, self.count[k]) for k in ("pe", "act", "dve", "pool") if self.count[k] > 0]
        for q in self.ring:
            n = self.ring_n[q]
            R = len(self.ring[q])
            for slot in range(R):
                cnt = (n - slot + R - 1) // R
                if cnt > 0:
                    toks.append((("d", q, slot), 16 * cnt))
        for e in self.engs:
            for t in toks:
                if t[0] != e:
                    self._wait(e, t)

    def emit(self):
        prog = self.prog
        with self.nc.Block() as block:
            @block.sync
            def _(eng):
                for f in prog["sp"]:
                    f(eng)

            @block.tensor
            def _(eng):
                for f in prog["pe"]:
                    f(eng)

            @block.scalar
            def _(eng):
                for f in prog["act"]:
                    f(eng)

            @block.vector
            def _(eng):
                for f in prog["dve"]:
                    f(eng)

            @block.gpsimd
            def _(eng):
                for f in prog["pool"]:
                    f(eng)
        self.prog = {k: [] for k in self.engs}


def bcast_mid(ap, n):
    a = ap.ap
    return bass.AP(ap.tensor, ap.offset, [list(a[0]), [0, n]] + [list(x) for x in a[1:]])


def bcast_last(ap, n):
    a = ap.ap
    return bass.AP(ap.tensor, ap.offset, [list(x) for x in a] + [[0, n]])


def host_consts(S):
    NT = S // 128
    NSEL = S // 64
    k = np.arange(128)[:, None]
    q = np.arange(128)[None, :]
    c = {}
    c["ident"] = np.eye(128, dtype=np.float32)
    c["tri"] = (k <= q).astype(np.float32)
    c["wlow"] = (k > q).astype(np.float32)
    mc = np.zeros((128, 16, 128), np.float32)
    for o in range(16):
        mc[:, o, :] = (16 * (k - 8 * o) + 31 <= q)
    c["maskc"] = mc
    n = np.arange(512)[:, None]
    m = np.arange(128)[None, :]
    ov = ((n >= 4 * m - 1) & (n <= 4 * m + 3) & (m < NSEL)).astype(np.float32)
    c["ovl"] = ov.reshape(4, 128, 128).transpose(1, 0, 2).copy()
    A = np.zeros((NT, 128, 128), np.float32)
    for i in range(NT):
        t = 128 * i + np.arange(128)[:, None]
        cur = t // 64
        mm = np.arange(128)[None, :]
        forced = (mm == 0) | (mm == cur) | (mm == cur - 1)
        causal = 64 * mm <= t
        A[i] = np.where(forced, 1e6, np.where(causal, 0.0, -1.0))
        A[i][:, NSEL:] = -1.0
    c["amask"] = A
    E = np.zeros((128, NT, 128), np.float32)
    for j in range(NT):
        for kk in range(128):
            mrow = 2 * j + (kk >= 64)
            if mrow < 128:
                E[mrow, j, kk] = 1.0
    c["emat"] = E
    f32 = np.float32
    inv_m = (f32(THETA) ** (-(np.arange(16, dtype=f32) / f32(16)))).astype(f32)
    inv_n = (f32(THETA) ** (-(np.arange(8, dtype=f32) / f32(8)))).astype(f32)
    fm = np.zeros((128, 1), f32)
    fm[0:16, 0] = inv_m
    fm[16:32, 0] = inv_m
    fn = np.zeros((128, 1), f32)
    for base in (0, 64):
        fn[base:base + 8, 0] = inv_n
        fn[base + 8:base + 16, 0] = inv_n
    c["invf"] = np.concatenate([fm, fn], axis=1)
    return c


def build(S, depth=DEPTH, stop_after=None):
    assert S % 512 == 0
    NT = S // 128
    NS = S // 512
    NSEL = S // 64
    NCMP = S // 16 - 1
    NCT = (NCMP + 127) // 128
    CW = 65 + 128
    nc = bass.Bass("TRN2", target_bir_lowering=False)

    def din(name, shape, dt=F32):
        return nc.dram_tensor(name, list(shape), dt, kind="ExternalInput").ap()

    def dscr(name, shape, dt=BF16):
        return nc.dram_tensor(name, list(shape), dt, kind="Internal").ap()

    x_in = din("x", [S, D])
    pos_in = din("positions", [1, S], I32)
    W = {}
    W["norm_mix_pre"] = din("norm_mix_pre", [depth, D])
    W["norm_mix_post"] = din("norm_mix_post", [depth, D])
    W["w_in"] = din("w_in", [depth, D, NIN])
    W["conv_w"] = din("conv_w", [depth, 3, 512])
    W["mla_q_norm"] = din("mla_q_norm", [depth, 384])
    W["mla_w_uq"] = din("mla_w_uq", [depth, 384, 768])
    W["mla_kv_norm"] = din("mla_kv_norm", [depth, 256])
    W["mla_w_ukv"] = din("mla_w_ukv", [depth, 256, 1024])
    W["nsa_cmp_pos_k"] = din("nsa_cmp_pos_k", [depth, 32, 64])
    W["nsa_cmp_pos_v"] = din("nsa_cmp_pos_v", [depth, 32, 64])
    W["nsa_cmp_w_k"] = din("nsa_cmp_w_k", [depth, 2048, 64])
    W["nsa_cmp_w_v"] = din("nsa_cmp_w_v", [depth, 2048, 64])
    W["w_branch_conv"] = din("w_branch_conv", [depth, 512, D])
    W["w_branch_mla"] = din("w_branch_mla", [depth, 512, D])
    W["w_branch_nsa"] = din("w_branch_nsa", [depth, 512, D])
    W["w_out"] = din("w_out", [depth, D, D])
    W["norm_ffn_pre"] = din("norm_ffn_pre", [depth, D])
    W["norm_ffn_post"] = din("norm_ffn_post", [depth, D])
    W["ffn_w_up"] = din("ffn_w_up", [depth, D, 2 * DFF])
    W["ffn_conv_w"] = din("ffn_conv_w", [depth, 3, DFF])
    W["ffn_conv_b"] = din("ffn_conv_b", [depth, DFF])
    W["ffn_w_down"] = din("ffn_w_down", [depth, DFF, D])
    c_ident = din("c_ident", [128, 128])
    c_tri = din("c_tri", [128, 128])
    c_wlow = din("c_wlow", [128, 128])
    c_maskc = din("c_maskc", [128, 16, 128])
    c_ovl = din("c_ovl", [128, 4, 128])
    c_amask = din("c_amask", [NT, 128, 128])
    c_emat = din("c_emat", [128, NT, 128])
    c_invf = din("c_invf", [128, 2])
    y_out = nc.dram_tensor("y", [S, D], F32, kind="ExternalOutput").ap()

    xs = dscr("xs", [S, D], F32)
    qmT = dscr("qmT", [8, 96, S])
    kmT = dscr("kmT", [8, 96, S])
    vm = dscr("vm", [S, 8, 65])
    qnT = dscr("qnT", [2, 64, 4, S])
    kcmpT = dscr("kcmpT", [2, 64, S])
    vcmpT = dscr("vcmpT", [2, 64, S])
    kslcT = dscr("kslcT", [2, 64, S])
    kwinT = dscr("kwinT", [2, 64, S])
    vslc = dscr("vslc", [S, 2, 65])
    vwin = dscr("vwin", [S, 2, 65])
    gat = dscr("gat", [S, 24], F32)
    ymla = dscr("ymla", [S, 512])
    ynsa = dscr("ynsa", [S, 512])
    wgin_b = dscr("wgin_b", [D, 4608])
    wup_b = dscr("wup_b", [D, 2 * DFF])
    wdn_b = dscr("wdn_b", [DFF, D])

    with ExitStack() as top:
        cx = Ctx(nc, top)
        banks = [top.enter_context(nc.psum_tensor("bank%d" % i, [128, 512], F32)) for i in range(8)]
        bb = [Buf("bank%d" % i, excl=True) for i in range(8)]

        uid = [0]

        def sbt(st, name, shape, dt):
            uid[0] += 1
            nm = "%s_u%d" % (name, uid[0])
            return st.enter_context(nc.sbuf_tensor(nm, list(shape), dt)), Buf(nm)

        def ACT(out, in_, func, reads, writes, **kw):
            cx.op("act", lambda e: e.activation(out=out, in_=in_, func=func, **kw), reads, writes)

        def TT(out, in0, in1, op, reads, writes, eng="dve"):
            cx.op(eng, lambda e: e.tensor_tensor(out=out, in0=in0, in1=in1, op=op), reads, writes)

        def TS(out, in0, s1, s2, op0, op1, reads, writes, eng="dve", **kw):
            if s2 is None:
                cx.op(eng, lambda e: e.tensor_scalar(out=out, in0=in0, scalar1=s1, scalar2=None, op0=op0, **kw),
                      reads, writes)
            else:
                cx.op(eng, lambda e: e.tensor_scalar(out=out, in0=in0, scalar1=s1, scalar2=s2, op0=op0, op1=op1, **kw),
                      reads, writes)

        def STT(out, in0, scalar, in1, op0, op1, reads, writes, eng="dve"):
            cx.op(eng, lambda e: e.scalar_tensor_tensor(out=out, in0=in0, scalar=scalar, in1=in1, op0=op0, op1=op1),
                  reads, writes)

        def CP(out, in_, reads, writes, eng="dve"):
            cx.op(eng, lambda e: e.tensor_copy(out=out, in_=in_), reads, writes)

        def MS(ap, val, writes, eng="pool"):
            cx.op(eng, lambda e: e.memset(ap, val), (), writes)

        def MM(out, lhsT, rhs, start, stop, reads, writes):
            cx.op("pe", lambda e: e.matmul(out, lhsT=lhsT, rhs=rhs, start=start, stop=stop, skip_group_check=True),
                  reads, writes)

        def TR(out, in_, ident, reads, writes):
            cx.op("pe", lambda e: e.transpose(out, in_, ident), reads, writes)

        LD = lambda out, in_, writes, reads=(), **kw: cx.dma("sp", out, in_, reads=reads, writes=writes, **kw)
        ST = lambda out, in_, reads, writes=(), **kw: cx.dma("pool", out, in_, reads=reads, writes=writes, **kw)

        B = {n: Buf(n) for n in ("xs", "qmT", "kmT", "vm", "qnT", "kcmpT", "vcmpT", "kslcT", "kwinT", "vslc",
                                 "vwin", "gat", "ymla", "ynsa", "wgin_b", "wup_b", "wdn_b", "y")}

        identf, b_identf = sbt(top, "identf", [128, 128], F32)
        identb, b_identb = sbt(top, "identb", [128, 128], BF16)
        ones_b, b_ones = sbt(top, "ones_b", [128, 128], BF16)
        invf, b_invf = sbt(top, "invf", [128, 2], F32)
        LD(identf[:], c_ident[:, :], [b_identf])
        LD(invf[:], c_invf[:, :], [b_invf])
        CP(identb[:], identf[:], [b_identf], [b_identb])
        MS(ones_b[:], 1.0, [b_ones])
        CONST = [b_identf, b_identb, b_ones, b_invf]

        epst, b_epst = sbt(top, "epst", [128, 1], F32)
        MS(epst[:], EPS, [b_epst])

        def rsqrt_ip(ap, b_ap, inv_n, src=None, b_src=None):
            p = ap.shape[0]
            if src is None:
                src, b_src = ap, b_ap
            ACT(ap, src, AF.Sqrt, [b_src, b_epst], [b_ap], bias=epst[0:p, 0:1], scale=float(inv_n))
            cx.op("dve", lambda e: e.reciprocal(out=ap, in_=ap), [b_ap], [b_ap])

        def vec_col(st, name, src_row, n):
            c = n // 128
            t, b = sbt(st, name, [128, c], F32)
            for k in range(c):
                LD(t[:, k:k + 1], src_row[k * 128:(k + 1) * 128].rearrange("(p o) -> p o", o=1), [b])
            return t, b

        def rms_tile(xt, b_xt, rs, b_rs, col, junk, b_junk, n):
            ACT(junk, xt, AF.Square, [b_xt], [b_junk, b_rs], accum_out=rs[:, col:col + 1])
            rsqrt_ip(rs[:, col:col + 1], b_rs, 1.0 / n)

        def norm_transpose(src, s0, xt4, b_xt4, hT, b_hT, rs, b_rs, hn, b_hn, junk, b_junk, b_src, bank, b_bank):
            for t in range(4):
                LD(xt4[:, t, :], src[s0 + t * 128:s0 + (t + 1) * 128, :], [b_xt4], reads=[b_src])
            for t in range(4):
                MS(rs[:, t:t + 1], 0.0, [b_rs], eng="dve")
                rms_tile(xt4[:, t, :], b_xt4, rs, b_rs, t, junk[:], b_junk, D)
                TS(hn[:], xt4[:, t, :], rs[:, t:t + 1], None, ALU.mult, None, [b_xt4, b_rs], [b_hn])
                pb = bank[:, :].bitcast(BF16)
                for k in range(8):
                    TR(pb[:, k * 128:(k + 1) * 128], hn[:, k * 128:(k + 1) * 128], identb[:],
                       [b_hn, b_identb], [b_bank])
                CP(hT[:, :, t * 128:(t + 1) * 128], pb[:, 0:1024].rearrange("p (k t) -> p k t", k=8),
                   [b_bank], [b_hT], eng="act" if False else "dve")

        def sincos(st_name, pos_f, b_pos, col, rows, shift, out, b_out, wk, b_wk, wki, b_wki):
            r = slice(0, rows)
            TS(wk[r, 0, :], pos_f[r, :], invf[r, col:col + 1], shift, ALU.mult, ALU.add, [b_pos, b_invf], [b_wk])
            TS(wk[r, 1, :], wk[r, 0, :], float(1.0 / (2 * np.pi)), None, ALU.mult, None, [b_wk], [b_wk])
            CP(wki[r, :], wk[r, 1, :], [b_wk], [b_wki])
            CP(wk[r, 1, :], wki[r, :], [b_wki], [b_wk])
            STT(wk[r, 0, :], wk[r, 1, :], float(-2 * np.pi), wk[r, 0, :], ALU.mult, ALU.add, [b_wk], [b_wk])
            TS(wk[r, 1, :], wk[r, 0, :], float(np.pi), None, ALU.is_gt, None, [b_wk], [b_wk])
            STT(wk[r, 0, :], wk[r, 1, :], float(-2 * np.pi), wk[r, 0, :], ALU.mult, ALU.add, [b_wk], [b_wk])
            TS(wk[r, 1, :], wk[r, 0, :], float(-np.pi), None, ALU.is_lt, None, [b_wk], [b_wk])
            STT(wk[r, 0, :], wk[r, 1, :], float(2 * np.pi), wk[r, 0, :], ALU.mult, ALU.add, [b_wk], [b_wk])
            ACT(out[r, :], wk[r, 0, :], AF.Sin, [b_wk], [b_out])

        PH = [0]
        for l in range(depth):
            x_src, b_xsrc = (x_in, Buf("x_in")) if l == 0 else (xs, B["xs"])
            x_dst, b_xdst = (y_out, B["y"]) if l == depth - 1 else (xs, B["xs"])
            w_in = W["w_in"][l]

            with ExitStack() as st:
                NA = 2904
                WA, b_WA = sbt(st, "WA", [128, 8, NA], BF16)
                WQ, b_WQ = sbt(st, "WQ", [128, 3, 1024], BF16)
                WKV, b_WKV = sbt(st, "WKV", [128, 2, 1024], BF16)
                stage, b_stage = sbt(st, "stage", [128, 1976], F32)
                gpre, b_gpre = vec_col(st, "gpre", W["norm_mix_pre"][l], D)
                gq, b_gq = vec_col(st, "gq", W["mla_q_norm"][l], 384)
                gkv, b_gkv = vec_col(st, "gkv", W["mla_kv_norm"][l], 256)
                ngpre, b_ngpre = sbt(st, "ngpre", [128, 8], F32)
                ngq, b_ngq = sbt(st, "ngq", [128, 3], F32)
                TS(ngpre[:], gpre[:], -1.0, None, ALU.mult, None, [b_gpre], [b_ngpre])
                TS(ngq[:], gq[:], -1.0, None, ALU.mult, None, [b_gq], [b_ngq])
                MS(WA[:, :, 1216:1728], 0.0, [b_WA])
                for c in (1856, 2112, 2368):
                    MS(WA[:, :, c:c + 128], 0.0, [b_WA])

                def SC(out, in_, sc, rd, wr):
                    TS(out, in_, sc, None, ALU.mult, None, rd, wr)

                for kc in range(8):
                    LD(stage[:], w_in[kc * 128:(kc + 1) * 128, 4608:6584], [b_stage])
                    g = gpre[:, kc:kc + 1]
                    ng = ngpre[:, kc:kc + 1]
                    rd = [b_stage, b_gpre, b_ngpre]
                    SC(WA[:, kc, 0:672], stage[:, 0:672], g, rd, [b_WA])
                    SC(WA[:, kc, 672:688], stage[:, 656:672], ng, rd, [b_WA])
                    SC(WA[:, kc, 688:704], stage[:, 640:656], g, rd, [b_WA])
                    SC(WA[:, kc, 704:1216], stage[:, 672:1184], g, rd, [b_WA])
                    sq = stage[:, 672:1184].rearrange("p (h d) -> p h d", d=64)
                    dq = WA[:, kc, 1216:1728].rearrange("p (h d) -> p h d", d=64)
                    SC(dq[:, :, 0:8], sq[:, :, 8:16], ng, rd, [b_WA])
                    SC(dq[:, :, 8:16], sq[:, :, 0:8], g, rd, [b_WA])
                    for (src_o, dst_o) in ((1184, 1728), (1440, 1984), (1696, 2240)):
                        SC(WA[:, kc, dst_o:dst_o + 128], stage[:, src_o:src_o + 128], g, rd, [b_WA])
                        sk = stage[:, src_o:src_o + 128].rearrange("p (h d) -> p h d", d=64)
                        dk = WA[:, kc, dst_o + 128:dst_o + 256].rearrange("p (h d) -> p h d", d=64)
                        SC(dk[:, :, 0:8], sk[:, :, 8:16], ng, rd, [b_WA])
                        SC(dk[:, :, 8:16], sk[:, :, 0:8], g, rd, [b_WA])
                    SC(WA[:, kc, 2496:2624], stage[:, 1312:1440], g, rd, [b_WA])
                    SC(WA[:, kc, 2624:2752], stage[:, 1568:1696], g, rd, [b_WA])
                    SC(WA[:, kc, 2752:2880], stage[:, 1824:1952], g, rd, [b_WA])
                    SC(WA[:, kc, 2880:2904], stage[:, 1952:1976], g, rd, [b_WA])
                for kc in range(3):
                    LD(stage[:, 0:768], W["mla_w_uq"][l][kc * 128:(kc + 1) * 128, :], [b_stage])
                    g = gq[:, kc:kc + 1]
                    ng = ngq[:, kc:kc + 1]
                    rd = [b_stage, b_gq, b_ngq]
                    s3 = stage[:, 0:768].rearrange("p (h d) -> p h d", d=96)
                    d3 = WQ[:, kc, 0:768].rearrange("p (h d) -> p h d", d=96)
                    SC(d3[:, :, 0:32], s3[:, :, 64:96], g, rd, [b_WQ])
                    SC(d3[:, :, 32:96], s3[:, :, 0:64], g, rd, [b_WQ])
                    r3 = WQ[:, kc, 768:1024].rearrange("p (h d) -> p h d", d=32)
                    SC(r3[:, :, 0:16], s3[:, :, 80:96], ng, rd, [b_WQ])
                    SC(r3[:, :, 16:32], s3[:, :, 64:80], g, rd, [b_WQ])
                for kc in range(2):
                    LD(stage[:, 0:1024], W["mla_w_ukv"][l][kc * 128:(kc + 1) * 128, :], [b_stage])
                    g = gkv[:, kc:kc + 1]
                    rd = [b_stage, b_gkv]
                    s3 = stage[:, 0:1024].rearrange("p (h d) -> p h d", d=128)
                    SC(WKV[:, kc, 0:512].rearrange("p (h d) -> p h d", d=64), s3[:, :, 0:64], g, rd, [b_WKV])
                    SC(WKV[:, kc, 512:1024].rearrange("p (h d) -> p h d", d=64), s3[:, :, 64:128], g, rd, [b_WKV])

                gf, b_gf = vec_col(st, "gf", W["norm_ffn_pre"][l], D)
                wst, b_wst = sbt(st, "wst", [128, 5632], F32)
                wsb, b_wsb = sbt(st, "wsb", [128, 5632], BF16)
                for kc in range(8):
                    LD(wst[:, 0:4608], w_in[kc * 128:(kc + 1) * 128, 0:4608], [b_wst])
                    SC(wsb[:, 0:4608], wst[:, 0:4608], gpre[:, kc:kc + 1], [b_wst, b_gpre], [b_wsb])
                    ST(wgin_b[kc * 128:(kc + 1) * 128, :], wsb[:, 0:4608], [b_wsb], [B["wgin_b"]])
                for kc in range(8):
                    LD(wst[:, :], W["ffn_w_up"][l][kc * 128:(kc + 1) * 128, :], [b_wst])
                    SC(wsb[:, :], wst[:, :], gf[:, kc:kc + 1], [b_wst, b_gf], [b_wsb])
                    ST(wup_b[kc * 128:(kc + 1) * 128, :], wsb[:, :], [b_wsb], [B["wup_b"]])
                for kc in range(22):
                    LD(wst[:, 0:1024], W["ffn_w_down"][l][kc * 128:(kc + 1) * 128, :], [b_wst])
                    CP(wsb[:, 0:1024], wst[:, 0:1024], [b_wst], [b_wsb])
                    ST(wdn_b[kc * 128:(kc + 1) * 128, :], wsb[:, 0:1024], [b_wsb], [B["wdn_b"]])

                xt4, b_xt4 = sbt(st, "xt4", [128, 4, D], F32)
                hT, b_hT = sbt(st, "hT", [128, 8, 512], BF16)
                hn, b_hn = sbt(st, "hn", [128, D], BF16)
                junk, b_junk = sbt(st, "junk", [128, D], BF16)
                rs, b_rs = sbt(st, "rs", [128, 8], F32)
                posi, b_posi = sbt(st, "posi", [128, 512], I32)
                posf, b_posf = sbt(st, "posf", [128, 512], F32)
                wk, b_wk = sbt(st, "wk", [128, 2, 512], F32)
                wki, b_wki = sbt(st, "wki", [128, 512], I32)
                Cm, b_Cm = sbt(st, "Cm", [128, 512], F32)
                Sm, b_Sm = sbt(st, "Sm", [128, 512], F32)
                Cn, b_Cn = sbt(st, "Cn", [128, 512], F32)
                Sn, b_Sn = sbt(st, "Sn", [128, 512], F32)
                cqT, b_cqT = sbt(st, "cqT", [128, 3, 512], BF16)
                ckvT, b_ckvT = sbt(st, "ckvT", [128, 2, 512], BF16)
                sqb, b_sqb = sbt(st, "sqb", [128, 3, 512], BF16)
                rq, b_rq = sbt(st, "rq", [128, 512], F32)
                rkv, b_rkv = sbt(st, "rkv", [128, 512], F32)
                rkt, b_rkt = sbt(st, "rkt", [128, 4], F32)
                CR, b_CR = sbt(st, "CR", [128, 512], F32)
                SR, b_SR = sbt(st, "SR", [128, 512], F32)
                t1, b_t1 = sbt(st, "t1", [128, 512], F32)
                t2, b_t2 = sbt(st, "t2", [128, 512], F32)
                ob = [sbt(st, "ob%d" % i, [128, 512], BF16) for i in range(3)]
                vt = [sbt(st, "vt%d" % i, [128, 8, 65], BF16) for i in range(2)]
                vs2 = [sbt(st, "vs2_%d" % i, [128, 2, 2, 65], BF16) for i in range(2)]
                gt = [sbt(st, "gt%d" % i, [128, 24], F32) for i in range(2)]
                for (t_, b_) in vt:
                    MS(t_[:], 1.0, [b_])
                for (t_, b_) in vs2:
                    MS(t_[:], 1.0, [b_])
                obi = [0]

                def next_ob():
                    obi[0] += 1
                    return ob[obi[0] % 3]

                for s in range(NS):
                    s0 = s * 512
                    norm_transpose(x_src, s0, xt4, b_xt4, hT, b_hT, rs, b_rs, hn, b_hn, junk, b_junk, b_xsrc,
                                   banks[7], bb[7])
                    LD(posi[:], pos_in[0:1, s0:s0 + 512].partition_broadcast(128), [b_posi])
                    CP(posf[:], posi[:], [b_posi], [b_posf])
                    sincos("cm", posf, b_posf, 0, 128, float(np.pi / 2), Cm, b_Cm, wk, b_wk, wki, b_wki)
                    sincos("sm", posf, b_posf, 0, 128, 0.0, Sm, b_Sm, wk, b_wk, wki, b_wki)
                    sincos("cn", posf, b_posf, 1, 128, float(np.pi / 2), Cn, b_Cn, wk, b_wk, wki, b_wki)
                    sincos("sn", posf, b_posf, 1, 128, 0.0, Sn, b_Sn, wk, b_wk, wki, b_wki)

                    def proj(bank_i, c0, m, rd_extra=()):
                        for kc in range(8):
                            MM(banks[bank_i][0:m, :], WA[:, kc, c0:c0 + m], hT[:, kc, :], kc == 0, kc == 7,
                               [b_WA, b_hT], [bb[bank_i]])

                    for c in range(3):
                        proj(c % 2, c * 128, 128)
                        CP(cqT[:, c, :], banks[c % 2][:, :], [bb[c % 2]], [b_cqT], eng="act" if False else "dve")
                        ACT(sqb[:, c, :], banks[c % 2][:, :], AF.Square, [bb[c % 2]], [b_sqb])
                    for c in range(3):
                        MM(banks[2][:, :], ones_b[:, :], sqb[:, c, :], c == 0, c == 2, [b_ones, b_sqb], [bb[2]])
                    rsqrt_ip(rq[:], b_rq, 1.0 / 384, src=banks[2][:, :], b_src=bb[2])
                    TT(CR[:], Cm[:], rq[:], ALU.mult, [b_Cm, b_rq], [b_CR])
                    TT(SR[:], Sm[:], rq[:], ALU.mult, [b_Sm, b_rq], [b_SR])
                    for h in range(8):
                        bq = 3 + (h % 2) * 2
                        for kc in range(3):
                            MM(banks[bq][0:96, :], WQ[:, kc, h * 96:(h + 1) * 96], cqT[:, kc, :], kc == 0, kc == 2,
                               [b_WQ, b_cqT], [bb[bq]])
                        for kc in range(3):
                            MM(banks[bq + 1][0:32, :], WQ[:, kc, 768 + h * 32:768 + (h + 1) * 32], cqT[:, kc, :],
                               kc == 0, kc == 2, [b_WQ, b_cqT], [bb[bq + 1]])
                        o_t, o_b = next_ob()
                        TT(t1[0:96, :], banks[bq][0:96, :], CR[0:96, :], ALU.mult, [bb[bq], b_CR], [b_t1])
                        TT(t2[0:32, :], banks[bq + 1][0:32, :], SR[0:32, :], ALU.mult, [bb[bq + 1], b_SR], [b_t2])
                        TT(o_t[0:32, :], t1[0:32, :], t2[0:32, :], ALU.add, [b_t1, b_t2], [o_b], eng="pool")
                        CP(o_t[32:64, :], t1[32:64, :], [b_t1], [o_b], eng="pool")
                        CP(o_t[64:96, :], t1[64:96, :], [b_t1], [o_b], eng="pool")
                        ST(qmT[h, :, s0:s0 + 512], o_t[0:96, :], [o_b], [B["qmT"]])
                    for c in range(2):
                        proj(c, 384 + c * 128, 128)
                        CP(ckvT[:, c, :], banks[c][:, :], [bb[c]], [b_ckvT])
                        ACT(sqb[:, c, :], banks[c][:, :], AF.Square, [bb[c]], [b_sqb])
                    for c in range(2):
                        MM(banks[2][:, :], ones_b[:, :], sqb[:, c, :], c == 0, c == 1, [b_ones, b_sqb], [bb[2]])
                    rsqrt_ip(rkv[:], b_rkv, 1.0 / 256, src=banks[2][:, :], b_src=bb[2])
                    for t in range(4):
                        for c in range(2):
                            MM(banks[2][:, t:t + 1], sqb[:, c, t * 128:(t + 1) * 128], ones_b[:, 0:1],
                               (t == 0 and c == 0), c == 1, [b_ones, b_sqb], [bb[2]])
                    rsqrt_ip(rkt[:], b_rkt, 1.0 / 256, src=banks[2][:, 0:4], b_src=bb[2])
                    for g2 in range(4):
                        bk = 3 + (g2 % 2)
                        for kc in range(2):
                            MM(banks[bk][:, :], WKV[:, kc, g2 * 128:(g2 + 1) * 128], ckvT[:, kc, :], kc == 0, kc == 1,
                               [b_WKV, b_ckvT], [bb[bk]])
                        o_t, o_b = next_ob()
                        TT(o_t[:, :], banks[bk][:, :], rkv[:, :], ALU.mult, [bb[bk], b_rkv], [o_b])
                        ST(kmT[2 * g2, 32:96, s0:s0 + 512], o_t[0:64, :], [o_b], [B["kmT"]])
                        ST(kmT[2 * g2 + 1, 32:96, s0:s0 + 512], o_t[64:128, :], [o_b], [B["kmT"]])
                    for t in range(4):
                        bv = 5 + (t % 2)
                        for kc in range(2):
                            MM(banks[bv][:, :], ckvT[:, kc, t * 128:(t + 1) * 128], WKV[:, kc, 512:1024], kc == 0,
                               kc == 1, [b_WKV, b_ckvT], [bb[bv]])
                        v_t, v_b = vt[t % 2]
                        TS(v_t[:, :, 0:64], banks[bv][:, :].rearrange("p (h d) -> p h d", d=64), rkt[:, t:t + 1], None,
                           ALU.mult, None, [bb[bv], b_rkt], [v_b])
                        ST(vm[s0 + t * 128:s0 + (t + 1) * 128, :, :], v_t[:], [v_b], [B["vm"]])
                    proj(0, 640, 32)
                    proj(1, 672, 32)
                    o_t, o_b = next_ob()
                    TT(t1[0:32, :], banks[0][0:32, :], Cm[0:32, :], ALU.mult, [bb[0], b_Cm], [b_t1])
                    TT(t2[0:32, :], banks[1][0:32, :], Sm[0:32, :], ALU.mult, [bb[1], b_Sm], [b_t2])
                    TT(o_t[0:32, :], t1[0:32, :], t2[0:32, :], ALU.add, [b_t1, b_t2], [o_b], eng="pool")
                    for h in range(8):
                        ST(kmT[h, 0:32, s0:s0 + 512], o_t[0:32, :], [o_b], [B["kmT"]])
                    def rope_group(c0, r0, dests):
                        bq = 3
                        proj(bq, c0, 128)
                        proj(bq + 1, r0, 128)
                        o_t, o_b = next_ob()
                        TT(t1[:, :], banks[bq][:, :], Cn[:, :], ALU.mult, [bb[bq], b_Cn], [b_t1])
                        TT(t2[:, :], banks[bq + 1][:, :], Sn[:, :], ALU.mult, [bb[bq + 1], b_Sn], [b_t2])
                        TT(o_t[:, :], t1[:, :], t2[:, :], ALU.add, [b_t1, b_t2], [o_b], eng="pool")
                        for (dst, bname, lo) in dests:
                            ST(dst, o_t[lo:lo + 64, :], [o_b], [B[bname]])
                    for pr in range(4):
                        h0 = 2 * pr
                        rope_group(704 + pr * 128, 1216 + pr * 128,
                                   [(qnT[h0 // 4, :, h0 % 4, s0:s0 + 512], "qnT", 0),
                                    (qnT[(h0 + 1) // 4, :, (h0 + 1) % 4, s0:s0 + 512], "qnT", 64)])
                    for (c0, dst, nm) in ((1728, kcmpT, "kcmpT"), (1984, kslcT, "kslcT"), (2240, kwinT, "kwinT")):
                        rope_group(c0, c0 + 128, [(dst[0, :, s0:s0 + 512], nm, 0), (dst[1, :, s0:s0 + 512], nm, 64)])
                    proj(5, 2496, 128)
                    o_t, o_b = next_ob()
                    CP(o_t[:, :], banks[5][:, :], [bb[5]], [o_b], eng="act" if False else "dve")
                    ST(vcmpT[0, :, s0:s0 + 512], o_t[0:64, :], [o_b], [B["vcmpT"]])
                    ST(vcmpT[1, :, s0:s0 + 512], o_t[64:128, :], [o_b], [B["vcmpT"]])
                    for t in range(4):
                        bv = 5 + (t % 2)
                        for kc in range(8):
                            MM(banks[bv][:, 0:280], hT[:, kc, t * 128:(t + 1) * 128], WA[:, kc, 2624:2904], kc == 0,
                               kc == 7, [b_WA, b_hT], [bb[bv]])
                        v_t, v_b = vs2[t % 2]
                        CP(v_t[:, :, :, 0:64], banks[bv][:, 0:256].rearrange("p (a g d) -> p a g d", a=2, g=2),
                           [bb[bv]], [v_b])
                        g_t, g_b = gt[t % 2]
                        ACT(g_t[:, :], banks[bv][:, 256:280], AF.Sigmoid, [bb[bv]], [g_b])
                        r0, r1 = s0 + t * 128, s0 + (t + 1) * 128
                        ST(vslc[r0:r1, :, :], v_t[:, 0, :, :], [v_b], [B["vslc"]])
                        ST(vwin[r0:r1, :, :], v_t[:, 1, :, :], [v_b], [B["vwin"]])
                        ST(gat[r0:r1, :], g_t[:, :], [g_b], [B["gat"]])
                cx.barrier()
                cx.emit()
            PH[0] += 1
            if stop_after is not None and PH[0] >= stop_after:
                break

            with ExitStack() as st:
                tri, b_tri = sbt(st, "tri", [128, 128], BF16)
                trif, b_trif = sbt(st, "trif", [128, 128], F32)
                LD(trif[:], c_tri[:, :], [b_trif])
                CP(tri[:], trif[:], [b_trif], [b_tri])
                KT, b_KT = sbt(st, "KT", [96, 4, S], BF16)
                VV, b_VV = sbt(st, "VV", [128, NT, 4, 65], BF16)
                QT = [sbt(st, "QT%d" % i, [96, 4, 128], BF16) for i in range(2)]
                PT = [sbt(st, "PT%d" % i, [128, 512], BF16) for i in range(4)]
                recs = [sbt(st, "rec%d" % i, [128, 4], F32) for i in range(2)]
                yo = [sbt(st, "yo%d" % i, [128, 4, 64], BF16) for i in range(2)]
                scale_m = float(96 ** -0.5)
                LAG = 2

                def run_pipeline(items):
                    n = len(items)
                    for k in range(n + LAG):
                        if k >= LAG:
                            items[k - LAG][2]()
                        if k < n:
                            items[k][0]()
                            items[k][1]()

                for hg in range(2):
                    for h in range(4):
                        LD(KT[:, h, :], kmT[hg * 4 + h, :, :], [b_KT], reads=[B["kmT"]])
                    for jt in range(NT):
                        LD(VV[:, jt, :, :], vm[jt * 128:(jt + 1) * 128, hg * 4:(hg + 1) * 4, :], [b_VV],
                           reads=[B["vm"]])
                    items = []
                    for i in range(NT):
                        for j in range(i + 1):
                            k = len(items)

                            def fS(i=i, j=j, k=k):
                                q_t, q_b = QT[i % 2]
                                if j == 0:
                                    for h in range(4):
                                        LD(q_t[:, h, :], qmT[hg * 4 + h, :, i * 128:(i + 1) * 128], [q_b],
                                           reads=[B["qmT"]])
                                sb_i = k % 3
                                for h in range(4):
                                    MM(banks[sb_i][:, h * 128:(h + 1) * 128], KT[:, h, j * 128:(j + 1) * 128],
                                       q_t[:, h, :], h == 0, h == 3, [b_KT, q_b], [bb[sb_i]])

                            def fE(i=i, j=j, k=k):
                                sb_i = k % 3
                                p_t, p_b = PT[k % 4]
                                ACT(p_t[:, :], banks[sb_i][:, :], AF.Exp, [bb[sb_i]], [p_b], scale=scale_m)
                                if j == i:
                                    TT(p_t[:, :].rearrange("p (h q) -> p h q", h=4),
                                       p_t[:, :].rearrange("p (h q) -> p h q", h=4), bcast_mid(tri[:, :], 4), ALU.mult,
                                       [p_b, b_tri], [p_b], eng="pool")

                            def fP(i=i, j=j, k=k):
                                p_t, p_b = PT[k % 4]
                                ab = 4 + (i % 2)
                                for h in range(4):
                                    MM(banks[ab][:, h * 65:(h + 1) * 65], p_t[:, h * 128:(h + 1) * 128],
                                       VV[:, j, h, :], (j == 0 and h == 0), j == i, [p_b, b_VV], [bb[ab]])
                                if j == i:
                                    rec, b_rec = recs[i % 2]
                                    acc = banks[ab][:, 0:260].rearrange("p (h c) -> p h c", c=65)
                                    TS(rec[:, :], acc[:, :, 64], 1e-30, None, ALU.max, None, [bb[ab]], [b_rec])
                                    cx.op("dve", lambda e, rec=rec: e.reciprocal(out=rec[:, :], in_=rec[:, :]),
                                          [b_rec], [b_rec])
                                    y_t, y_b = yo[i % 2]
                                    TT(y_t[:, :, :], acc[:, :, 0:64], bcast_last(rec[:, :], 64), ALU.mult,
                                       [bb[ab], b_rec], [y_b])
                                    ST(ymla[i * 128:(i + 1) * 128, hg * 256:(hg + 1) * 256],
                                       y_t[:, :, :].rearrange("p h d -> p (h d)"), [y_b], [B["ymla"]])

                            items.append((fS, fE, fP))
                    run_pipeline(items)
                cx.barrier()
                cx.emit()
            PH[0] += 1
            if stop_after is not None and PH[0] >= stop_after:
                break

            with ExitStack() as st:
                cst, b_cst = sbt(st, "cst", [128, 16 * 128], F32)
                tri, b_tri = sbt(st, "tri", [128, 128], BF16)
                wlow, b_wlow = sbt(st, "wlow", [128, 128], BF16)
                maskc, b_maskc = sbt(st, "maskc", [128, 16, 128], BF16)
                emat, b_emat = sbt(st, "emat", [128, NT, 128], BF16)
                LD(cst[:, 0:128], c_tri[:, :], [b_cst])
                CP(tri[:], cst[:, 0:128], [b_cst], [b_tri])
                LD(cst[:, 0:128], c_wlow[:, :], [b_cst])
                CP(wlow[:], cst[:, 0:128], [b_cst], [b_wlow])
                LD(cst[:, :].rearrange("p (o q) -> p o q", o=16), c_maskc[:, :, :], [b_cst])
                CP(maskc[:], cst[:, :].rearrange("p (o q) -> p o q", o=16), [b_cst], [b_maskc])
                for j0 in range(0, NT, 16):
                    jn = min(16, NT - j0)
                    LD(cst[:, 0:jn * 128].rearrange("p (o q) -> p o q", o=jn), c_emat[:, j0:j0 + jn, :], [b_cst])
                    CP(emat[:, j0:j0 + jn, :], cst[:, 0:jn * 128].rearrange("p (o q) -> p o q", o=jn), [b_cst],
                       [b_emat])
                wck, b_wck = sbt(st, "wck", [64, 32, 64], BF16)
                wcv, b_wcv = sbt(st, "wcv", [64, 32, 64], BF16)
                wcs, b_wcs = sbt(st, "wcs", [64, 32, 64], F32)
                LD(wcs[:], W["nsa_cmp_w_k"][l].rearrange("(r d) e -> d r e", d=64), [b_wcs])
                CP(wck[:], wcs[:], [b_wcs], [b_wck])
                LD(wcs[:], W["nsa_cmp_w_v"][l].rearrange("(r d) e -> d r e", d=64), [b_wcs])
                CP(wcv[:], wcs[:], [b_wcs], [b_wcv])
                pkv, b_pkv = sbt(st, "pkv", [32, 2, 64], F32)
                LD(pkv[:, 0, :], W["nsa_cmp_pos_k"][l], [b_pkv])
                LD(pkv[:, 1, :], W["nsa_cmp_pos_v"][l], [b_pkv])
                posT, b_posT = sbt(st, "posT", [64, 2, 32], BF16)
                for a in range(2):
                    TR(banks[0][0:64, a * 32:(a + 1) * 32], pkv[:, a, :], identf[0:32, 0:32], [b_pkv, b_identf],
                       [bb[0]])
                CP(posT[:, :, :], banks[0][0:64, 0:64].rearrange("p (a r) -> p a r", a=2), [bb[0]], [b_posT])
                biask, b_biask = sbt(st, "biask", [64, 1], F32)
                biasv, b_biasv = sbt(st, "biasv", [1, 64], BF16)
                for r in range(32):
                    MM(banks[1][0:64, 0:1], wck[:, r, :], posT[:, 0, r:r + 1], r == 0, r == 31, [b_wck, b_posT],
                       [bb[1]])
                CP(biask[:, :], banks[1][0:64, 0:1], [bb[1]], [b_biask])
                for r in range(32):
                    MM(banks[2][0:1, 0:64], posT[:, 1, r:r + 1], wcv[:, r, :], r == 0, r == 31, [b_wcv, b_posT],
                       [bb[2]])
                CP(biasv[:, :], banks[2][0:1, 0:64], [bb[2]], [b_biasv])
                KS, b_KS = sbt(st, "KS", [64, 2, S], BF16)
                KW, b_KW = sbt(st, "KW", [64, 2, S], BF16)
                kcT, b_kcT = sbt(st, "kcT", [64, 2, 512], BF16)
                VC, b_VC = sbt(st, "VC", [128, 4, 2, CW], BF16)
                MS(kcT[:], 0.0, [b_kcT])
                MS(VC[:], 0.0, [b_VC])
                for g in range(2):
                    LD(KS[:, g, :], kcmpT[g, :, :], [b_KS], reads=[B["kcmpT"]])
                    LD(KW[:, g, :], vcmpT[g, :, :], [b_KW], reads=[B["vcmpT"]])
                LD(cst[:, 0:512].rearrange("p (t m) -> p t m", t=4), c_ovl[:, :, :], [b_cst])
                for g in range(2):
                    MS(VC[:, :, g, 64:65], 1.0, [b_VC])
                    CP(VC[:, :, g, 65:CW], cst[:, 0:512].rearrange("p (t m) -> p t m", t=4), [b_cst], [b_VC])
                for g in range(2):
                    n0 = 0
                    while n0 < NCMP:
                        nn = min(512, NCMP - n0)
                        for r in range(32):
                            src = KS[:, g, r + 16 * n0:r + 16 * n0 + 16 * (nn - 1) + 1]
                            rhs = bass.AP(src.tensor, src.offset, [list(src.ap[0]), [16, nn]])
                            MM(banks[3][0:64, 0:nn], wck[:, r, :], rhs, r == 0, r == 31, [b_wck, b_KS], [bb[3]])
                        ACT(kcT[:, g, n0:n0 + nn], banks[3][0:64, 0:nn], AF.Identity, [bb[3], b_biask], [b_kcT],
                            bias=biask[:, 0:1], scale=1.0)
                        n0 += nn
                    for jt in range(NCT):
                        nn = min(128, NCMP - jt * 128)
                        for r in range(32):
                            o = r + 16 * jt * 128
                            src = KW[:, g, o:o + 16 * (nn - 1) + 1]
                            lhsT = bass.AP(src.tensor, src.offset, [list(src.ap[0]), [16, nn]])
                            MM(banks[4][0:nn, 0:64], lhsT, wcv[:, r, :], r == 0, False, [b_wcv, b_KW], [bb[4]])
                        MM(banks[4][0:nn, 0:64], ones_b[0:1, 0:nn], biasv[0:1, :], False, True, [b_ones, b_biasv],
                           [bb[4]])
                        CP(VC[0:nn, jt, g, 0:64], banks[4][0:nn, 0:64], [bb[4]], [b_VC])
                VS, b_VS = sbt(st, "VS", [128, NT, 2, 65], BF16)
                VW, b_VW = sbt(st, "VW", [128, NT, 2, 65], BF16)
                for g in range(2):
                    LD(KS[:, g, :], kslcT[g, :, :], [b_KS], reads=[B["kslcT"]])
                    LD(KW[:, g, :], kwinT[g, :, :], [b_KW], reads=[B["kwinT"]])
                for jt in range(NT):
                    LD(VS[:, jt, :, :], vslc[jt * 128:(jt + 1) * 128, :, :], [b_VS], reads=[B["vslc"]])
                    LD(VW[:, jt, :, :], vwin[jt * 128:(jt + 1) * 128, :, :], [b_VW], reads=[B["vwin"]])
                QN = [sbt(st, "QN%d" % i, [64, 2, 4, 128], BF16) for i in range(2)]
                GA = [sbt(st, "GA%d" % i, [128, 24], F32) for i in range(2)]
                AM = [sbt(st, "AM%d" % i, [128, 128], F32) for i in range(2)]
                PT = [sbt(st, "PTn%d" % i, [128, 512], BF16) for i in range(4)]
                imps = [sbt(st, "imp%d" % i, [128, 128], F32) for i in range(2)]
                impws = [sbt(st, "impw%d" % i, [128, 128], F32) for i in range(2)]
                top16s = [sbt(st, "top16_%d" % i, [128, 16], F32) for i in range(2)]
                negms = [sbt(st, "negm%d" % i, [128, 128], BF16) for i in range(2)]
                negTs = [sbt(st, "negT%d" % i, [128, 128], BF16) for i in range(2)]
                recs = [sbt(st, "recn%d" % i, [128, 4], F32) for i in range(6)]
                wgts = [sbt(st, "wgt%d" % i, [128, 4], F32) for i in range(6)]
                yaccs = [sbt(st, "yacc%d" % i, [128, 8, 64], F32) for i in range(2)]
                ytmps = [sbt(st, "ytmp%d" % i, [128, 4, 64], F32) for i in range(2)]
                yo = [sbt(st, "yon%d" % i, [128, 512], BF16) for i in range(2)]
                scale_n = 0.125
                LAG = 2
                rot = [0]

                def small():
                    rot[0] += 1
                    return recs[rot[0] % 6], wgts[rot[0] % 6]

                def exp_mask(k, masks):
                    sb_i = k % 3
                    p_t, p_b = PT[k % 4]
                    ACT(p_t[:, :], banks[sb_i][:, :], AF.Exp, [bb[sb_i]], [p_b], scale=scale_n)
                    for (m_ap, m_b) in masks:
                        TT(p_t[:, :].rearrange("p (h q) -> p h q", h=4),
                           p_t[:, :].rearrange("p (h q) -> p h q", h=4), bcast_mid(m_ap, 4), ALU.mult,
                           [p_b, m_b], [p_b], eng="pool")

                def branch_out(i, g, bank_i, br):
                    acc = banks[bank_i][:, 0:260].rearrange("p (h c) -> p h c", c=65)
                    acc_b = bb[bank_i]
                    (rec, b_rec), (wgt, b_wgt) = small()
                    g_t, g_b = GA[i % 2]
                    gv = g_t[:, :].rearrange("p (h b) -> p h b", b=3)
                    yacc, b_yacc = yaccs[i % 2]
                    ytmp, b_ytmp = ytmps[(2 * i + g + br) % 2]
                    TS(rec[:, :], acc[:, :, 64], 1e-30, None, ALU.max, None, [acc_b], [b_rec])
                    cx.op("dve", lambda e, rec=rec: e.reciprocal(out=rec[:, :], in_=rec[:, :]), [b_rec], [b_rec])
                    TT(wgt[:, :], rec[:, :], gv[:, g * 4:(g + 1) * 4, br], ALU.mult, [b_rec, g_b], [b_wgt])
                    TT(ytmp[:, :, :], acc[:, :, 0:64], bcast_last(wgt[:, :], 64), ALU.mult, [acc_b, b_wgt], [b_ytmp])
                    TT(yacc[:, g * 4:(g + 1) * 4, :], yacc[:, g * 4:(g + 1) * 4, :], ytmp[:, :, :], ALU.add,
                       [b_yacc, b_ytmp], [b_yacc], eng="pool")

                def cmp_final(i, g):
                    (rec, b_rec), (wgt, b_wgt) = small()
                    par = (2 * i + g) % 2
                    imp, b_imp = imps[par]
                    impw, b_impw = impws[par]
                    top16, b_top16 = top16s[par]
                    negm, b_negm = negms[par]
                    g_t, g_b = GA[i % 2]
                    a_t, a_b = AM[i % 2]
                    gv = g_t[:, :].rearrange("p (h b) -> p h b", b=3)
                    yacc, b_yacc = yaccs[i % 2]
                    for h in range(4):
                        bk = 3 + h // 2
                        a0 = (h % 2) * CW
                        TS(rec[:, h:h + 1], banks[bk][:, a0 + 64:a0 + 65], 1e-30, None, ALU.max, None, [bb[bk]],
                           [b_rec])
                    cx.op("dve", lambda e, rec=rec: e.reciprocal(out=rec[:, :], in_=rec[:, :]), [b_rec], [b_rec])
                    for h in range(4):
                        bk = 3 + h // 2
                        a0 = (h % 2) * CW
                        if h == 0:
                            TS(imp[:, :], banks[bk][:, a0 + 65:a0 + CW], rec[:, 0:1], None, ALU.mult, None,
                               [bb[bk], b_rec], [b_imp])
                        else:
                            STT(imp[:, :], banks[bk][:, a0 + 65:a0 + CW], rec[:, h:h + 1], imp[:, :], ALU.mult,
                                ALU.add, [bb[bk], b_rec, b_imp], [b_imp])
                    TT(wgt[:, :], rec[:, :], gv[:, g * 4:(g + 1) * 4, 0], ALU.mult, [b_rec, g_b], [b_wgt])
                    for h in range(4):
                        bk = 3 + h // 2
                        a0 = (h % 2) * CW
                        TS(yacc[:, g * 4 + h, :], banks[bk][:, a0:a0 + 64], wgt[:, h:h + 1], None, ALU.mult, None,
                           [bb[bk], b_wgt], [b_yacc])
                    TT(imp[:, :], imp[:, :], a_t[:, :], ALU.add, [b_imp, a_b], [b_imp])
                    cx.op("dve", lambda e, top16=top16, imp=imp: e.max(out=top16[:, 0:8], in_=imp[:, :]),
                          [b_imp], [b_top16])
                    cx.op("dve", lambda e, top16=top16, imp=imp, impw=impw:
                          e.match_replace(out=impw[:, :], in_to_replace=top16[:, 0:8], in_values=imp[:, :],
                                          imm_value=-3.0), [b_imp, b_top16], [b_impw])
                    cx.op("dve", lambda e, top16=top16, impw=impw: e.max(out=top16[:, 8:16], in_=impw[:, :]),
                          [b_impw], [b_top16])
                    TS(impw[:, :], imp[:, :], top16[:, 15:16], -1.0, ALU.is_ge, ALU.add, [b_imp, b_top16], [b_impw])
                    TS(negm[:, :], impw[:, :], -NEG, None, ALU.mult, None, [b_impw], [b_negm])

                def neg_transpose(i, g):
                    par = (2 * i + g) % 2
                    negm, b_negm = negms[par]
                    negT, b_negT = negTs[par]
                    pb = banks[7][:, :].bitcast(BF16)
                    TR(pb[:, 0:128], negm[:, :], identb[:, :], [b_negm, b_identb], [bb[7]])
                    CP(negT[:, :], pb[:, 0:128], [bb[7]], [b_negT])

                items = []
                for i in range(NT):
                    for g in range(2):
                        jl = i // 16
                        j_lo = max(0, i - 4)
                        seq = [("c", jt) for jt in range(jl + 1)] + [("w", j) for j in range(j_lo, i + 1)] + \
                              [("s", j) for j in range(i + 1)]
                        for (kind, j) in seq:
                            k = len(items)

                            def fS(i=i, g=g, kind=kind, j=j, k=k):
                                q_t, q_b = QN[i % 2]
                                if g == 0 and kind == "c" and j == 0:
                                    g_t, g_b = GA[i % 2]
                                    a_t, a_b = AM[i % 2]
                                    for gg in range(2):
                                        LD(q_t[:, gg, :, :], qnT[gg, :, :, i * 128:(i + 1) * 128], [q_b],
                                           reads=[B["qnT"]])
                                    LD(g_t[:, :], gat[i * 128:(i + 1) * 128, :], [g_b], reads=[B["gat"]])
                                    LD(a_t[:, :], c_amask[i, :, :], [a_b])
                                qrhs = q_t[:, g, :, :]
                                sb_i = k % 3
                                out3 = banks[sb_i][:, :].rearrange("p (h q) -> p h q", h=4)
                                if kind == "c":
                                    MM(out3, kcT[:, g, j * 128:(j + 1) * 128], qrhs, True, True, [b_kcT, q_b],
                                       [bb[sb_i]])
                                elif kind == "w":
                                    MM(out3, KW[:, g, j * 128:(j + 1) * 128], qrhs, True, True, [b_KW, q_b],
                                       [bb[sb_i]])
                                else:
                                    if j == 0:
                                        neg_transpose(i, g)
                                    negT, b_negT = negTs[(2 * i + g) % 2]
                                    MM(out3, KS[:, g, j * 128:(j + 1) * 128], qrhs, True, False, [b_KS, q_b],
                                       [bb[sb_i]])
                                    MM(out3, emat[:, j, :], bcast_mid(negT[:, :], 4), False, True,
                                       [b_emat, b_negT], [bb[sb_i]])

                            def fE(i=i, g=g, kind=kind, j=j, k=k):
                                masks = []
                                if kind == "c":
                                    if j == i // 16:
                                        masks.append((maskc[:, i % 16, :], b_maskc))
                                else:
                                    if j == i:
                                        masks.append((tri[:, :], b_tri))
                                    if kind == "w" and j == i - 4:
                                        masks.append((wlow[:, :], b_wlow))
                                exp_mask(k, masks)

                            def fP(i=i, g=g, kind=kind, j=j, k=k):
                                p_t, p_b = PT[k % 4]
                                if kind == "c":
                                    jl = i // 16
                                    for h in range(4):
                                        bk = 3 + h // 2
                                        MM(banks[bk][:, (h % 2) * CW:(h % 2 + 1) * CW], p_t[:, h * 128:(h + 1) * 128],
                                           VC[:, j, g, :], (j == 0 and h % 2 == 0), j == jl, [p_b, b_VC], [bb[bk]])
                                    if j == jl:
                                        cmp_final(i, g)
                                elif kind == "w":
                                    j_lo = max(0, i - 4)
                                    for h in range(4):
                                        MM(banks[6][:, h * 65:(h + 1) * 65], p_t[:, h * 128:(h + 1) * 128],
                                           VW[:, j, g, :], (j == j_lo and h == 0), j == i, [p_b, b_VW], [bb[6]])
                                    if j == i:
                                        branch_out(i, g, 6, 2)
                                else:
                                    for h in range(4):
                                        MM(banks[5][:, h * 65:(h + 1) * 65], p_t[:, h * 128:(h + 1) * 128],
                                           VS[:, j, g, :], (j == 0 and h == 0), j == i, [p_b, b_VS], [bb[5]])
                                    if j == i:
                                        branch_out(i, g, 5, 1)
                                        if g == 1:
                                            yacc, b_yacc = yaccs[i % 2]
                                            y_t, y_b = yo[i % 2]
                                            CP(y_t[:, :], yacc[:, :, :].rearrange("p h d -> p (h d)"), [b_yacc], [y_b])
                                            ST(ynsa[i * 128:(i + 1) * 128, :], y_t[:, :], [y_b], [B["ynsa"]])

                            items.append((fS, fE, fP))
                n_items = len(items)
                for k in range(n_items + LAG):
                    if k >= LAG:
                        items[k - LAG][2]()
                    if k < n_items:
                        items[k][0]()
                        items[k][1]()
                cx.barrier()
                cx.emit()
            PH[0] += 1
            if stop_after is not None and PH[0] >= stop_after:
                break

            with ExitStack() as st:
                WB, b_WB = sbt(st, "WB", [128, 12, D], BF16)
                WO, b_WO = sbt(st, "WO", [128, 8, D], BF16)
                wst, b_wst = sbt(st, "wstT", [128, D], F32)
                for bi, nm in enumerate(("w_branch_conv", "w_branch_mla", "w_branch_nsa")):
                    for kc in range(4):
                        LD(wst[:], W[nm][l][kc * 128:(kc + 1) * 128, :], [b_wst])
                        CP(WB[:, bi * 4 + kc, :], wst[:], [b_wst], [b_WB])
                for kc in range(8):
                    LD(wst[:], W["w_out"][l][kc * 128:(kc + 1) * 128, :], [b_wst])
                    CP(WO[:, kc, :], wst[:], [b_wst], [b_WO])
                gpost, b_gpost = sbt(st, "gpost", [128, D], F32)
                gfpost, b_gfpost = sbt(st, "gfpost", [128, D], F32)
                LD(gpost[:], W["norm_mix_post"][l:l + 1, :].partition_broadcast(128), [b_gpost])
                LD(gfpost[:], W["norm_ffn_post"][l:l + 1, :].partition_broadcast(128), [b_gfpost])
                cw, b_cw = sbt(st, "cw", [128, 4, 3], F32)
                fw, b_fw = sbt(st, "fw", [128, 22, 3], F32)
                fb, b_fb = sbt(st, "fb", [128, 22], F32)
                for k in range(3):
                    for c in range(4):
                        LD(cw[:, c, k:k + 1], W["conv_w"][l][k, c * 128:(c + 1) * 128].rearrange("(p o) -> p o", o=1),
                           [b_cw])
                    for c in range(22):
                        LD(fw[:, c, k:k + 1],
                           W["ffn_conv_w"][l][k, c * 128:(c + 1) * 128].rearrange("(p o) -> p o", o=1), [b_fw])
                for c in range(22):
                    LD(fb[:, c:c + 1], W["ffn_conv_b"][l][c * 128:(c + 1) * 128].rearrange("(p o) -> p o", o=1), [b_fb])
                xt4, b_xt4 = sbt(st, "xt4T", [128, 4, D], F32)
                hT, b_hT = sbt(st, "hTT", [128, 8, 512], BF16)
                hn, b_hn = sbt(st, "hnT", [128, D], BF16)
                junk, b_junk = sbt(st, "junkT", [128, D], BF16)
                rs, b_rs = sbt(st, "rsT", [128, 8], F32)
                WS = [sbt(st, "WS%d" % i, [128, 8, 256], BF16) for i in range(3)]
                WD = [sbt(st, "WD%d" % i, [128, D], BF16) for i in range(3)]
                cin, b_cin = sbt(st, "cin", [128, 12, 512], BF16)
                uu, b_uu = sbt(st, "uu", [128, 4, 514], F32)
                cacc, b_cacc = sbt(st, "cacc", [128, 512], F32)
                ycT, b_ycT = sbt(st, "ycT", [128, 4, 512], BF16)
                ymT, b_ymT = sbt(st, "ymT", [128, 4, 512], BF16)
                ynT, b_ynT = sbt(st, "ynT", [128, 4, 512], BF16)
                ytk, b_ytk = sbt(st, "ytk", [128, 512], BF16)
                gsb, b_gsb = sbt(st, "gsb", [128, 3, 512], F32)
                mrg, b_mrg = sbt(st, "mrg", [128, 512], F32)
                mT, b_mT = sbt(st, "mT", [128, 8, 512], BF16)
                aa, b_aa = sbt(st, "aa", [128, 514], F32)
                halo, b_halo = sbt(st, "halo", [128, 22, 2], F32)
                z1, b_z1 = sbt(st, "z1", [128, 512], F32)
                z2, b_z2 = sbt(st, "z2", [128, 512], F32)
                gT, b_gT = sbt(st, "gT", [128, 22, 512], BF16)
                xo, b_xo = sbt(st, "xo", [128, D], F32)
                MS(uu[:], 0.0, [b_uu])
                MS(halo[:], 0.0, [b_halo])
                wsi = [0]

                def load_ws(src, c0):
                    wsi[0] += 1
                    w_t, w_b = WS[wsi[0] % 3]
                    LD(w_t[:, :, :], src[:, c0:c0 + 256].rearrange("(k p) c -> p k c", p=128), [w_b],
                       reads=[B["wgin_b"], B["wup_b"]])
                    return w_t, w_b

                def post_norm_residual(t, gtile, b_gt, first_bank):
                    MS(rs[:, 4 + t:5 + t], 0.0, [b_rs], eng="dve")
                    MS(rs[:, 0:1], 0.0, [b_rs], eng="dve")
                    ACT(junk[:, 0:512], banks[first_bank][:, :], AF.Square, [bb[first_bank]], [b_junk, b_rs],
                        accum_out=rs[:, 4 + t:5 + t])
                    ACT(junk[:, 512:1024], banks[first_bank + 1][:, :], AF.Square, [bb[first_bank + 1]],
                        [b_junk, b_rs], accum_out=rs[:, 0:1])
                    TT(rs[:, 4 + t:5 + t], rs[:, 4 + t:5 + t], rs[:, 0:1], ALU.add, [b_rs], [b_rs])
                    rsqrt_ip(rs[:, 4 + t:5 + t], b_rs, 1.0 / D)
                    for hf in range(2):
                        STT(xo[:, hf * 512:(hf + 1) * 512], banks[first_bank + hf][:, :], rs[:, 4 + t:5 + t],
                            gtile[:, hf * 512:(hf + 1) * 512], ALU.mult, ALU.mult,
                            [bb[first_bank + hf], b_rs, b_gt], [b_xo])
                    TT(xt4[:, t, :], xt4[:, t, :], xo[:, :], ALU.add, [b_xt4, b_xo], [b_xt4], eng="pool")

                for s in range(NS):
                    s0 = s * 512
                    norm_transpose(x_src, s0, xt4, b_xt4, hT, b_hT, rs, b_rs, hn, b_hn, junk, b_junk, b_xsrc,
                                   banks[7], bb[7])
                    for (ysrc, bname, yT, b_yT) in ((ymla, "ymla", ymT, b_ymT), (ynsa, "ynsa", ynT, b_ynT)):
                        for t in range(4):
                            LD(ytk[:, :], ysrc[s0 + t * 128:s0 + (t + 1) * 128, :], [b_ytk], reads=[B[bname]])
                            pb = banks[7][:, :].bitcast(BF16)
                            for k in range(4):
                                TR(pb[:, k * 128:(k + 1) * 128], ytk[:, k * 128:(k + 1) * 128], identb[:],
                                   [b_ytk, b_identb], [bb[7]])
                            CP(yT[:, :, t * 128:(t + 1) * 128], pb[:, 0:512].rearrange("p (k t) -> p k t", k=4),
                               [bb[7]], [b_yT])
                    for c2 in range(6):
                        w_t, w_b = load_ws(wgin_b, 3072 + c2 * 256)
                        for c in range(2):
                            bk = c % 2
                            for kc in range(8):
                                MM(banks[bk][:, :], w_t[:, kc, c * 128:(c + 1) * 128], hT[:, kc, :], kc == 0, kc == 7,
                                   [w_b, b_hT], [bb[bk]])
                            CP(cin[:, c2 * 2 + c, :], banks[bk][:, :], [bb[bk]], [b_cin])
                    for c in range(4):
                        CP(uu[:, c, 0:2], uu[:, c, 512:514], [b_uu], [b_uu])
                    for c in range(4):
                        TT(uu[:, c, 2:514], cin[:, 4 + c, :], cin[:, 8 + c, :], ALU.mult, [b_cin], [b_uu])
                        TS(cacc[:, :], uu[:, c, 0:512], cw[:, c, 0:1], None, ALU.mult, None, [b_uu, b_cw], [b_cacc])
                        STT(cacc[:, :], uu[:, c, 1:513], cw[:, c, 1:2], cacc[:, :], ALU.mult, ALU.add,
                            [b_uu, b_cw, b_cacc], [b_cacc])
                        STT(cacc[:, :], uu[:, c, 2:514], cw[:, c, 2:3], cacc[:, :], ALU.mult, ALU.add,
                            [b_uu, b_cw, b_cacc], [b_cacc])
                        TT(ycT[:, c, :], cacc[:, :], cin[:, c, :], ALU.mult, [b_cacc, b_cin], [b_ycT])
                    for n2 in range(4):
                        wts = [load_ws(wgin_b, bi * 1024 + n2 * 256) for bi in range(3)]
                        for nn in range(2):
                            n = n2 * 2 + nn
                            for bi in range(3):
                                w_t, w_b = wts[bi]
                                for kc in range(8):
                                    MM(banks[bi][:, :], w_t[:, kc, nn * 128:(nn + 1) * 128], hT[:, kc, :], kc == 0,
                                       kc == 7, [w_b, b_hT], [bb[bi]])
                                ACT(gsb[:, bi, :], banks[bi][:, :], AF.Sigmoid, [bb[bi]], [b_gsb])
                            for bi, (yT, b_yT) in enumerate(((ycT, b_ycT), (ymT, b_ymT), (ynT, b_ynT))):
                                for kc in range(4):
                                    MM(banks[3 + bi][:, :], WB[:, bi * 4 + kc, n * 128:(n + 1) * 128], yT[:, kc, :],
                                       kc == 0, kc == 3, [b_WB, b_yT], [bb[3 + bi]])
                            TT(mrg[:, :], banks[3][:, :], gsb[:, 0, :], ALU.mult, [bb[3], b_gsb], [b_mrg])
                            TT(z1[:, :], banks[4][:, :], gsb[:, 1, :], ALU.mult, [bb[4], b_gsb], [b_z1])
                            TT(z2[:, :], banks[5][:, :], gsb[:, 2, :], ALU.mult, [bb[5], b_gsb], [b_z2])
                            TT(mrg[:, :], mrg[:, :], z1[:, :], ALU.add, [b_mrg, b_z1], [b_mrg], eng="pool")
                            TT(mT[:, n, :], mrg[:, :], z2[:, :], ALU.add, [b_mrg, b_z2], [b_mT], eng="pool")
                    for t in range(4):
                        for hf in range(2):
                            for kc in range(8):
                                MM(banks[hf][:, :], mT[:, kc, t * 128:(t + 1) * 128], WO[:, kc, hf * 512:(hf + 1) * 512],
                                   kc == 0, kc == 7, [b_mT, b_WO], [bb[hf]])
                        post_norm_residual(t, gpost, b_gpost, 0)
                    for t in range(4):
                        MS(rs[:, t:t + 1], 0.0, [b_rs], eng="dve")
                        rms_tile(xt4[:, t, :], b_xt4, rs, b_rs, t, junk[:], b_junk, D)
                        TS(hn[:], xt4[:, t, :], rs[:, t:t + 1], None, ALU.mult, None, [b_xt4, b_rs], [b_hn])
                        pb = banks[7][:, :].bitcast(BF16)
                        for k in range(8):
                            TR(pb[:, k * 128:(k + 1) * 128], hn[:, k * 128:(k + 1) * 128], identb[:],
                               [b_hn, b_identb], [bb[7]])
                        CP(hT[:, :, t * 128:(t + 1) * 128], pb[:, 0:1024].rearrange("p (k t) -> p k t", k=8),
                           [bb[7]], [b_hT])
                    for c2 in range(11):
                        wa_t, wa_b = load_ws(wup_b, c2 * 256)
                        wb_t, wb_b = load_ws(wup_b, DFF + c2 * 256)
                        for cc in range(2):
                            c = c2 * 2 + cc
                            for kc in range(8):
                                MM(banks[0][:, :], wa_t[:, kc, cc * 128:(cc + 1) * 128], hT[:, kc, :],
                                   kc == 0, kc == 7, [wa_b, b_hT], [bb[0]])
                            for kc in range(8):
                                MM(banks[1][:, :], wb_t[:, kc, cc * 128:(cc + 1) * 128], hT[:, kc, :],
                                   kc == 0, kc == 7, [wb_b, b_hT], [bb[1]])
                            CP(aa[:, 0:2], halo[:, c, :], [b_halo], [b_aa], eng="pool")
                            CP(aa[:, 2:514], banks[0][:, :], [bb[0]], [b_aa], eng="act" if False else "dve")
                            CP(halo[:, c, :], aa[:, 512:514], [b_aa], [b_halo], eng="pool")
                            TS(z1[:, :], aa[:, 0:512], fw[:, c, 0:1], fb[:, c:c + 1], ALU.mult, ALU.add,
                               [b_aa, b_fw, b_fb], [b_z1])
                            STT(z1[:, :], aa[:, 1:513], fw[:, c, 1:2], z1[:, :], ALU.mult, ALU.add,
                                [b_aa, b_fw, b_z1], [b_z1])
                            STT(z1[:, :], aa[:, 2:514], fw[:, c, 2:3], z1[:, :], ALU.mult, ALU.add,
                                [b_aa, b_fw, b_z1], [b_z1])
                            TT(z2[:, :], z1[:, :], z1[:, :], ALU.mult, [b_z1], [b_z2], eng="pool")
                            TS(z2[:, :], z2[:, :], 0.044715, 1.0, ALU.mult, ALU.add, [b_z2], [b_z2], eng="pool")
                            TT(z2[:, :], z2[:, :], z1[:, :], ALU.mult, [b_z2, b_z1], [b_z2], eng="pool")
                            ACT(z2[:, :], z2[:, :], AF.Sigmoid, [b_z2], [b_z2], scale=float(2.0 * np.sqrt(2.0 / np.pi)))
                            TT(z2[:, :], z2[:, :], z1[:, :], ALU.mult, [b_z2, b_z1], [b_z2], eng="pool")
                            TT(gT[:, c, :], z2[:, :], banks[1][:, :], ALU.mult, [b_z2, bb[1]], [b_gT])
                    wdi = 0
                    for tp in range(2):
                        for kc in range(22):
                            w_t, w_b = WD[wdi % 3]
                            wdi += 1
                            LD(w_t[:, :], wdn_b[kc * 128:(kc + 1) * 128, :], [w_b], reads=[B["wdn_b"]])
                            for tt in range(2):
                                t = tp * 2 + tt
                                for hf in range(2):
                                    bk = 2 + tt * 2 + hf
                                    MM(banks[bk][:, :], gT[:, kc, t * 128:(t + 1) * 128],
                                       w_t[:, hf * 512:(hf + 1) * 512], kc == 0, kc == 21, [b_gT, w_b], [bb[bk]])
                        for tt in range(2):
                            t = tp * 2 + tt
                            post_norm_residual(t, gfpost, b_gfpost, 2 + tt * 2)
                            ST(x_dst[s0 + t * 128:s0 + (t + 1) * 128, :], xt4[:, t, :], [b_xt4], [b_xdst])
                cx.barrier()
                cx.emit()
            PH[0] += 1
            if stop_after is not None and PH[0] >= stop_after:
                break
    return nc


_CACHE = {}


def _in_maps(inputs):
    x = np.asarray(inputs["x"])
    Bn, S, _ = x.shape
    consts = host_consts(S)
    shared = {}
    for k, v in inputs.items():
        if k in ("x", "positions"):
            continue
        shared[k] = np.ascontiguousarray(np.asarray(v, dtype=np.float32))
    for k, v in consts.items():
        shared["c_" + k] = np.ascontiguousarray(v.astype(np.float32))
    pos = np.asarray(inputs["positions"]).astype(np.int32)
    in_maps = []
    for c in range(8):
        b = c % Bn
        m = dict(shared)
        m["x"] = np.ascontiguousarray(x[b].astype(np.float32))
        m["positions"] = np.ascontiguousarray(pos[b][None, :])
        in_maps.append(m)
    return in_maps


def kernel(_stop_after=None, **inputs):
    x = np.asarray(inputs["x"])
    Bn, S, _ = x.shape
    depth = np.asarray(inputs["w_in"]).shape[0]
    key = (S, depth, _stop_after)
    if key not in _CACHE:
        _CACHE[key] = build(S, depth, _stop_after)
    nc = _CACHE[key]
    in_maps = _in_maps(inputs)
    res = run_bass_kernel_spmd(nc, in_maps, core_ids=list(range(8)))
    out = np.stack([np.asarray(res.results[b]["y"]) for b in range(Bn)], axis=0)
    return out.astype(np.float32)
```

```python
from contextlib import ExitStack
import numpy as np
import concourse.bass as bass
import concourse.mybir as mybir
from concourse.bass_utils import run_bass_kernel_spmd

F32 = mybir.dt.float32
BF16 = mybir.dt.bfloat16
I32 = mybir.dt.int32
AF = mybir.ActivationFunctionType
ALU = mybir.AluOpType

D = 1024
DEPTH = 2
THETA = 500000.0
EPS = 1e-6
DFF = 2816
NIN = 6584
NEG = -30000.0


class Buf:
    __slots__ = ("name", "last_w", "readers", "excl")

    def __init__(self, name, excl=False):
        self.name = name
        self.last_w = None
        self.readers = []
        self.excl = excl


class Ctx:
    def __init__(self, nc, stack, n_dma_ring=8):
        self.nc = nc
        self.engs = ("pe", "act", "dve", "pool", "sp")
        self.sems = {}
        self.count = {}
        self.waited = {k: {} for k in self.engs}
        for k in ("pe", "act", "dve", "pool"):
            self.sems[k] = stack.enter_context(nc.semaphore("s_" + k))
            self.count[k] = 0
        self.ring = {}
        self.ring_n = {}
        for q in ("sp", "pool", "act"):
            self.ring[q] = []
            for i in range(n_dma_ring):
                s = stack.enter_context(nc.semaphore("d_%s%d" % (q, i)))
                self.ring[q].append(s)
                self.sems[("d", q, i)] = s
            self.ring_n[q] = 0
        self.prog = {k: [] for k in self.engs}
        self.n_inst = 0
        self.n_wait = 0

    def _wait(self, e, tok):
        key, val = tok
        if key == e and self.count[e] - val >= 3:
            return
        w = self.waited[e]
        if w.get(key, 0) >= val:
            return
        w[key] = val
        sem = self.sems[key]
        self.prog[e].append(lambda eng, sem=sem, val=val: eng.wait_ge(sem, val))
        self.n_wait += 1

    def _deps(self, e, reads, writes, same_ok):
        for b in reads:
            if b.last_w is not None and not (same_ok and b.last_w[0] == e):
                self._wait(e, b.last_w)
            if b.excl:
                for t in b.readers:
                    if t[0] != e:
                        self._wait(e, t)
        for b in writes:
            if b.last_w is not None and not (same_ok and b.last_w[0] == e):
                self._wait(e, b.last_w)
            for t in b.readers:
                if not (same_ok and t[0] == e):
                    self._wait(e, t)

    def _record(self, tok, reads, writes):
        for b in reads:
            b.readers.append(tok)
            if len(b.readers) > 24:
                d = {}
                for k, v in b.readers:
                    if d.get(k, 0) < v:
                        d[k] = v
                b.readers = list(d.items())
        for b in writes:
            b.last_w = tok
            b.readers = []

    def op(self, e, fn, reads=(), writes=()):
        self._deps(e, reads, writes, same_ok=(e == "pe"))
        self.count[e] += 1
        sem = self.sems[e]
        self.prog[e].append(lambda eng, fn=fn, sem=sem: fn(eng).then_inc(sem, 1))
        tok = (e, self.count[e])
        self._record(tok, reads, writes)
        self.n_inst += 1
        return tok

    def dma(self, q, out, in_, reads=(), writes=(), **kw):
        n = self.ring_n[q]
        R = len(self.ring[q])
        key = ("d", q, n % R)
        prev = n // R
        if prev > 0:
            self._wait(q, (key, 16 * prev))
        self._deps(q, reads, writes, same_ok=False)
        sem = self.sems[key]
        self.prog[q].append(lambda eng, out=out, in_=in_, kw=kw, sem=sem:
                            eng.dma_start(out=out, in_=in_, **kw).then_inc(sem, 16))
        self.ring_n[q] = n + 1
        tok = (key, 16 * (prev + 1))
        self._record(tok, reads, writes)
        self.n_inst += 1
        return tok

    def barrier(self):
        toks = [(k, self.count[k]) for k in ("pe", "act", "dve", "pool") if self.count[k] > 0]
        for q in self.ring:
            n = self.ring_n[q]
            R = len(self.ring[q])
            for slot in range(R):
                cnt = (n - slot + R - 1) // R
                if cnt > 0:
                    toks.append((("d", q, slot), 16 * cnt))
        for e in self.engs:
            for t in toks:
                if t[0] != e:
                    self._wait(e, t)

    def emit(self):
        prog = self.prog
        with self.nc.Block() as block:
            @block.sync
            def _(eng):
                for f in prog["sp"]:
                    f(eng)

            @block.tensor
            def _(eng):
                for f in prog["pe"]:
                    f(eng)

            @block.scalar
            def _(eng):
                for f in prog["act"]:
                    f(eng)

            @block.vector
            def _(eng):
                for f in prog["dve"]:
                    f(eng)

            @block.gpsimd
            def _(eng):
                for f in prog["pool"]:
                    f(eng)
        self.prog = {k: [] for k in self.engs}


def bcast_mid(ap, n):
    a = ap.ap
    return bass.AP(ap.tensor, ap.offset, [list(a[0]), [0, n]] + [list(x) for x in a[1:]])


def bcast_last(ap, n):
    a = ap.ap
    return bass.AP(ap.tensor, ap.offset, [list(x) for x in a] + [[0, n]])


def host_consts(S):
    NT = S // 128
    NSEL = S // 64
    k = np.arange(128)[:, None]
    q = np.arange(128)[None, :]
    c = {}
    c["ident"] = np.eye(128, dtype=np.float32)
    c["tri"] = (k <= q).astype(np.float32)
    c["wlow"] = (k > q).astype(np.float32)
    mc = np.zeros((128, 16, 128), np.float32)
    for o in range(16):
        mc[:, o, :] = (16 * (k - 8 * o) + 31 <= q)
    c["maskc"] = mc
    n = np.arange(512)[:, None]
    m = np.arange(128)[None, :]
    ov = ((n >= 4 * m - 1) & (n <= 4 * m + 3) & (m < NSEL)).astype(np.float32)
    c["ovl"] = ov.reshape(4, 128, 128).transpose(1, 0, 2).copy()
    A = np.zeros((NT, 128, 128), np.float32)
    for i in range(NT):
        t = 128 * i + np.arange(128)[:, None]
        cur = t // 64
        mm = np.arange(128)[None, :]
        forced = (mm == 0) | (mm == cur) | (mm == cur - 1)
        causal = 64 * mm <= t
        A[i] = np.where(forced, 1e6, np.where(causal, 0.0, -1.0))
        A[i][:, NSEL:] = -1.0
    c["amask"] = A
    E = np.zeros((128, NT, 128), np.float32)
    for j in range(NT):
        for kk in range(128):
            mrow = 2 * j + (kk >= 64)
            if mrow < 128:
                E[mrow, j, kk] = 1.0
    c["emat"] = E
    f32 = np.float32
    inv_m = (f32(THETA) ** (-(np.arange(16, dtype=f32) / f32(16)))).astype(f32)
    inv_n = (f32(THETA) ** (-(np.arange(8, dtype=f32) / f32(8)))).astype(f32)
    fm = np.zeros((128, 1), f32)
    fm[0:16, 0] = inv_m
    fm[16:32, 0] = inv_m
    fn = np.zeros((128, 1), f32)
    for base in (0, 64):
        fn[base:base + 8, 0] = inv_n
        fn[base + 8:base + 16, 0] = inv_n
    c["invf"] = np.concatenate([fm, fn], axis=1)
    return c


def build(S, depth=DEPTH, stop_after=None):
    assert S % 512 == 0
    NT = S // 128
    NS = S // 512
    NSEL = S // 64
    NCMP = S // 16 - 1
    NCT = (NCMP + 127) // 128
    CW = 65 + 128
    nc = bass.Bass("TRN2", target_bir_lowering=False)

    def din(name, shape, dt=F32):
        return nc.dram_tensor(name, list(shape), dt, kind="ExternalInput").ap()

    def dscr(name, shape, dt=BF16):
        return nc.dram_tensor(name, list(shape), dt, kind="Internal").ap()

    x_in = din("x", [S, D])
    pos_in = din("positions", [1, S], I32)
    W = {}
    W["norm_mix_pre"] = din("norm_mix_pre", [depth, D])
    W["norm_mix_post"] = din("norm_mix_post", [depth, D])
    W["w_in"] = din("w_in", [depth, D, NIN])
    W["conv_w"] = din("conv_w", [depth, 3, 512])
    W["mla_q_norm"] = din("mla_q_norm", [depth, 384])
    W["mla_w_uq"] = din("mla_w_uq", [depth, 384, 768])
    W["mla_kv_norm"] = din("mla_kv_norm", [depth, 256])
    W["mla_w_ukv"] = din("mla_w_ukv", [depth, 256, 1024])
    W["nsa_cmp_pos_k"] = din("nsa_cmp_pos_k", [depth, 32, 64])
    W["nsa_cmp_pos_v"] = din("nsa_cmp_pos_v", [depth, 32, 64])
    W["nsa_cmp_w_k"] = din("nsa_cmp_w_k", [depth, 2048, 64])
    W["nsa_cmp_w_v"] = din("nsa_cmp_w_v", [depth, 2048, 64])
    W["w_branch_conv"] = din("w_branch_conv", [depth, 512, D])
    W["w_branch_mla"] = din("w_branch_mla", [depth, 512, D])
    W["w_branch_nsa"] = din("w_branch_nsa", [depth, 512, D])
    W["w_out"] = din("w_out", [depth, D, D])
    W["norm_ffn_pre"] = din("norm_ffn_pre", [depth, D])
    W["norm_ffn_post"] = din("norm_ffn_post", [depth, D])
    W["ffn_w_up"] = din("ffn_w_up", [depth, D, 2 * DFF])
    W["ffn_conv_w"] = din("ffn_conv_w", [depth, 3, DFF])
    W["ffn_conv_b"] = din("ffn_conv_b", [depth, DFF])
    W["ffn_w_down"] = din("ffn_w_down", [depth, DFF, D])
    c_ident = din("c_ident", [128, 128])
    c_tri = din("c_tri", [128, 128])
    c_wlow = din("c_wlow", [128, 128])
    c_maskc = din("c_maskc", [128, 16, 128])
    c_ovl = din("c_ovl", [128, 4, 128])
    c_amask = din("c_amask", [NT, 128, 128])
    c_emat = din("c_emat", [128, NT, 128])
    c_invf = din("c_invf", [128, 2])
    y_out = nc.dram_tensor("y", [S, D], F32, kind="ExternalOutput").ap()

    xs = dscr("xs", [S, D], F32)
    qmT = dscr("qmT", [8, 96, S])
    kmT = dscr("kmT", [8, 96, S])
    vm = dscr("vm", [S, 8, 65])
    qnT = dscr("qnT", [2, 64, 4, S])
    kcmpT = dscr("kcmpT", [2, 64, S])
    vcmpT = dscr("vcmpT", [2, 64, S])
    kslcT = dscr("kslcT", [2, 64, S])
    kwinT = dscr("kwinT", [2, 64, S])
    vslc = dscr("vslc", [S, 2, 65])
    vwin = dscr("vwin", [S, 2, 65])
    gat = dscr("gat", [S, 24], F32)
    ymla = dscr("ymla", [S, 512])
    ynsa = dscr("ynsa", [S, 512])
    wgin_b = dscr("wgin_b", [D, 4608])
    wup_b = dscr("wup_b", [D, 2 * DFF])
    wdn_b = dscr("wdn_b", [DFF, D])

    with ExitStack() as top:
        cx = Ctx(nc, top)
        banks = [top.enter_context(nc.psum_tensor("bank%d" % i, [128, 512], F32)) for i in range(8)]
        bb = [Buf("bank%d" % i, excl=True) for i in range(8)]

        uid = [0]

        def sbt(st, name, shape, dt):
            uid[0] += 1
            nm = "%s_u%d" % (name, uid[0])
            return st.enter_context(nc.sbuf_tensor(nm, list(shape), dt)), Buf(nm)

        def ACT(out, in_, func, reads, writes, **kw):
            cx.op("act", lambda e: e.activation(out=out, in_=in_, func=func, **kw), reads, writes)

        def TT(out, in0, in1, op, reads, writes, eng="dve"):
            cx.op(eng, lambda e: e.tensor_tensor(out=out, in0=in0, in1=in1, op=op), reads, writes)

        def TS(out, in0, s1, s2, op0, op1, reads, writes, eng="dve", **kw):
            if s2 is None:
                cx.op(eng, lambda e: e.tensor_scalar(out=out, in0=in0, scalar1=s1, scalar2=None, op0=op0, **kw),
                      reads, writes)
            else:
                cx.op(eng, lambda e: e.tensor_scalar(out=out, in0=in0, scalar1=s1, scalar2=s2, op0=op0, op1=op1, **kw),
                      reads, writes)

        def STT(out, in0, scalar, in1, op0, op1, reads, writes, eng="dve"):
            cx.op(eng, lambda e: e.scalar_tensor_tensor(out=out, in0=in0, scalar=scalar, in1=in1, op0=op0, op1=op1),
                  reads, writes)

        def CP(out, in_, reads, writes, eng="dve"):
            cx.op(eng, lambda e: e.tensor_copy(out=out, in_=in_), reads, writes)

        def MS(ap, val, writes, eng="pool"):
            cx.op(eng, lambda e: e.memset(ap, val), (), writes)

        def MM(out, lhsT, rhs, start, stop, reads, writes):
            cx.op("pe", lambda e: e.matmul(out, lhsT=lhsT, rhs=rhs, start=start, stop=stop, skip_group_check=True),
                  reads, writes)

        def TR(out, in_, ident, reads, writes):
            cx.op("pe", lambda e: e.transpose(out, in_, ident), reads, writes)

        LD = lambda out, in_, writes, reads=(), **kw: cx.dma("sp", out, in_, reads=reads, writes=writes, **kw)
        ST = lambda out, in_, reads, writes=(), **kw: cx.dma("pool", out, in_, reads=reads, writes=writes, **kw)

        B = {n: Buf(n) for n in ("xs", "qmT", "kmT", "vm", "qnT", "kcmpT", "vcmpT", "kslcT", "kwinT", "vslc",
                                 "vwin", "gat", "ymla", "ynsa", "wgin_b", "wup_b", "wdn_b", "y")}

        identf, b_identf = sbt(top, "identf", [128, 128], F32)
        identb, b_identb = sbt(top, "identb", [128, 128], BF16)
        ones_b, b_ones = sbt(top, "ones_b", [128, 128], BF16)
        invf, b_invf = sbt(top, "invf", [128, 2], F32)
        LD(identf[:], c_ident[:, :], [b_identf])
        LD(invf[:], c_invf[:, :], [b_invf])
        CP(identb[:], identf[:], [b_identf], [b_identb])
        MS(ones_b[:], 1.0, [b_ones])
        CONST = [b_identf, b_identb, b_ones, b_invf]

        epst, b_epst = sbt(top, "epst", [128, 1], F32)
        MS(epst[:], EPS, [b_epst])

        def rsqrt_ip(ap, b_ap, inv_n, src=None, b_src=None):
            p = ap.shape[0]
            if src is None:
                src, b_src = ap, b_ap
            ACT(ap, src, AF.Sqrt, [b_src, b_epst], [b_ap], bias=epst[0:p, 0:1], scale=float(inv_n))
            cx.op("dve", lambda e: e.reciprocal(out=ap, in_=ap), [b_ap], [b_ap])

        def vec_col(st, name, src_row, n):
            c = n // 128
            t, b = sbt(st, name, [128, c], F32)
            for k in range(c):
                LD(t[:, k:k + 1], src_row[k * 128:(k + 1) * 128].rearrange("(p o) -> p o", o=1), [b])
            return t, b

        def rms_tile(xt, b_xt, rs, b_rs, col, junk, b_junk, n):
            ACT(junk, xt, AF.Square, [b_xt], [b_junk, b_rs], accum_out=rs[:, col:col + 1])
            rsqrt_ip(rs[:, col:col + 1], b_rs, 1.0 / n)

        def norm_transpose(src, s0, xt4, b_xt4, hT, b_hT, rs, b_rs, hn, b_hn, junk, b_junk, b_src, bank, b_bank):
            for t in range(4):
                LD(xt4[:, t, :], src[s0 + t * 128:s0 + (t + 1) * 128, :], [b_xt4], reads=[b_src])
            for t in range(4):
                MS(rs[:, t:t + 1], 0.0, [b_rs], eng="dve")
                rms_tile(xt4[:, t, :], b_xt4, rs, b_rs, t, junk[:], b_junk, D)
                TS(hn[:], xt4[:, t, :], rs[:, t:t + 1], None, ALU.mult, None, [b_xt4, b_rs], [b_hn])
                pb = bank[:, :].bitcast(BF16)
                for k in range(8):
                    TR(pb[:, k * 128:(k + 1) * 128], hn[:, k * 128:(k + 1) * 128], identb[:],
                       [b_hn, b_identb], [b_bank])
                CP(hT[:, :, t * 128:(t + 1) * 128], pb[:, 0:1024].rearrange("p (k t) -> p k t", k=8),
                   [b_bank], [b_hT], eng="act" if False else "dve")

        def sincos(st_name, pos_f, b_pos, col, rows, shift, out, b_out, wk, b_wk, wki, b_wki):
            r = slice(0, rows)
            TS(wk[r, 0, :], pos_f[r, :], invf[r, col:col + 1], shift, ALU.mult, ALU.add, [b_pos, b_invf], [b_wk])
            TS(wk[r, 1, :], wk[r, 0, :], float(1.0 / (2 * np.pi)), None, ALU.mult, None, [b_wk], [b_wk])
            CP(wki[r, :], wk[r, 1, :], [b_wk], [b_wki])
            CP(wk[r, 1, :], wki[r, :], [b_wki], [b_wk])
            STT(wk[r, 0, :], wk[r, 1, :], float(-2 * np.pi), wk[r, 0, :], ALU.mult, ALU.add, [b_wk], [b_wk])
            TS(wk[r, 1, :], wk[r, 0, :], float(np.pi), None, ALU.is_gt, None, [b_wk], [b_wk])
            STT(wk[r, 0, :], wk[r, 1, :], float(-2 * np.pi), wk[r, 0, :], ALU.mult, ALU.add, [b_wk], [b_wk])
            TS(wk[r, 1, :], wk[r, 0, :], float(-np.pi), None, ALU.is_lt, None, [b_wk], [b_wk])
            STT(wk[r, 0, :], wk[r, 1, :], float(2 * np.pi), wk[r, 0, :], ALU.mult, ALU.add, [b_wk], [b_wk])
            ACT(out[r, :], wk[r, 0, :], AF.Sin, [b_wk], [b_out])

        PH = [0]
        for l in range(depth):
            x_src, b_xsrc = (x_in, Buf("x_in")) if l == 0 else (xs, B["xs"])
            x_dst, b_xdst = (y_out, B["y"]) if l == depth - 1 else (xs, B["xs"])
            w_in = W["w_in"][l]

            with ExitStack() as st:
                NA = 2904
                WA, b_WA = sbt(st, "WA", [128, 8, NA], BF16)
                WQ, b_WQ = sbt(st, "WQ", [128, 3, 1024], BF16)
                WKV, b_WKV = sbt(st, "WKV", [128, 2, 1024], BF16)
                stage, b_stage = sbt(st, "stage", [128, 1976], F32)
                gpre, b_gpre = vec_col(st, "gpre", W["norm_mix_pre"][l], D)
                gq, b_gq = vec_col(st, "gq", W["mla_q_norm"][l], 384)
                gkv, b_gkv = vec_col(st, "gkv", W["mla_kv_norm"][l], 256)
                ngpre, b_ngpre = sbt(st, "ngpre", [128, 8], F32)
                ngq, b_ngq = sbt(st, "ngq", [128, 3], F32)
                TS(ngpre[:], gpre[:], -1.0, None, ALU.mult, None, [b_gpre], [b_ngpre])
                TS(ngq[:], gq[:], -1.0, None, ALU.mult, None, [b_gq], [b_ngq])
                MS(WA[:, :, 1216:1728], 0.0, [b_WA])
                for c in (1856, 2112, 2368):
                    MS(WA[:, :, c:c + 128], 0.0, [b_WA])

                def SC(out, in_, sc, rd, wr):
                    TS(out, in_, sc, None, ALU.mult, None, rd, wr)

                for kc in range(8):
                    LD(stage[:], w_in[kc * 128:(kc + 1) * 128, 4608:6584], [b_stage])
                    g = gpre[:, kc:kc + 1]
                    ng = ngpre[:, kc:kc + 1]
                    rd = [b_stage, b_gpre, b_ngpre]
                    SC(WA[:, kc, 0:672], stage[:, 0:672], g, rd, [b_WA])
                    SC(WA[:, kc, 672:688], stage[:, 656:672], ng, rd, [b_WA])
                    SC(WA[:, kc, 688:704], stage[:, 640:656], g, rd, [b_WA])
                    SC(WA[:, kc, 704:1216], stage[:, 672:1184], g, rd, [b_WA])
                    sq = stage[:, 672:1184].rearrange("p (h d) -> p h d", d=64)
                    dq = WA[:, kc, 1216:1728].rearrange("p (h d) -> p h d", d=64)
                    SC(dq[:, :, 0:8], sq[:, :, 8:16], ng, rd, [b_WA])
                    SC(dq[:, :, 8:16], sq[:, :, 0:8], g, rd, [b_WA])
                    for (src_o, dst_o) in ((1184, 1728), (1440, 1984), (1696, 2240)):
                        SC(WA[:, kc, dst_o:dst_o + 128], stage[:, src_o:src_o + 128], g, rd, [b_WA])
                        sk = stage[:, src_o:src_o + 128].rearrange("p (h d) -> p h d", d=64)
                        dk = WA[:, kc, dst_o + 128:dst_o + 256].rearrange("p (h d) -> p h d", d=64)
                        SC(dk[:, :, 0:8], sk[:, :, 8:16], ng, rd, [b_WA])
                        SC(dk[:, :, 8:16], sk[:, :, 0:8], g, rd, [b_WA])
                    SC(WA[:, kc, 2496:2624], stage[:, 1312:1440], g, rd, [b_WA])
                    SC(WA[:, kc, 2624:2752], stage[:, 1568:1696], g, rd, [b_WA])
                    SC(WA[:, kc, 2752:2880], stage[:, 1824:1952], g, rd, [b_WA])
                    SC(WA[:, kc, 2880:2904], stage[:, 1952:1976], g, rd, [b_WA])
                for kc in range(3):
                    LD(stage[:, 0:768], W["mla_w_uq"][l][kc * 128:(kc + 1) * 128, :], [b_stage])
                    g = gq[:, kc:kc + 1]
                    ng = ngq[:, kc:kc + 1]
                    rd = [b_stage, b_gq, b_ngq]
                    s3 = stage[:, 0:768].rearrange("p (h d) -> p h d", d=96)
                    d3 = WQ[:, kc, 0:768].rearrange("p (h d) -> p h d", d=96)
                    SC(d3[:, :, 0:32], s3[:, :, 64:96], g, rd, [b_WQ])
                    SC(d3[:, :, 32:96], s3[:, :, 0:64], g, rd, [b_WQ])
                    r3 = WQ[:, kc, 768:1024].rearrange("p (h d) -> p h d", d=32)
                    SC(r3[:, :, 0:16], s3[:, :, 80:96], ng, rd, [b_WQ])
                    SC(r3[:, :, 16:32], s3[:, :, 64:80], g, rd, [b_WQ])
                for kc in range(2):
                    LD(stage[:, 0:1024], W["mla_w_ukv"][l][kc * 128:(kc + 1) * 128, :], [b_stage])
                    g = gkv[:, kc:kc + 1]
                    rd = [b_stage, b_gkv]
                    s3 = stage[:, 0:1024].rearrange("p (h d) -> p h d", d=128)
                    SC(WKV[:, kc, 0:512].rearrange("p (h d) -> p h d", d=64), s3[:, :, 0:64], g, rd, [b_WKV])
                    SC(WKV[:, kc, 512:1024].rearrange("p (h d) -> p h d", d=64), s3[:, :, 64:128], g, rd, [b_WKV])

                gf, b_gf = vec_col(st, "gf", W["norm_ffn_pre"][l], D)
                wst, b_wst = sbt(st, "wst", [128, 5632], F32)
                wsb, b_wsb = sbt(st, "wsb", [128, 5632], BF16)
                for kc in range(8):
                    LD(wst[:, 0:4608], w_in[kc * 128:(kc + 1) * 128, 0:4608], [b_wst])
                    SC(wsb[:, 0:4608], wst[:, 0:4608], gpre[:, kc:kc + 1], [b_wst, b_gpre], [b_wsb])
                    ST(wgin_b[kc * 128:(kc + 1) * 128, :], wsb[:, 0:4608], [b_wsb], [B["wgin_b"]])
                for kc in range(8):
                    LD(wst[:, :], W["ffn_w_up"][l][kc * 128:(kc + 1) * 128, :], [b_wst])
                    SC(wsb[:, :], wst[:, :], gf[:, kc:kc + 1], [b_wst, b_gf], [b_wsb])
                    ST(wup_b[kc * 128:(kc + 1) * 128, :], wsb[:, :], [b_wsb], [B["wup_b"]])
                for kc in range(22):
                    LD(wst[:, 0:1024], W["ffn_w_down"][l][kc * 128:(kc + 1) * 128, :], [b_wst])
                    CP(wsb[:, 0:1024], wst[:, 0:1024], [b_wst], [b_wsb])
                    ST(wdn_b[kc * 128:(kc + 1) * 128, :], wsb[:, 0:1024], [b_wsb], [B["wdn_b"]])

                xt4, b_xt4 = sbt(st, "xt4", [128, 4, D], F32)
                hT, b_hT = sbt(st, "hT", [128, 8, 512], BF16)
                hn, b_hn = sbt(st, "hn", [128, D], BF16)
                junk, b_junk = sbt(st, "junk", [128, D], BF16)
                rs, b_rs = sbt(st, "rs", [128, 8], F32)
                posi, b_posi = sbt(st, "posi", [128, 512], I32)
                posf, b_posf = sbt(st, "posf", [128, 512], F32)
                wk, b_wk = sbt(st, "wk", [128, 2, 512], F32)
                wki, b_wki = sbt(st, "wki", [128, 512], I32)
                Cm, b_Cm = sbt(st, "Cm", [128, 512], F32)
                Sm, b_Sm = sbt(st, "Sm", [128, 512], F32)
                Cn, b_Cn = sbt(st, "Cn", [128, 512], F32)
                Sn, b_Sn = sbt(st, "Sn", [128, 512], F32)
                cqT, b_cqT = sbt(st, "cqT", [128, 3, 512], BF16)
                ckvT, b_ckvT = sbt(st, "ckvT", [128, 2, 512], BF16)
                sqb, b_sqb = sbt(st, "sqb", [128, 3, 512], BF16)
                rq, b_rq = sbt(st, "rq", [128, 512], F32)
                rkv, b_rkv = sbt(st, "rkv", [128, 512], F32)
                rkt, b_rkt = sbt(st, "rkt", [128, 4], F32)
                CR, b_CR = sbt(st, "CR", [128, 512], F32)
                SR, b_SR = sbt(st, "SR", [128, 512], F32)
                t1, b_t1 = sbt(st, "t1", [128, 512], F32)
                t2, b_t2 = sbt(st, "t2", [128, 512], F32)
                ob = [sbt(st, "ob%d" % i, [128, 512], BF16) for i in range(3)]
                vt = [sbt(st, "vt%d" % i, [128, 8, 65], BF16) for i in range(2)]
                vs2 = [sbt(st, "vs2_%d" % i, [128, 2, 2, 65], BF16) for i in range(2)]
                gt = [sbt(st, "gt%d" % i, [128, 24], F32) for i in range(2)]
                for (t_, b_) in vt:
                    MS(t_[:], 1.0, [b_])
                for (t_, b_) in vs2:
                    MS(t_[:], 1.0, [b_])
                obi = [0]

                def next_ob():
                    obi[0] += 1
                    return ob[obi[0] % 3]

                for s in range(NS):
                    s0 = s * 512
                    norm_transpose(x_src, s0, xt4, b_xt4, hT, b_hT, rs, b_rs, hn, b_hn, junk, b_junk, b_xsrc,
                                   banks[7], bb[7])
                    LD(posi[:], pos_in[0:1, s0:s0 + 512].partition_broadcast(128), [b_posi])
                    CP(posf[:], posi[:], [b_posi], [b_posf])
                    sincos("cm", posf, b_posf, 0, 128, float(np.pi / 2), Cm, b_Cm, wk, b_wk, wki, b_wki)
                    sincos("sm", posf, b_posf, 0, 128, 0.0, Sm, b_Sm, wk, b_wk, wki, b_wki)
                    sincos("cn", posf, b_posf, 1, 128, float(np.pi / 2), Cn, b_Cn, wk, b_wk, wki, b_wki)
                    sincos("sn", posf, b_posf, 1, 128, 0.0, Sn, b_Sn, wk, b_wk, wki, b_wki)

                    def proj(bank_i, c0, m, rd_extra=()):
                        for kc in range(8):
                            MM(banks[bank_i][0:m, :], WA[:, kc, c0:c0 + m], hT[:, kc, :], kc == 0, kc == 7,
                               [b_WA, b_hT], [bb[bank_i]])

                    for c in range(3):
                        proj(c % 2, c * 128, 128)
                        CP(cqT[:, c, :], banks[c % 2][:, :], [bb[c % 2]], [b_cqT], eng="act" if False else "dve")
                        ACT(sqb[:, c, :], banks[c % 2][:, :], AF.Square, [bb[c % 2]], [b_sqb])
                    for c in range(3):
                        MM(banks[2][:, :], ones_b[:, :], sqb[:, c, :], c == 0, c == 2, [b_ones, b_sqb], [bb[2]])
                    rsqrt_ip(rq[:], b_rq, 1.0 / 384, src=banks[2][:, :], b_src=bb[2])
                    TT(CR[:], Cm[:], rq[:], ALU.mult, [b_Cm, b_rq], [b_CR])
                    TT(SR[:], Sm[:], rq[:], ALU.mult, [b_Sm, b_rq], [b_SR])
                    for h in range(8):
                        bq = 3 + (h % 2) * 2
                        for kc in range(3):
                            MM(banks[bq][0:96, :], WQ[:, kc, h * 96:(h + 1) * 96], cqT[:, kc, :], kc == 0, kc == 2,
                               [b_WQ, b_cqT], [bb[bq]])
                        for kc in range(3):
                            MM(banks[bq + 1][0:32, :], WQ[:, kc, 768 + h * 32:768 + (h + 1) * 32], cqT[:, kc, :],
                               kc == 0, kc == 2, [b_WQ, b_cqT], [bb[bq + 1]])
                        o_t, o_b = next_ob()
                        TT(t1[0:96, :], banks[bq][0:96, :], CR[0:96, :], ALU.mult, [bb[bq], b_CR], [b_t1])
                        TT(t2[0:32, :], banks[bq + 1][0:32, :], SR[0:32, :], ALU.mult, [bb[bq + 1], b_SR], [b_t2])
                        TT(o_t[0:32, :], t1[0:32, :], t2[0:32, :], ALU.add, [b_t1, b_t2], [o_b], eng="pool")
                        CP(o_t[32:64, :], t1[32:64, :], [b_t1], [o_b], eng="pool")
                        CP(o_t[64:96, :], t1[64:96, :], [b_t1], [o_b], eng="pool")
                        ST(qmT[h, :, s0:s0 + 512], o_t[0:96, :], [o_b], [B["qmT"]])
                    for c in range(2):
                        proj(c, 384 + c * 128, 128)
                        CP(ckvT[:, c, :], banks[c][:, :], [bb[c]], [b_ckvT])
                        ACT(sqb[:, c, :], banks[c][:, :], AF.Square, [bb[c]], [b_sqb])
                    for c in range(2):
                        MM(banks[2][:, :], ones_b[:, :], sqb[:, c, :], c == 0, c == 1, [b_ones, b_sqb], [bb[2]])
                    rsqrt_ip(rkv[:], b_rkv, 1.0 / 256, src=banks[2][:, :], b_src=bb[2])
                    for t in range(4):
                        for c in range(2):
                            MM(banks[2][:, t:t + 1], sqb[:, c, t * 128:(t + 1) * 128], ones_b[:, 0:1],
                               (t == 0 and c == 0), c == 1, [b_ones, b_sqb], [bb[2]])
                    rsqrt_ip(rkt[:], b_rkt, 1.0 / 256, src=banks[2][:, 0:4], b_src=bb[2])
                    for g2 in range(4):
                        bk = 3 + (g2 % 2)
                        for kc in range(2):
                            MM(banks[bk][:, :], WKV[:, kc, g2 * 128:(g2 + 1) * 128], ckvT[:, kc, :], kc == 0, kc == 1,
                               [b_WKV, b_ckvT], [bb[bk]])
                        o_t, o_b = next_ob()
                        TT(o_t[:, :], banks[bk][:, :], rkv[:, :], ALU.mult, [bb[bk], b_rkv], [o_b])
                        ST(kmT[2 * g2, 32:96, s0:s0 + 512], o_t[0:64, :], [o_b], [B["kmT"]])
                        ST(kmT[2 * g2 + 1, 32:96, s0:s0 + 512], o_t[64:128, :], [o_b], [B["kmT"]])
                    for t in range(4):
                        bv = 5 + (t % 2)
                        for kc in range(2):
                            MM(banks[bv][:, :], ckvT[:, kc, t * 128:(t + 1) * 128], WKV[:, kc, 512:1024], kc == 0,
                               kc == 1, [b_WKV, b_ckvT], [bb[bv]])
                        v_t, v_b = vt[t % 2]
                        TS(v_t[:, :, 0:64], banks[bv][:, :].rearrange("p (h d) -> p h d", d=64), rkt[:, t:t + 1], None,
                           ALU.mult, None, [bb[bv], b_rkt], [v_b])
                        ST(vm[s0 + t * 128:s0 + (t + 1) * 128, :, :], v_t[:], [v_b], [B["vm"]])
                    proj(0, 640, 32)
                    proj(1, 672, 32)
                    o_t, o_b = next_ob()
                    TT(t1[0:32, :], banks[0][0:32, :], Cm[0:32, :], ALU.mult, [bb[0], b_Cm], [b_t1])
                    TT(t2[0:32, :], banks[1][0:32, :], Sm[0:32, :], ALU.mult, [bb[1], b_Sm], [b_t2])
                    TT(o_t[0:32, :], t1[0:32, :], t2[0:32, :], ALU.add, [b_t1, b_t2], [o_b], eng="pool")
                    for h in range(8):
                        ST(kmT[h, 0:32, s0:s0 + 512], o_t[0:32, :], [o_b], [B["kmT"]])
                    def rope_group(c0, r0, dests):
                        bq = 3
                        proj(bq, c0, 128)
                        proj(bq + 1, r0, 128)
                        o_t, o_b = next_ob()
                        TT(t1[:, :], banks[bq][:, :], Cn[:, :], ALU.mult, [bb[bq], b_Cn], [b_t1])
                        TT(t2[:, :], banks[bq + 1][:, :], Sn[:, :], ALU.mult, [bb[bq + 1], b_Sn], [b_t2])
                        TT(o_t[:, :], t1[:, :], t2[:, :], ALU.add, [b_t1, b_t2], [o_b], eng="pool")
                        for (dst, bname, lo) in dests:
                            ST(dst, o_t[lo:lo + 64, :], [o_b], [B[bname]])
                    for pr in range(4):
                        h0 = 2 * pr
                        rope_group(704 + pr * 128, 1216 + pr * 128,
                                   [(qnT[h0 // 4, :, h0 % 4, s0:s0 + 512], "qnT", 0),
                                    (qnT[(h0 + 1) // 4, :, (h0 + 1) % 4, s0:s0 + 512], "qnT", 64)])
                    for (c0, dst, nm) in ((1728, kcmpT, "kcmpT"), (1984, kslcT, "kslcT"), (2240, kwinT, "kwinT")):
                        rope_group(c0, c0 + 128, [(dst[0, :, s0:s0 + 512], nm, 0), (dst[1, :, s0:s0 + 512], nm, 64)])
                    proj(5, 2496, 128)
                    o_t, o_b = next_ob()
                    CP(o_t[:, :], banks[5][:, :], [bb[5]], [o_b], eng="act" if False else "dve")
                    ST(vcmpT[0, :, s0:s0 + 512], o_t[0:64, :], [o_b], [B["vcmpT"]])
                    ST(vcmpT[1, :, s0:s0 + 512], o_t[64:128, :], [o_b], [B["vcmpT"]])
                    for t in range(4):
                        bv = 5 + (t % 2)
                        for kc in range(8):
                            MM(banks[bv][:, 0:280], hT[:, kc, t * 128:(t + 1) * 128], WA[:, kc, 2624:2904], kc == 0,
                               kc == 7, [b_WA, b_hT], [bb[bv]])
                        v_t, v_b = vs2[t % 2]
                        CP(v_t[:, :, :, 0:64], banks[bv][:, 0:256].rearrange("p (a g d) -> p a g d", a=2, g=2),
                           [bb[bv]], [v_b])
                        g_t, g_b = gt[t % 2]
                        ACT(g_t[:, :], banks[bv][:, 256:280], AF.Sigmoid, [bb[bv]], [g_b])
                        r0, r1 = s0 + t * 128, s0 + (t + 1) * 128
                        ST(vslc[r0:r1, :, :], v_t[:, 0, :, :], [v_b], [B["vslc"]])
                        ST(vwin[r0:r1, :, :], v_t[:, 1, :, :], [v_b], [B["vwin"]])
                        ST(gat[r0:r1, :], g_t[:, :], [g_b], [B["gat"]])
                cx.barrier()
                cx.emit()
            PH[0] += 1
            if stop_after is not None and PH[0] >= stop_after:
                break

            with ExitStack() as st:
                tri, b_tri = sbt(st, "tri", [128, 128], BF16)
                trif, b_trif = sbt(st, "trif", [128, 128], F32)
                LD(trif[:], c_tri[:, :], [b_trif])
                CP(tri[:], trif[:], [b_trif], [b_tri])
                KT, b_KT = sbt(st, "KT", [96, 4, S], BF16)
                VV, b_VV = sbt(st, "VV", [128, NT, 4, 65], BF16)
                QT = [sbt(st, "QT%d" % i, [96, 4, 128], BF16) for i in range(2)]
                PT = [sbt(st, "PT%d" % i, [128, 512], BF16) for i in range(4)]
                recs = [sbt(st, "rec%d" % i, [128, 4], F32) for i in range(2)]
                yo = [sbt(st, "yo%d" % i, [128, 4, 64], BF16) for i in range(2)]
                scale_m = float(96 ** -0.5)
                LAG = 2

                def run_pipeline(items):
                    n = len(items)
                    for k in range(n + LAG):
                        if k >= LAG:
                            items[k - LAG][2]()
                        if k < n:
                            items[k][0]()
                            items[k][1]()

                for hg in range(2):
                    for h in range(4):
                        LD(KT[:, h, :], kmT[hg * 4 + h, :, :], [b_KT], reads=[B["kmT"]])
                    for jt in range(NT):
                        LD(VV[:, jt, :, :], vm[jt * 128:(jt + 1) * 128, hg * 4:(hg + 1) * 4, :], [b_VV],
                           reads=[B["vm"]])
                    items = []
                    for i in range(NT):
                        for j in range(i + 1):
                            k = len(items)

                            def fS(i=i, j=j, k=k):
                                q_t, q_b = QT[i % 2]
                                if j == 0:
                                    for h in range(4):
                                        LD(q_t[:, h, :], qmT[hg * 4 + h, :, i * 128:(i + 1) * 128], [q_b],
                                           reads=[B["qmT"]])
                                sb_i = k % 3
                                for h in range(4):
                                    MM(banks[sb_i][:, h * 128:(h + 1) * 128], KT[:, h, j * 128:(j + 1) * 128],
                                       q_t[:, h, :], h == 0, h == 3, [b_KT, q_b], [bb[sb_i]])

                            def fE(i=i, j=j, k=k):
                                sb_i = k % 3
                                p_t, p_b = PT[k % 4]
                                ACT(p_t[:, :], banks[sb_i][:, :], AF.Exp, [bb[sb_i]], [p_b], scale=scale_m)
                                if j == i:
                                    TT(p_t[:, :].rearrange("p (h q) -> p h q", h=4),
                                       p_t[:, :].rearrange("p (h q) -> p h q", h=4), bcast_mid(tri[:, :], 4), ALU.mult,
                                       [p_b, b_tri], [p_b], eng="pool")

                            def fP(i=i, j=j, k=k):
                                p_t, p_b = PT[k % 4]
                                ab = 4 + (i % 2)
                                for h in range(4):
                                    MM(banks[ab][:, h * 65:(h + 1) * 65], p_t[:, h * 128:(h + 1) * 128],
                                       VV[:, j, h, :], (j == 0 and h == 0), j == i, [p_b, b_VV], [bb[ab]])
                                if j == i:
                                    rec, b_rec = recs[i % 2]
                                    acc = banks[ab][:, 0:260].rearrange("p (h c) -> p h c", c=65)
                                    TS(rec[:, :], acc[:, :, 64], 1e-30, None, ALU.max, None, [bb[ab]], [b_rec])
                                    cx.op("dve", lambda e, rec=rec: e.reciprocal(out=rec[:, :], in_=rec[:, :]),
                                          [b_rec], [b_rec])
                                    y_t, y_b = yo[i % 2]
                                    TT(y_t[:, :, :], acc[:, :, 0:64], bcast_last(rec[:, :], 64), ALU.mult,
                                       [bb[ab], b_rec], [y_b])
                                    ST(ymla[i * 128:(i + 1) * 128, hg * 256:(hg + 1) * 256],
                                       y_t[:, :, :].rearrange("p h d -> p (h d)"), [y_b], [B["ymla"]])

                            items.append((fS, fE, fP))
                    run_pipeline(items)
                cx.barrier()
                cx.emit()
            PH[0] += 1
            if stop_after is not None and PH[0] >= stop_after:
                break

            with ExitStack() as st:
                cst, b_cst = sbt(st, "cst", [128, 16 * 128], F32)
                tri, b_tri = sbt(st, "tri", [128, 128], BF16)
                wlow, b_wlow = sbt(st, "wlow", [128, 128], BF16)
                maskc, b_maskc = sbt(st, "maskc", [128, 16, 128], BF16)
                emat, b_emat = sbt(st, "emat", [128, NT, 128], BF16)
                LD(cst[:, 0:128], c_tri[:, :], [b_cst])
                CP(tri[:], cst[:, 0:128], [b_cst], [b_tri])
                LD(cst[:, 0:128], c_wlow[:, :], [b_cst])
                CP(wlow[:], cst[:, 0:128], [b_cst], [b_wlow])
                LD(cst[:, :].rearrange("p (o q) -> p o q", o=16), c_maskc[:, :, :], [b_cst])
                CP(maskc[:], cst[:, :].rearrange("p (o q) -> p o q", o=16), [b_cst], [b_maskc])
                for j0 in range(0, NT, 16):
                    jn = min(16, NT - j0)
                    LD(cst[:, 0:jn * 128].rearrange("p (o q) -> p o q", o=jn), c_emat[:, j0:j0 + jn, :], [b_cst])
                    CP(emat[:, j0:j0 + jn, :], cst[:, 0:jn * 128].rearrange("p (o q) -> p o q", o=jn), [b_cst],
                       [b_emat])
                wck, b_wck = sbt(st, "wck", [64, 32, 64], BF16)
                wcv, b_wcv = sbt(st, "wcv", [64, 32, 64], BF16)
                wcs, b_wcs = sbt(st, "wcs", [64, 32, 64], F32)
                LD(wcs[:], W["nsa_cmp_w_k"][l].rearrange("(r d) e -> d r e", d=64), [b_wcs])
                CP(wck[:], wcs[:], [b_wcs], [b_wck])
                LD(wcs[:], W["nsa_cmp_w_v"][l].rearrange("(r d) e -> d r e", d=64), [b_wcs])
                CP(wcv[:], wcs[:], [b_wcs], [b_wcv])
                pkv, b_pkv = sbt(st, "pkv", [32, 2, 64], F32)
                LD(pkv[:, 0, :], W["nsa_cmp_pos_k"][l], [b_pkv])
                LD(pkv[:, 1, :], W["nsa_cmp_pos_v"][l], [b_pkv])
                posT, b_posT = sbt(st, "posT", [64, 2, 32], BF16)
                for a in range(2):
                    TR(banks[0][0:64, a * 32:(a + 1) * 32], pkv[:, a, :], identf[0:32, 0:32], [b_pkv, b_identf],
                       [bb[0]])
                CP(posT[:, :, :], banks[0][0:64, 0:64].rearrange("p (a r) -> p a r", a=2), [bb[0]], [b_posT])
                biask, b_biask = sbt(st, "biask", [64, 1], F32)
                biasv, b_biasv = sbt(st, "biasv", [1, 64], BF16)
                for r in range(32):
                    MM(banks[1][0:64, 0:1], wck[:, r, :], posT[:, 0, r:r + 1], r == 0, r == 31, [b_wck, b_posT],
                       [bb[1]])
                CP(biask[:, :], banks[1][0:64, 0:1], [bb[1]], [b_biask])
                for r in range(32):
                    MM(banks[2][0:1, 0:64], posT[:, 1, r:r + 1], wcv[:, r, :], r == 0, r == 31, [b_wcv, b_posT],
                       [bb[2]])
                CP(biasv[:, :], banks[2][0:1, 0:64], [bb[2]], [b_biasv])
                KS, b_KS = sbt(st, "KS", [64, 2, S], BF16)
                KW, b_KW = sbt(st, "KW", [64, 2, S], BF16)
                kcT, b_kcT = sbt(st, "kcT", [64, 2, 512], BF16)
                VC, b_VC = sbt(st, "VC", [128, 4, 2, CW], BF16)
                MS(kcT[:], 0.0, [b_kcT])
                MS(VC[:], 0.0, [b_VC])
                for g in range(2):
                    LD(KS[:, g, :], kcmpT[g, :, :], [b_KS], reads=[B["kcmpT"]])
                    LD(KW[:, g, :], vcmpT[g, :, :], [b_KW], reads=[B["vcmpT"]])
                LD(cst[:, 0:512].rearrange("p (t m) -> p t m", t=4), c_ovl[:, :, :], [b_cst])
                for g in range(2):
                    MS(VC[:, :, g, 64:65], 1.0, [b_VC])
                    CP(VC[:, :, g, 65:CW], cst[:, 0:512].rearrange("p (t m) -> p t m", t=4), [b_cst], [b_VC])
                for g in range(2):
                    n0 = 0
                    while n0 < NCMP:
                        nn = min(512, NCMP - n0)
                        for r in range(32):
                            src = KS[:, g, r + 16 * n0:r + 16 * n0 + 16 * (nn - 1) + 1]
                            rhs = bass.AP(src.tensor, src.offset, [list(src.ap[0]), [16, nn]])
                            MM(banks[3][0:64, 0:nn], wck[:, r, :], rhs, r == 0, r == 31, [b_wck, b_KS], [bb[3]])
                        ACT(kcT[:, g, n0:n0 + nn], banks[3][0:64, 0:nn], AF.Identity, [bb[3], b_biask], [b_kcT],
                            bias=biask[:, 0:1], scale=1.0)
                        n0 += nn
                    for jt in range(NCT):
                        nn = min(128, NCMP - jt * 128)
                        for r in range(32):
                            o = r + 16 * jt * 128
                            src = KW[:, g, o:o + 16 * (nn - 1) + 1]
                            lhsT = bass.AP(src.tensor, src.offset, [list(src.ap[0]), [16, nn]])
                            MM(banks[4][0:nn, 0:64], lhsT, wcv[:, r, :], r == 0, False, [b_wcv, b_KW], [bb[4]])
                        MM(banks[4][0:nn, 0:64], ones_b[0:1, 0:nn], biasv[0:1, :], False, True, [b_ones, b_biasv],
                           [bb[4]])
                        CP(VC[0:nn, jt, g, 0:64], banks[4][0:nn, 0:64], [bb[4]], [b_VC])
                VS, b_VS = sbt(st, "VS", [128, NT, 2, 65], BF16)
                VW, b_VW = sbt(st, "VW", [128, NT, 2, 65], BF16)
                for g in range(2):
                    LD(KS[:, g, :], kslcT[g, :, :], [b_KS], reads=[B["kslcT"]])
                    LD(KW[:, g, :], kwinT[g, :, :], [b_KW], reads=[B["kwinT"]])
                for jt in range(NT):
                    LD(VS[:, jt, :, :], vslc[jt * 128:(jt + 1) * 128, :, :], [b_VS], reads=[B["vslc"]])
                    LD(VW[:, jt, :, :], vwin[jt * 128:(jt + 1) * 128, :, :], [b_VW], reads=[B["vwin"]])
                QN = [sbt(st, "QN%d" % i, [64, 2, 4, 128], BF16) for i in range(2)]
                GA = [sbt(st, "GA%d" % i, [128, 24], F32) for i in range(2)]
                AM = [sbt(st, "AM%d" % i, [128, 128], F32) for i in range(2)]
                PT = [sbt(st, "PTn%d" % i, [128, 512], BF16) for i in range(4)]
                imps = [sbt(st, "imp%d" % i, [128, 128], F32) for i in range(2)]
                impws = [sbt(st, "impw%d" % i, [128, 128], F32) for i in range(2)]
                top16s = [sbt(st, "top16_%d" % i, [128, 16], F32) for i in range(2)]
                negms = [sbt(st, "negm%d" % i, [128, 128], BF16) for i in range(2)]
                negTs = [sbt(st, "negT%d" % i, [128, 128], BF16) for i in range(2)]
                recs = [sbt(st, "recn%d" % i, [128, 4], F32) for i in range(6)]
                wgts = [sbt(st, "wgt%d" % i, [128, 4], F32) for i in range(6)]
                yaccs = [sbt(st, "yacc%d" % i, [128, 8, 64], F32) for i in range(2)]
                ytmps = [sbt(st, "ytmp%d" % i, [128, 4, 64], F32) for i in range(2)]
                yo = [sbt(st, "yon%d" % i, [128, 512], BF16) for i in range(2)]
                scale_n = 0.125
                LAG = 2
                rot = [0]

                def small():
                    rot[0] += 1
                    return recs[rot[0] % 6], wgts[rot[0] % 6]

                def exp_mask(k, masks):
                    sb_i = k % 3
                    p_t, p_b = PT[k % 4]
                    ACT(p_t[:, :], banks[sb_i][:, :], AF.Exp, [bb[sb_i]], [p_b], scale=scale_n)
                    for (m_ap, m_b) in masks:
                        TT(p_t[:, :].rearrange("p (h q) -> p h q", h=4),
                           p_t[:, :].rearrange("p (h q) -> p h q", h=4), bcast_mid(m_ap, 4), ALU.mult,
                           [p_b, m_b], [p_b], eng="pool")

                def branch_out(i, g, bank_i, br):
                    acc = banks[bank_i][:, 0:260].rearrange("p (h c) -> p h c", c=65)
                    acc_b = bb[bank_i]
                    (rec, b_rec), (wgt, b_wgt) = small()
                    g_t, g_b = GA[i % 2]
                    gv = g_t[:, :].rearrange("p (h b) -> p h b", b=3)
                    yacc, b_yacc = yaccs[i % 2]
                    ytmp, b_ytmp = ytmps[(2 * i + g + br) % 2]
                    TS(rec[:, :], acc[:, :, 64], 1e-30, None, ALU.max, None, [acc_b], [b_rec])
                    cx.op("dve", lambda e, rec=rec: e.reciprocal(out=rec[:, :], in_=rec[:, :]), [b_rec], [b_rec])
                    TT(wgt[:, :], rec[:, :], gv[:, g * 4:(g + 1) * 4, br], ALU.mult, [b_rec, g_b], [b_wgt])
                    TT(ytmp[:, :, :], acc[:, :, 0:64], bcast_last(wgt[:, :], 64), ALU.mult, [acc_b, b_wgt], [b_ytmp])
                    TT(yacc[:, g * 4:(g + 1) * 4, :], yacc[:, g * 4:(g + 1) * 4, :], ytmp[:, :, :], ALU.add,
                       [b_yacc, b_ytmp], [b_yacc], eng="pool")

                def cmp_final(i, g):
                    (rec, b_rec), (wgt, b_wgt) = small()
                    par = (2 * i + g) % 2
                    imp, b_imp = imps[par]
                    impw, b_impw = impws[par]
                    top16, b_top16 = top16s[par]
                    negm, b_negm = negms[par]
                    g_t, g_b = GA[i % 2]
                    a_t, a_b = AM[i % 2]
                    gv = g_t[:, :].rearrange("p (h b) -> p h b", b=3)
                    yacc, b_yacc = yaccs[i % 2]
                    for h in range(4):
                        bk = 3 + h // 2
                        a0 = (h % 2) * CW
                        TS(rec[:, h:h + 1], banks[bk][:, a0 + 64:a0 + 65], 1e-30, None, ALU.max, None, [bb[bk]],
                           [b_rec])
                    cx.op("dve", lambda e, rec=rec: e.reciprocal(out=rec[:, :], in_=rec[:, :]), [b_rec], [b_rec])
                    for h in range(4):
                        bk = 3 + h // 2
                        a0 = (h % 2) * CW
                        if h == 0:
                            TS(imp[:, :], banks[bk][:, a0 + 65:a0 + CW], rec[:, 0:1], None, ALU.mult, None,
                               [bb[bk], b_rec], [b_imp])
                        else:
                            STT(imp[:, :], banks[bk][:, a0 + 65:a0 + CW], rec[:, h:h + 1], imp[:, :], ALU.mult,
                                ALU.add, [bb[bk], b_rec, b_imp], [b_imp])
                    TT(wgt[:, :], rec[:, :], gv[:, g * 4:(g + 1) * 4, 0], ALU.mult, [b_rec, g_b], [b_wgt])
                    for h in range(4):
                        bk = 3 + h // 2
                        a0 = (h % 2) * CW
                        TS(yacc[:, g * 4 + h, :], banks[bk][:, a0:a0 + 64], wgt[:, h:h + 1], None, ALU.mult, None,
                           [bb[bk], b_wgt], [b_yacc])
                    TT(imp[:, :], imp[:, :], a_t[:, :], ALU.add, [b_imp, a_b], [b_imp])
                    cx.op("dve", lambda e, top16=top16, imp=imp: e.max(out=top16[:, 0:8], in_=imp[:, :]),
                          [b_imp], [b_top16])
                    cx.op("dve", lambda e, top16=top16, imp=imp, impw=impw:
                          e.match_replace(out=impw[:, :], in_to_replace=top16[:, 0:8], in_values=imp[:, :],
                                          imm_value=-3.0), [b_imp, b_top16], [b_impw])
                    cx.op("dve", lambda e, top16=top16, impw=impw: e.max(out=top16[:, 8:16], in_=impw[:, :]),
                          [b_impw], [b_top16])
                    TS(impw[:, :], imp[:, :], top16[:, 15:16], -1.0, ALU.is_ge, ALU.add, [b_imp, b_top16], [b_impw])
                    TS(negm[:, :], impw[:, :], -NEG, None, ALU.mult, None, [b_impw], [b_negm])

                def neg_transpose(i, g):
                    par = (2 * i + g) % 2
                    negm, b_negm = negms[par]
                    negT, b_negT = negTs[par]
                    pb = banks[7][:, :].bitcast(BF16)
                    TR(pb[:, 0:128], negm[:, :], identb[:, :], [b_negm, b_identb], [bb[7]])
                    CP(negT[:, :], pb[:, 0:128], [bb[7]], [b_negT])

                items = []
                for i in range(NT):
                    for g in range(2):
                        jl = i // 16
                        j_lo = max(0, i - 4)
                        seq = [("c", jt) for jt in range(jl + 1)] + [("w", j) for j in range(j_lo, i + 1)] + \
                              [("s", j) for j in range(i + 1)]
                        for (kind, j) in seq:
                            k = len(items)

                            def fS(i=i, g=g, kind=kind, j=j, k=k):
                                q_t, q_b = QN[i % 2]
                                if g == 0 and kind == "c" and j == 0:
                                    g_t, g_b = GA[i % 2]
                                    a_t, a_b = AM[i % 2]
                                    for gg in range(2):
                                        LD(q_t[:, gg, :, :], qnT[gg, :, :, i * 128:(i + 1) * 128], [q_b],
                                           reads=[B["qnT"]])
                                    LD(g_t[:, :], gat[i * 128:(i + 1) * 128, :], [g_b], reads=[B["gat"]])
                                    LD(a_t[:, :], c_amask[i, :, :], [a_b])
                                qrhs = q_t[:, g, :, :]
                                sb_i = k % 3
                                out3 = banks[sb_i][:, :].rearrange("p (h q) -> p h q", h=4)
                                if kind == "c":
                                    MM(out3, kcT[:, g, j * 128:(j + 1) * 128], qrhs, True, True, [b_kcT, q_b],
                                       [bb[sb_i]])
                                elif kind == "w":
                                    MM(out3, KW[:, g, j * 128:(j + 1) * 128], qrhs, True, True, [b_KW, q_b],
                                       [bb[sb_i]])
                                else:
                                    if j == 0:
                                        neg_transpose(i, g)
                                    negT, b_negT = negTs[(2 * i + g) % 2]
                                    MM(out3, KS[:, g, j * 128:(j + 1) * 128], qrhs, True, False, [b_KS, q_b],
                                       [bb[sb_i]])
                                    MM(out3, emat[:, j, :], bcast_mid(negT[:, :], 4), False, True,
                                       [b_emat, b_negT], [bb[sb_i]])

                            def fE(i=i, g=g, kind=kind, j=j, k=k):
                                masks = []
                                if kind == "c":
                                    if j == i // 16:
                                        masks.append((maskc[:, i % 16, :], b_maskc))
                                else:
                                    if j == i:
                                        masks.append((tri[:, :], b_tri))
                                    if kind == "w" and j == i - 4:
                                        masks.append((wlow[:, :], b_wlow))
                                exp_mask(k, masks)

                            def fP(i=i, g=g, kind=kind, j=j, k=k):
                                p_t, p_b = PT[k % 4]
                                if kind == "c":
                                    jl = i // 16
                                    for h in range(4):
                                        bk = 3 + h // 2
                                        MM(banks[bk][:, (h % 2) * CW:(h % 2 + 1) * CW], p_t[:, h * 128:(h + 1) * 128],
                                           VC[:, j, g, :], (j == 0 and h % 2 == 0), j == jl, [p_b, b_VC], [bb[bk]])
                                    if j == jl:
                                        cmp_final(i, g)
                                elif kind == "w":
                                    j_lo = max(0, i - 4)
                                    for h in range(4):
                                        MM(banks[6][:, h * 65:(h + 1) * 65], p_t[:, h * 128:(h + 1) * 128],
                                           VW[:, j, g, :], (j == j_lo and h == 0), j == i, [p_b, b_VW], [bb[6]])
                                    if j == i:
                                        branch_out(i, g, 6, 2)
                                else:
                                    for h in range(4):
                                        MM(banks[5][:, h * 65:(h + 1) * 65], p_t[:, h * 128:(h + 1) * 128],
                                           VS[:, j, g, :], (j == 0 and h == 0), j == i, [p_b, b_VS], [bb[5]])
                                    if j == i:
                                        branch_out(i, g, 5, 1)
                                        if g == 1:
                                            yacc, b_yacc = yaccs[i % 2]
                                            y_t, y_b = yo[i % 2]
                                            CP(y_t[:, :], yacc[:, :, :].rearrange("p h d -> p (h d)"), [b_yacc], [y_b])
                                            ST(ynsa[i * 128:(i + 1) * 128, :], y_t[:, :], [y_b], [B["ynsa"]])

                            items.append((fS, fE, fP))
                n_items = len(items)
                for k in range(n_items + LAG):
                    if k >= LAG:
                        items[k - LAG][2]()
                    if k < n_items:
                        items[k][0]()
                        items[k][1]()
                cx.barrier()
                cx.emit()
            PH[0] += 1
            if stop_after is not None and PH[0] >= stop_after:
                break

            with ExitStack() as st:
                WB, b_WB = sbt(st, "WB", [128, 12, D], BF16)
                WO, b_WO = sbt(st, "WO", [128, 8, D], BF16)
                wst, b_wst = sbt(st, "wstT", [128, D], F32)
                for bi, nm in enumerate(("w_branch_conv", "w_branch_mla", "w_branch_nsa")):
                    for kc in range(4):
                        LD(wst[:], W[nm][l][kc * 128:(kc + 1) * 128, :], [b_wst])
                        CP(WB[:, bi * 4 + kc, :], wst[:], [b_wst], [b_WB])
                for kc in range(8):
                    LD(wst[:], W["w_out"][l][kc * 128:(kc + 1) * 128, :], [b_wst])
                    CP(WO[:, kc, :], wst[:], [b_wst], [b_WO])
                gpost, b_gpost = sbt(st, "gpost", [128, D], F32)
                gfpost, b_gfpost = sbt(st, "gfpost", [128, D], F32)
                LD(gpost[:], W["norm_mix_post"][l:l + 1, :].partition_broadcast(128), [b_gpost])
                LD(gfpost[:], W["norm_ffn_post"][l:l + 1, :].partition_broadcast(128), [b_gfpost])
                cw, b_cw = sbt(st, "cw", [128, 4, 3], F32)
                fw, b_fw = sbt(st, "fw", [128, 22, 3], F32)
                fb, b_fb = sbt(st, "fb", [128, 22], F32)
                for k in range(3):
                    for c in range(4):
                        LD(cw[:, c, k:k + 1], W["conv_w"][l][k, c * 128:(c + 1) * 128].rearrange("(p o) -> p o", o=1),
                           [b_cw])
                    for c in range(22):
                        LD(fw[:, c, k:k + 1],
                           W["ffn_conv_w"][l][k, c * 128:(c + 1) * 128].rearrange("(p o) -> p o", o=1), [b_fw])
                for c in range(22):
                    LD(fb[:, c:c + 1], W["ffn_conv_b"][l][c * 128:(c + 1) * 128].rearrange("(p o) -> p o", o=1), [b_fb])
                xt4, b_xt4 = sbt(st, "xt4T", [128, 4, D], F32)
                hT, b_hT = sbt(st, "hTT", [128, 8, 512], BF16)
                hns = [sbt(st, "hnT%d" % i, [128, D], BF16) for i in range(2)]
                junk, b_junk = sbt(st, "junkT", [128, D], BF16)
                rs, b_rs = sbt(st, "rsT", [128, 16], F32)
                WS = [sbt(st, "WS%d" % i, [128, 8, 256], BF16) for i in range(3)]
                WD = [sbt(st, "WD%d" % i, [128, D], BF16) for i in range(2)]
                gT, b_gT = sbt(st, "gT", [128, 22, 512], BF16)
                cin, b_cin = gT, b_gT
                uu, b_uu = sbt(st, "uu", [128, 4, 514], F32)
                ycT, b_ycT = sbt(st, "ycT", [128, 4, 512], BF16)
                ymT, b_ymT = sbt(st, "ymT", [128, 4, 512], BF16)
                ynT, b_ynT = sbt(st, "ynT", [128, 4, 512], BF16)
                ytks = [sbt(st, "ytk%d" % i, [128, 512], BF16) for i in range(2)]
                gsbs = [[sbt(st, "gsb%d_%d" % (i, k), [128, 512], BF16) for k in range(3)] for i in range(2)]
                zA, b_zA = sbt(st, "zA", [128, 512], F32)
                zB, b_zB = sbt(st, "zB", [128, 512], F32)
                mT, b_mT = sbt(st, "mT", [128, 8, 512], BF16)
                aas = [sbt(st, "aa%d" % i, [128, 514], F32) for i in range(2)]
                halo, b_halo = sbt(st, "halo", [128, 22, 2], F32)
                z1s = [sbt(st, "z1_%d" % i, [128, 512], F32) for i in range(2)]
                z2s = [sbt(st, "z2_%d" % i, [128, 512], F32) for i in range(2)]
                sgs = [sbt(st, "sg%d" % i, [128, 512], F32) for i in range(2)]
                xos = [sbt(st, "xo%d" % i, [128, D], F32) for i in range(2)]
                MS(uu[:], 0.0, [b_uu])
                MS(halo[:], 0.0, [b_halo])
                wsi = [0]

                def load_ws(src, c0):
                    wsi[0] += 1
                    w_t, w_b = WS[wsi[0] % 3]
                    LD(w_t[:, :, :], src[:, c0:c0 + 256].rearrange("(k p) c -> p k c", p=128), [w_b],
                       reads=[B["wgin_b"], B["wup_b"]])
                    return w_t, w_b

                pnr = [0]

                def post_norm_residual(t, gtile, b_gt, first_bank):
                    pnr[0] += 1
                    xo, b_xo = xos[pnr[0] % 2]
                    c0 = 4 + 2 * (pnr[0] % 4)
                    MS(rs[:, c0:c0 + 2], 0.0, [b_rs], eng="dve")
                    ACT(junk[:, 0:512], banks[first_bank][:, :], AF.Square, [bb[first_bank]], [b_junk, b_rs],
                        accum_out=rs[:, c0:c0 + 1])
                    ACT(junk[:, 512:1024], banks[first_bank + 1][:, :], AF.Square, [bb[first_bank + 1]],
                        [b_junk, b_rs], accum_out=rs[:, c0 + 1:c0 + 2])
                    TT(rs[:, c0:c0 + 1], rs[:, c0:c0 + 1], rs[:, c0 + 1:c0 + 2], ALU.add, [b_rs], [b_rs])
                    rsqrt_ip(rs[:, c0:c0 + 1], b_rs, 1.0 / D)
                    for hf in range(2):
                        STT(xo[:, hf * 512:(hf + 1) * 512], banks[first_bank + hf][:, :], rs[:, c0:c0 + 1],
                            gtile[:, hf * 512:(hf + 1) * 512], ALU.mult, ALU.mult,
                            [bb[first_bank + hf], b_rs, b_gt], [b_xo])
                    TT(xt4[:, t, :], xt4[:, t, :], xo[:, :], ALU.add, [b_xt4, b_xo], [b_xt4], eng="pool")

                def norm_tr(t, tb):
                    hn, b_hn = hns[t % 2]
                    MS(rs[:, t:t + 1], 0.0, [b_rs], eng="dve")
                    rms_tile(xt4[:, t, :], b_xt4, rs, b_rs, t, junk[:], b_junk, D)
                    TS(hn[:], xt4[:, t, :], rs[:, t:t + 1], None, ALU.mult, None, [b_xt4, b_rs], [b_hn])
                    pb = banks[tb][:, :].bitcast(BF16)
                    for k in range(8):
                        TR(pb[:, k * 128:(k + 1) * 128], hn[:, k * 128:(k + 1) * 128], identb[:],
                           [b_hn, b_identb], [bb[tb]])
                    ACT(hT[:, :, t * 128:(t + 1) * 128], pb[:, 0:1024].rearrange("p (k t) -> p k t", k=8),
                        AF.Copy, [bb[tb]], [b_hT])

                for s in range(NS):
                    s0 = s * 512
                    for t in range(4):
                        LD(xt4[:, t, :], x_src[s0 + t * 128:s0 + (t + 1) * 128, :], [b_xt4], reads=[b_xsrc])
                    for t in range(4):
                        norm_tr(t, 6 + t % 2)
                    yi = 0
                    for (ysrc, bname, yT, b_yT) in ((ymla, "ymla", ymT, b_ymT), (ynsa, "ynsa", ynT, b_ynT)):
                        for t in range(4):
                            ytk, b_ytk = ytks[yi % 2]
                            tb = 4 + yi % 2
                            yi += 1
                            LD(ytk[:, :], ysrc[s0 + t * 128:s0 + (t + 1) * 128, :], [b_ytk], reads=[B[bname]])
                            pb = banks[tb][:, :].bitcast(BF16)
                            for k in range(4):
                                TR(pb[:, k * 128:(k + 1) * 128], ytk[:, k * 128:(k + 1) * 128], identb[:],
                                   [b_ytk, b_identb], [bb[tb]])
                            ACT(yT[:, :, t * 128:(t + 1) * 128], pb[:, 0:512].rearrange("p (k t) -> p k t", k=4),
                                AF.Copy, [bb[tb]], [b_yT])
                    for c2 in range(6):
                        w_t, w_b = load_ws(wgin_b, 3072 + c2 * 256)
                        for c in range(2):
                            bk = (c2 * 2 + c) % 4
                            for kc in range(8):
                                MM(banks[bk][:, :], w_t[:, kc, c * 128:(c + 1) * 128], hT[:, kc, :], kc == 0, kc == 7,
                                   [w_b, b_hT], [bb[bk]])
                            if c == 0:
                                ACT(cin[:, c2 * 2 + c, :], banks[bk][:, :], AF.Copy, [bb[bk]], [b_cin])
                            else:
                                CP(cin[:, c2 * 2 + c, :], banks[bk][:, :], [bb[bk]], [b_cin])
                    for c in range(4):
                        CP(uu[:, c, 0:2], uu[:, c, 512:514], [b_uu], [b_uu])
                    for c in range(4):
                        TT(uu[:, c, 2:514], cin[:, 4 + c, :], cin[:, 8 + c, :], ALU.mult, [b_cin], [b_uu])
                        TS(zA[:, :], uu[:, c, 0:512], cw[:, c, 0:1], None, ALU.mult, None, [b_uu, b_cw], [b_zA])
                        STT(zA[:, :], uu[:, c, 1:513], cw[:, c, 1:2], zA[:, :], ALU.mult, ALU.add,
                            [b_uu, b_cw, b_zA], [b_zA])
                        STT(zA[:, :], uu[:, c, 2:514], cw[:, c, 2:3], zA[:, :], ALU.mult, ALU.add,
                            [b_uu, b_cw, b_zA], [b_zA])
                        TT(ycT[:, c, :], zA[:, :], cin[:, c, :], ALU.mult, [b_zA, b_cin], [b_ycT])
                    for n2 in range(4):
                        wts = [load_ws(wgin_b, bi * 1024 + n2 * 256) for bi in range(3)]
                        for nn in range(2):
                            n = n2 * 2 + nn
                            gs = gsbs[n % 2]
                            for bi in range(3):
                                w_t, w_b = wts[bi]
                                for kc in range(8):
                                    MM(banks[bi][:, :], w_t[:, kc, nn * 128:(nn + 1) * 128], hT[:, kc, :], kc == 0,
                                       kc == 7, [w_b, b_hT], [bb[bi]])
                                ACT(gs[bi][0][:, :], banks[bi][:, :], AF.Sigmoid, [bb[bi]], [gs[bi][1]])
                            for bi, (yT, b_yT) in enumerate(((ycT, b_ycT), (ymT, b_ymT), (ynT, b_ynT))):
                                for kc in range(4):
                                    MM(banks[3 + bi][:, :], WB[:, bi * 4 + kc, n * 128:(n + 1) * 128], yT[:, kc, :],
                                       kc == 0, kc == 3, [b_WB, b_yT], [bb[3 + bi]])
                            TT(zA[:, :], banks[3][:, :], gs[0][0][:, :], ALU.mult, [bb[3], gs[0][1]], [b_zA])
                            TT(zB[:, :], banks[4][:, :], gs[1][0][:, :], ALU.mult, [bb[4], gs[1][1]], [b_zB])
                            TT(zA[:, :], zA[:, :], zB[:, :], ALU.add, [b_zA, b_zB], [b_zA])
                            TT(zB[:, :], banks[5][:, :], gs[2][0][:, :], ALU.mult, [bb[5], gs[2][1]], [b_zB])
                            TT(mT[:, n, :], zA[:, :], zB[:, :], ALU.add, [b_zA, b_zB], [b_mT])
                    for t in range(4):
                        fbk = (t % 2) * 2
                        for hf in range(2):
                            for kc in range(8):
                                MM(banks[fbk + hf][:, :], mT[:, kc, t * 128:(t + 1) * 128],
                                   WO[:, kc, hf * 512:(hf + 1) * 512], kc == 0, kc == 7, [b_mT, b_WO], [bb[fbk + hf]])
                        post_norm_residual(t, gpost, b_gpost, fbk)
                    for t in range(4):
                        norm_tr(t, 6 + t % 2)
                    for c2 in range(11):
                        wa_t, wa_b = load_ws(wup_b, c2 * 256)
                        wb_t, wb_b = load_ws(wup_b, DFF + c2 * 256)
                        for cc in range(2):
                            c = c2 * 2 + cc
                            par = c % 2
                            ba, bbk = par * 2, par * 2 + 1
                            aa, b_aa = aas[par]
                            z1, b_z1 = z1s[par]
                            z2, b_z2 = z2s[par]
                            sg, b_sg = sgs[par]
                            for kc in range(8):
                                MM(banks[ba][:, :], wa_t[:, kc, cc * 128:(cc + 1) * 128], hT[:, kc, :],
                                   kc == 0, kc == 7, [wa_b, b_hT], [bb[ba]])
                            for kc in range(8):
                                MM(banks[bbk][:, :], wb_t[:, kc, cc * 128:(cc + 1) * 128], hT[:, kc, :],
                                   kc == 0, kc == 7, [wb_b, b_hT], [bb[bbk]])
                            CP(aa[:, 0:2], halo[:, c, :], [b_halo], [b_aa])
                            ACT(aa[:, 2:514], banks[ba][:, :], AF.Copy, [bb[ba]], [b_aa])
                            CP(halo[:, c, :], aa[:, 512:514], [b_aa], [b_halo])
                            TS(z1[:, :], aa[:, 0:512], fw[:, c, 0:1], fb[:, c:c + 1], ALU.mult, ALU.add,
                               [b_aa, b_fw, b_fb], [b_z1])
                            STT(z1[:, :], aa[:, 1:513], fw[:, c, 1:2], z1[:, :], ALU.mult, ALU.add,
                                [b_aa, b_fw, b_z1], [b_z1])
                            STT(z1[:, :], aa[:, 2:514], fw[:, c, 2:3], z1[:, :], ALU.mult, ALU.add,
                                [b_aa, b_fw, b_z1], [b_z1])
                            ACT(z2[:, :], z1[:, :], AF.Square, [b_z1], [b_z2], scale=float(np.sqrt(0.044715)))
                            STT(z2[:, :], z2[:, :], 1.0, z1[:, :], ALU.add, ALU.mult, [b_z2, b_z1], [b_z2])
                            ACT(sg[:, :], z2[:, :], AF.Sigmoid, [b_z2], [b_sg], scale=float(2.0 * np.sqrt(2.0 / np.pi)))
                            TT(sg[:, :], sg[:, :], z1[:, :], ALU.mult, [b_sg, b_z1], [b_sg], eng="pool")
                            TT(gT[:, c, :], sg[:, :], banks[bbk][:, :], ALU.mult, [b_sg, bb[bbk]], [b_gT])
                    wdi = 0
                    for tp in range(2):
                        for kc in range(22):
                            w_t, w_b = WD[wdi % 2]
                            wdi += 1
                            LD(w_t[:, :], wdn_b[kc * 128:(kc + 1) * 128, :], [w_b], reads=[B["wdn_b"]])
                            for tt in range(2):
                                t = tp * 2 + tt
                                for hf in range(2):
                                    bk = 4 + tt * 2 + hf
                                    MM(banks[bk][:, :], gT[:, kc, t * 128:(t + 1) * 128],
                                       w_t[:, hf * 512:(hf + 1) * 512], kc == 0, kc == 21, [b_gT, w_b], [bb[bk]])
                        for tt in range(2):
                            t = tp * 2 + tt
                            post_norm_residual(t, gfpost, b_gfpost, 4 + tt * 2)
                            ST(x_dst[s0 + t * 128:s0 + (t + 1) * 128, :], xt4[:, t, :], [b_xt4], [b_xdst])
                cx.barrier()
                cx.emit()
            PH[0] += 1
            if stop_after is not None and PH[0] >= stop_after:
                break
    return nc


_CACHE = {}


def _in_maps(inputs):
    x = np.asarray(inputs["x"])
    Bn, S, _ = x.shape
    consts = host_consts(S)
    shared = {}
    for k, v in inputs.items():
        if k in ("x", "positions"):
            continue
        shared[k] = np.ascontiguousarray(np.asarray(v, dtype=np.float32))
    for k, v in consts.items():
        shared["c_" + k] = np.ascontiguousarray(v.astype(np.float32))
    pos = np.asarray(inputs["positions"]).astype(np.int32)
    in_maps = []
    for c in range(8):
        b = c % Bn
        m = dict(shared)
        m["x"] = np.ascontiguousarray(x[b].astype(np.float32))
        m["positions"] = np.ascontiguousarray(pos[b][None, :])
        in_maps.append(m)
    return in_maps


def kernel(_stop_after=None, **inputs):
    x = np.asarray(inputs["x"])
    Bn, S, _ = x.shape
    depth = np.asarray(inputs["w_in"]).shape[0]
    key = (S, depth, _stop_after)
    if key not in _CACHE:
        _CACHE[key] = build(S, depth, _stop_after)
    nc = _CACHE[key]
    in_maps = _in_maps(inputs)
    res = run_bass_kernel_spmd(nc, in_maps, core_ids=list(range(8)))
    out = np.stack([np.asarray(res.results[b]["y"]) for b in range(Bn)], axis=0)
    return out.astype(np.float32)
```

```python
from contextlib import ExitStack
import numpy as np
import concourse.bass as bass
import concourse.mybir as mybir
from concourse.bass_utils import run_bass_kernel_spmd

F32 = mybir.dt.float32
BF16 = mybir.dt.bfloat16
I32 = mybir.dt.int32
AF = mybir.ActivationFunctionType
ALU = mybir.AluOpType

D = 1024
DEPTH = 2
THETA = 500000.0
EPS = 1e-6
DFF = 2816
NIN = 6584
NEG = -30000.0


class Buf:
    __slots__ = ("name", "last_w", "readers", "excl")

    def __init__(self, name, excl=False):
        self.name = name
        self.last_w = None
        self.readers = []
        self.excl = excl


class Ctx:
    LOOK = 40

    def __init__(self, nc, stack, n_dma_ring=8):
        self.nc = nc
        self.engs = ("pe", "act", "dve", "pool", "sp")
        self.sems = {}
        self.count = {}
        self.waited = {k: {} for k in self.engs}
        for k in ("pe", "act", "dve", "pool"):
            self.sems[k] = stack.enter_context(nc.semaphore("s_" + k))
            self.count[k] = 0
        self.ring = {}
        self.ring_n = {}
        for q in ("sp", "pool"):
            self.ring[q] = []
            for i in range(n_dma_ring):
                s = stack.enter_context(nc.semaphore("d_%s%d" % (q, i)))
                self.ring[q].append(s)
                self.sems[("d", q, i)] = s
            self.ring_n[q] = 0
        self.nodes = []
        self.bufs = set()
        self.n_inst = 0
        self.n_wait = 0

    def _collect(self, reads, writes):
        deps = set()
        for b in reads:
            if b.last_w is not None:
                deps.add(b.last_w)
            if b.excl:
                deps.update(b.readers)
        for b in writes:
            if b.last_w is not None:
                deps.add(b.last_w)
            deps.update(b.readers)
        return deps

    def _record(self, nid, reads, writes):
        for b in reads:
            b.readers.append(nid)
            self.bufs.add(b)
        for b in writes:
            b.last_w = nid
            b.readers = []
            self.bufs.add(b)

    def op(self, e, fn, reads=(), writes=(), cost=500.0):
        deps = self._collect(reads, writes)
        nid = len(self.nodes)
        self.nodes.append((e, "op", fn, deps, float(cost), float(cost)))
        self._record(nid, reads, writes)
        return nid

    def dma(self, q, out, in_, reads=(), writes=(), nbytes=65536, **kw):
        deps = self._collect(reads, writes)
        nid = len(self.nodes)
        issue = 120.0 if q == "sp" else 700.0
        lat = 2200.0 + nbytes / 0.12
        self.nodes.append((q, "dma", (out, in_, kw), deps, issue, lat))
        self._record(nid, reads, writes)
        return nid

    def barrier(self):
        pass

    def emit(self):
        nodes = self.nodes
        n = len(nodes)
        pend = {e: [] for e in self.engs}
        for i, nd in enumerate(nodes):
            pend[nd[0]].append(i)
        head = {e: 0 for e in self.engs}
        done_t = [None] * n
        t_eng = {e: 0.0 for e in self.engs}
        order = {e: [] for e in self.engs}
        sched = [False] * n
        remaining = n
        LOOK = self.LOOK
        while remaining:
            best = None
            for e in self.engs:
                lst = pend[e]
                h = head[e]
                while h < len(lst) and sched[lst[h]]:
                    h += 1
                head[e] = h
                cnt = 0
                k = h
                while k < len(lst) and cnt < LOOK:
                    i = lst[k]
                    k += 1
                    if sched[i]:
                        continue
                    cnt += 1
                    ready = 0.0
                    ok = True
                    for d in nodes[i][3]:
                        dt_ = done_t[d]
                        if dt_ is None:
                            ok = False
                            break
                        if dt_ > ready:
                            ready = dt_
                    if not ok:
                        continue
                    start = ready if ready > t_eng[e] else t_eng[e]
                    key = (start, i)
                    if best is None or key < best[0]:
                        best = (key, e, i)
                    if ready <= t_eng[e]:
                        break
            (start, _), e, i = best
            nd = nodes[i]
            t_eng[e] = start + nd[4]
            done_t[i] = start + nd[5]
            sched[i] = True
            order[e].append(i)
            remaining -= 1
        tok = [None] * n
        pos = [0] * n
        ring_prev = {}
        for e in self.engs:
            for p, i in enumerate(order[e]):
                nd = nodes[i]
                pos[i] = p
                if nd[1] == "op":
                    self.count[e] += 1
                    tok[i] = (e, self.count[e])
                else:
                    q = e
                    m = self.ring_n[q]
                    R = len(self.ring[q])
                    key = ("d", q, m % R)
                    ring_prev[i] = (key, 16 * (m // R)) if m >= R else None
                    tok[i] = (key, 16 * (m // R + 1))
                    self.ring_n[q] = m + 1
        prog = {e: [] for e in self.engs}

        def wait(e, t):
            key, val = t
            w = self.waited[e]
            if w.get(key, 0) >= val:
                return
            w[key] = val
            sem = self.sems[key]
            prog[e].append(lambda eng, sem=sem, val=val: eng.wait_ge(sem, val))
            self.n_wait += 1

        for e in self.engs:
            for i in order[e]:
                nd = nodes[i]
                if nd[1] == "dma" and ring_prev[i] is not None:
                    wait(e, ring_prev[i])
                for d in nd[3]:
                    dn = nodes[d]
                    if dn[1] == "op" and dn[0] == e:
                        if e == "pe":
                            continue
                        if tok[i] is not None and nd[1] == "op" and tok[i][1] - tok[d][1] >= 3:
                            continue
                    wait(e, tok[d])
                if nd[1] == "op":
                    sem = self.sems[e]
                    prog[e].append(lambda eng, fn=nd[2], sem=sem: fn(eng).then_inc(sem, 1))
                else:
                    out, in_, kw = nd[2]
                    sem = self.sems[tok[i][0]]
                    prog[e].append(lambda eng, out=out, in_=in_, kw=kw, sem=sem:
                                   eng.dma_start(out=out, in_=in_, **kw).then_inc(sem, 16))
                self.n_inst += 1
        toks = [(k, self.count[k]) for k in ("pe", "act", "dve", "pool") if self.count[k] > 0]
        for q in self.ring:
            m = self.ring_n[q]
            R = len(self.ring[q])
            for slot in range(R):
                cnt = (m - slot + R - 1) // R
                if cnt > 0:
                    toks.append((("d", q, slot), 16 * cnt))
        for e in self.engs:
            for t in toks:
                if t[0] != e:
                    wait(e, t)
        with self.nc.Block() as block:
            @block.sync
            def _(eng):
                for f in prog["sp"]:
                    f(eng)

            @block.tensor
            def _(eng):
                for f in prog["pe"]:
                    f(eng)

            @block.scalar
            def _(eng):
                for f in prog["act"]:
                    f(eng)

            @block.vector
            def _(eng):
                for f in prog["dve"]:
                    f(eng)

            @block.gpsimd
            def _(eng):
                for f in prog["pool"]:
                    f(eng)
        for b_ in self.bufs:
            b_.last_w = None
            b_.readers = []
        self.bufs = set()
        self.nodes = []


def bcast_mid(ap, n):
    a = ap.ap
    return bass.AP(ap.tensor, ap.offset, [list(a[0]), [0, n]] + [list(x) for x in a[1:]])


def bcast_last(ap, n):
    a = ap.ap
    return bass.AP(ap.tensor, ap.offset, [list(x) for x in a] + [[0, n]])


def host_consts(S):
    NT = S // 128
    NSEL = S // 64
    k = np.arange(128)[:, None]
    q = np.arange(128)[None, :]
    c = {}
    c["ident"] = np.eye(128, dtype=np.float32)
    c["tri"] = (k <= q).astype(np.float32)
    c["wlow"] = (k > q).astype(np.float32)
    mc = np.zeros((128, 16, 128), np.float32)
    for o in range(16):
        mc[:, o, :] = (16 * (k - 8 * o) + 31 <= q)
    c["maskc"] = mc
    n = np.arange(512)[:, None]
    m = np.arange(128)[None, :]
    ov = ((n >= 4 * m - 1) & (n <= 4 * m + 3) & (m < NSEL)).astype(np.float32)
    c["ovl"] = ov.reshape(4, 128, 128).transpose(1, 0, 2).copy()
    A = np.zeros((NT, 128, 128), np.float32)
    for i in range(NT):
        t = 128 * i + np.arange(128)[:, None]
        cur = t // 64
        mm = np.arange(128)[None, :]
        forced = (mm == 0) | (mm == cur) | (mm == cur - 1)
        causal = 64 * mm <= t
        A[i] = np.where(forced, 1e6, np.where(causal, 0.0, -1.0))
        A[i][:, NSEL:] = -1.0
    c["amask"] = A
    E = np.zeros((128, NT, 128), np.float32)
    for j in range(NT):
        for kk in range(128):
            mrow = 2 * j + (kk >= 64)
            if mrow < 128:
                E[mrow, j, kk] = 1.0
    c["emat"] = E
    f32 = np.float32
    inv_m = (f32(THETA) ** (-(np.arange(16, dtype=f32) / f32(16)))).astype(f32)
    inv_n = (f32(THETA) ** (-(np.arange(8, dtype=f32) / f32(8)))).astype(f32)
    fm = np.zeros((128, 1), f32)
    fm[0:16, 0] = inv_m
    fm[16:32, 0] = inv_m
    fn = np.zeros((128, 1), f32)
    for base in (0, 64):
        fn[base:base + 8, 0] = inv_n
        fn[base + 8:base + 16, 0] = inv_n
    c["invf"] = np.concatenate([fm, fn], axis=1)
    return c


def build(S, depth=DEPTH, stop_after=None):
    assert S % 512 == 0
    NT = S // 128
    NS = S // 512
    NSEL = S // 64
    NCMP = S // 16 - 1
    NCT = (NCMP + 127) // 128
    CW = 65 + 128
    nc = bass.Bass("TRN2", target_bir_lowering=False)

    def din(name, shape, dt=F32):
        return nc.dram_tensor(name, list(shape), dt, kind="ExternalInput").ap()

    def dscr(name, shape, dt=BF16):
        return nc.dram_tensor(name, list(shape), dt, kind="Internal").ap()

    x_in = din("x", [S, D])
    pos_in = din("positions", [1, S], I32)
    W = {}
    W["norm_mix_pre"] = din("norm_mix_pre", [depth, D])
    W["norm_mix_post"] = din("norm_mix_post", [depth, D])
    W["w_in"] = din("w_in", [depth, D, NIN])
    W["conv_w"] = din("conv_w", [depth, 3, 512])
    W["mla_q_norm"] = din("mla_q_norm", [depth, 384])
    W["mla_w_uq"] = din("mla_w_uq", [depth, 384, 768])
    W["mla_kv_norm"] = din("mla_kv_norm", [depth, 256])
    W["mla_w_ukv"] = din("mla_w_ukv", [depth, 256, 1024])
    W["nsa_cmp_pos_k"] = din("nsa_cmp_pos_k", [depth, 32, 64])
    W["nsa_cmp_pos_v"] = din("nsa_cmp_pos_v", [depth, 32, 64])
    W["nsa_cmp_w_k"] = din("nsa_cmp_w_k", [depth, 2048, 64])
    W["nsa_cmp_w_v"] = din("nsa_cmp_w_v", [depth, 2048, 64])
    W["w_branch_conv"] = din("w_branch_conv", [depth, 512, D])
    W["w_branch_mla"] = din("w_branch_mla", [depth, 512, D])
    W["w_branch_nsa"] = din("w_branch_nsa", [depth, 512, D])
    W["w_out"] = din("w_out", [depth, D, D])
    W["norm_ffn_pre"] = din("norm_ffn_pre", [depth, D])
    W["norm_ffn_post"] = din("norm_ffn_post", [depth, D])
    W["ffn_w_up"] = din("ffn_w_up", [depth, D, 2 * DFF])
    W["ffn_conv_w"] = din("ffn_conv_w", [depth, 3, DFF])
    W["ffn_conv_b"] = din("ffn_conv_b", [depth, DFF])
    W["ffn_w_down"] = din("ffn_w_down", [depth, DFF, D])
    c_ident = din("c_ident", [128, 128])
    c_tri = din("c_tri", [128, 128])
    c_wlow = din("c_wlow", [128, 128])
    c_maskc = din("c_maskc", [128, 16, 128])
    c_ovl = din("c_ovl", [128, 4, 128])
    c_amask = din("c_amask", [NT, 128, 128])
    c_emat = din("c_emat", [128, NT, 128])
    c_invf = din("c_invf", [128, 2])
    y_out = nc.dram_tensor("y", [S, D], F32, kind="ExternalOutput").ap()

    xs = dscr("xs", [S, D], F32)
    qmT = dscr("qmT", [8, 96, S])
    kmT = dscr("kmT", [8, 96, S])
    vm = dscr("vm", [S, 8, 65])
    qnT = dscr("qnT", [2, 64, 4, S])
    kcmpT = dscr("kcmpT", [2, 64, S])
    vcmpT = dscr("vcmpT", [2, 64, S])
    kslcT = dscr("kslcT", [2, 64, S])
    kwinT = dscr("kwinT", [2, 64, S])
    vslc = dscr("vslc", [S, 2, 65])
    vwin = dscr("vwin", [S, 2, 65])
    gat = dscr("gat", [S, 24], F32)
    ymla = dscr("ymla", [S, 512])
    ynsa = dscr("ynsa", [S, 512])
    wgin_b = dscr("wgin_b", [D, 4608])
    wup_b = dscr("wup_b", [D, 2 * DFF])
    wdn_b = dscr("wdn_b", [DFF, D])

    with ExitStack() as top:
        cx = Ctx(nc, top)
        banks = [top.enter_context(nc.psum_tensor("bank%d" % i, [128, 512], F32)) for i in range(8)]
        bb = [Buf("bank%d" % i, excl=True) for i in range(8)]

        uid = [0]

        def sbt(st, name, shape, dt):
            uid[0] += 1
            nm = "%s_u%d" % (name, uid[0])
            return st.enter_context(nc.sbuf_tensor(nm, list(shape), dt)), Buf(nm)

        def fsz(ap):
            n = 1
            for d in ap.shape[1:]:
                n *= d
            return n

        def ACT(out, in_, func, reads, writes, **kw):
            cx.op("act", lambda e: e.activation(out=out, in_=in_, func=func, **kw), reads, writes,
                  cost=200 + 0.85 * fsz(out))

        def TT(out, in0, in1, op, reads, writes, eng="dve"):
            c = (90 + 1.2 * fsz(out)) if eng == "dve" else (150 + 2.3 * fsz(out))
            cx.op(eng, lambda e: e.tensor_tensor(out=out, in0=in0, in1=in1, op=op), reads, writes, cost=c)

        def TS(out, in0, s1, s2, op0, op1, reads, writes, eng="dve", **kw):
            c = (90 + 1.0 * fsz(out)) if eng == "dve" else (150 + 2.3 * fsz(out))
            if s2 is None:
                cx.op(eng, lambda e: e.tensor_scalar(out=out, in0=in0, scalar1=s1, scalar2=None, op0=op0, **kw),
                      reads, writes, cost=c)
            else:
                cx.op(eng, lambda e: e.tensor_scalar(out=out, in0=in0, scalar1=s1, scalar2=s2, op0=op0, op1=op1, **kw),
                      reads, writes, cost=c)

        def STT(out, in0, scalar, in1, op0, op1, reads, writes, eng="dve"):
            c = (90 + 1.4 * fsz(out)) if eng == "dve" else (150 + 2.3 * fsz(out))
            cx.op(eng, lambda e: e.scalar_tensor_tensor(out=out, in0=in0, scalar=scalar, in1=in1, op0=op0, op1=op1),
                  reads, writes, cost=c)

        def CP(out, in_, reads, writes, eng="dve"):
            c = (90 + 1.0 * fsz(out)) if eng == "dve" else ((200 + 0.85 * fsz(out)) if eng == "act"
                                                          else (150 + 2.3 * fsz(out)))
            cx.op(eng, lambda e: e.tensor_copy(out=out, in_=in_), reads, writes, cost=c)

        def MS(ap, val, writes, eng="pool"):
            c = (90 + 0.6 * fsz(ap)) if eng == "dve" else (150 + 1.2 * fsz(ap))
            cx.op(eng, lambda e: e.memset(ap, val), (), writes, cost=c)

        def MM(out, lhsT, rhs, start, stop, reads, writes):
            cx.op("pe", lambda e: e.matmul(out, lhsT=lhsT, rhs=rhs, start=start, stop=stop, skip_group_check=True),
                  reads, writes, cost=70 + 0.42 * max(fsz(rhs), 64))

        def TR(out, in_, ident, reads, writes):
            cx.op("pe", lambda e: e.transpose(out, in_, ident), reads, writes, cost=120)

        def nby(ap):
            n = ap.shape[0]
            for d in ap.shape[1:]:
                n *= d
            return n * (4 if ap.dtype in (F32, I32) else 2)

        LD = lambda out, in_, writes, reads=(), **kw: cx.dma("sp", out, in_, reads=reads, writes=writes,
                                                             nbytes=nby(out), **kw)
        ST = lambda out, in_, reads, writes=(), **kw: cx.dma("pool", out, in_, reads=reads, writes=writes,
                                                             nbytes=nby(in_), **kw)

        B = {n: Buf(n) for n in ("xs", "qmT", "kmT", "vm", "qnT", "kcmpT", "vcmpT", "kslcT", "kwinT", "vslc",
                                 "vwin", "gat", "ymla", "ynsa", "wgin_b", "wup_b", "wdn_b", "y")}

        identf, b_identf = sbt(top, "identf", [128, 128], F32)
        identb, b_identb = sbt(top, "identb", [128, 128], BF16)
        ones_b, b_ones = sbt(top, "ones_b", [128, 128], BF16)
        invf, b_invf = sbt(top, "invf", [128, 2], F32)
        LD(identf[:], c_ident[:, :], [b_identf])
        LD(invf[:], c_invf[:, :], [b_invf])
        CP(identb[:], identf[:], [b_identf], [b_identb])
        MS(ones_b[:], 1.0, [b_ones])
        CONST = [b_identf, b_identb, b_ones, b_invf]

        epst, b_epst = sbt(top, "epst", [128, 1], F32)
        MS(epst[:], EPS, [b_epst])

        def rsqrt_ip(ap, b_ap, inv_n, src=None, b_src=None):
            p = ap.shape[0]
            if src is None:
                src, b_src = ap, b_ap
            ACT(ap, src, AF.Sqrt, [b_src, b_epst], [b_ap], bias=epst[0:p, 0:1], scale=float(inv_n))
            cx.op("dve", lambda e: e.reciprocal(out=ap, in_=ap), [b_ap], [b_ap])

        def vec_col(st, name, src_row, n):
            c = n // 128
            t, b = sbt(st, name, [128, c], F32)
            for k in range(c):
                LD(t[:, k:k + 1], src_row[k * 128:(k + 1) * 128].rearrange("(p o) -> p o", o=1), [b])
            return t, b

        def rms_tile(xt, b_xt, rs, b_rs, col, junk, b_junk, n):
            ACT(junk, xt, AF.Square, [b_xt], [b_junk, b_rs], accum_out=rs[:, col:col + 1])
            rsqrt_ip(rs[:, col:col + 1], b_rs, 1.0 / n)

        def norm_transpose(src, s0, xt4, b_xt4, hT, b_hT, rs, b_rs, hn, b_hn, junk, b_junk, b_src, bank, b_bank):
            for t in range(4):
                LD(xt4[:, t, :], src[s0 + t * 128:s0 + (t + 1) * 128, :], [b_xt4], reads=[b_src])
            for t in range(4):
                MS(rs[:, t:t + 1], 0.0, [b_rs], eng="dve")
                rms_tile(xt4[:, t, :], b_xt4, rs, b_rs, t, junk[:], b_junk, D)
                TS(hn[:], xt4[:, t, :], rs[:, t:t + 1], None, ALU.mult, None, [b_xt4, b_rs], [b_hn])
                pb = bank[:, :].bitcast(BF16)
                for k in range(8):
                    TR(pb[:, k * 128:(k + 1) * 128], hn[:, k * 128:(k + 1) * 128], identb[:],
                       [b_hn, b_identb], [b_bank])
                CP(hT[:, :, t * 128:(t + 1) * 128], pb[:, 0:1024].rearrange("p (k t) -> p k t", k=8),
                   [b_bank], [b_hT], eng="act" if False else "dve")

        def sincos(st_name, pos_f, b_pos, col, rows, shift, out, b_out, wk, b_wk, wki, b_wki):
            r = slice(0, rows)
            TS(wk[r, 0, :], pos_f[r, :], invf[r, col:col + 1], shift, ALU.mult, ALU.add, [b_pos, b_invf], [b_wk])
            TS(wk[r, 1, :], wk[r, 0, :], float(1.0 / (2 * np.pi)), None, ALU.mult, None, [b_wk], [b_wk])
            CP(wki[r, :], wk[r, 1, :], [b_wk], [b_wki])
            CP(wk[r, 1, :], wki[r, :], [b_wki], [b_wk])
            STT(wk[r, 0, :], wk[r, 1, :], float(-2 * np.pi), wk[r, 0, :], ALU.mult, ALU.add, [b_wk], [b_wk])
            TS(wk[r, 1, :], wk[r, 0, :], float(np.pi), None, ALU.is_gt, None, [b_wk], [b_wk])
            STT(wk[r, 0, :], wk[r, 1, :], float(-2 * np.pi), wk[r, 0, :], ALU.mult, ALU.add, [b_wk], [b_wk])
            TS(wk[r, 1, :], wk[r, 0, :], float(-np.pi), None, ALU.is_lt, None, [b_wk], [b_wk])
            STT(wk[r, 0, :], wk[r, 1, :], float(2 * np.pi), wk[r, 0, :], ALU.mult, ALU.add, [b_wk], [b_wk])
            ACT(out[r, :], wk[r, 0, :], AF.Sin, [b_wk], [b_out])

        PH = [0]
        for l in range(depth):
            x_src, b_xsrc = (x_in, Buf("x_in")) if l == 0 else (xs, B["xs"])
            x_dst, b_xdst = (y_out, B["y"]) if l == depth - 1 else (xs, B["xs"])
            w_in = W["w_in"][l]

            with ExitStack() as st:
                NA = 2904
                WA, b_WA = sbt(st, "WA", [128, 8, NA], BF16)
                WQ, b_WQ = sbt(st, "WQ", [128, 3, 1024], BF16)
                WKV, b_WKV = sbt(st, "WKV", [128, 2, 1024], BF16)
                stage, b_stage = sbt(st, "stage", [128, 1976], F32)
                gpre, b_gpre = vec_col(st, "gpre", W["norm_mix_pre"][l], D)
                gq, b_gq = vec_col(st, "gq", W["mla_q_norm"][l], 384)
                gkv, b_gkv = vec_col(st, "gkv", W["mla_kv_norm"][l], 256)
                ngpre, b_ngpre = sbt(st, "ngpre", [128, 8], F32)
                ngq, b_ngq = sbt(st, "ngq", [128, 3], F32)
                TS(ngpre[:], gpre[:], -1.0, None, ALU.mult, None, [b_gpre], [b_ngpre])
                TS(ngq[:], gq[:], -1.0, None, ALU.mult, None, [b_gq], [b_ngq])
                MS(WA[:, :, 1216:1728], 0.0, [b_WA])
                for c in (1856, 2112, 2368):
                    MS(WA[:, :, c:c + 128], 0.0, [b_WA])

                def SC(out, in_, sc, rd, wr):
                    TS(out, in_, sc, None, ALU.mult, None, rd, wr)

                for kc in range(8):
                    LD(stage[:], w_in[kc * 128:(kc + 1) * 128, 4608:6584], [b_stage])
                    g = gpre[:, kc:kc + 1]
                    ng = ngpre[:, kc:kc + 1]
                    rd = [b_stage, b_gpre, b_ngpre]
                    SC(WA[:, kc, 0:672], stage[:, 0:672], g, rd, [b_WA])
                    SC(WA[:, kc, 672:688], stage[:, 656:672], ng, rd, [b_WA])
                    SC(WA[:, kc, 688:704], stage[:, 640:656], g, rd, [b_WA])
                    SC(WA[:, kc, 704:1216], stage[:, 672:1184], g, rd, [b_WA])
                    sq = stage[:, 672:1184].rearrange("p (h d) -> p h d", d=64)
                    dq = WA[:, kc, 1216:1728].rearrange("p (h d) -> p h d", d=64)
                    SC(dq[:, :, 0:8], sq[:, :, 8:16], ng, rd, [b_WA])
                    SC(dq[:, :, 8:16], sq[:, :, 0:8], g, rd, [b_WA])
                    for (src_o, dst_o) in ((1184, 1728), (1440, 1984), (1696, 2240)):
                        SC(WA[:, kc, dst_o:dst_o + 128], stage[:, src_o:src_o + 128], g, rd, [b_WA])
                        sk = stage[:, src_o:src_o + 128].rearrange("p (h d) -> p h d", d=64)
                        dk = WA[:, kc, dst_o + 128:dst_o + 256].rearrange("p (h d) -> p h d", d=64)
                        SC(dk[:, :, 0:8], sk[:, :, 8:16], ng, rd, [b_WA])
                        SC(dk[:, :, 8:16], sk[:, :, 0:8], g, rd, [b_WA])
                    SC(WA[:, kc, 2496:2624], stage[:, 1312:1440], g, rd, [b_WA])
                    SC(WA[:, kc, 2624:2752], stage[:, 1568:1696], g, rd, [b_WA])
                    SC(WA[:, kc, 2752:2880], stage[:, 1824:1952], g, rd, [b_WA])
                    SC(WA[:, kc, 2880:2904], stage[:, 1952:1976], g, rd, [b_WA])
                for kc in range(3):
                    LD(stage[:, 0:768], W["mla_w_uq"][l][kc * 128:(kc + 1) * 128, :], [b_stage])
                    g = gq[:, kc:kc + 1]
                    ng = ngq[:, kc:kc + 1]
                    rd = [b_stage, b_gq, b_ngq]
                    s3 = stage[:, 0:768].rearrange("p (h d) -> p h d", d=96)
                    d3 = WQ[:, kc, 0:768].rearrange("p (h d) -> p h d", d=96)
                    SC(d3[:, :, 0:32], s3[:, :, 64:96], g, rd, [b_WQ])
                    SC(d3[:, :, 32:96], s3[:, :, 0:64], g, rd, [b_WQ])
                    r3 = WQ[:, kc, 768:1024].rearrange("p (h d) -> p h d", d=32)
                    SC(r3[:, :, 0:16], s3[:, :, 80:96], ng, rd, [b_WQ])
                    SC(r3[:, :, 16:32], s3[:, :, 64:80], g, rd, [b_WQ])
                for kc in range(2):
                    LD(stage[:, 0:1024], W["mla_w_ukv"][l][kc * 128:(kc + 1) * 128, :], [b_stage])
                    g = gkv[:, kc:kc + 1]
                    rd = [b_stage, b_gkv]
                    s3 = stage[:, 0:1024].rearrange("p (h d) -> p h d", d=128)
                    SC(WKV[:, kc, 0:512].rearrange("p (h d) -> p h d", d=64), s3[:, :, 0:64], g, rd, [b_WKV])
                    SC(WKV[:, kc, 512:1024].rearrange("p (h d) -> p h d", d=64), s3[:, :, 64:128], g, rd, [b_WKV])

                gf, b_gf = vec_col(st, "gf", W["norm_ffn_pre"][l], D)
                wst, b_wst = sbt(st, "wst", [128, 5632], F32)
                wsb, b_wsb = sbt(st, "wsb", [128, 5632], BF16)
                for kc in range(8):
                    LD(wst[:, 0:4608], w_in[kc * 128:(kc + 1) * 128, 0:4608], [b_wst])
                    SC(wsb[:, 0:4608], wst[:, 0:4608], gpre[:, kc:kc + 1], [b_wst, b_gpre], [b_wsb])
                    ST(wgin_b[kc * 128:(kc + 1) * 128, :], wsb[:, 0:4608], [b_wsb], [B["wgin_b"]])
                for kc in range(8):
                    LD(wst[:, :], W["ffn_w_up"][l][kc * 128:(kc + 1) * 128, :], [b_wst])
                    SC(wsb[:, :], wst[:, :], gf[:, kc:kc + 1], [b_wst, b_gf], [b_wsb])
                    ST(wup_b[kc * 128:(kc + 1) * 128, :], wsb[:, :], [b_wsb], [B["wup_b"]])
                for kc in range(22):
                    LD(wst[:, 0:1024], W["ffn_w_down"][l][kc * 128:(kc + 1) * 128, :], [b_wst])
                    CP(wsb[:, 0:1024], wst[:, 0:1024], [b_wst], [b_wsb])
                    ST(wdn_b[kc * 128:(kc + 1) * 128, :], wsb[:, 0:1024], [b_wsb], [B["wdn_b"]])

                xt4, b_xt4 = sbt(st, "xt4", [128, 4, D], F32)
                hT, b_hT = sbt(st, "hT", [128, 8, 512], BF16)
                hn, b_hn = sbt(st, "hn", [128, D], BF16)
                junk, b_junk = sbt(st, "junk", [128, D], BF16)
                rs, b_rs = sbt(st, "rs", [128, 8], F32)
                posi, b_posi = sbt(st, "posi", [128, 512], I32)
                posf, b_posf = sbt(st, "posf", [128, 512], F32)
                wk, b_wk = sbt(st, "wk", [128, 2, 512], F32)
                wki, b_wki = sbt(st, "wki", [128, 512], I32)
                Cm, b_Cm = sbt(st, "Cm", [128, 512], F32)
                Sm, b_Sm = sbt(st, "Sm", [128, 512], F32)
                Cn, b_Cn = sbt(st, "Cn", [128, 512], F32)
                Sn, b_Sn = sbt(st, "Sn", [128, 512], F32)
                cqT, b_cqT = sbt(st, "cqT", [128, 3, 512], BF16)
                ckvT, b_ckvT = sbt(st, "ckvT", [128, 2, 512], BF16)
                sqb, b_sqb = sbt(st, "sqb", [128, 3, 512], BF16)
                rq, b_rq = sbt(st, "rq", [128, 512], F32)
                rkv, b_rkv = sbt(st, "rkv", [128, 512], F32)
                rkt, b_rkt = sbt(st, "rkt", [128, 4], F32)
                CR, b_CR = sbt(st, "CR", [128, 512], F32)
                SR, b_SR = sbt(st, "SR", [128, 512], F32)
                t1, b_t1 = sbt(st, "t1", [128, 512], F32)
                t2, b_t2 = sbt(st, "t2", [128, 512], F32)
                ob = [sbt(st, "ob%d" % i, [128, 512], BF16) for i in range(3)]
                vt = [sbt(st, "vt%d" % i, [128, 8, 65], BF16) for i in range(2)]
                vs2 = [sbt(st, "vs2_%d" % i, [128, 2, 2, 65], BF16) for i in range(2)]
                gt = [sbt(st, "gt%d" % i, [128, 24], F32) for i in range(2)]
                for (t_, b_) in vt:
                    MS(t_[:], 1.0, [b_])
                for (t_, b_) in vs2:
                    MS(t_[:], 1.0, [b_])
                obi = [0]

                def next_ob():
                    obi[0] += 1
                    return ob[obi[0] % 3]

                for s in range(NS):
                    s0 = s * 512
                    norm_transpose(x_src, s0, xt4, b_xt4, hT, b_hT, rs, b_rs, hn, b_hn, junk, b_junk, b_xsrc,
                                   banks[7], bb[7])
                    LD(posi[:], pos_in[0:1, s0:s0 + 512].partition_broadcast(128), [b_posi])
                    CP(posf[:], posi[:], [b_posi], [b_posf])
                    sincos("cm", posf, b_posf, 0, 128, float(np.pi / 2), Cm, b_Cm, wk, b_wk, wki, b_wki)
                    sincos("sm", posf, b_posf, 0, 128, 0.0, Sm, b_Sm, wk, b_wk, wki, b_wki)
                    sincos("cn", posf, b_posf, 1, 128, float(np.pi / 2), Cn, b_Cn, wk, b_wk, wki, b_wki)
                    sincos("sn", posf, b_posf, 1, 128, 0.0, Sn, b_Sn, wk, b_wk, wki, b_wki)

                    def proj(bank_i, c0, m, rd_extra=()):
                        for kc in range(8):
                            MM(banks[bank_i][0:m, :], WA[:, kc, c0:c0 + m], hT[:, kc, :], kc == 0, kc == 7,
                               [b_WA, b_hT], [bb[bank_i]])

                    for c in range(3):
                        proj(c % 2, c * 128, 128)
                        CP(cqT[:, c, :], banks[c % 2][:, :], [bb[c % 2]], [b_cqT], eng="act" if False else "dve")
                        ACT(sqb[:, c, :], banks[c % 2][:, :], AF.Square, [bb[c % 2]], [b_sqb])
                    for c in range(3):
                        MM(banks[2][:, :], ones_b[:, :], sqb[:, c, :], c == 0, c == 2, [b_ones, b_sqb], [bb[2]])
                    rsqrt_ip(rq[:], b_rq, 1.0 / 384, src=banks[2][:, :], b_src=bb[2])
                    TT(CR[:], Cm[:], rq[:], ALU.mult, [b_Cm, b_rq], [b_CR])
                    TT(SR[:], Sm[:], rq[:], ALU.mult, [b_Sm, b_rq], [b_SR])
                    for h in range(8):
                        bq = 3 + (h % 2) * 2
                        for kc in range(3):
                            MM(banks[bq][0:96, :], WQ[:, kc, h * 96:(h + 1) * 96], cqT[:, kc, :], kc == 0, kc == 2,
                               [b_WQ, b_cqT], [bb[bq]])
                        for kc in range(3):
                            MM(banks[bq + 1][0:32, :], WQ[:, kc, 768 + h * 32:768 + (h + 1) * 32], cqT[:, kc, :],
                               kc == 0, kc == 2, [b_WQ, b_cqT], [bb[bq + 1]])
                        o_t, o_b = next_ob()
                        TT(t1[0:96, :], banks[bq][0:96, :], CR[0:96, :], ALU.mult, [bb[bq], b_CR], [b_t1])
                        TT(t2[0:32, :], banks[bq + 1][0:32, :], SR[0:32, :], ALU.mult, [bb[bq + 1], b_SR], [b_t2])
                        TT(o_t[0:32, :], t1[0:32, :], t2[0:32, :], ALU.add, [b_t1, b_t2], [o_b], eng="pool")
                        CP(o_t[32:64, :], t1[32:64, :], [b_t1], [o_b], eng="pool")
                        CP(o_t[64:96, :], t1[64:96, :], [b_t1], [o_b], eng="pool")
                        ST(qmT[h, :, s0:s0 + 512], o_t[0:96, :], [o_b], [B["qmT"]])
                    for c in range(2):
                        proj(c, 384 + c * 128, 128)
                        CP(ckvT[:, c, :], banks[c][:, :], [bb[c]], [b_ckvT])
                        ACT(sqb[:, c, :], banks[c][:, :], AF.Square, [bb[c]], [b_sqb])
                    for c in range(2):
                        MM(banks[2][:, :], ones_b[:, :], sqb[:, c, :], c == 0, c == 1, [b_ones, b_sqb], [bb[2]])
                    rsqrt_ip(rkv[:], b_rkv, 1.0 / 256, src=banks[2][:, :], b_src=bb[2])
                    for t in range(4):
                        for c in range(2):
                            MM(banks[2][:, t:t + 1], sqb[:, c, t * 128:(t + 1) * 128], ones_b[:, 0:1],
                               (t == 0 and c == 0), c == 1, [b_ones, b_sqb], [bb[2]])
                    rsqrt_ip(rkt[:], b_rkt, 1.0 / 256, src=banks[2][:, 0:4], b_src=bb[2])
                    for g2 in range(4):
                        bk = 3 + (g2 % 2)
                        for kc in range(2):
                            MM(banks[bk][:, :], WKV[:, kc, g2 * 128:(g2 + 1) * 128], ckvT[:, kc, :], kc == 0, kc == 1,
                               [b_WKV, b_ckvT], [bb[bk]])
                        o_t, o_b = next_ob()
                        TT(o_t[:, :], banks[bk][:, :], rkv[:, :], ALU.mult, [bb[bk], b_rkv], [o_b])
                        ST(kmT[2 * g2, 32:96, s0:s0 + 512], o_t[0:64, :], [o_b], [B["kmT"]])
                        ST(kmT[2 * g2 + 1, 32:96, s0:s0 + 512], o_t[64:128, :], [o_b], [B["kmT"]])
                    for t in range(4):
                        bv = 5 + (t % 2)
                        for kc in range(2):
                            MM(banks[bv][:, :], ckvT[:, kc, t * 128:(t + 1) * 128], WKV[:, kc, 512:1024], kc == 0,
                               kc == 1, [b_WKV, b_ckvT], [bb[bv]])
                        v_t, v_b = vt[t % 2]
                        TS(v_t[:, :, 0:64], banks[bv][:, :].rearrange("p (h d) -> p h d", d=64), rkt[:, t:t + 1], None,
                           ALU.mult, None, [bb[bv], b_rkt], [v_b])
                        ST(vm[s0 + t * 128:s0 + (t + 1) * 128, :, :], v_t[:], [v_b], [B["vm"]])
                    proj(0, 640, 32)
                    proj(1, 672, 32)
                    o_t, o_b = next_ob()
                    TT(t1[0:32, :], banks[0][0:32, :], Cm[0:32, :], ALU.mult, [bb[0], b_Cm], [b_t1])
                    TT(t2[0:32, :], banks[1][0:32, :], Sm[0:32, :], ALU.mult, [bb[1], b_Sm], [b_t2])
                    TT(o_t[0:32, :], t1[0:32, :], t2[0:32, :], ALU.add, [b_t1, b_t2], [o_b], eng="pool")
                    for h in range(8):
                        ST(kmT[h, 0:32, s0:s0 + 512], o_t[0:32, :], [o_b], [B["kmT"]])
                    def rope_group(c0, r0, dests):
                        bq = 3
                        proj(bq, c0, 128)
                        proj(bq + 1, r0, 128)
                        o_t, o_b = next_ob()
                        TT(t1[:, :], banks[bq][:, :], Cn[:, :], ALU.mult, [bb[bq], b_Cn], [b_t1])
                        TT(t2[:, :], banks[bq + 1][:, :], Sn[:, :], ALU.mult, [bb[bq + 1], b_Sn], [b_t2])
                        TT(o_t[:, :], t1[:, :], t2[:, :], ALU.add, [b_t1, b_t2], [o_b], eng="pool")
                        for (dst, bname, lo) in dests:
                            ST(dst, o_t[lo:lo + 64, :], [o_b], [B[bname]])
                    for pr in range(4):
                        h0 = 2 * pr
                        rope_group(704 + pr * 128, 1216 + pr * 128,
                                   [(qnT[h0 // 4, :, h0 % 4, s0:s0 + 512], "qnT", 0),
                                    (qnT[(h0 + 1) // 4, :, (h0 + 1) % 4, s0:s0 + 512], "qnT", 64)])
                    for (c0, dst, nm) in ((1728, kcmpT, "kcmpT"), (1984, kslcT, "kslcT"), (2240, kwinT, "kwinT")):
                        rope_group(c0, c0 + 128, [(dst[0, :, s0:s0 + 512], nm, 0), (dst[1, :, s0:s0 + 512], nm, 64)])
                    proj(5, 2496, 128)
                    o_t, o_b = next_ob()
                    CP(o_t[:, :], banks[5][:, :], [bb[5]], [o_b], eng="act" if False else "dve")
                    ST(vcmpT[0, :, s0:s0 + 512], o_t[0:64, :], [o_b], [B["vcmpT"]])
                    ST(vcmpT[1, :, s0:s0 + 512], o_t[64:128, :], [o_b], [B["vcmpT"]])
                    for t in range(4):
                        bv = 5 + (t % 2)
                        for kc in range(8):
                            MM(banks[bv][:, 0:280], hT[:, kc, t * 128:(t + 1) * 128], WA[:, kc, 2624:2904], kc == 0,
                               kc == 7, [b_WA, b_hT], [bb[bv]])
                        v_t, v_b = vs2[t % 2]
                        CP(v_t[:, :, :, 0:64], banks[bv][:, 0:256].rearrange("p (a g d) -> p a g d", a=2, g=2),
                           [bb[bv]], [v_b])
                        g_t, g_b = gt[t % 2]
                        ACT(g_t[:, :], banks[bv][:, 256:280], AF.Sigmoid, [bb[bv]], [g_b])
                        r0, r1 = s0 + t * 128, s0 + (t + 1) * 128
                        ST(vslc[r0:r1, :, :], v_t[:, 0, :, :], [v_b], [B["vslc"]])
                        ST(vwin[r0:r1, :, :], v_t[:, 1, :, :], [v_b], [B["vwin"]])
                        ST(gat[r0:r1, :], g_t[:, :], [g_b], [B["gat"]])
                cx.barrier()
                cx.emit()
            PH[0] += 1
            if stop_after is not None and PH[0] >= stop_after:
                break

            with ExitStack() as st:
                tri, b_tri = sbt(st, "tri", [128, 128], BF16)
                trif, b_trif = sbt(st, "trif", [128, 128], F32)
                LD(trif[:], c_tri[:, :], [b_trif])
                CP(tri[:], trif[:], [b_trif], [b_tri])
                KT, b_KT = sbt(st, "KT", [96, 4, S], BF16)
                VV, b_VV = sbt(st, "VV", [128, NT, 4, 65], BF16)
                QT = [sbt(st, "QT%d" % i, [96, 4, 128], BF16) for i in range(2)]
                PT = [sbt(st, "PT%d" % i, [128, 512], BF16) for i in range(4)]
                recs = [sbt(st, "rec%d" % i, [128, 4], F32) for i in range(2)]
                yo = [sbt(st, "yo%d" % i, [128, 4, 64], BF16) for i in range(2)]
                scale_m = float(96 ** -0.5)
                LAG = 2

                def run_pipeline(items):
                    n = len(items)
                    for k in range(n + LAG):
                        if k >= LAG:
                            items[k - LAG][2]()
                        if k < n:
                            items[k][0]()
                            items[k][1]()

                for hg in range(2):
                    for h in range(4):
                        LD(KT[:, h, :], kmT[hg * 4 + h, :, :], [b_KT], reads=[B["kmT"]])
                    for jt in range(NT):
                        LD(VV[:, jt, :, :], vm[jt * 128:(jt + 1) * 128, hg * 4:(hg + 1) * 4, :], [b_VV],
                           reads=[B["vm"]])
                    items = []
                    for i in range(NT):
                        for j in range(i + 1):
                            k = len(items)

                            def fS(i=i, j=j, k=k):
                                q_t, q_b = QT[i % 2]
                                if j == 0:
                                    for h in range(4):
                                        LD(q_t[:, h, :], qmT[hg * 4 + h, :, i * 128:(i + 1) * 128], [q_b],
                                           reads=[B["qmT"]])
                                sb_i = k % 3
                                for h in range(4):
                                    MM(banks[sb_i][:, h * 128:(h + 1) * 128], KT[:, h, j * 128:(j + 1) * 128],
                                       q_t[:, h, :], h == 0, h == 3, [b_KT, q_b], [bb[sb_i]])

                            def fE(i=i, j=j, k=k):
                                sb_i = k % 3
                                p_t, p_b = PT[k % 4]
                                ACT(p_t[:, :], banks[sb_i][:, :], AF.Exp, [bb[sb_i]], [p_b], scale=scale_m)
                                if j == i:
                                    TT(p_t[:, :].rearrange("p (h q) -> p h q", h=4),
                                       p_t[:, :].rearrange("p (h q) -> p h q", h=4), bcast_mid(tri[:, :], 4), ALU.mult,
                                       [p_b, b_tri], [p_b], eng="pool")

                            def fP(i=i, j=j, k=k):
                                p_t, p_b = PT[k % 4]
                                ab = 4 + (i % 2)
                                for h in range(4):
                                    MM(banks[ab][:, h * 65:(h + 1) * 65], p_t[:, h * 128:(h + 1) * 128],
                                       VV[:, j, h, :], (j == 0 and h == 0), j == i, [p_b, b_VV], [bb[ab]])
                                if j == i:
                                    rec, b_rec = recs[i % 2]
                                    acc = banks[ab][:, 0:260].rearrange("p (h c) -> p h c", c=65)
                                    TS(rec[:, :], acc[:, :, 64], 1e-30, None, ALU.max, None, [bb[ab]], [b_rec])
                                    cx.op("dve", lambda e, rec=rec: e.reciprocal(out=rec[:, :], in_=rec[:, :]),
                                          [b_rec], [b_rec])
                                    y_t, y_b = yo[i % 2]
                                    TT(y_t[:, :, :], acc[:, :, 0:64], bcast_last(rec[:, :], 64), ALU.mult,
                                       [bb[ab], b_rec], [y_b])
                                    ST(ymla[i * 128:(i + 1) * 128, hg * 256:(hg + 1) * 256],
                                       y_t[:, :, :].rearrange("p h d -> p (h d)"), [y_b], [B["ymla"]])

                            items.append((fS, fE, fP))
                    run_pipeline(items)
                cx.barrier()
                cx.emit()
            PH[0] += 1
            if stop_after is not None and PH[0] >= stop_after:
                break

            with ExitStack() as st:
                cst, b_cst = sbt(st, "cst", [128, 16 * 128], F32)
                tri, b_tri = sbt(st, "tri", [128, 128], BF16)
                wlow, b_wlow = sbt(st, "wlow", [128, 128], BF16)
                maskc, b_maskc = sbt(st, "maskc", [128, 16, 128], BF16)
                emat, b_emat = sbt(st, "emat", [128, NT, 128], BF16)
                LD(cst[:, 0:128], c_tri[:, :], [b_cst])
                CP(tri[:], cst[:, 0:128], [b_cst], [b_tri])
                LD(cst[:, 0:128], c_wlow[:, :], [b_cst])
                CP(wlow[:], cst[:, 0:128], [b_cst], [b_wlow])
                LD(cst[:, :].rearrange("p (o q) -> p o q", o=16), c_maskc[:, :, :], [b_cst])
                CP(maskc[:], cst[:, :].rearrange("p (o q) -> p o q", o=16), [b_cst], [b_maskc])
                for j0 in range(0, NT, 16):
                    jn = min(16, NT - j0)
                    LD(cst[:, 0:jn * 128].rearrange("p (o q) -> p o q", o=jn), c_emat[:, j0:j0 + jn, :], [b_cst])
                    CP(emat[:, j0:j0 + jn, :], cst[:, 0:jn * 128].rearrange("p (o q) -> p o q", o=jn), [b_cst],
                       [b_emat])
                wck, b_wck = sbt(st, "wck", [64, 32, 64], BF16)
                wcv, b_wcv = sbt(st, "wcv", [64, 32, 64], BF16)
                wcs, b_wcs = sbt(st, "wcs", [64, 32, 64], F32)
                LD(wcs[:], W["nsa_cmp_w_k"][l].rearrange("(r d) e -> d r e", d=64), [b_wcs])
                CP(wck[:], wcs[:], [b_wcs], [b_wck])
                LD(wcs[:], W["nsa_cmp_w_v"][l].rearrange("(r d) e -> d r e", d=64), [b_wcs])
                CP(wcv[:], wcs[:], [b_wcs], [b_wcv])
                pkv, b_pkv = sbt(st, "pkv", [32, 2, 64], F32)
                LD(pkv[:, 0, :], W["nsa_cmp_pos_k"][l], [b_pkv])
                LD(pkv[:, 1, :], W["nsa_cmp_pos_v"][l], [b_pkv])
                posT, b_posT = sbt(st, "posT", [64, 2, 32], BF16)
                for a in range(2):
                    TR(banks[0][0:64, a * 32:(a + 1) * 32], pkv[:, a, :], identf[0:32, 0:32], [b_pkv, b_identf],
                       [bb[0]])
                CP(posT[:, :, :], banks[0][0:64, 0:64].rearrange("p (a r) -> p a r", a=2), [bb[0]], [b_posT])
                biask, b_biask = sbt(st, "biask", [64, 1], F32)
                biasv, b_biasv = sbt(st, "biasv", [1, 64], BF16)
                for r in range(32):
                    MM(banks[1][0:64, 0:1], wck[:, r, :], posT[:, 0, r:r + 1], r == 0, r == 31, [b_wck, b_posT],
                       [bb[1]])
                CP(biask[:, :], banks[1][0:64, 0:1], [bb[1]], [b_biask])
                for r in range(32):
                    MM(banks[2][0:1, 0:64], posT[:, 1, r:r + 1], wcv[:, r, :], r == 0, r == 31, [b_wcv, b_posT],
                       [bb[2]])
                CP(biasv[:, :], banks[2][0:1, 0:64], [bb[2]], [b_biasv])
                KS, b_KS = sbt(st, "KS", [64, 2, S], BF16)
                KW, b_KW = sbt(st, "KW", [64, 2, S], BF16)
                kcT, b_kcT = sbt(st, "kcT", [64, 2, 512], BF16)
                VC, b_VC = sbt(st, "VC", [128, 4, 2, CW], BF16)
                MS(kcT[:], 0.0, [b_kcT])
                MS(VC[:], 0.0, [b_VC])
                for g in range(2):
                    LD(KS[:, g, :], kcmpT[g, :, :], [b_KS], reads=[B["kcmpT"]])
                    LD(KW[:, g, :], vcmpT[g, :, :], [b_KW], reads=[B["vcmpT"]])
                LD(cst[:, 0:512].rearrange("p (t m) -> p t m", t=4), c_ovl[:, :, :], [b_cst])
                for g in range(2):
                    MS(VC[:, :, g, 64:65], 1.0, [b_VC])
                    CP(VC[:, :, g, 65:CW], cst[:, 0:512].rearrange("p (t m) -> p t m", t=4), [b_cst], [b_VC])
                for g in range(2):
                    n0 = 0
                    while n0 < NCMP:
                        nn = min(512, NCMP - n0)
                        for r in range(32):
                            src = KS[:, g, r + 16 * n0:r + 16 * n0 + 16 * (nn - 1) + 1]
                            rhs = bass.AP(src.tensor, src.offset, [list(src.ap[0]), [16, nn]])
                            MM(banks[3][0:64, 0:nn], wck[:, r, :], rhs, r == 0, r == 31, [b_wck, b_KS], [bb[3]])
                        ACT(kcT[:, g, n0:n0 + nn], banks[3][0:64, 0:nn], AF.Identity, [bb[3], b_biask], [b_kcT],
                            bias=biask[:, 0:1], scale=1.0)
                        n0 += nn
                    for jt in range(NCT):
                        nn = min(128, NCMP - jt * 128)
                        for r in range(32):
                            o = r + 16 * jt * 128
                            src = KW[:, g, o:o + 16 * (nn - 1) + 1]
                            lhsT = bass.AP(src.tensor, src.offset, [list(src.ap[0]), [16, nn]])
                            MM(banks[4][0:nn, 0:64], lhsT, wcv[:, r, :], r == 0, False, [b_wcv, b_KW], [bb[4]])
                        MM(banks[4][0:nn, 0:64], ones_b[0:1, 0:nn], biasv[0:1, :], False, True, [b_ones, b_biasv],
                           [bb[4]])
                        CP(VC[0:nn, jt, g, 0:64], banks[4][0:nn, 0:64], [bb[4]], [b_VC])
                VS, b_VS = sbt(st, "VS", [128, NT, 2, 65], BF16)
                VW, b_VW = sbt(st, "VW", [128, NT, 2, 65], BF16)
                for g in range(2):
                    LD(KS[:, g, :], kslcT[g, :, :], [b_KS], reads=[B["kslcT"]])
                    LD(KW[:, g, :], kwinT[g, :, :], [b_KW], reads=[B["kwinT"]])
                for jt in range(NT):
                    LD(VS[:, jt, :, :], vslc[jt * 128:(jt + 1) * 128, :, :], [b_VS], reads=[B["vslc"]])
                    LD(VW[:, jt, :, :], vwin[jt * 128:(jt + 1) * 128, :, :], [b_VW], reads=[B["vwin"]])
                QN = [sbt(st, "QN%d" % i, [64, 2, 4, 128], BF16) for i in range(2)]
                GA = [sbt(st, "GA%d" % i, [128, 24], F32) for i in range(2)]
                AM = [sbt(st, "AM%d" % i, [128, 128], F32) for i in range(2)]
                PT = [sbt(st, "PTn%d" % i, [128, 512], BF16) for i in range(4)]
                imps = [sbt(st, "imp%d" % i, [128, 128], F32) for i in range(2)]
                impws = [sbt(st, "impw%d" % i, [128, 128], F32) for i in range(2)]
                top16s = [sbt(st, "top16_%d" % i, [128, 16], F32) for i in range(2)]
                negms = [sbt(st, "negm%d" % i, [128, 128], BF16) for i in range(2)]
                negTs = [sbt(st, "negT%d" % i, [128, 128], BF16) for i in range(2)]
                recs = [sbt(st, "recn%d" % i, [128, 4], F32) for i in range(6)]
                wgts = [sbt(st, "wgt%d" % i, [128, 4], F32) for i in range(6)]
                yaccs = [sbt(st, "yacc%d" % i, [128, 8, 64], F32) for i in range(2)]
                ytmps = [sbt(st, "ytmp%d" % i, [128, 4, 64], F32) for i in range(2)]
                yo = [sbt(st, "yon%d" % i, [128, 512], BF16) for i in range(2)]
                scale_n = 0.125
                LAG = 2
                rot = [0]

                def small():
                    rot[0] += 1
                    return recs[rot[0] % 6], wgts[rot[0] % 6]

                def exp_mask(k, masks):
                    sb_i = k % 3
                    p_t, p_b = PT[k % 4]
                    ACT(p_t[:, :], banks[sb_i][:, :], AF.Exp, [bb[sb_i]], [p_b], scale=scale_n)
                    for (m_ap, m_b) in masks:
                        TT(p_t[:, :].rearrange("p (h q) -> p h q", h=4),
                           p_t[:, :].rearrange("p (h q) -> p h q", h=4), bcast_mid(m_ap, 4), ALU.mult,
                           [p_b, m_b], [p_b], eng="pool")

                def branch_out(i, g, bank_i, br):
                    acc = banks[bank_i][:, 0:260].rearrange("p (h c) -> p h c", c=65)
                    acc_b = bb[bank_i]
                    (rec, b_rec), (wgt, b_wgt) = small()
                    g_t, g_b = GA[i % 2]
                    gv = g_t[:, :].rearrange("p (h b) -> p h b", b=3)
                    yacc, b_yacc = yaccs[i % 2]
                    ytmp, b_ytmp = ytmps[(2 * i + g + br) % 2]
                    TS(rec[:, :], acc[:, :, 64], 1e-30, None, ALU.max, None, [acc_b], [b_rec])
                    cx.op("dve", lambda e, rec=rec: e.reciprocal(out=rec[:, :], in_=rec[:, :]), [b_rec], [b_rec])
                    TT(wgt[:, :], rec[:, :], gv[:, g * 4:(g + 1) * 4, br], ALU.mult, [b_rec, g_b], [b_wgt])
                    TT(ytmp[:, :, :], acc[:, :, 0:64], bcast_last(wgt[:, :], 64), ALU.mult, [acc_b, b_wgt], [b_ytmp])
                    TT(yacc[:, g * 4:(g + 1) * 4, :], yacc[:, g * 4:(g + 1) * 4, :], ytmp[:, :, :], ALU.add,
                       [b_yacc, b_ytmp], [b_yacc], eng="pool")

                def cmp_final(i, g):
                    (rec, b_rec), (wgt, b_wgt) = small()
                    par = (2 * i + g) % 2
                    imp, b_imp = imps[par]
                    impw, b_impw = impws[par]
                    top16, b_top16 = top16s[par]
                    negm, b_negm = negms[par]
                    g_t, g_b = GA[i % 2]
                    a_t, a_b = AM[i % 2]
                    gv = g_t[:, :].rearrange("p (h b) -> p h b", b=3)
                    yacc, b_yacc = yaccs[i % 2]
                    for h in range(4):
                        bk = 3 + h // 2
                        a0 = (h % 2) * CW
                        TS(rec[:, h:h + 1], banks[bk][:, a0 + 64:a0 + 65], 1e-30, None, ALU.max, None, [bb[bk]],
                           [b_rec])
                    cx.op("dve", lambda e, rec=rec: e.reciprocal(out=rec[:, :], in_=rec[:, :]), [b_rec], [b_rec])
                    for h in range(4):
                        bk = 3 + h // 2
                        a0 = (h % 2) * CW
                        if h == 0:
                            TS(imp[:, :], banks[bk][:, a0 + 65:a0 + CW], rec[:, 0:1], None, ALU.mult, None,
                               [bb[bk], b_rec], [b_imp])
                        else:
                            STT(imp[:, :], banks[bk][:, a0 + 65:a0 + CW], rec[:, h:h + 1], imp[:, :], ALU.mult,
                                ALU.add, [bb[bk], b_rec, b_imp], [b_imp])
                    TT(wgt[:, :], rec[:, :], gv[:, g * 4:(g + 1) * 4, 0], ALU.mult, [b_rec, g_b], [b_wgt])
                    for h in range(4):
                        bk = 3 + h // 2
                        a0 = (h % 2) * CW
                        TS(yacc[:, g * 4 + h, :], banks[bk][:, a0:a0 + 64], wgt[:, h:h + 1], None, ALU.mult, None,
                           [bb[bk], b_wgt], [b_yacc])
                    TT(imp[:, :], imp[:, :], a_t[:, :], ALU.add, [b_imp, a_b], [b_imp])
                    cx.op("dve", lambda e, top16=top16, imp=imp: e.max(out=top16[:, 0:8], in_=imp[:, :]),
                          [b_imp], [b_top16])
                    cx.op("dve", lambda e, top16=top16, imp=imp, impw=impw:
                          e.match_replace(out=impw[:, :], in_to_replace=top16[:, 0:8], in_values=imp[:, :],
                                          imm_value=-3.0), [b_imp, b_top16], [b_impw])
                    cx.op("dve", lambda e, top16=top16, impw=impw: e.max(out=top16[:, 8:16], in_=impw[:, :]),
                          [b_impw], [b_top16])
                    TS(impw[:, :], imp[:, :], top16[:, 15:16], -1.0, ALU.is_ge, ALU.add, [b_imp, b_top16], [b_impw])
                    TS(negm[:, :], impw[:, :], -NEG, None, ALU.mult, None, [b_impw], [b_negm])

                def neg_transpose(i, g):
                    par = (2 * i + g) % 2
                    negm, b_negm = negms[par]
                    negT, b_negT = negTs[par]
                    pb = banks[7][:, :].bitcast(BF16)
                    TR(pb[:, 0:128], negm[:, :], identb[:, :], [b_negm, b_identb], [bb[7]])
                    CP(negT[:, :], pb[:, 0:128], [bb[7]], [b_negT])

                items = []
                for i in range(NT):
                    for g in range(2):
                        jl = i // 16
                        j_lo = max(0, i - 4)
                        seq = [("c", jt) for jt in range(jl + 1)] + [("w", j) for j in range(j_lo, i + 1)] + \
                              [("s", j) for j in range(i + 1)]
                        for (kind, j) in seq:
                            k = len(items)

                            def fS(i=i, g=g, kind=kind, j=j, k=k):
                                q_t, q_b = QN[i % 2]
                                if g == 0 and kind == "c" and j == 0:
                                    g_t, g_b = GA[i % 2]
                                    a_t, a_b = AM[i % 2]
                                    for gg in range(2):
                                        LD(q_t[:, gg, :, :], qnT[gg, :, :, i * 128:(i + 1) * 128], [q_b],
                                           reads=[B["qnT"]])
                                    LD(g_t[:, :], gat[i * 128:(i + 1) * 128, :], [g_b], reads=[B["gat"]])
                                    LD(a_t[:, :], c_amask[i, :, :], [a_b])
                                qrhs = q_t[:, g, :, :]
                                sb_i = k % 3
                                out3 = banks[sb_i][:, :].rearrange("p (h q) -> p h q", h=4)
                                if kind == "c":
                                    MM(out3, kcT[:, g, j * 128:(j + 1) * 128], qrhs, True, True, [b_kcT, q_b],
                                       [bb[sb_i]])
                                elif kind == "w":
                                    MM(out3, KW[:, g, j * 128:(j + 1) * 128], qrhs, True, True, [b_KW, q_b],
                                       [bb[sb_i]])
                                else:
                                    if j == 0:
                                        neg_transpose(i, g)
                                    negT, b_negT = negTs[(2 * i + g) % 2]
                                    MM(out3, KS[:, g, j * 128:(j + 1) * 128], qrhs, True, False, [b_KS, q_b],
                                       [bb[sb_i]])
                                    MM(out3, emat[:, j, :], bcast_mid(negT[:, :], 4), False, True,
                                       [b_emat, b_negT], [bb[sb_i]])

                            def fE(i=i, g=g, kind=kind, j=j, k=k):
                                masks = []
                                if kind == "c":
                                    if j == i // 16:
                                        masks.append((maskc[:, i % 16, :], b_maskc))
                                else:
                                    if j == i:
                                        masks.append((tri[:, :], b_tri))
                                    if kind == "w" and j == i - 4:
                                        masks.append((wlow[:, :], b_wlow))
                                exp_mask(k, masks)

                            def fP(i=i, g=g, kind=kind, j=j, k=k):
                                p_t, p_b = PT[k % 4]
                                if kind == "c":
                                    jl = i // 16
                                    for h in range(4):
                                        bk = 3 + h // 2
                                        MM(banks[bk][:, (h % 2) * CW:(h % 2 + 1) * CW], p_t[:, h * 128:(h + 1) * 128],
                                           VC[:, j, g, :], (j == 0 and h % 2 == 0), j == jl, [p_b, b_VC], [bb[bk]])
                                    if j == jl:
                                        cmp_final(i, g)
                                elif kind == "w":
                                    j_lo = max(0, i - 4)
                                    for h in range(4):
                                        MM(banks[6][:, h * 65:(h + 1) * 65], p_t[:, h * 128:(h + 1) * 128],
                                           VW[:, j, g, :], (j == j_lo and h == 0), j == i, [p_b, b_VW], [bb[6]])
                                    if j == i:
                                        branch_out(i, g, 6, 2)
                                else:
                                    for h in range(4):
                                        MM(banks[5][:, h * 65:(h + 1) * 65], p_t[:, h * 128:(h + 1) * 128],
                                           VS[:, j, g, :], (j == 0 and h == 0), j == i, [p_b, b_VS], [bb[5]])
                                    if j == i:
                                        branch_out(i, g, 5, 1)
                                        if g == 1:
                                            yacc, b_yacc = yaccs[i % 2]
                                            y_t, y_b = yo[i % 2]
                                            CP(y_t[:, :], yacc[:, :, :].rearrange("p h d -> p (h d)"), [b_yacc], [y_b])
                                            ST(ynsa[i * 128:(i + 1) * 128, :], y_t[:, :], [y_b], [B["ynsa"]])

                            items.append((fS, fE, fP))
                n_items = len(items)
                for k in range(n_items + LAG):
                    if k >= LAG:
                        items[k - LAG][2]()
                    if k < n_items:
                        items[k][0]()
                        items[k][1]()
                cx.barrier()
                cx.emit()
            PH[0] += 1
            if stop_after is not None and PH[0] >= stop_after:
                break

            with ExitStack() as st:
                WB, b_WB = sbt(st, "WB", [128, 12, D], BF16)
                WO, b_WO = sbt(st, "WO", [128, 8, D], BF16)
                wst, b_wst = sbt(st, "wstT", [128, D], F32)
                for bi, nm in enumerate(("w_branch_conv", "w_branch_mla", "w_branch_nsa")):
                    for kc in range(4):
                        LD(wst[:], W[nm][l][kc * 128:(kc + 1) * 128, :], [b_wst])
                        CP(WB[:, bi * 4 + kc, :], wst[:], [b_wst], [b_WB])
                for kc in range(8):
                    LD(wst[:], W["w_out"][l][kc * 128:(kc + 1) * 128, :], [b_wst])
                    CP(WO[:, kc, :], wst[:], [b_wst], [b_WO])
                gpost, b_gpost = sbt(st, "gpost", [128, D], F32)
                gfpost, b_gfpost = sbt(st, "gfpost", [128, D], F32)
                LD(gpost[:], W["norm_mix_post"][l:l + 1, :].partition_broadcast(128), [b_gpost])
                LD(gfpost[:], W["norm_ffn_post"][l:l + 1, :].partition_broadcast(128), [b_gfpost])
                cw, b_cw = sbt(st, "cw", [128, 4, 3], F32)
                fw, b_fw = sbt(st, "fw", [128, 22, 3], F32)
                fb, b_fb = sbt(st, "fb", [128, 22], F32)
                for k in range(3):
                    for c in range(4):
                        LD(cw[:, c, k:k + 1], W["conv_w"][l][k, c * 128:(c + 1) * 128].rearrange("(p o) -> p o", o=1),
                           [b_cw])
                    for c in range(22):
                        LD(fw[:, c, k:k + 1],
                           W["ffn_conv_w"][l][k, c * 128:(c + 1) * 128].rearrange("(p o) -> p o", o=1), [b_fw])
                for c in range(22):
                    LD(fb[:, c:c + 1], W["ffn_conv_b"][l][c * 128:(c + 1) * 128].rearrange("(p o) -> p o", o=1), [b_fb])
                xt4, b_xt4 = sbt(st, "xt4T", [128, 4, D], F32)
                hT, b_hT = sbt(st, "hTT", [128, 8, 512], BF16)
                hns = [sbt(st, "hnT%d" % i, [128, D], BF16) for i in range(2)]
                junk, b_junk = sbt(st, "junkT", [128, D], BF16)
                rs, b_rs = sbt(st, "rsT", [128, 16], F32)
                WS = [sbt(st, "WS%d" % i, [128, 8, 256], BF16) for i in range(3)]
                WD = [sbt(st, "WD%d" % i, [128, D], BF16) for i in range(2)]
                gT, b_gT = sbt(st, "gT", [128, 22, 512], BF16)
                cin, b_cin = gT, b_gT
                uu, b_uu = sbt(st, "uu", [128, 4, 514], F32)
                ycT, b_ycT = sbt(st, "ycT", [128, 4, 512], BF16)
                ymT, b_ymT = sbt(st, "ymT", [128, 4, 512], BF16)
                ynT, b_ynT = sbt(st, "ynT", [128, 4, 512], BF16)
                ytks = [sbt(st, "ytk%d" % i, [128, 512], BF16) for i in range(2)]
                gsbs = [[sbt(st, "gsb%d_%d" % (i, k), [128, 512], BF16) for k in range(3)] for i in range(2)]
                zA, b_zA = sbt(st, "zA", [128, 512], F32)
                zB, b_zB = sbt(st, "zB", [128, 512], F32)
                mT, b_mT = sbt(st, "mT", [128, 8, 512], BF16)
                aas = [sbt(st, "aa%d" % i, [128, 514], F32) for i in range(2)]
                halo, b_halo = sbt(st, "halo", [128, 22, 2], F32)
                z1s = [sbt(st, "z1_%d" % i, [128, 512], F32) for i in range(2)]
                z2s = [sbt(st, "z2_%d" % i, [128, 512], F32) for i in range(2)]
                sgs = [sbt(st, "sg%d" % i, [128, 512], F32) for i in range(2)]
                xos = [sbt(st, "xo%d" % i, [128, D], F32) for i in range(2)]
                MS(uu[:], 0.0, [b_uu])
                MS(halo[:], 0.0, [b_halo])
                wsi = [0]

                def load_ws(src, c0):
                    wsi[0] += 1
                    w_t, w_b = WS[wsi[0] % 3]
                    LD(w_t[:, :, :], src[:, c0:c0 + 256].rearrange("(k p) c -> p k c", p=128), [w_b],
                       reads=[B["wgin_b"], B["wup_b"]])
                    return w_t, w_b

                pnr = [0]

                def post_norm_residual(t, gtile, b_gt, first_bank):
                    pnr[0] += 1
                    xo, b_xo = xos[pnr[0] % 2]
                    c0 = 4 + 2 * (pnr[0] % 4)
                    MS(rs[:, c0:c0 + 2], 0.0, [b_rs], eng="dve")
                    ACT(junk[:, 0:512], banks[first_bank][:, :], AF.Square, [bb[first_bank]], [b_junk, b_rs],
                        accum_out=rs[:, c0:c0 + 1])
                    ACT(junk[:, 512:1024], banks[first_bank + 1][:, :], AF.Square, [bb[first_bank + 1]],
                        [b_junk, b_rs], accum_out=rs[:, c0 + 1:c0 + 2])
                    TT(rs[:, c0:c0 + 1], rs[:, c0:c0 + 1], rs[:, c0 + 1:c0 + 2], ALU.add, [b_rs], [b_rs])
                    rsqrt_ip(rs[:, c0:c0 + 1], b_rs, 1.0 / D)
                    for hf in range(2):
                        STT(xo[:, hf * 512:(hf + 1) * 512], banks[first_bank + hf][:, :], rs[:, c0:c0 + 1],
                            gtile[:, hf * 512:(hf + 1) * 512], ALU.mult, ALU.mult,
                            [bb[first_bank + hf], b_rs, b_gt], [b_xo])
                    TT(xt4[:, t, :], xt4[:, t, :], xo[:, :], ALU.add, [b_xt4, b_xo], [b_xt4], eng="pool")

                def norm_tr(t, tb):
                    hn, b_hn = hns[t % 2]
                    MS(rs[:, t:t + 1], 0.0, [b_rs], eng="dve")
                    rms_tile(xt4[:, t, :], b_xt4, rs, b_rs, t, junk[:], b_junk, D)
                    TS(hn[:], xt4[:, t, :], rs[:, t:t + 1], None, ALU.mult, None, [b_xt4, b_rs], [b_hn])
                    pb = banks[tb][:, :].bitcast(BF16)
                    for k in range(8):
                        TR(pb[:, k * 128:(k + 1) * 128], hn[:, k * 128:(k + 1) * 128], identb[:],
                           [b_hn, b_identb], [bb[tb]])
                    ACT(hT[:, :, t * 128:(t + 1) * 128], pb[:, 0:1024].rearrange("p (k t) -> p k t", k=8),
                        AF.Copy, [bb[tb]], [b_hT])

                for s in range(NS):
                    s0 = s * 512
                    for t in range(4):
                        LD(xt4[:, t, :], x_src[s0 + t * 128:s0 + (t + 1) * 128, :], [b_xt4], reads=[b_xsrc])
                    for t in range(4):
                        norm_tr(t, 6 + t % 2)
                    yi = 0
                    for (ysrc, bname, yT, b_yT) in ((ymla, "ymla", ymT, b_ymT), (ynsa, "ynsa", ynT, b_ynT)):
                        for t in range(4):
                            ytk, b_ytk = ytks[yi % 2]
                            tb = 4 + yi % 2
                            yi += 1
                            LD(ytk[:, :], ysrc[s0 + t * 128:s0 + (t + 1) * 128, :], [b_ytk], reads=[B[bname]])
                            pb = banks[tb][:, :].bitcast(BF16)
                            for k in range(4):
                                TR(pb[:, k * 128:(k + 1) * 128], ytk[:, k * 128:(k + 1) * 128], identb[:],
                                   [b_ytk, b_identb], [bb[tb]])
                            ACT(yT[:, :, t * 128:(t + 1) * 128], pb[:, 0:512].rearrange("p (k t) -> p k t", k=4),
                                AF.Copy, [bb[tb]], [b_yT])
                    for c2 in range(6):
                        w_t, w_b = load_ws(wgin_b, 3072 + c2 * 256)
                        for c in range(2):
                            bk = (c2 * 2 + c) % 4
                            for kc in range(8):
                                MM(banks[bk][:, :], w_t[:, kc, c * 128:(c + 1) * 128], hT[:, kc, :], kc == 0, kc == 7,
                                   [w_b, b_hT], [bb[bk]])
                            if c == 0:
                                ACT(cin[:, c2 * 2 + c, :], banks[bk][:, :], AF.Copy, [bb[bk]], [b_cin])
                            else:
                                CP(cin[:, c2 * 2 + c, :], banks[bk][:, :], [bb[bk]], [b_cin])
                    for c in range(4):
                        CP(uu[:, c, 0:2], uu[:, c, 512:514], [b_uu], [b_uu])
                    for c in range(4):
                        TT(uu[:, c, 2:514], cin[:, 4 + c, :], cin[:, 8 + c, :], ALU.mult, [b_cin], [b_uu])
                        TS(zA[:, :], uu[:, c, 0:512], cw[:, c, 0:1], None, ALU.mult, None, [b_uu, b_cw], [b_zA])
                        STT(zA[:, :], uu[:, c, 1:513], cw[:, c, 1:2], zA[:, :], ALU.mult, ALU.add,
                            [b_uu, b_cw, b_zA], [b_zA])
                        STT(zA[:, :], uu[:, c, 2:514], cw[:, c, 2:3], zA[:, :], ALU.mult, ALU.add,
                            [b_uu, b_cw, b_zA], [b_zA])
                        TT(ycT[:, c, :], zA[:, :], cin[:, c, :], ALU.mult, [b_zA, b_cin], [b_ycT])
                    for n2 in range(4):
                        wts = [load_ws(wgin_b, bi * 1024 + n2 * 256) for bi in range(3)]
                        for nn in range(2):
                            n = n2 * 2 + nn
                            gs = gsbs[n % 2]
                            for bi in range(3):
                                w_t, w_b = wts[bi]
                                for kc in range(8):
                                    MM(banks[bi][:, :], w_t[:, kc, nn * 128:(nn + 1) * 128], hT[:, kc, :], kc == 0,
                                       kc == 7, [w_b, b_hT], [bb[bi]])
                                ACT(gs[bi][0][:, :], banks[bi][:, :], AF.Sigmoid, [bb[bi]], [gs[bi][1]])
                            for bi, (yT, b_yT) in enumerate(((ycT, b_ycT), (ymT, b_ymT), (ynT, b_ynT))):
                                for kc in range(4):
                                    MM(banks[3 + bi][:, :], WB[:, bi * 4 + kc, n * 128:(n + 1) * 128], yT[:, kc, :],
                                       kc == 0, kc == 3, [b_WB, b_yT], [bb[3 + bi]])
                            TT(zA[:, :], banks[3][:, :], gs[0][0][:, :], ALU.mult, [bb[3], gs[0][1]], [b_zA])
                            TT(zB[:, :], banks[4][:, :], gs[1][0][:, :], ALU.mult, [bb[4], gs[1][1]], [b_zB])
                            TT(zA[:, :], zA[:, :], zB[:, :], ALU.add, [b_zA, b_zB], [b_zA])
                            TT(zB[:, :], banks[5][:, :], gs[2][0][:, :], ALU.mult, [bb[5], gs[2][1]], [b_zB])
                            TT(mT[:, n, :], zA[:, :], zB[:, :], ALU.add, [b_zA, b_zB], [b_mT])
                    for t in range(4):
                        fbk = (t % 2) * 2
                        for hf in range(2):
                            for kc in range(8):
                                MM(banks[fbk + hf][:, :], mT[:, kc, t * 128:(t + 1) * 128],
                                   WO[:, kc, hf * 512:(hf + 1) * 512], kc == 0, kc == 7, [b_mT, b_WO], [bb[fbk + hf]])
                        post_norm_residual(t, gpost, b_gpost, fbk)
                    for t in range(4):
                        norm_tr(t, 6 + t % 2)
                    for c2 in range(11):
                        wa_t, wa_b = load_ws(wup_b, c2 * 256)
                        wb_t, wb_b = load_ws(wup_b, DFF + c2 * 256)
                        for cc in range(2):
                            c = c2 * 2 + cc
                            par = c % 2
                            ba, bbk = par * 2, par * 2 + 1
                            aa, b_aa = aas[par]
                            z1, b_z1 = z1s[par]
                            z2, b_z2 = z2s[par]
                            sg, b_sg = sgs[par]
                            for kc in range(8):
                                MM(banks[ba][:, :], wa_t[:, kc, cc * 128:(cc + 1) * 128], hT[:, kc, :],
                                   kc == 0, kc == 7, [wa_b, b_hT], [bb[ba]])
                            for kc in range(8):
                                MM(banks[bbk][:, :], wb_t[:, kc, cc * 128:(cc + 1) * 128], hT[:, kc, :],
                                   kc == 0, kc == 7, [wb_b, b_hT], [bb[bbk]])
                            CP(aa[:, 0:2], halo[:, c, :], [b_halo], [b_aa])
                            ACT(aa[:, 2:514], banks[ba][:, :], AF.Copy, [bb[ba]], [b_aa])
                            CP(halo[:, c, :], aa[:, 512:514], [b_aa], [b_halo])
                            TS(z1[:, :], aa[:, 0:512], fw[:, c, 0:1], fb[:, c:c + 1], ALU.mult, ALU.add,
                               [b_aa, b_fw, b_fb], [b_z1])
                            STT(z1[:, :], aa[:, 1:513], fw[:, c, 1:2], z1[:, :], ALU.mult, ALU.add,
                                [b_aa, b_fw, b_z1], [b_z1])
                            STT(z1[:, :], aa[:, 2:514], fw[:, c, 2:3], z1[:, :], ALU.mult, ALU.add,
                                [b_aa, b_fw, b_z1], [b_z1])
                            ACT(z2[:, :], z1[:, :], AF.Square, [b_z1], [b_z2], scale=float(np.sqrt(0.044715)))
                            STT(z2[:, :], z2[:, :], 1.0, z1[:, :], ALU.add, ALU.mult, [b_z2, b_z1], [b_z2])
                            ACT(sg[:, :], z2[:, :], AF.Sigmoid, [b_z2], [b_sg], scale=float(2.0 * np.sqrt(2.0 / np.pi)))
                            TT(sg[:, :], sg[:, :], z1[:, :], ALU.mult, [b_sg, b_z1], [b_sg], eng="pool")
                            TT(gT[:, c, :], sg[:, :], banks[bbk][:, :], ALU.mult, [b_sg, bb[bbk]], [b_gT])
                    wdi = 0
                    for tp in range(2):
                        for kc in range(22):
                            w_t, w_b = WD[wdi % 2]
                            wdi += 1
                            LD(w_t[:, :], wdn_b[kc * 128:(kc + 1) * 128, :], [w_b], reads=[B["wdn_b"]])
                            for tt in range(2):
                                t = tp * 2 + tt
                                for hf in range(2):
                                    bk = 4 + tt * 2 + hf
                                    MM(banks[bk][:, :], gT[:, kc, t * 128:(t + 1) * 128],
                                       w_t[:, hf * 512:(hf + 1) * 512], kc == 0, kc == 21, [b_gT, w_b], [bb[bk]])
                        for tt in range(2):
                            t = tp * 2 + tt
                            post_norm_residual(t, gfpost, b_gfpost, 4 + tt * 2)
                            ST(x_dst[s0 + t * 128:s0 + (t + 1) * 128, :], xt4[:, t, :], [b_xt4], [b_xdst])
                cx.barrier()
                cx.emit()
            PH[0] += 1
            if stop_after is not None and PH[0] >= stop_after:
                break
    return nc


_CACHE = {}


def _in_maps(inputs):
    x = np.asarray(inputs["x"])
    Bn, S, _ = x.shape
    consts = host_consts(S)
    shared = {}
    for k, v in inputs.items():
        if k in ("x", "positions"):
            continue
        shared[k] = np.ascontiguousarray(np.asarray(v, dtype=np.float32))
    for k, v in consts.items():
        shared["c_" + k] = np.ascontiguousarray(v.astype(np.float32))
    pos = np.asarray(inputs["positions"]).astype(np.int32)
    in_maps = []
    for c in range(8):
        b = c % Bn
        m = dict(shared)
        m["x"] = np.ascontiguousarray(x[b].astype(np.float32))
        m["positions"] = np.ascontiguousarray(pos[b][None, :])
        in_maps.append(m)
    return in_maps


def kernel(_stop_after=None, **inputs):
    x = np.asarray(inputs["x"])
    Bn, S, _ = x.shape
    depth = np.asarray(inputs["w_in"]).shape[0]
    key = (S, depth, _stop_after)
    if key not in _CACHE:
        _CACHE[key] = build(S, depth, _stop_after)
    nc = _CACHE[key]
    in_maps = _in_maps(inputs)
    res = run_bass_kernel_spmd(nc, in_maps, core_ids=list(range(8)))
    out = np.stack([np.asarray(res.results[b]["y"]) for b in range(Bn)], axis=0)
    return out.astype(np.float32)
```

```python
from contextlib import ExitStack
import numpy as np
import concourse.bass as bass
import concourse.mybir as mybir
from concourse.bass_utils import run_bass_kernel_spmd

F32 = mybir.dt.float32
BF16 = mybir.dt.bfloat16
I32 = mybir.dt.int32
AF = mybir.ActivationFunctionType
ALU = mybir.AluOpType

D = 1024
DEPTH = 2
THETA = 500000.0
EPS = 1e-6
DFF = 2816
NIN = 6584
NEG = -30000.0


class Buf:
    __slots__ = ("name", "last_w", "readers", "excl")

    def __init__(self, name, excl=False):
        self.name = name
        self.last_w = None
        self.readers = []
        self.excl = excl


class Ctx:
    LOOK = 120

    def __init__(self, nc, stack, n_dma_ring=8):
        self.nc = nc
        self.engs = ("pe", "act", "dve", "pool", "sp")
        self.sems = {}
        self.count = {}
        self.waited = {k: {} for k in self.engs}
        for k in ("pe", "act", "dve", "pool"):
            self.sems[k] = stack.enter_context(nc.semaphore("s_" + k))
            self.count[k] = 0
        self.ring = {}
        self.ring_n = {}
        for q in ("sp", "pool"):
            self.ring[q] = []
            for i in range(n_dma_ring):
                s = stack.enter_context(nc.semaphore("d_%s%d" % (q, i)))
                self.ring[q].append(s)
                self.sems[("d", q, i)] = s
            self.ring_n[q] = 0
        self.nodes = []
        self.bufs = set()
        self.n_inst = 0
        self.n_wait = 0

    def _collect(self, reads, writes):
        deps = set()
        for b in reads:
            if b.last_w is not None:
                deps.add(b.last_w)
            if b.excl:
                deps.update(b.readers)
        for b in writes:
            if b.last_w is not None:
                deps.add(b.last_w)
            deps.update(b.readers)
        return deps

    def _record(self, nid, reads, writes):
        for b in reads:
            b.readers.append(nid)
            self.bufs.add(b)
        for b in writes:
            b.last_w = nid
            b.readers = []
            self.bufs.add(b)

    def op(self, e, fn, reads=(), writes=(), cost=500.0):
        deps = self._collect(reads, writes)
        nid = len(self.nodes)
        self.nodes.append((e, "op", fn, deps, float(cost), float(cost)))
        self._record(nid, reads, writes)
        return nid

    def dma(self, q, out, in_, reads=(), writes=(), nbytes=65536, **kw):
        deps = self._collect(reads, writes)
        nid = len(self.nodes)
        issue = 120.0 if q == "sp" else 700.0
        lat = 2200.0 + nbytes / 0.12
        self.nodes.append((q, "dma", (out, in_, kw), deps, issue, lat))
        self._record(nid, reads, writes)
        return nid

    def barrier(self):
        pass

    def emit(self):
        nodes = self.nodes
        n = len(nodes)
        pend = {e: [] for e in self.engs}
        for i, nd in enumerate(nodes):
            pend[nd[0]].append(i)
        head = {e: 0 for e in self.engs}
        done_t = [None] * n
        t_eng = {e: 0.0 for e in self.engs}
        order = {e: [] for e in self.engs}
        sched = [False] * n
        remaining = n
        LOOK = self.LOOK
        while remaining:
            best = None
            for e in self.engs:
                lst = pend[e]
                h = head[e]
                while h < len(lst) and sched[lst[h]]:
                    h += 1
                head[e] = h
                cnt = 0
                k = h
                while k < len(lst) and cnt < LOOK:
                    i = lst[k]
                    k += 1
                    if sched[i]:
                        continue
                    cnt += 1
                    ready = 0.0
                    ok = True
                    for d in nodes[i][3]:
                        dt_ = done_t[d]
                        if dt_ is None:
                            ok = False
                            break
                        if dt_ > ready:
                            ready = dt_
                    if not ok:
                        continue
                    start = ready if ready > t_eng[e] else t_eng[e]
                    key = (start, i)
                    if best is None or key < best[0]:
                        best = (key, e, i)
                    if ready <= t_eng[e]:
                        break
            (start, _), e, i = best
            nd = nodes[i]
            t_eng[e] = start + nd[4]
            done_t[i] = start + nd[5]
            sched[i] = True
            order[e].append(i)
            remaining -= 1
        tok = [None] * n
        pos = [0] * n
        ring_prev = {}
        for e in self.engs:
            for p, i in enumerate(order[e]):
                nd = nodes[i]
                pos[i] = p
                if nd[1] == "op":
                    self.count[e] += 1
                    tok[i] = (e, self.count[e])
                else:
                    q = e
                    m = self.ring_n[q]
                    R = len(self.ring[q])
                    key = ("d", q, m % R)
                    ring_prev[i] = (key, 16 * (m // R)) if m >= R else None
                    tok[i] = (key, 16 * (m // R + 1))
                    self.ring_n[q] = m + 1
        prog = {e: [] for e in self.engs}

        def wait(e, t):
            key, val = t
            w = self.waited[e]
            if w.get(key, 0) >= val:
                return
            w[key] = val
            sem = self.sems[key]
            prog[e].append(lambda eng, sem=sem, val=val: eng.wait_ge(sem, val))
            self.n_wait += 1

        for e in self.engs:
            for i in order[e]:
                nd = nodes[i]
                if nd[1] == "dma" and ring_prev[i] is not None:
                    wait(e, ring_prev[i])
                for d in nd[3]:
                    dn = nodes[d]
                    if dn[1] == "op" and dn[0] == e:
                        if e == "pe":
                            continue
                        if tok[i] is not None and nd[1] == "op" and tok[i][1] - tok[d][1] >= 3:
                            continue
                    wait(e, tok[d])
                if nd[1] == "op":
                    sem = self.sems[e]
                    prog[e].append(lambda eng, fn=nd[2], sem=sem: fn(eng).then_inc(sem, 1))
                else:
                    out, in_, kw = nd[2]
                    sem = self.sems[tok[i][0]]
                    prog[e].append(lambda eng, out=out, in_=in_, kw=kw, sem=sem:
                                   eng.dma_start(out=out, in_=in_, **kw).then_inc(sem, 16))
                self.n_inst += 1
        toks = [(k, self.count[k]) for k in ("pe", "act", "dve", "pool") if self.count[k] > 0]
        for q in self.ring:
            m = self.ring_n[q]
            R = len(self.ring[q])
            for slot in range(R):
                cnt = (m - slot + R - 1) // R
                if cnt > 0:
                    toks.append((("d", q, slot), 16 * cnt))
        for e in self.engs:
            for t in toks:
                if t[0] != e:
                    wait(e, t)
        with self.nc.Block() as block:
            @block.sync
            def _(eng):
                for f in prog["sp"]:
                    f(eng)

            @block.tensor
            def _(eng):
                for f in prog["pe"]:
                    f(eng)

            @block.scalar
            def _(eng):
                for f in prog["act"]:
                    f(eng)

            @block.vector
            def _(eng):
                for f in prog["dve"]:
                    f(eng)

            @block.gpsimd
            def _(eng):
                for f in prog["pool"]:
                    f(eng)
        for b_ in self.bufs:
            b_.last_w = None
            b_.readers = []
        self.bufs = set()
        self.nodes = []


def bcast_mid(ap, n):
    a = ap.ap
    return bass.AP(ap.tensor, ap.offset, [list(a[0]), [0, n]] + [list(x) for x in a[1:]])


def bcast_last(ap, n):
    a = ap.ap
    return bass.AP(ap.tensor, ap.offset, [list(x) for x in a] + [[0, n]])


def host_consts(S):
    NT = S // 128
    NSEL = S // 64
    k = np.arange(128)[:, None]
    q = np.arange(128)[None, :]
    c = {}
    c["ident"] = np.eye(128, dtype=np.float32)
    c["tri"] = (k <= q).astype(np.float32)
    c["wlow"] = (k > q).astype(np.float32)
    mc = np.zeros((128, 16, 128), np.float32)
    for o in range(16):
        mc[:, o, :] = (16 * (k - 8 * o) + 31 <= q)
    c["maskc"] = mc
    n = np.arange(512)[:, None]
    m = np.arange(128)[None, :]
    ov = ((n >= 4 * m - 1) & (n <= 4 * m + 3) & (m < NSEL)).astype(np.float32)
    c["ovl"] = ov.reshape(4, 128, 128).transpose(1, 0, 2).copy()
    A = np.zeros((NT, 128, 128), np.float32)
    for i in range(NT):
        t = 128 * i + np.arange(128)[:, None]
        cur = t // 64
        mm = np.arange(128)[None, :]
        forced = (mm == 0) | (mm == cur) | (mm == cur - 1)
        causal = 64 * mm <= t
        A[i] = np.where(forced, 1e6, np.where(causal, 0.0, -1.0))
        A[i][:, NSEL:] = -1.0
    c["amask"] = A
    E = np.zeros((128, NT, 128), np.float32)
    for j in range(NT):
        for kk in range(128):
            mrow = 2 * j + (kk >= 64)
            if mrow < 128:
                E[mrow, j, kk] = 1.0
    c["emat"] = E
    f32 = np.float32
    inv_m = (f32(THETA) ** (-(np.arange(16, dtype=f32) / f32(16)))).astype(f32)
    inv_n = (f32(THETA) ** (-(np.arange(8, dtype=f32) / f32(8)))).astype(f32)
    fm = np.zeros((128, 1), f32)
    fm[0:16, 0] = inv_m
    fm[16:32, 0] = inv_m
    fn = np.zeros((128, 1), f32)
    for base in (0, 64):
        fn[base:base + 8, 0] = inv_n
        fn[base + 8:base + 16, 0] = inv_n
    c["invf"] = np.concatenate([fm, fn], axis=1)
    return c


def build(S, depth=DEPTH, stop_after=None):
    assert S % 512 == 0
    NT = S // 128
    NS = S // 512
    NSEL = S // 64
    NCMP = S // 16 - 1
    NCT = (NCMP + 127) // 128
    CW = 65 + 128
    nc = bass.Bass("TRN2", target_bir_lowering=False)

    def din(name, shape, dt=F32):
        return nc.dram_tensor(name, list(shape), dt, kind="ExternalInput").ap()

    def dscr(name, shape, dt=BF16):
        return nc.dram_tensor(name, list(shape), dt, kind="Internal").ap()

    x_in = din("x", [S, D])
    pos_in = din("positions", [1, S], I32)
    W = {}
    W["norm_mix_pre"] = din("norm_mix_pre", [depth, D])
    W["norm_mix_post"] = din("norm_mix_post", [depth, D])
    W["w_in"] = din("w_in", [depth, D, NIN])
    W["conv_w"] = din("conv_w", [depth, 3, 512])
    W["mla_q_norm"] = din("mla_q_norm", [depth, 384])
    W["mla_w_uq"] = din("mla_w_uq", [depth, 384, 768])
    W["mla_kv_norm"] = din("mla_kv_norm", [depth, 256])
    W["mla_w_ukv"] = din("mla_w_ukv", [depth, 256, 1024])
    W["nsa_cmp_pos_k"] = din("nsa_cmp_pos_k", [depth, 32, 64])
    W["nsa_cmp_pos_v"] = din("nsa_cmp_pos_v", [depth, 32, 64])
    W["nsa_cmp_w_k"] = din("nsa_cmp_w_k", [depth, 2048, 64])
    W["nsa_cmp_w_v"] = din("nsa_cmp_w_v", [depth, 2048, 64])
    W["w_branch_conv"] = din("w_branch_conv", [depth, 512, D])
    W["w_branch_mla"] = din("w_branch_mla", [depth, 512, D])
    W["w_branch_nsa"] = din("w_branch_nsa", [depth, 512, D])
    W["w_out"] = din("w_out", [depth, D, D])
    W["norm_ffn_pre"] = din("norm_ffn_pre", [depth, D])
    W["norm_ffn_post"] = din("norm_ffn_post", [depth, D])
    W["ffn_w_up"] = din("ffn_w_up", [depth, D, 2 * DFF])
    W["ffn_conv_w"] = din("ffn_conv_w", [depth, 3, DFF])
    W["ffn_conv_b"] = din("ffn_conv_b", [depth, DFF])
    W["ffn_w_down"] = din("ffn_w_down", [depth, DFF, D])
    c_ident = din("c_ident", [128, 128])
    c_tri = din("c_tri", [128, 128])
    c_wlow = din("c_wlow", [128, 128])
    c_maskc = din("c_maskc", [128, 16, 128])
    c_ovl = din("c_ovl", [128, 4, 128])
    c_amask = din("c_amask", [NT, 128, 128])
    c_emat = din("c_emat", [128, NT, 128])
    c_invf = din("c_invf", [128, 2])
    y_out = nc.dram_tensor("y", [S, D], F32, kind="ExternalOutput").ap()

    xs = dscr("xs", [S, D], F32)
    qmT = dscr("qmT", [8, 96, S])
    kmT = dscr("kmT", [8, 96, S])
    vm = dscr("vm", [S, 8, 65])
    qnT = dscr("qnT", [2, 64, 4, S])
    kcmpT = dscr("kcmpT", [2, 64, S])
    vcmpT = dscr("vcmpT", [2, 64, S])
    kslcT = dscr("kslcT", [2, 64, S])
    kwinT = dscr("kwinT", [2, 64, S])
    vslc = dscr("vslc", [S, 2, 65])
    vwin = dscr("vwin", [S, 2, 65])
    gat = dscr("gat", [S, 24], F32)
    ymla = dscr("ymla", [S, 512])
    ynsa = dscr("ynsa", [S, 512])
    wgin_b = dscr("wgin_b", [D, 4608])
    wup_b = dscr("wup_b", [D, 2 * DFF])
    wdn_b = dscr("wdn_b", [DFF, D])

    with ExitStack() as top:
        cx = Ctx(nc, top)
        banks = [top.enter_context(nc.psum_tensor("bank%d" % i, [128, 512], F32)) for i in range(8)]
        bb = [Buf("bank%d" % i, excl=True) for i in range(8)]

        uid = [0]

        def sbt(st, name, shape, dt):
            uid[0] += 1
            nm = "%s_u%d" % (name, uid[0])
            return st.enter_context(nc.sbuf_tensor(nm, list(shape), dt)), Buf(nm)

        def fsz(ap):
            n = 1
            for d in ap.shape[1:]:
                n *= d
            return n

        def ACT(out, in_, func, reads, writes, **kw):
            cx.op("act", lambda e: e.activation(out=out, in_=in_, func=func, **kw), reads, writes,
                  cost=200 + 0.85 * fsz(out))

        def TT(out, in0, in1, op, reads, writes, eng="dve"):
            c = (90 + 1.2 * fsz(out)) if eng == "dve" else (150 + 2.3 * fsz(out))
            cx.op(eng, lambda e: e.tensor_tensor(out=out, in0=in0, in1=in1, op=op), reads, writes, cost=c)

        def TS(out, in0, s1, s2, op0, op1, reads, writes, eng="dve", **kw):
            c = (90 + 1.0 * fsz(out)) if eng == "dve" else (150 + 2.3 * fsz(out))
            if s2 is None:
                cx.op(eng, lambda e: e.tensor_scalar(out=out, in0=in0, scalar1=s1, scalar2=None, op0=op0, **kw),
                      reads, writes, cost=c)
            else:
                cx.op(eng, lambda e: e.tensor_scalar(out=out, in0=in0, scalar1=s1, scalar2=s2, op0=op0, op1=op1, **kw),
                      reads, writes, cost=c)

        def STT(out, in0, scalar, in1, op0, op1, reads, writes, eng="dve"):
            c = (90 + 1.4 * fsz(out)) if eng == "dve" else (150 + 2.3 * fsz(out))
            cx.op(eng, lambda e: e.scalar_tensor_tensor(out=out, in0=in0, scalar=scalar, in1=in1, op0=op0, op1=op1),
                  reads, writes, cost=c)

        def CP(out, in_, reads, writes, eng="dve"):
            c = (90 + 1.0 * fsz(out)) if eng == "dve" else ((200 + 0.85 * fsz(out)) if eng == "act"
                                                          else (150 + 2.3 * fsz(out)))
            cx.op(eng, lambda e: e.tensor_copy(out=out, in_=in_), reads, writes, cost=c)

        def MS(ap, val, writes, eng="pool"):
            c = (90 + 0.6 * fsz(ap)) if eng == "dve" else (150 + 1.2 * fsz(ap))
            cx.op(eng, lambda e: e.memset(ap, val), (), writes, cost=c)

        def MM(out, lhsT, rhs, start, stop, reads, writes):
            cx.op("pe", lambda e: e.matmul(out, lhsT=lhsT, rhs=rhs, start=start, stop=stop, skip_group_check=True),
                  reads, writes, cost=70 + 0.42 * max(fsz(rhs), 64))

        def TR(out, in_, ident, reads, writes):
            cx.op("pe", lambda e: e.transpose(out, in_, ident), reads, writes, cost=120)

        def nby(ap):
            n = ap.shape[0]
            for d in ap.shape[1:]:
                n *= d
            return n * (4 if ap.dtype in (F32, I32) else 2)

        LD = lambda out, in_, writes, reads=(), **kw: cx.dma("sp", out, in_, reads=reads, writes=writes,
                                                             nbytes=nby(out), **kw)
        ST = lambda out, in_, reads, writes=(), **kw: cx.dma("pool", out, in_, reads=reads, writes=writes,
                                                             nbytes=nby(in_), **kw)

        B = {n: Buf(n) for n in ("xs", "qmT", "kmT", "vm", "qnT", "kcmpT", "vcmpT", "kslcT", "kwinT", "vslc",
                                 "vwin", "gat", "ymla", "ynsa", "wgin_b", "wup_b", "wdn_b", "y")}

        identf, b_identf = sbt(top, "identf", [128, 128], F32)
        identb, b_identb = sbt(top, "identb", [128, 128], BF16)
        ones_b, b_ones = sbt(top, "ones_b", [128, 128], BF16)
        invf, b_invf = sbt(top, "invf", [128, 2], F32)
        LD(identf[:], c_ident[:, :], [b_identf])
        LD(invf[:], c_invf[:, :], [b_invf])
        CP(identb[:], identf[:], [b_identf], [b_identb])
        MS(ones_b[:], 1.0, [b_ones])
        CONST = [b_identf, b_identb, b_ones, b_invf]

        epst, b_epst = sbt(top, "epst", [128, 1], F32)
        MS(epst[:], EPS, [b_epst])

        def rsqrt_ip(ap, b_ap, inv_n, src=None, b_src=None):
            p = ap.shape[0]
            if src is None:
                src, b_src = ap, b_ap
            ACT(ap, src, AF.Sqrt, [b_src, b_epst], [b_ap], bias=epst[0:p, 0:1], scale=float(inv_n))
            cx.op("dve", lambda e: e.reciprocal(out=ap, in_=ap), [b_ap], [b_ap])

        def vec_col(st, name, src_row, n):
            c = n // 128
            t, b = sbt(st, name, [128, c], F32)
            for k in range(c):
                LD(t[:, k:k + 1], src_row[k * 128:(k + 1) * 128].rearrange("(p o) -> p o", o=1), [b])
            return t, b

        def rms_tile(xt, b_xt, rs, b_rs, col, junk, b_junk, n):
            ACT(junk, xt, AF.Square, [b_xt], [b_junk, b_rs], accum_out=rs[:, col:col + 1])
            rsqrt_ip(rs[:, col:col + 1], b_rs, 1.0 / n)

        def norm_transpose(src, s0, xt4, b_xt4, hT, b_hT, rs, b_rs, hn, b_hn, junk, b_junk, b_src, bank, b_bank):
            for t in range(4):
                LD(xt4[:, t, :], src[s0 + t * 128:s0 + (t + 1) * 128, :], [b_xt4], reads=[b_src])
            for t in range(4):
                MS(rs[:, t:t + 1], 0.0, [b_rs], eng="dve")
                rms_tile(xt4[:, t, :], b_xt4, rs, b_rs, t, junk[:], b_junk, D)
                TS(hn[:], xt4[:, t, :], rs[:, t:t + 1], None, ALU.mult, None, [b_xt4, b_rs], [b_hn])
                pb = bank[:, :].bitcast(BF16)
                for k in range(8):
                    TR(pb[:, k * 128:(k + 1) * 128], hn[:, k * 128:(k + 1) * 128], identb[:],
                       [b_hn, b_identb], [b_bank])
                CP(hT[:, :, t * 128:(t + 1) * 128], pb[:, 0:1024].rearrange("p (k t) -> p k t", k=8),
                   [b_bank], [b_hT], eng="act" if False else "dve")

        def sincos(st_name, pos_f, b_pos, col, rows, shift, out, b_out, wk, b_wk, wki, b_wki):
            r = slice(0, rows)
            TS(wk[r, 0, :], pos_f[r, :], invf[r, col:col + 1], shift, ALU.mult, ALU.add, [b_pos, b_invf], [b_wk])
            TS(wk[r, 1, :], wk[r, 0, :], float(1.0 / (2 * np.pi)), None, ALU.mult, None, [b_wk], [b_wk])
            CP(wki[r, :], wk[r, 1, :], [b_wk], [b_wki])
            CP(wk[r, 1, :], wki[r, :], [b_wki], [b_wk])
            STT(wk[r, 0, :], wk[r, 1, :], float(-2 * np.pi), wk[r, 0, :], ALU.mult, ALU.add, [b_wk], [b_wk])
            TS(wk[r, 1, :], wk[r, 0, :], float(np.pi), None, ALU.is_gt, None, [b_wk], [b_wk])
            STT(wk[r, 0, :], wk[r, 1, :], float(-2 * np.pi), wk[r, 0, :], ALU.mult, ALU.add, [b_wk], [b_wk])
            TS(wk[r, 1, :], wk[r, 0, :], float(-np.pi), None, ALU.is_lt, None, [b_wk], [b_wk])
            STT(wk[r, 0, :], wk[r, 1, :], float(2 * np.pi), wk[r, 0, :], ALU.mult, ALU.add, [b_wk], [b_wk])
            ACT(out[r, :], wk[r, 0, :], AF.Sin, [b_wk], [b_out])

        PH = [0]
        for l in range(depth):
            x_src, b_xsrc = (x_in, Buf("x_in")) if l == 0 else (xs, B["xs"])
            x_dst, b_xdst = (y_out, B["y"]) if l == depth - 1 else (xs, B["xs"])
            w_in = W["w_in"][l]

            with ExitStack() as st:
                NA = 2904
                WA, b_WA = sbt(st, "WA", [128, 8, NA], BF16)
                WQ, b_WQ = sbt(st, "WQ", [128, 3, 1024], BF16)
                WKV, b_WKV = sbt(st, "WKV", [128, 2, 1024], BF16)
                stage, b_stage = sbt(st, "stage", [128, 1976], F32)
                gpre, b_gpre = vec_col(st, "gpre", W["norm_mix_pre"][l], D)
                gq, b_gq = vec_col(st, "gq", W["mla_q_norm"][l], 384)
                gkv, b_gkv = vec_col(st, "gkv", W["mla_kv_norm"][l], 256)
                ngpre, b_ngpre = sbt(st, "ngpre", [128, 8], F32)
                ngq, b_ngq = sbt(st, "ngq", [128, 3], F32)
                TS(ngpre[:], gpre[:], -1.0, None, ALU.mult, None, [b_gpre], [b_ngpre])
                TS(ngq[:], gq[:], -1.0, None, ALU.mult, None, [b_gq], [b_ngq])
                MS(WA[:, :, 1216:1728], 0.0, [b_WA])
                for c in (1856, 2112, 2368):
                    MS(WA[:, :, c:c + 128], 0.0, [b_WA])

                def SC(out, in_, sc, rd, wr):
                    TS(out, in_, sc, None, ALU.mult, None, rd, wr)

                for kc in range(8):
                    LD(stage[:], w_in[kc * 128:(kc + 1) * 128, 4608:6584], [b_stage])
                    g = gpre[:, kc:kc + 1]
                    ng = ngpre[:, kc:kc + 1]
                    rd = [b_stage, b_gpre, b_ngpre]
                    SC(WA[:, kc, 0:672], stage[:, 0:672], g, rd, [b_WA])
                    SC(WA[:, kc, 672:688], stage[:, 656:672], ng, rd, [b_WA])
                    SC(WA[:, kc, 688:704], stage[:, 640:656], g, rd, [b_WA])
                    SC(WA[:, kc, 704:1216], stage[:, 672:1184], g, rd, [b_WA])
                    sq = stage[:, 672:1184].rearrange("p (h d) -> p h d", d=64)
                    dq = WA[:, kc, 1216:1728].rearrange("p (h d) -> p h d", d=64)
                    SC(dq[:, :, 0:8], sq[:, :, 8:16], ng, rd, [b_WA])
                    SC(dq[:, :, 8:16], sq[:, :, 0:8], g, rd, [b_WA])
                    for (src_o, dst_o) in ((1184, 1728), (1440, 1984), (1696, 2240)):
                        SC(WA[:, kc, dst_o:dst_o + 128], stage[:, src_o:src_o + 128], g, rd, [b_WA])
                        sk = stage[:, src_o:src_o + 128].rearrange("p (h d) -> p h d", d=64)
                        dk = WA[:, kc, dst_o + 128:dst_o + 256].rearrange("p (h d) -> p h d", d=64)
                        SC(dk[:, :, 0:8], sk[:, :, 8:16], ng, rd, [b_WA])
                        SC(dk[:, :, 8:16], sk[:, :, 0:8], g, rd, [b_WA])
                    SC(WA[:, kc, 2496:2624], stage[:, 1312:1440], g, rd, [b_WA])
                    SC(WA[:, kc, 2624:2752], stage[:, 1568:1696], g, rd, [b_WA])
                    SC(WA[:, kc, 2752:2880], stage[:, 1824:1952], g, rd, [b_WA])
                    SC(WA[:, kc, 2880:2904], stage[:, 1952:1976], g, rd, [b_WA])
                for kc in range(3):
                    LD(stage[:, 0:768], W["mla_w_uq"][l][kc * 128:(kc + 1) * 128, :], [b_stage])
                    g = gq[:, kc:kc + 1]
                    ng = ngq[:, kc:kc + 1]
                    rd = [b_stage, b_gq, b_ngq]
                    s3 = stage[:, 0:768].rearrange("p (h d) -> p h d", d=96)
                    d3 = WQ[:, kc, 0:768].rearrange("p (h d) -> p h d", d=96)
                    SC(d3[:, :, 0:32], s3[:, :, 64:96], g, rd, [b_WQ])
                    SC(d3[:, :, 32:96], s3[:, :, 0:64], g, rd, [b_WQ])
                    r3 = WQ[:, kc, 768:1024].rearrange("p (h d) -> p h d", d=32)
                    SC(r3[:, :, 0:16], s3[:, :, 80:96], ng, rd, [b_WQ])
                    SC(r3[:, :, 16:32], s3[:, :, 64:80], g, rd, [b_WQ])
                for kc in range(2):
                    LD(stage[:, 0:1024], W["mla_w_ukv"][l][kc * 128:(kc + 1) * 128, :], [b_stage])
                    g = gkv[:, kc:kc + 1]
                    rd = [b_stage, b_gkv]
                    s3 = stage[:, 0:1024].rearrange("p (h d) -> p h d", d=128)
                    SC(WKV[:, kc, 0:512].rearrange("p (h d) -> p h d", d=64), s3[:, :, 0:64], g, rd, [b_WKV])
                    SC(WKV[:, kc, 512:1024].rearrange("p (h d) -> p h d", d=64), s3[:, :, 64:128], g, rd, [b_WKV])

                gf, b_gf = vec_col(st, "gf", W["norm_ffn_pre"][l], D)
                wst, b_wst = sbt(st, "wst", [128, 5632], F32)
                wsb, b_wsb = sbt(st, "wsb", [128, 5632], BF16)
                for kc in range(8):
                    LD(wst[:, 0:4608], w_in[kc * 128:(kc + 1) * 128, 0:4608], [b_wst])
                    SC(wsb[:, 0:4608], wst[:, 0:4608], gpre[:, kc:kc + 1], [b_wst, b_gpre], [b_wsb])
                    ST(wgin_b[kc * 128:(kc + 1) * 128, :], wsb[:, 0:4608], [b_wsb], [B["wgin_b"]])
                for kc in range(8):
                    LD(wst[:, :], W["ffn_w_up"][l][kc * 128:(kc + 1) * 128, :], [b_wst])
                    SC(wsb[:, :], wst[:, :], gf[:, kc:kc + 1], [b_wst, b_gf], [b_wsb])
                    ST(wup_b[kc * 128:(kc + 1) * 128, :], wsb[:, :], [b_wsb], [B["wup_b"]])
                for kc in range(22):
                    LD(wst[:, 0:1024], W["ffn_w_down"][l][kc * 128:(kc + 1) * 128, :], [b_wst])
                    CP(wsb[:, 0:1024], wst[:, 0:1024], [b_wst], [b_wsb])
                    ST(wdn_b[kc * 128:(kc + 1) * 128, :], wsb[:, 0:1024], [b_wsb], [B["wdn_b"]])

                xt4, b_xt4 = sbt(st, "xt4", [128, 4, D], F32)
                hT, b_hT = sbt(st, "hT", [128, 8, 512], BF16)
                hn, b_hn = sbt(st, "hn", [128, D], BF16)
                junk, b_junk = sbt(st, "junk", [128, D], BF16)
                rs, b_rs = sbt(st, "rs", [128, 8], F32)
                posi, b_posi = sbt(st, "posi", [128, 512], I32)
                posf, b_posf = sbt(st, "posf", [128, 512], F32)
                wk, b_wk = sbt(st, "wk", [128, 2, 512], F32)
                wki, b_wki = sbt(st, "wki", [128, 512], I32)
                Cm, b_Cm = sbt(st, "Cm", [128, 512], F32)
                Sm, b_Sm = sbt(st, "Sm", [128, 512], F32)
                Cn, b_Cn = sbt(st, "Cn", [128, 512], F32)
                Sn, b_Sn = sbt(st, "Sn", [128, 512], F32)
                cqT, b_cqT = sbt(st, "cqT", [128, 3, 512], BF16)
                ckvT, b_ckvT = sbt(st, "ckvT", [128, 2, 512], BF16)
                sqb, b_sqb = sbt(st, "sqb", [128, 3, 512], BF16)
                rq, b_rq = sbt(st, "rq", [128, 512], F32)
                rkv, b_rkv = sbt(st, "rkv", [128, 512], F32)
                rkt, b_rkt = sbt(st, "rkt", [128, 4], F32)
                CR, b_CR = sbt(st, "CR", [128, 512], F32)
                SR, b_SR = sbt(st, "SR", [128, 512], F32)
                t1, b_t1 = sbt(st, "t1", [128, 512], F32)
                t2, b_t2 = sbt(st, "t2", [128, 512], F32)
                ob = [sbt(st, "ob%d" % i, [128, 512], BF16) for i in range(3)]
                vt = [sbt(st, "vt%d" % i, [128, 8, 65], BF16) for i in range(2)]
                vs2 = [sbt(st, "vs2_%d" % i, [128, 2, 2, 65], BF16) for i in range(2)]
                gt = [sbt(st, "gt%d" % i, [128, 24], F32) for i in range(2)]
                for (t_, b_) in vt:
                    MS(t_[:], 1.0, [b_])
                for (t_, b_) in vs2:
                    MS(t_[:], 1.0, [b_])
                obi = [0]

                def next_ob():
                    obi[0] += 1
                    return ob[obi[0] % 3]

                for s in range(NS):
                    s0 = s * 512
                    norm_transpose(x_src, s0, xt4, b_xt4, hT, b_hT, rs, b_rs, hn, b_hn, junk, b_junk, b_xsrc,
                                   banks[7], bb[7])
                    LD(posi[:], pos_in[0:1, s0:s0 + 512].partition_broadcast(128), [b_posi])
                    CP(posf[:], posi[:], [b_posi], [b_posf])
                    sincos("cm", posf, b_posf, 0, 128, float(np.pi / 2), Cm, b_Cm, wk, b_wk, wki, b_wki)
                    sincos("sm", posf, b_posf, 0, 128, 0.0, Sm, b_Sm, wk, b_wk, wki, b_wki)
                    sincos("cn", posf, b_posf, 1, 128, float(np.pi / 2), Cn, b_Cn, wk, b_wk, wki, b_wki)
                    sincos("sn", posf, b_posf, 1, 128, 0.0, Sn, b_Sn, wk, b_wk, wki, b_wki)

                    def proj(bank_i, c0, m, rd_extra=()):
                        for kc in range(8):
                            MM(banks[bank_i][0:m, :], WA[:, kc, c0:c0 + m], hT[:, kc, :], kc == 0, kc == 7,
                               [b_WA, b_hT], [bb[bank_i]])

                    for c in range(3):
                        proj(c % 2, c * 128, 128)
                        CP(cqT[:, c, :], banks[c % 2][:, :], [bb[c % 2]], [b_cqT], eng="act" if False else "dve")
                        ACT(sqb[:, c, :], banks[c % 2][:, :], AF.Square, [bb[c % 2]], [b_sqb])
                    for c in range(3):
                        MM(banks[2][:, :], ones_b[:, :], sqb[:, c, :], c == 0, c == 2, [b_ones, b_sqb], [bb[2]])
                    rsqrt_ip(rq[:], b_rq, 1.0 / 384, src=banks[2][:, :], b_src=bb[2])
                    TT(CR[:], Cm[:], rq[:], ALU.mult, [b_Cm, b_rq], [b_CR])
                    TT(SR[:], Sm[:], rq[:], ALU.mult, [b_Sm, b_rq], [b_SR])
                    for h in range(8):
                        bq = 3 + (h % 2) * 2
                        for kc in range(3):
                            MM(banks[bq][0:96, :], WQ[:, kc, h * 96:(h + 1) * 96], cqT[:, kc, :], kc == 0, kc == 2,
                               [b_WQ, b_cqT], [bb[bq]])
                        for kc in range(3):
                            MM(banks[bq + 1][0:32, :], WQ[:, kc, 768 + h * 32:768 + (h + 1) * 32], cqT[:, kc, :],
                               kc == 0, kc == 2, [b_WQ, b_cqT], [bb[bq + 1]])
                        o_t, o_b = next_ob()
                        TT(t1[0:96, :], banks[bq][0:96, :], CR[0:96, :], ALU.mult, [bb[bq], b_CR], [b_t1])
                        TT(t2[0:32, :], banks[bq + 1][0:32, :], SR[0:32, :], ALU.mult, [bb[bq + 1], b_SR], [b_t2])
                        TT(o_t[0:32, :], t1[0:32, :], t2[0:32, :], ALU.add, [b_t1, b_t2], [o_b], eng="pool")
                        CP(o_t[32:64, :], t1[32:64, :], [b_t1], [o_b], eng="pool")
                        CP(o_t[64:96, :], t1[64:96, :], [b_t1], [o_b], eng="pool")
                        ST(qmT[h, :, s0:s0 + 512], o_t[0:96, :], [o_b], [B["qmT"]])
                    for c in range(2):
                        proj(c, 384 + c * 128, 128)
                        CP(ckvT[:, c, :], banks[c][:, :], [bb[c]], [b_ckvT])
                        ACT(sqb[:, c, :], banks[c][:, :], AF.Square, [bb[c]], [b_sqb])
                    for c in range(2):
                        MM(banks[2][:, :], ones_b[:, :], sqb[:, c, :], c == 0, c == 1, [b_ones, b_sqb], [bb[2]])
                    rsqrt_ip(rkv[:], b_rkv, 1.0 / 256, src=banks[2][:, :], b_src=bb[2])
                    for t in range(4):
                        for c in range(2):
                            MM(banks[2][:, t:t + 1], sqb[:, c, t * 128:(t + 1) * 128], ones_b[:, 0:1],
                               (t == 0 and c == 0), c == 1, [b_ones, b_sqb], [bb[2]])
                    rsqrt_ip(rkt[:], b_rkt, 1.0 / 256, src=banks[2][:, 0:4], b_src=bb[2])
                    for g2 in range(4):
                        bk = 3 + (g2 % 2)
                        for kc in range(2):
                            MM(banks[bk][:, :], WKV[:, kc, g2 * 128:(g2 + 1) * 128], ckvT[:, kc, :], kc == 0, kc == 1,
                               [b_WKV, b_ckvT], [bb[bk]])
                        o_t, o_b = next_ob()
                        TT(o_t[:, :], banks[bk][:, :], rkv[:, :], ALU.mult, [bb[bk], b_rkv], [o_b])
                        ST(kmT[2 * g2, 32:96, s0:s0 + 512], o_t[0:64, :], [o_b], [B["kmT"]])
                        ST(kmT[2 * g2 + 1, 32:96, s0:s0 + 512], o_t[64:128, :], [o_b], [B["kmT"]])
                    for t in range(4):
                        bv = 5 + (t % 2)
                        for kc in range(2):
                            MM(banks[bv][:, :], ckvT[:, kc, t * 128:(t + 1) * 128], WKV[:, kc, 512:1024], kc == 0,
                               kc == 1, [b_WKV, b_ckvT], [bb[bv]])
                        v_t, v_b = vt[t % 2]
                        TS(v_t[:, :, 0:64], banks[bv][:, :].rearrange("p (h d) -> p h d", d=64), rkt[:, t:t + 1], None,
                           ALU.mult, None, [bb[bv], b_rkt], [v_b])
                        ST(vm[s0 + t * 128:s0 + (t + 1) * 128, :, :], v_t[:], [v_b], [B["vm"]])
                    proj(0, 640, 32)
                    proj(1, 672, 32)
                    o_t, o_b = next_ob()
                    TT(t1[0:32, :], banks[0][0:32, :], Cm[0:32, :], ALU.mult, [bb[0], b_Cm], [b_t1])
                    TT(t2[0:32, :], banks[1][0:32, :], Sm[0:32, :], ALU.mult, [bb[1], b_Sm], [b_t2])
                    TT(o_t[0:32, :], t1[0:32, :], t2[0:32, :], ALU.add, [b_t1, b_t2], [o_b], eng="pool")
                    for h in range(8):
                        ST(kmT[h, 0:32, s0:s0 + 512], o_t[0:32, :], [o_b], [B["kmT"]])
                    def rope_group(c0, r0, dests):
                        bq = 3
                        proj(bq, c0, 128)
                        proj(bq + 1, r0, 128)
                        o_t, o_b = next_ob()
                        TT(t1[:, :], banks[bq][:, :], Cn[:, :], ALU.mult, [bb[bq], b_Cn], [b_t1])
                        TT(t2[:, :], banks[bq + 1][:, :], Sn[:, :], ALU.mult, [bb[bq + 1], b_Sn], [b_t2])
                        TT(o_t[:, :], t1[:, :], t2[:, :], ALU.add, [b_t1, b_t2], [o_b], eng="pool")
                        for (dst, bname, lo) in dests:
                            ST(dst, o_t[lo:lo + 64, :], [o_b], [B[bname]])
                    for pr in range(4):
                        h0 = 2 * pr
                        rope_group(704 + pr * 128, 1216 + pr * 128,
                                   [(qnT[h0 // 4, :, h0 % 4, s0:s0 + 512], "qnT", 0),
                                    (qnT[(h0 + 1) // 4, :, (h0 + 1) % 4, s0:s0 + 512], "qnT", 64)])
                    for (c0, dst, nm) in ((1728, kcmpT, "kcmpT"), (1984, kslcT, "kslcT"), (2240, kwinT, "kwinT")):
                        rope_group(c0, c0 + 128, [(dst[0, :, s0:s0 + 512], nm, 0), (dst[1, :, s0:s0 + 512], nm, 64)])
                    proj(5, 2496, 128)
                    o_t, o_b = next_ob()
                    CP(o_t[:, :], banks[5][:, :], [bb[5]], [o_b], eng="act" if False else "dve")
                    ST(vcmpT[0, :, s0:s0 + 512], o_t[0:64, :], [o_b], [B["vcmpT"]])
                    ST(vcmpT[1, :, s0:s0 + 512], o_t[64:128, :], [o_b], [B["vcmpT"]])
                    for t in range(4):
                        bv = 5 + (t % 2)
                        for kc in range(8):
                            MM(banks[bv][:, 0:280], hT[:, kc, t * 128:(t + 1) * 128], WA[:, kc, 2624:2904], kc == 0,
                               kc == 7, [b_WA, b_hT], [bb[bv]])
                        v_t, v_b = vs2[t % 2]
                        CP(v_t[:, :, :, 0:64], banks[bv][:, 0:256].rearrange("p (a g d) -> p a g d", a=2, g=2),
                           [bb[bv]], [v_b])
                        g_t, g_b = gt[t % 2]
                        ACT(g_t[:, :], banks[bv][:, 256:280], AF.Sigmoid, [bb[bv]], [g_b])
                        r0, r1 = s0 + t * 128, s0 + (t + 1) * 128
                        ST(vslc[r0:r1, :, :], v_t[:, 0, :, :], [v_b], [B["vslc"]])
                        ST(vwin[r0:r1, :, :], v_t[:, 1, :, :], [v_b], [B["vwin"]])
                        ST(gat[r0:r1, :], g_t[:, :], [g_b], [B["gat"]])
                cx.barrier()
                cx.emit()
            PH[0] += 1
            if stop_after is not None and PH[0] >= stop_after:
                break

            with ExitStack() as st:
                tri, b_tri = sbt(st, "tri", [128, 128], BF16)
                trif, b_trif = sbt(st, "trif", [128, 128], F32)
                LD(trif[:], c_tri[:, :], [b_trif])
                CP(tri[:], trif[:], [b_trif], [b_tri])
                KT, b_KT = sbt(st, "KT", [96, 4, S], BF16)
                VV, b_VV = sbt(st, "VV", [128, NT, 4, 65], BF16)
                QT = [sbt(st, "QT%d" % i, [96, 4, 128], BF16) for i in range(2)]
                PT = [sbt(st, "PT%d" % i, [128, 512], BF16) for i in range(4)]
                recs = [sbt(st, "rec%d" % i, [128, 4], F32) for i in range(2)]
                yo = [sbt(st, "yo%d" % i, [128, 4, 64], BF16) for i in range(2)]
                scale_m = float(96 ** -0.5)
                LAG = 2

                def run_pipeline(items):
                    n = len(items)
                    for k in range(n + LAG):
                        if k >= LAG:
                            items[k - LAG][2]()
                        if k < n:
                            items[k][0]()
                            items[k][1]()

                for hg in range(2):
                    for h in range(4):
                        LD(KT[:, h, :], kmT[hg * 4 + h, :, :], [b_KT], reads=[B["kmT"]])
                    for jt in range(NT):
                        LD(VV[:, jt, :, :], vm[jt * 128:(jt + 1) * 128, hg * 4:(hg + 1) * 4, :], [b_VV],
                           reads=[B["vm"]])
                    items = []
                    for i in range(NT):
                        for j in range(i + 1):
                            k = len(items)

                            def fS(i=i, j=j, k=k):
                                q_t, q_b = QT[i % 2]
                                if j == 0:
                                    for h in range(4):
                                        LD(q_t[:, h, :], qmT[hg * 4 + h, :, i * 128:(i + 1) * 128], [q_b],
                                           reads=[B["qmT"]])
                                sb_i = k % 3
                                for h in range(4):
                                    MM(banks[sb_i][:, h * 128:(h + 1) * 128], KT[:, h, j * 128:(j + 1) * 128],
                                       q_t[:, h, :], h == 0, h == 3, [b_KT, q_b], [bb[sb_i]])

                            def fE(i=i, j=j, k=k):
                                sb_i = k % 3
                                p_t, p_b = PT[k % 4]
                                ACT(p_t[:, :], banks[sb_i][:, :], AF.Exp, [bb[sb_i]], [p_b], scale=scale_m)
                                if j == i:
                                    TT(p_t[:, :].rearrange("p (h q) -> p h q", h=4),
                                       p_t[:, :].rearrange("p (h q) -> p h q", h=4), bcast_mid(tri[:, :], 4), ALU.mult,
                                       [p_b, b_tri], [p_b], eng="pool")

                            def fP(i=i, j=j, k=k):
                                p_t, p_b = PT[k % 4]
                                ab = 4 + (i % 2)
                                for h in range(4):
                                    MM(banks[ab][:, h * 65:(h + 1) * 65], p_t[:, h * 128:(h + 1) * 128],
                                       VV[:, j, h, :], (j == 0 and h == 0), j == i, [p_b, b_VV], [bb[ab]])
                                if j == i:
                                    rec, b_rec = recs[i % 2]
                                    acc = banks[ab][:, 0:260].rearrange("p (h c) -> p h c", c=65)
                                    TS(rec[:, :], acc[:, :, 64], 1e-30, None, ALU.max, None, [bb[ab]], [b_rec])
                                    cx.op("dve", lambda e, rec=rec: e.reciprocal(out=rec[:, :], in_=rec[:, :]),
                                          [b_rec], [b_rec])
                                    y_t, y_b = yo[i % 2]
                                    TT(y_t[:, :, :], acc[:, :, 0:64], bcast_last(rec[:, :], 64), ALU.mult,
                                       [bb[ab], b_rec], [y_b])
                                    ST(ymla[i * 128:(i + 1) * 128, hg * 256:(hg + 1) * 256],
                                       y_t[:, :, :].rearrange("p h d -> p (h d)"), [y_b], [B["ymla"]])

                            items.append((fS, fE, fP))
                    run_pipeline(items)
                cx.barrier()
                cx.emit()
            PH[0] += 1
            if stop_after is not None and PH[0] >= stop_after:
                break

            with ExitStack() as st:
                cst, b_cst = sbt(st, "cst", [128, 16 * 128], F32)
                tri, b_tri = sbt(st, "tri", [128, 128], BF16)
                wlow, b_wlow = sbt(st, "wlow", [128, 128], BF16)
                maskc, b_maskc = sbt(st, "maskc", [128, 16, 128], BF16)
                emat, b_emat = sbt(st, "emat", [128, NT, 128], BF16)
                LD(cst[:, 0:128], c_tri[:, :], [b_cst])
                CP(tri[:], cst[:, 0:128], [b_cst], [b_tri])
                LD(cst[:, 0:128], c_wlow[:, :], [b_cst])
                CP(wlow[:], cst[:, 0:128], [b_cst], [b_wlow])
                LD(cst[:, :].rearrange("p (o q) -> p o q", o=16), c_maskc[:, :, :], [b_cst])
                CP(maskc[:], cst[:, :].rearrange("p (o q) -> p o q", o=16), [b_cst], [b_maskc])
                for j0 in range(0, NT, 16):
                    jn = min(16, NT - j0)
                    LD(cst[:, 0:jn * 128].rearrange("p (o q) -> p o q", o=jn), c_emat[:, j0:j0 + jn, :], [b_cst])
                    CP(emat[:, j0:j0 + jn, :], cst[:, 0:jn * 128].rearrange("p (o q) -> p o q", o=jn), [b_cst],
                       [b_emat])
                wck, b_wck = sbt(st, "wck", [64, 32, 64], BF16)
                wcv, b_wcv = sbt(st, "wcv", [64, 32, 64], BF16)
                wcs, b_wcs = sbt(st, "wcs", [64, 32, 64], F32)
                LD(wcs[:], W["nsa_cmp_w_k"][l].rearrange("(r d) e -> d r e", d=64), [b_wcs])
                CP(wck[:], wcs[:], [b_wcs], [b_wck])
                LD(wcs[:], W["nsa_cmp_w_v"][l].rearrange("(r d) e -> d r e", d=64), [b_wcs])
                CP(wcv[:], wcs[:], [b_wcs], [b_wcv])
                pkv, b_pkv = sbt(st, "pkv", [32, 2, 64], F32)
                LD(pkv[:, 0, :], W["nsa_cmp_pos_k"][l], [b_pkv])
                LD(pkv[:, 1, :], W["nsa_cmp_pos_v"][l], [b_pkv])
                posT, b_posT = sbt(st, "posT", [64, 2, 32], BF16)
                for a in range(2):
                    TR(banks[0][0:64, a * 32:(a + 1) * 32], pkv[:, a, :], identf[0:32, 0:32], [b_pkv, b_identf],
                       [bb[0]])
                CP(posT[:, :, :], banks[0][0:64, 0:64].rearrange("p (a r) -> p a r", a=2), [bb[0]], [b_posT])
                biask, b_biask = sbt(st, "biask", [64, 1], F32)
                biasv, b_biasv = sbt(st, "biasv", [1, 64], BF16)
                for r in range(32):
                    MM(banks[1][0:64, 0:1], wck[:, r, :], posT[:, 0, r:r + 1], r == 0, r == 31, [b_wck, b_posT],
                       [bb[1]])
                CP(biask[:, :], banks[1][0:64, 0:1], [bb[1]], [b_biask])
                for r in range(32):
                    MM(banks[2][0:1, 0:64], posT[:, 1, r:r + 1], wcv[:, r, :], r == 0, r == 31, [b_wcv, b_posT],
                       [bb[2]])
                CP(biasv[:, :], banks[2][0:1, 0:64], [bb[2]], [b_biasv])
                KS, b_KS = sbt(st, "KS", [64, 2, S], BF16)
                KW, b_KW = sbt(st, "KW", [64, 2, S], BF16)
                kcT, b_kcT = sbt(st, "kcT", [64, 2, 512], BF16)
                VC, b_VC = sbt(st, "VC", [128, 4, 2, CW], BF16)
                MS(kcT[:], 0.0, [b_kcT])
                MS(VC[:], 0.0, [b_VC])
                for g in range(2):
                    LD(KS[:, g, :], kcmpT[g, :, :], [b_KS], reads=[B["kcmpT"]])
                    LD(KW[:, g, :], vcmpT[g, :, :], [b_KW], reads=[B["vcmpT"]])
                LD(cst[:, 0:512].rearrange("p (t m) -> p t m", t=4), c_ovl[:, :, :], [b_cst])
                for g in range(2):
                    MS(VC[:, :, g, 64:65], 1.0, [b_VC])
                    CP(VC[:, :, g, 65:CW], cst[:, 0:512].rearrange("p (t m) -> p t m", t=4), [b_cst], [b_VC])
                for g in range(2):
                    n0 = 0
                    while n0 < NCMP:
                        nn = min(512, NCMP - n0)
                        for r in range(32):
                            src = KS[:, g, r + 16 * n0:r + 16 * n0 + 16 * (nn - 1) + 1]
                            rhs = bass.AP(src.tensor, src.offset, [list(src.ap[0]), [16, nn]])
                            MM(banks[3][0:64, 0:nn], wck[:, r, :], rhs, r == 0, r == 31, [b_wck, b_KS], [bb[3]])
                        ACT(kcT[:, g, n0:n0 + nn], banks[3][0:64, 0:nn], AF.Identity, [bb[3], b_biask], [b_kcT],
                            bias=biask[:, 0:1], scale=1.0)
                        n0 += nn
                    for jt in range(NCT):
                        nn = min(128, NCMP - jt * 128)
                        for r in range(32):
                            o = r + 16 * jt * 128
                            src = KW[:, g, o:o + 16 * (nn - 1) + 1]
                            lhsT = bass.AP(src.tensor, src.offset, [list(src.ap[0]), [16, nn]])
                            MM(banks[4][0:nn, 0:64], lhsT, wcv[:, r, :], r == 0, False, [b_wcv, b_KW], [bb[4]])
                        MM(banks[4][0:nn, 0:64], ones_b[0:1, 0:nn], biasv[0:1, :], False, True, [b_ones, b_biasv],
                           [bb[4]])
                        CP(VC[0:nn, jt, g, 0:64], banks[4][0:nn, 0:64], [bb[4]], [b_VC])
                VS, b_VS = sbt(st, "VS", [128, NT, 2, 65], BF16)
                VW, b_VW = sbt(st, "VW", [128, NT, 2, 65], BF16)
                for g in range(2):
                    LD(KS[:, g, :], kslcT[g, :, :], [b_KS], reads=[B["kslcT"]])
                    LD(KW[:, g, :], kwinT[g, :, :], [b_KW], reads=[B["kwinT"]])
                for jt in range(NT):
                    LD(VS[:, jt, :, :], vslc[jt * 128:(jt + 1) * 128, :, :], [b_VS], reads=[B["vslc"]])
                    LD(VW[:, jt, :, :], vwin[jt * 128:(jt + 1) * 128, :, :], [b_VW], reads=[B["vwin"]])
                QN = [sbt(st, "QN%d" % i, [64, 2, 4, 128], BF16) for i in range(2)]
                GA = [sbt(st, "GA%d" % i, [128, 24], F32) for i in range(2)]
                AM = [sbt(st, "AM%d" % i, [128, 128], F32) for i in range(2)]
                PT = [sbt(st, "PTn%d" % i, [128, 512], BF16) for i in range(4)]
                imps = [sbt(st, "imp%d" % i, [128, 128], F32) for i in range(2)]
                impws = [sbt(st, "impw%d" % i, [128, 128], F32) for i in range(2)]
                top16s = [sbt(st, "top16_%d" % i, [128, 16], F32) for i in range(2)]
                negms = [sbt(st, "negm%d" % i, [128, 128], BF16) for i in range(2)]
                negTs = [sbt(st, "negT%d" % i, [128, 128], BF16) for i in range(2)]
                recs = [sbt(st, "recn%d" % i, [128, 4], F32) for i in range(6)]
                wgts = [sbt(st, "wgt%d" % i, [128, 4], F32) for i in range(6)]
                yaccs = [sbt(st, "yacc%d" % i, [128, 8, 64], F32) for i in range(2)]
                ytmps = [sbt(st, "ytmp%d" % i, [128, 4, 64], F32) for i in range(2)]
                yo = [sbt(st, "yon%d" % i, [128, 512], BF16) for i in range(2)]
                scale_n = 0.125
                LAG = 2
                rot = [0]

                def small():
                    rot[0] += 1
                    return recs[rot[0] % 6], wgts[rot[0] % 6]

                def exp_mask(k, masks):
                    sb_i = k % 3
                    p_t, p_b = PT[k % 4]
                    ACT(p_t[:, :], banks[sb_i][:, :], AF.Exp, [bb[sb_i]], [p_b], scale=scale_n)
                    for (m_ap, m_b) in masks:
                        TT(p_t[:, :].rearrange("p (h q) -> p h q", h=4),
                           p_t[:, :].rearrange("p (h q) -> p h q", h=4), bcast_mid(m_ap, 4), ALU.mult,
                           [p_b, m_b], [p_b], eng="pool")

                def branch_out(i, g, bank_i, br):
                    acc = banks[bank_i][:, 0:260].rearrange("p (h c) -> p h c", c=65)
                    acc_b = bb[bank_i]
                    (rec, b_rec), (wgt, b_wgt) = small()
                    g_t, g_b = GA[i % 2]
                    gv = g_t[:, :].rearrange("p (h b) -> p h b", b=3)
                    yacc, b_yacc = yaccs[i % 2]
                    ytmp, b_ytmp = ytmps[(2 * i + g + br) % 2]
                    TS(rec[:, :], acc[:, :, 64], 1e-30, None, ALU.max, None, [acc_b], [b_rec])
                    cx.op("dve", lambda e, rec=rec: e.reciprocal(out=rec[:, :], in_=rec[:, :]), [b_rec], [b_rec])
                    TT(wgt[:, :], rec[:, :], gv[:, g * 4:(g + 1) * 4, br], ALU.mult, [b_rec, g_b], [b_wgt])
                    TT(ytmp[:, :, :], acc[:, :, 0:64], bcast_last(wgt[:, :], 64), ALU.mult, [acc_b, b_wgt], [b_ytmp])
                    TT(yacc[:, g * 4:(g + 1) * 4, :], yacc[:, g * 4:(g + 1) * 4, :], ytmp[:, :, :], ALU.add,
                       [b_yacc, b_ytmp], [b_yacc], eng="pool")

                def cmp_final(i, g):
                    (rec, b_rec), (wgt, b_wgt) = small()
                    par = (2 * i + g) % 2
                    imp, b_imp = imps[par]
                    impw, b_impw = impws[par]
                    top16, b_top16 = top16s[par]
                    negm, b_negm = negms[par]
                    g_t, g_b = GA[i % 2]
                    a_t, a_b = AM[i % 2]
                    gv = g_t[:, :].rearrange("p (h b) -> p h b", b=3)
                    yacc, b_yacc = yaccs[i % 2]
                    for h in range(4):
                        bk = 3 + h // 2
                        a0 = (h % 2) * CW
                        TS(rec[:, h:h + 1], banks[bk][:, a0 + 64:a0 + 65], 1e-30, None, ALU.max, None, [bb[bk]],
                           [b_rec])
                    cx.op("dve", lambda e, rec=rec: e.reciprocal(out=rec[:, :], in_=rec[:, :]), [b_rec], [b_rec])
                    for h in range(4):
                        bk = 3 + h // 2
                        a0 = (h % 2) * CW
                        if h == 0:
                            TS(imp[:, :], banks[bk][:, a0 + 65:a0 + CW], rec[:, 0:1], None, ALU.mult, None,
                               [bb[bk], b_rec], [b_imp])
                        else:
                            STT(imp[:, :], banks[bk][:, a0 + 65:a0 + CW], rec[:, h:h + 1], imp[:, :], ALU.mult,
                                ALU.add, [bb[bk], b_rec, b_imp], [b_imp])
                    TT(wgt[:, :], rec[:, :], gv[:, g * 4:(g + 1) * 4, 0], ALU.mult, [b_rec, g_b], [b_wgt])
                    for h in range(4):
                        bk = 3 + h // 2
                        a0 = (h % 2) * CW
                        TS(yacc[:, g * 4 + h, :], banks[bk][:, a0:a0 + 64], wgt[:, h:h + 1], None, ALU.mult, None,
                           [bb[bk], b_wgt], [b_yacc])
                    TT(imp[:, :], imp[:, :], a_t[:, :], ALU.add, [b_imp, a_b], [b_imp])
                    cx.op("dve", lambda e, top16=top16, imp=imp: e.max(out=top16[:, 0:8], in_=imp[:, :]),
                          [b_imp], [b_top16])
                    cx.op("dve", lambda e, top16=top16, imp=imp, impw=impw:
                          e.match_replace(out=impw[:, :], in_to_replace=top16[:, 0:8], in_values=imp[:, :],
                                          imm_value=-3.0), [b_imp, b_top16], [b_impw])
                    cx.op("dve", lambda e, top16=top16, impw=impw: e.max(out=top16[:, 8:16], in_=impw[:, :]),
                          [b_impw], [b_top16])
                    TS(impw[:, :], imp[:, :], top16[:, 15:16], -1.0, ALU.is_ge, ALU.add, [b_imp, b_top16], [b_impw])
                    TS(negm[:, :], impw[:, :], -NEG, None, ALU.mult, None, [b_impw], [b_negm])

                def neg_transpose(i, g):
                    par = (2 * i + g) % 2
                    negm, b_negm = negms[par]
                    negT, b_negT = negTs[par]
                    pb = banks[7][:, :].bitcast(BF16)
                    TR(pb[:, 0:128], negm[:, :], identb[:, :], [b_negm, b_identb], [bb[7]])
                    CP(negT[:, :], pb[:, 0:128], [bb[7]], [b_negT])

                items = []
                for i in range(NT):
                    for g in range(2):
                        jl = i // 16
                        j_lo = max(0, i - 4)
                        seq = [("c", jt) for jt in range(jl + 1)] + [("w", j) for j in range(j_lo, i + 1)] + \
                              [("s", j) for j in range(i + 1)]
                        for (kind, j) in seq:
                            k = len(items)

                            def fS(i=i, g=g, kind=kind, j=j, k=k):
                                q_t, q_b = QN[i % 2]
                                if g == 0 and kind == "c" and j == 0:
                                    g_t, g_b = GA[i % 2]
                                    a_t, a_b = AM[i % 2]
                                    for gg in range(2):
                                        LD(q_t[:, gg, :, :], qnT[gg, :, :, i * 128:(i + 1) * 128], [q_b],
                                           reads=[B["qnT"]])
                                    LD(g_t[:, :], gat[i * 128:(i + 1) * 128, :], [g_b], reads=[B["gat"]])
                                    LD(a_t[:, :], c_amask[i, :, :], [a_b])
                                qrhs = q_t[:, g, :, :]
                                sb_i = k % 3
                                out3 = banks[sb_i][:, :].rearrange("p (h q) -> p h q", h=4)
                                if kind == "c":
                                    MM(out3, kcT[:, g, j * 128:(j + 1) * 128], qrhs, True, True, [b_kcT, q_b],
                                       [bb[sb_i]])
                                elif kind == "w":
                                    MM(out3, KW[:, g, j * 128:(j + 1) * 128], qrhs, True, True, [b_KW, q_b],
                                       [bb[sb_i]])
                                else:
                                    if j == 0:
                                        neg_transpose(i, g)
                                    negT, b_negT = negTs[(2 * i + g) % 2]
                                    MM(out3, KS[:, g, j * 128:(j + 1) * 128], qrhs, True, False, [b_KS, q_b],
                                       [bb[sb_i]])
                                    MM(out3, emat[:, j, :], bcast_mid(negT[:, :], 4), False, True,
                                       [b_emat, b_negT], [bb[sb_i]])

                            def fE(i=i, g=g, kind=kind, j=j, k=k):
                                masks = []
                                if kind == "c":
                                    if j == i // 16:
                                        masks.append((maskc[:, i % 16, :], b_maskc))
                                else:
                                    if j == i:
                                        masks.append((tri[:, :], b_tri))
                                    if kind == "w" and j == i - 4:
                                        masks.append((wlow[:, :], b_wlow))
                                exp_mask(k, masks)

                            def fP(i=i, g=g, kind=kind, j=j, k=k):
                                p_t, p_b = PT[k % 4]
                                if kind == "c":
                                    jl = i // 16
                                    for h in range(4):
                                        bk = 3 + h // 2
                                        MM(banks[bk][:, (h % 2) * CW:(h % 2 + 1) * CW], p_t[:, h * 128:(h + 1) * 128],
                                           VC[:, j, g, :], (j == 0 and h % 2 == 0), j == jl, [p_b, b_VC], [bb[bk]])
                                    if j == jl:
                                        cmp_final(i, g)
                                elif kind == "w":
                                    j_lo = max(0, i - 4)
                                    for h in range(4):
                                        MM(banks[6][:, h * 65:(h + 1) * 65], p_t[:, h * 128:(h + 1) * 128],
                                           VW[:, j, g, :], (j == j_lo and h == 0), j == i, [p_b, b_VW], [bb[6]])
                                    if j == i:
                                        branch_out(i, g, 6, 2)
                                else:
                                    for h in range(4):
                                        MM(banks[5][:, h * 65:(h + 1) * 65], p_t[:, h * 128:(h + 1) * 128],
                                           VS[:, j, g, :], (j == 0 and h == 0), j == i, [p_b, b_VS], [bb[5]])
                                    if j == i:
                                        branch_out(i, g, 5, 1)
                                        if g == 1:
                                            yacc, b_yacc = yaccs[i % 2]
                                            y_t, y_b = yo[i % 2]
                                            CP(y_t[:, :], yacc[:, :, :].rearrange("p h d -> p (h d)"), [b_yacc], [y_b])
                                            ST(ynsa[i * 128:(i + 1) * 128, :], y_t[:, :], [y_b], [B["ynsa"]])

                            items.append((fS, fE, fP))
                n_items = len(items)
                for k in range(n_items + LAG):
                    if k >= LAG:
                        items[k - LAG][2]()
                    if k < n_items:
                        items[k][0]()
                        items[k][1]()
                cx.barrier()
                cx.emit()
            PH[0] += 1
            if stop_after is not None and PH[0] >= stop_after:
                break

            with ExitStack() as st:
                WB, b_WB = sbt(st, "WB", [128, 12, D], BF16)
                WO, b_WO = sbt(st, "WO", [128, 8, D], BF16)
                wst, b_wst = sbt(st, "wstT", [128, D], F32)
                for bi, nm in enumerate(("w_branch_conv", "w_branch_mla", "w_branch_nsa")):
                    for kc in range(4):
                        LD(wst[:], W[nm][l][kc * 128:(kc + 1) * 128, :], [b_wst])
                        CP(WB[:, bi * 4 + kc, :], wst[:], [b_wst], [b_WB])
                for kc in range(8):
                    LD(wst[:], W["w_out"][l][kc * 128:(kc + 1) * 128, :], [b_wst])
                    CP(WO[:, kc, :], wst[:], [b_wst], [b_WO])
                gpost, b_gpost = sbt(st, "gpost", [128, D], F32)
                gfpost, b_gfpost = sbt(st, "gfpost", [128, D], F32)
                LD(gpost[:], W["norm_mix_post"][l:l + 1, :].partition_broadcast(128), [b_gpost])
                LD(gfpost[:], W["norm_ffn_post"][l:l + 1, :].partition_broadcast(128), [b_gfpost])
                cw, b_cw = sbt(st, "cw", [128, 4, 3], F32)
                fw, b_fw = sbt(st, "fw", [128, 22, 3], F32)
                fb, b_fb = sbt(st, "fb", [128, 22], F32)
                for k in range(3):
                    for c in range(4):
                        LD(cw[:, c, k:k + 1], W["conv_w"][l][k, c * 128:(c + 1) * 128].rearrange("(p o) -> p o", o=1),
                           [b_cw])
                    for c in range(22):
                        LD(fw[:, c, k:k + 1],
                           W["ffn_conv_w"][l][k, c * 128:(c + 1) * 128].rearrange("(p o) -> p o", o=1), [b_fw])
                for c in range(22):
                    LD(fb[:, c:c + 1], W["ffn_conv_b"][l][c * 128:(c + 1) * 128].rearrange("(p o) -> p o", o=1), [b_fb])
                xt4, b_xt4 = sbt(st, "xt4T", [128, 4, D], F32)
                hT, b_hT = sbt(st, "hTT", [128, 8, 512], BF16)
                hns = [sbt(st, "hnT%d" % i, [128, D], BF16) for i in range(2)]
                junk, b_junk = sbt(st, "junkT", [128, D], BF16)
                rs, b_rs = sbt(st, "rsT", [128, 16], F32)
                WS = [sbt(st, "WS%d" % i, [128, 8, 256], BF16) for i in range(3)]
                WD = [sbt(st, "WD%d" % i, [128, D], BF16) for i in range(2)]
                gT, b_gT = sbt(st, "gT", [128, 22, 512], BF16)
                cin, b_cin = gT, b_gT
                uu, b_uu = sbt(st, "uu", [128, 4, 514], F32)
                ycT, b_ycT = sbt(st, "ycT", [128, 4, 512], BF16)
                ymT, b_ymT = sbt(st, "ymT", [128, 4, 512], BF16)
                ynT, b_ynT = sbt(st, "ynT", [128, 4, 512], BF16)
                ytks = [sbt(st, "ytk%d" % i, [128, 512], BF16) for i in range(2)]
                gsbs = [[sbt(st, "gsb%d_%d" % (i, k), [128, 512], BF16) for k in range(3)] for i in range(2)]
                zA, b_zA = sbt(st, "zA", [128, 512], F32)
                zB, b_zB = sbt(st, "zB", [128, 512], F32)
                mT, b_mT = sbt(st, "mT", [128, 8, 512], BF16)
                aas = [sbt(st, "aa%d" % i, [128, 514], F32) for i in range(3)]
                halo, b_halo = sbt(st, "halo", [128, 22, 2], F32)
                z1s = [sbt(st, "z1_%d" % i, [128, 512], F32) for i in range(3)]
                z2s = [sbt(st, "z2_%d" % i, [128, 512], F32) for i in range(3)]
                xos = [sbt(st, "xo%d" % i, [128, D], F32) for i in range(2)]
                MS(uu[:], 0.0, [b_uu])
                MS(halo[:], 0.0, [b_halo])
                wsi = [0]

                def load_ws(src, c0):
                    wsi[0] += 1
                    w_t, w_b = WS[wsi[0] % 3]
                    LD(w_t[:, :, :], src[:, c0:c0 + 256].rearrange("(k p) c -> p k c", p=128), [w_b],
                       reads=[B["wgin_b"], B["wup_b"]])
                    return w_t, w_b

                pnr = [0]

                def post_norm_residual(t, gtile, b_gt, first_bank):
                    pnr[0] += 1
                    xo, b_xo = xos[pnr[0] % 2]
                    c0 = 4 + 2 * (pnr[0] % 4)
                    MS(rs[:, c0:c0 + 2], 0.0, [b_rs], eng="dve")
                    ACT(junk[:, 0:512], banks[first_bank][:, :], AF.Square, [bb[first_bank]], [b_junk, b_rs],
                        accum_out=rs[:, c0:c0 + 1])
                    ACT(junk[:, 512:1024], banks[first_bank + 1][:, :], AF.Square, [bb[first_bank + 1]],
                        [b_junk, b_rs], accum_out=rs[:, c0 + 1:c0 + 2])
                    TT(rs[:, c0:c0 + 1], rs[:, c0:c0 + 1], rs[:, c0 + 1:c0 + 2], ALU.add, [b_rs], [b_rs])
                    rsqrt_ip(rs[:, c0:c0 + 1], b_rs, 1.0 / D)
                    for hf in range(2):
                        STT(xo[:, hf * 512:(hf + 1) * 512], banks[first_bank + hf][:, :], rs[:, c0:c0 + 1],
                            gtile[:, hf * 512:(hf + 1) * 512], ALU.mult, ALU.mult,
                            [bb[first_bank + hf], b_rs, b_gt], [b_xo])
                    TT(xt4[:, t, :], xt4[:, t, :], xo[:, :], ALU.add, [b_xt4, b_xo], [b_xt4])

                def norm_tr(t, tb):
                    hn, b_hn = hns[t % 2]
                    MS(rs[:, t:t + 1], 0.0, [b_rs], eng="dve")
                    rms_tile(xt4[:, t, :], b_xt4, rs, b_rs, t, junk[:], b_junk, D)
                    TS(hn[:], xt4[:, t, :], rs[:, t:t + 1], None, ALU.mult, None, [b_xt4, b_rs], [b_hn])
                    pb = banks[tb][:, :].bitcast(BF16)
                    for k in range(8):
                        TR(pb[:, k * 128:(k + 1) * 128], hn[:, k * 128:(k + 1) * 128], identb[:],
                           [b_hn, b_identb], [bb[tb]])
                    ACT(hT[:, :, t * 128:(t + 1) * 128], pb[:, 0:1024].rearrange("p (k t) -> p k t", k=8),
                        AF.Copy, [bb[tb]], [b_hT])

                for s in range(NS):
                    s0 = s * 512
                    for t in range(4):
                        LD(xt4[:, t, :], x_src[s0 + t * 128:s0 + (t + 1) * 128, :], [b_xt4], reads=[b_xsrc])
                    for t in range(4):
                        norm_tr(t, 6 + t % 2)
                    yi = 0
                    for (ysrc, bname, yT, b_yT) in ((ymla, "ymla", ymT, b_ymT), (ynsa, "ynsa", ynT, b_ynT)):
                        for t in range(4):
                            ytk, b_ytk = ytks[yi % 2]
                            tb = 4 + yi % 2
                            yi += 1
                            LD(ytk[:, :], ysrc[s0 + t * 128:s0 + (t + 1) * 128, :], [b_ytk], reads=[B[bname]])
                            pb = banks[tb][:, :].bitcast(BF16)
                            for k in range(4):
                                TR(pb[:, k * 128:(k + 1) * 128], ytk[:, k * 128:(k + 1) * 128], identb[:],
                                   [b_ytk, b_identb], [bb[tb]])
                            ACT(yT[:, :, t * 128:(t + 1) * 128], pb[:, 0:512].rearrange("p (k t) -> p k t", k=4),
                                AF.Copy, [bb[tb]], [b_yT])
                    for c2 in range(6):
                        w_t, w_b = load_ws(wgin_b, 3072 + c2 * 256)
                        for c in range(2):
                            bk = (c2 * 2 + c) % 4
                            for kc in range(8):
                                MM(banks[bk][:, :], w_t[:, kc, c * 128:(c + 1) * 128], hT[:, kc, :], kc == 0, kc == 7,
                                   [w_b, b_hT], [bb[bk]])
                            if c == 0:
                                ACT(cin[:, c2 * 2 + c, :], banks[bk][:, :], AF.Copy, [bb[bk]], [b_cin])
                            else:
                                CP(cin[:, c2 * 2 + c, :], banks[bk][:, :], [bb[bk]], [b_cin])
                    for c in range(4):
                        CP(uu[:, c, 0:2], uu[:, c, 512:514], [b_uu], [b_uu])
                    for c in range(4):
                        TT(uu[:, c, 2:514], cin[:, 4 + c, :], cin[:, 8 + c, :], ALU.mult, [b_cin], [b_uu])
                        TS(zA[:, :], uu[:, c, 0:512], cw[:, c, 0:1], None, ALU.mult, None, [b_uu, b_cw], [b_zA])
                        STT(zA[:, :], uu[:, c, 1:513], cw[:, c, 1:2], zA[:, :], ALU.mult, ALU.add,
                            [b_uu, b_cw, b_zA], [b_zA])
                        STT(zA[:, :], uu[:, c, 2:514], cw[:, c, 2:3], zA[:, :], ALU.mult, ALU.add,
                            [b_uu, b_cw, b_zA], [b_zA])
                        TT(ycT[:, c, :], zA[:, :], cin[:, c, :], ALU.mult, [b_zA, b_cin], [b_ycT])
                    for n2 in range(4):
                        wts = [load_ws(wgin_b, bi * 1024 + n2 * 256) for bi in range(3)]
                        for nn in range(2):
                            n = n2 * 2 + nn
                            gs = gsbs[n % 2]
                            for bi in range(3):
                                w_t, w_b = wts[bi]
                                for kc in range(8):
                                    MM(banks[bi][:, :], w_t[:, kc, nn * 128:(nn + 1) * 128], hT[:, kc, :], kc == 0,
                                       kc == 7, [w_b, b_hT], [bb[bi]])
                                ACT(gs[bi][0][:, :], banks[bi][:, :], AF.Sigmoid, [bb[bi]], [gs[bi][1]])
                            for bi, (yT, b_yT) in enumerate(((ycT, b_ycT), (ymT, b_ymT), (ynT, b_ynT))):
                                for kc in range(4):
                                    MM(banks[3 + bi][:, :], WB[:, bi * 4 + kc, n * 128:(n + 1) * 128], yT[:, kc, :],
                                       kc == 0, kc == 3, [b_WB, b_yT], [bb[3 + bi]])
                            TT(zA[:, :], banks[3][:, :], gs[0][0][:, :], ALU.mult, [bb[3], gs[0][1]], [b_zA])
                            TT(zB[:, :], banks[4][:, :], gs[1][0][:, :], ALU.mult, [bb[4], gs[1][1]], [b_zB])
                            TT(zA[:, :], zA[:, :], zB[:, :], ALU.add, [b_zA, b_zB], [b_zA])
                            TT(zB[:, :], banks[5][:, :], gs[2][0][:, :], ALU.mult, [bb[5], gs[2][1]], [b_zB])
                            TT(mT[:, n, :], zA[:, :], zB[:, :], ALU.add, [b_zA, b_zB], [b_mT])
                    for t in range(4):
                        fbk = (t % 2) * 2
                        for hf in range(2):
                            for kc in range(8):
                                MM(banks[fbk + hf][:, :], mT[:, kc, t * 128:(t + 1) * 128],
                                   WO[:, kc, hf * 512:(hf + 1) * 512], kc == 0, kc == 7, [b_mT, b_WO], [bb[fbk + hf]])
                        post_norm_residual(t, gpost, b_gpost, fbk)
                    for t in range(4):
                        norm_tr(t, 6 + t % 2)
                    for c2 in range(11):
                        wa_t, wa_b = load_ws(wup_b, c2 * 256)
                        wb_t, wb_b = load_ws(wup_b, DFF + c2 * 256)
                        for cc in range(2):
                            c = c2 * 2 + cc
                            par = c % 2
                            p3 = c % 3
                            ba, bbk = par * 2, par * 2 + 1
                            aa, b_aa = aas[p3]
                            z1, b_z1 = z1s[p3]
                            z2, b_z2 = z2s[p3]
                            for kc in range(8):
                                MM(banks[ba][:, :], wa_t[:, kc, cc * 128:(cc + 1) * 128], hT[:, kc, :],
                                   kc == 0, kc == 7, [wa_b, b_hT], [bb[ba]])
                            for kc in range(8):
                                MM(banks[bbk][:, :], wb_t[:, kc, cc * 128:(cc + 1) * 128], hT[:, kc, :],
                                   kc == 0, kc == 7, [wb_b, b_hT], [bb[bbk]])
                            CP(aa[:, 0:2], halo[:, c, :], [b_halo], [b_aa], eng="pool")
                            ACT(aa[:, 2:514], banks[ba][:, :], AF.Copy, [bb[ba]], [b_aa])
                            CP(halo[:, c, :], aa[:, 512:514], [b_aa], [b_halo], eng="pool")
                            ACT(z1[:, :], aa[:, 0:512], AF.Identity, [b_aa, b_fw, b_fb], [b_z1],
                                scale=fw[:, c, 0:1], bias=fb[:, c:c + 1])
                            STT(z1[:, :], aa[:, 1:513], fw[:, c, 1:2], z1[:, :], ALU.mult, ALU.add,
                                [b_aa, b_fw, b_z1], [b_z1])
                            STT(z1[:, :], aa[:, 2:514], fw[:, c, 2:3], z1[:, :], ALU.mult, ALU.add,
                                [b_aa, b_fw, b_z1], [b_z1])
                            ACT(z2[:, :], z1[:, :], AF.Square, [b_z1], [b_z2], scale=float(np.sqrt(0.044715)))
                            STT(z2[:, :], z2[:, :], 1.0, z1[:, :], ALU.add, ALU.mult, [b_z2, b_z1], [b_z2])
                            ACT(z2[:, :], z2[:, :], AF.Sigmoid, [b_z2], [b_z2], scale=float(2.0 * np.sqrt(2.0 / np.pi)))
                            TT(z1[:, :], z1[:, :], banks[bbk][:, :], ALU.mult, [b_z1, bb[bbk]], [b_z1])
                            TT(gT[:, c, :], z2[:, :], z1[:, :], ALU.mult, [b_z2, b_z1], [b_gT])
                    wdi = 0
                    for tp in range(2):
                        for kc in range(22):
                            w_t, w_b = WD[wdi % 2]
                            wdi += 1
                            LD(w_t[:, :], wdn_b[kc * 128:(kc + 1) * 128, :], [w_b], reads=[B["wdn_b"]])
                            for tt in range(2):
                                t = tp * 2 + tt
                                for hf in range(2):
                                    bk = 4 + tt * 2 + hf
                                    MM(banks[bk][:, :], gT[:, kc, t * 128:(t + 1) * 128],
                                       w_t[:, hf * 512:(hf + 1) * 512], kc == 0, kc == 21, [b_gT, w_b], [bb[bk]])
                        for tt in range(2):
                            t = tp * 2 + tt
                            post_norm_residual(t, gfpost, b_gfpost, 4 + tt * 2)
                            ST(x_dst[s0 + t * 128:s0 + (t + 1) * 128, :], xt4[:, t, :], [b_xt4], [b_xdst])
                cx.barrier()
                cx.emit()
            PH[0] += 1
            if stop_after is not None and PH[0] >= stop_after:
                break
    return nc


_CACHE = {}


def _in_maps(inputs):
    x = np.asarray(inputs["x"])
    Bn, S, _ = x.shape
    consts = host_consts(S)
    shared = {}
    for k, v in inputs.items():
        if k in ("x", "positions"):
            continue
        shared[k] = np.ascontiguousarray(np.asarray(v, dtype=np.float32))
    for k, v in consts.items():
        shared["c_" + k] = np.ascontiguousarray(v.astype(np.float32))
    pos = np.asarray(inputs["positions"]).astype(np.int32)
    in_maps = []
    for c in range(8):
        b = c % Bn
        m = dict(shared)
        m["x"] = np.ascontiguousarray(x[b].astype(np.float32))
        m["positions"] = np.ascontiguousarray(pos[b][None, :])
        in_maps.append(m)
    return in_maps


def kernel(_stop_after=None, **inputs):
    x = np.asarray(inputs["x"])
    Bn, S, _ = x.shape
    depth = np.asarray(inputs["w_in"]).shape[0]
    key = (S, depth, _stop_after)
    if key not in _CACHE:
        _CACHE[key] = build(S, depth, _stop_after)
    nc = _CACHE[key]
    in_maps = _in_maps(inputs)
    res = run_bass_kernel_spmd(nc, in_maps, core_ids=list(range(8)))
    out = np.stack([np.asarray(res.results[b]["y"]) for b in range(Bn)], axis=0)
    return out.astype(np.float32)
```

```python
from contextlib import ExitStack
import numpy as np
import concourse.bass as bass
import concourse.mybir as mybir
from concourse.bass_utils import run_bass_kernel_spmd

F32 = mybir.dt.float32
BF16 = mybir.dt.bfloat16
I32 = mybir.dt.int32
AF = mybir.ActivationFunctionType
ALU = mybir.AluOpType

D = 1024
DEPTH = 2
THETA = 500000.0
EPS = 1e-6
DFF = 2816
NIN = 6584
NEG = -30000.0


class Buf:
    __slots__ = ("name", "last_w", "readers", "excl")

    def __init__(self, name, excl=False):
        self.name = name
        self.last_w = None
        self.readers = []
        self.excl = excl


class Ctx:
    LOOK = 120

    def __init__(self, nc, stack, n_dma_ring=8):
        self.nc = nc
        self.engs = ("pe", "act", "dve", "pool", "sp")
        self.sems = {}
        self.count = {}
        self.waited = {k: {} for k in self.engs}
        for k in ("pe", "act", "dve", "pool"):
            self.sems[k] = stack.enter_context(nc.semaphore("s_" + k))
            self.count[k] = 0
        self.ring = {}
        self.ring_n = {}
        for q in ("sp", "pool"):
            self.ring[q] = []
            for i in range(n_dma_ring):
                s = stack.enter_context(nc.semaphore("d_%s%d" % (q, i)))
                self.ring[q].append(s)
                self.sems[("d", q, i)] = s
            self.ring_n[q] = 0
        self.nodes = []
        self.bufs = set()
        self.n_inst = 0
        self.n_wait = 0

    def _collect(self, reads, writes):
        deps = set()
        for b in reads:
            if b.last_w is not None:
                deps.add(b.last_w)
            if b.excl:
                deps.update(b.readers)
        for b in writes:
            if b.last_w is not None:
                deps.add(b.last_w)
            deps.update(b.readers)
        return deps

    def _record(self, nid, reads, writes):
        for b in reads:
            b.readers.append(nid)
            self.bufs.add(b)
        for b in writes:
            b.last_w = nid
            b.readers = []
            self.bufs.add(b)

    def op(self, e, fn, reads=(), writes=(), cost=500.0):
        deps = self._collect(reads, writes)
        nid = len(self.nodes)
        self.nodes.append((e, "op", fn, deps, float(cost), float(cost)))
        self._record(nid, reads, writes)
        return nid

    def dma(self, q, out, in_, reads=(), writes=(), nbytes=65536, **kw):
        deps = self._collect(reads, writes)
        nid = len(self.nodes)
        issue = 120.0 if q == "sp" else 700.0
        lat = 2200.0 + nbytes / 0.12
        self.nodes.append((q, "dma", (out, in_, kw), deps, issue, lat))
        self._record(nid, reads, writes)
        return nid

    def barrier(self):
        pass

    def emit(self):
        nodes = self.nodes
        n = len(nodes)
        pend = {e: [] for e in self.engs}
        for i, nd in enumerate(nodes):
            pend[nd[0]].append(i)
        head = {e: 0 for e in self.engs}
        done_t = [None] * n
        t_eng = {e: 0.0 for e in self.engs}
        order = {e: [] for e in self.engs}
        sched = [False] * n
        remaining = n
        LOOK = self.LOOK
        while remaining:
            best = None
            for e in self.engs:
                lst = pend[e]
                h = head[e]
                while h < len(lst) and sched[lst[h]]:
                    h += 1
                head[e] = h
                cnt = 0
                k = h
                while k < len(lst) and cnt < LOOK:
                    i = lst[k]
                    k += 1
                    if sched[i]:
                        continue
                    cnt += 1
                    ready = 0.0
                    ok = True
                    for d in nodes[i][3]:
                        dt_ = done_t[d]
                        if dt_ is None:
                            ok = False
                            break
                        if dt_ > ready:
                            ready = dt_
                    if not ok:
                        continue
                    start = ready if ready > t_eng[e] else t_eng[e]
                    key = (start, i)
                    if best is None or key < best[0]:
                        best = (key, e, i)
                    if ready <= t_eng[e]:
                        break
            (start, _), e, i = best
            nd = nodes[i]
            t_eng[e] = start + nd[4]
            done_t[i] = start + nd[5]
            sched[i] = True
            order[e].append(i)
            remaining -= 1
        tok = [None] * n
        pos = [0] * n
        ring_prev = {}
        for e in self.engs:
            for p, i in enumerate(order[e]):
                nd = nodes[i]
                pos[i] = p
                if nd[1] == "op":
                    self.count[e] += 1
                    tok[i] = (e, self.count[e])
                else:
                    q = e
                    m = self.ring_n[q]
                    R = len(self.ring[q])
                    key = ("d", q, m % R)
                    ring_prev[i] = (key, 16 * (m // R)) if m >= R else None
                    tok[i] = (key, 16 * (m // R + 1))
                    self.ring_n[q] = m + 1
        prog = {e: [] for e in self.engs}

        def wait(e, t):
            key, val = t
            w = self.waited[e]
            if w.get(key, 0) >= val:
                return
            w[key] = val
            sem = self.sems[key]
            prog[e].append(lambda eng, sem=sem, val=val: eng.wait_ge(sem, val))
            self.n_wait += 1

        for e in self.engs:
            for i in order[e]:
                nd = nodes[i]
                if nd[1] == "dma" and ring_prev[i] is not None:
                    wait(e, ring_prev[i])
                for d in nd[3]:
                    dn = nodes[d]
                    if dn[1] == "op" and dn[0] == e:
                        if e == "pe":
                            continue
                        if tok[i] is not None and nd[1] == "op" and tok[i][1] - tok[d][1] >= 3:
                            continue
                    wait(e, tok[d])
                if nd[1] == "op":
                    sem = self.sems[e]
                    prog[e].append(lambda eng, fn=nd[2], sem=sem: fn(eng).then_inc(sem, 1))
                else:
                    out, in_, kw = nd[2]
                    sem = self.sems[tok[i][0]]
                    prog[e].append(lambda eng, out=out, in_=in_, kw=kw, sem=sem:
                                   eng.dma_start(out=out, in_=in_, **kw).then_inc(sem, 16))
                self.n_inst += 1
        toks = [(k, self.count[k]) for k in ("pe", "act", "dve", "pool") if self.count[k] > 0]
        for q in self.ring:
            m = self.ring_n[q]
            R = len(self.ring[q])
            for slot in range(R):
                cnt = (m - slot + R - 1) // R
                if cnt > 0:
                    toks.append((("d", q, slot), 16 * cnt))
        for e in self.engs:
            for t in toks:
                if t[0] != e:
                    wait(e, t)
        with self.nc.Block() as block:
            @block.sync
            def _(eng):
                for f in prog["sp"]:
                    f(eng)

            @block.tensor
            def _(eng):
                for f in prog["pe"]:
                    f(eng)

            @block.scalar
            def _(eng):
                for f in prog["act"]:
                    f(eng)

            @block.vector
            def _(eng):
                for f in prog["dve"]:
                    f(eng)

            @block.gpsimd
            def _(eng):
                for f in prog["pool"]:
                    f(eng)
        for b_ in self.bufs:
            b_.last_w = None
            b_.readers = []
        self.bufs = set()
        self.nodes = []


def bcast_mid(ap, n):
    a = ap.ap
    return bass.AP(ap.tensor, ap.offset, [list(a[0]), [0, n]] + [list(x) for x in a[1:]])


def bcast_last(ap, n):
    a = ap.ap
    return bass.AP(ap.tensor, ap.offset, [list(x) for x in a] + [[0, n]])


def host_consts(S):
    NT = S // 128
    NSEL = S // 64
    k = np.arange(128)[:, None]
    q = np.arange(128)[None, :]
    c = {}
    c["ident"] = np.eye(128, dtype=np.float32)
    c["tri"] = (k <= q).astype(np.float32)
    c["wlow"] = (k > q).astype(np.float32)
    mc = np.zeros((128, 16, 128), np.float32)
    for o in range(16):
        mc[:, o, :] = (16 * (k - 8 * o) + 31 <= q)
    c["maskc"] = mc
    n = np.arange(512)[:, None]
    m = np.arange(128)[None, :]
    ov = ((n >= 4 * m - 1) & (n <= 4 * m + 3) & (m < NSEL)).astype(np.float32)
    c["ovl"] = ov.reshape(4, 128, 128).transpose(1, 0, 2).copy()
    A = np.zeros((NT, 128, 128), np.float32)
    for i in range(NT):
        t = 128 * i + np.arange(128)[:, None]
        cur = t // 64
        mm = np.arange(128)[None, :]
        forced = (mm == 0) | (mm == cur) | (mm == cur - 1)
        causal = 64 * mm <= t
        A[i] = np.where(forced, 1e6, np.where(causal, 0.0, -1.0))
        A[i][:, NSEL:] = -1.0
    c["amask"] = A
    E = np.zeros((128, NT, 128), np.float32)
    for j in range(NT):
        for kk in range(128):
            mrow = 2 * j + (kk >= 64)
            if mrow < 128:
                E[mrow, j, kk] = 1.0
    c["emat"] = E
    f32 = np.float32
    inv_m = (f32(THETA) ** (-(np.arange(16, dtype=f32) / f32(16)))).astype(f32)
    inv_n = (f32(THETA) ** (-(np.arange(8, dtype=f32) / f32(8)))).astype(f32)
    fm = np.zeros((128, 1), f32)
    fm[0:16, 0] = inv_m
    fm[16:32, 0] = inv_m
    fn = np.zeros((128, 1), f32)
    for base in (0, 64):
        fn[base:base + 8, 0] = inv_n
        fn[base + 8:base + 16, 0] = inv_n
    c["invf"] = np.concatenate([fm, fn], axis=1)
    return c


def build(S, depth=DEPTH, stop_after=None):
    assert S % 512 == 0
    NT = S // 128
    NS = S // 512
    NSEL = S // 64
    NCMP = S // 16 - 1
    NCT = (NCMP + 127) // 128
    CW = 65 + 128
    nc = bass.Bass("TRN2", target_bir_lowering=False)

    def din(name, shape, dt=F32):
        return nc.dram_tensor(name, list(shape), dt, kind="ExternalInput").ap()

    def dscr(name, shape, dt=BF16):
        return nc.dram_tensor(name, list(shape), dt, kind="Internal").ap()

    x_in = din("x", [S, D])
    pos_in = din("positions", [1, S], I32)
    W = {}
    W["norm_mix_pre"] = din("norm_mix_pre", [depth, D])
    W["norm_mix_post"] = din("norm_mix_post", [depth, D])
    W["w_in"] = din("w_in", [depth, D, NIN])
    W["conv_w"] = din("conv_w", [depth, 3, 512])
    W["mla_q_norm"] = din("mla_q_norm", [depth, 384])
    W["mla_w_uq"] = din("mla_w_uq", [depth, 384, 768])
    W["mla_kv_norm"] = din("mla_kv_norm", [depth, 256])
    W["mla_w_ukv"] = din("mla_w_ukv", [depth, 256, 1024])
    W["nsa_cmp_pos_k"] = din("nsa_cmp_pos_k", [depth, 32, 64])
    W["nsa_cmp_pos_v"] = din("nsa_cmp_pos_v", [depth, 32, 64])
    W["nsa_cmp_w_k"] = din("nsa_cmp_w_k", [depth, 2048, 64])
    W["nsa_cmp_w_v"] = din("nsa_cmp_w_v", [depth, 2048, 64])
    W["w_branch_conv"] = din("w_branch_conv", [depth, 512, D])
    W["w_branch_mla"] = din("w_branch_mla", [depth, 512, D])
    W["w_branch_nsa"] = din("w_branch_nsa", [depth, 512, D])
    W["w_out"] = din("w_out", [depth, D, D])
    W["norm_ffn_pre"] = din("norm_ffn_pre", [depth, D])
    W["norm_ffn_post"] = din("norm_ffn_post", [depth, D])
    W["ffn_w_up"] = din("ffn_w_up", [depth, D, 2 * DFF])
    W["ffn_conv_w"] = din("ffn_conv_w", [depth, 3, DFF])
    W["ffn_conv_b"] = din("ffn_conv_b", [depth, DFF])
    W["ffn_w_down"] = din("ffn_w_down", [depth, DFF, D])
    c_ident = din("c_ident", [128, 128])
    c_tri = din("c_tri", [128, 128])
    c_wlow = din("c_wlow", [128, 128])
    c_maskc = din("c_maskc", [128, 16, 128])
    c_ovl = din("c_ovl", [128, 4, 128])
    c_amask = din("c_amask", [NT, 128, 128])
    c_emat = din("c_emat", [128, NT, 128])
    c_invf = din("c_invf", [128, 2])
    y_out = nc.dram_tensor("y", [S, D], F32, kind="ExternalOutput").ap()

    xs = dscr("xs", [S, D], F32)
    qmT = dscr("qmT", [8, 96, S])
    kmT = dscr("kmT", [8, 96, S])
    vm = dscr("vm", [S, 8, 65])
    qnT = dscr("qnT", [2, 64, 4, S])
    kcmpT = dscr("kcmpT", [2, 64, S])
    vcmpT = dscr("vcmpT", [2, 64, S])
    kslcT = dscr("kslcT", [2, 64, S])
    kwinT = dscr("kwinT", [2, 64, S])
    vslc = dscr("vslc", [S, 2, 65])
    vwin = dscr("vwin", [S, 2, 65])
    gat = dscr("gat", [S, 24], F32)
    ymla = dscr("ymla", [S, 512])
    ynsa = dscr("ynsa", [S, 512])
    wgin_b = dscr("wgin_b", [D, 4608])
    wup_b = dscr("wup_b", [D, 2 * DFF])
    wdn_b = dscr("wdn_b", [DFF, D])

    with ExitStack() as top:
        cx = Ctx(nc, top)
        banks = [top.enter_context(nc.psum_tensor("bank%d" % i, [128, 512], F32)) for i in range(8)]
        bb = [Buf("bank%d" % i, excl=True) for i in range(8)]

        uid = [0]

        def sbt(st, name, shape, dt):
            uid[0] += 1
            nm = "%s_u%d" % (name, uid[0])
            return st.enter_context(nc.sbuf_tensor(nm, list(shape), dt)), Buf(nm)

        def fsz(ap):
            n = 1
            for d in ap.shape[1:]:
                n *= d
            return n

        def ACT(out, in_, func, reads, writes, **kw):
            cx.op("act", lambda e: e.activation(out=out, in_=in_, func=func, **kw), reads, writes,
                  cost=200 + 0.85 * fsz(out))

        def TT(out, in0, in1, op, reads, writes, eng="dve"):
            c = (90 + 1.2 * fsz(out)) if eng == "dve" else (150 + 2.3 * fsz(out))
            cx.op(eng, lambda e: e.tensor_tensor(out=out, in0=in0, in1=in1, op=op), reads, writes, cost=c)

        def TS(out, in0, s1, s2, op0, op1, reads, writes, eng="dve", **kw):
            c = (90 + 1.0 * fsz(out)) if eng == "dve" else (150 + 2.3 * fsz(out))
            if s2 is None:
                cx.op(eng, lambda e: e.tensor_scalar(out=out, in0=in0, scalar1=s1, scalar2=None, op0=op0, **kw),
                      reads, writes, cost=c)
            else:
                cx.op(eng, lambda e: e.tensor_scalar(out=out, in0=in0, scalar1=s1, scalar2=s2, op0=op0, op1=op1, **kw),
                      reads, writes, cost=c)

        def STT(out, in0, scalar, in1, op0, op1, reads, writes, eng="dve"):
            c = (90 + 1.4 * fsz(out)) if eng == "dve" else (150 + 2.3 * fsz(out))
            cx.op(eng, lambda e: e.scalar_tensor_tensor(out=out, in0=in0, scalar=scalar, in1=in1, op0=op0, op1=op1),
                  reads, writes, cost=c)

        def CP(out, in_, reads, writes, eng="dve"):
            c = (90 + 1.0 * fsz(out)) if eng == "dve" else ((200 + 0.85 * fsz(out)) if eng == "act"
                                                          else (150 + 2.3 * fsz(out)))
            cx.op(eng, lambda e: e.tensor_copy(out=out, in_=in_), reads, writes, cost=c)

        def MS(ap, val, writes, eng="pool"):
            c = (90 + 0.6 * fsz(ap)) if eng == "dve" else (150 + 1.2 * fsz(ap))
            cx.op(eng, lambda e: e.memset(ap, val), (), writes, cost=c)

        def MM(out, lhsT, rhs, start, stop, reads, writes):
            cx.op("pe", lambda e: e.matmul(out, lhsT=lhsT, rhs=rhs, start=start, stop=stop, skip_group_check=True),
                  reads, writes, cost=70 + 0.42 * max(fsz(rhs), 64))

        def TR(out, in_, ident, reads, writes):
            cx.op("pe", lambda e: e.transpose(out, in_, ident), reads, writes, cost=120)

        def nby(ap):
            n = ap.shape[0]
            for d in ap.shape[1:]:
                n *= d
            return n * (4 if ap.dtype in (F32, I32) else 2)

        LD = lambda out, in_, writes, reads=(), **kw: cx.dma("sp", out, in_, reads=reads, writes=writes,
                                                             nbytes=nby(out), **kw)
        ST = lambda out, in_, reads, writes=(), **kw: cx.dma("pool", out, in_, reads=reads, writes=writes,
                                                             nbytes=nby(in_), **kw)

        B = {n: Buf(n) for n in ("xs", "qmT", "kmT", "vm", "qnT", "kcmpT", "vcmpT", "kslcT", "kwinT", "vslc",
                                 "vwin", "gat", "ymla", "ynsa", "wgin_b", "wup_b", "wdn_b", "y")}

        identf, b_identf = sbt(top, "identf", [128, 128], F32)
        identb, b_identb = sbt(top, "identb", [128, 128], BF16)
        ones_b, b_ones = sbt(top, "ones_b", [128, 128], BF16)
        invf, b_invf = sbt(top, "invf", [128, 2], F32)
        LD(identf[:], c_ident[:, :], [b_identf])
        LD(invf[:], c_invf[:, :], [b_invf])
        CP(identb[:], identf[:], [b_identf], [b_identb])
        MS(ones_b[:], 1.0, [b_ones])
        CONST = [b_identf, b_identb, b_ones, b_invf]

        epst, b_epst = sbt(top, "epst", [128, 1], F32)
        MS(epst[:], EPS, [b_epst])

        def rsqrt_ip(ap, b_ap, inv_n, src=None, b_src=None):
            p = ap.shape[0]
            if src is None:
                src, b_src = ap, b_ap
            ACT(ap, src, AF.Sqrt, [b_src, b_epst], [b_ap], bias=epst[0:p, 0:1], scale=float(inv_n))
            cx.op("dve", lambda e: e.reciprocal(out=ap, in_=ap), [b_ap], [b_ap])

        def vec_col(st, name, src_row, n):
            c = n // 128
            t, b = sbt(st, name, [128, c], F32)
            for k in range(c):
                LD(t[:, k:k + 1], src_row[k * 128:(k + 1) * 128].rearrange("(p o) -> p o", o=1), [b])
            return t, b

        def rms_tile(xt, b_xt, rs, b_rs, col, junk, b_junk, n):
            ACT(junk, xt, AF.Square, [b_xt], [b_junk, b_rs], accum_out=rs[:, col:col + 1])
            rsqrt_ip(rs[:, col:col + 1], b_rs, 1.0 / n)

        def norm_transpose(src, s0, xt4, b_xt4, hT, b_hT, rs, b_rs, hn, b_hn, junk, b_junk, b_src, bank, b_bank):
            for t in range(4):
                LD(xt4[:, t, :], src[s0 + t * 128:s0 + (t + 1) * 128, :], [b_xt4], reads=[b_src])
            for t in range(4):
                MS(rs[:, t:t + 1], 0.0, [b_rs], eng="dve")
                rms_tile(xt4[:, t, :], b_xt4, rs, b_rs, t, junk[:], b_junk, D)
                TS(hn[:], xt4[:, t, :], rs[:, t:t + 1], None, ALU.mult, None, [b_xt4, b_rs], [b_hn])
                pb = bank[:, :].bitcast(BF16)
                for k in range(8):
                    TR(pb[:, k * 128:(k + 1) * 128], hn[:, k * 128:(k + 1) * 128], identb[:],
                       [b_hn, b_identb], [b_bank])
                CP(hT[:, :, t * 128:(t + 1) * 128], pb[:, 0:1024].rearrange("p (k t) -> p k t", k=8),
                   [b_bank], [b_hT], eng="act" if False else "dve")

        def sincos(st_name, pos_f, b_pos, col, rows, shift, out, b_out, wk, b_wk, wki, b_wki):
            r = slice(0, rows)
            TS(wk[r, 0, :], pos_f[r, :], invf[r, col:col + 1], shift, ALU.mult, ALU.add, [b_pos, b_invf], [b_wk])
            TS(wk[r, 1, :], wk[r, 0, :], float(1.0 / (2 * np.pi)), None, ALU.mult, None, [b_wk], [b_wk])
            CP(wki[r, :], wk[r, 1, :], [b_wk], [b_wki])
            CP(wk[r, 1, :], wki[r, :], [b_wki], [b_wk])
            STT(wk[r, 0, :], wk[r, 1, :], float(-2 * np.pi), wk[r, 0, :], ALU.mult, ALU.add, [b_wk], [b_wk])
            TS(wk[r, 1, :], wk[r, 0, :], float(np.pi), None, ALU.is_gt, None, [b_wk], [b_wk])
            STT(wk[r, 0, :], wk[r, 1, :], float(-2 * np.pi), wk[r, 0, :], ALU.mult, ALU.add, [b_wk], [b_wk])
            TS(wk[r, 1, :], wk[r, 0, :], float(-np.pi), None, ALU.is_lt, None, [b_wk], [b_wk])
            STT(wk[r, 0, :], wk[r, 1, :], float(2 * np.pi), wk[r, 0, :], ALU.mult, ALU.add, [b_wk], [b_wk])
            ACT(out[r, :], wk[r, 0, :], AF.Sin, [b_wk], [b_out])

        PH = [0]
        for l in range(depth):
            x_src, b_xsrc = (x_in, Buf("x_in")) if l == 0 else (xs, B["xs"])
            x_dst, b_xdst = (y_out, B["y"]) if l == depth - 1 else (xs, B["xs"])
            w_in = W["w_in"][l]

            with ExitStack() as st:
                NA = 2904
                WA, b_WA = sbt(st, "WA", [128, 8, NA], BF16)
                WQ, b_WQ = sbt(st, "WQ", [128, 3, 1024], BF16)
                WKV, b_WKV = sbt(st, "WKV", [128, 2, 1024], BF16)
                stage, b_stage = sbt(st, "stage", [128, 1976], F32)
                gpre, b_gpre = vec_col(st, "gpre", W["norm_mix_pre"][l], D)
                gq, b_gq = vec_col(st, "gq", W["mla_q_norm"][l], 384)
                gkv, b_gkv = vec_col(st, "gkv", W["mla_kv_norm"][l], 256)
                ngpre, b_ngpre = sbt(st, "ngpre", [128, 8], F32)
                ngq, b_ngq = sbt(st, "ngq", [128, 3], F32)
                TS(ngpre[:], gpre[:], -1.0, None, ALU.mult, None, [b_gpre], [b_ngpre])
                TS(ngq[:], gq[:], -1.0, None, ALU.mult, None, [b_gq], [b_ngq])
                MS(WA[:, :, 1216:1728], 0.0, [b_WA])
                for c in (1856, 2112, 2368):
                    MS(WA[:, :, c:c + 128], 0.0, [b_WA])

                def SC(out, in_, sc, rd, wr):
                    TS(out, in_, sc, None, ALU.mult, None, rd, wr)

                for kc in range(8):
                    LD(stage[:], w_in[kc * 128:(kc + 1) * 128, 4608:6584], [b_stage])
                    g = gpre[:, kc:kc + 1]
                    ng = ngpre[:, kc:kc + 1]
                    rd = [b_stage, b_gpre, b_ngpre]
                    SC(WA[:, kc, 0:672], stage[:, 0:672], g, rd, [b_WA])
                    SC(WA[:, kc, 672:688], stage[:, 656:672], ng, rd, [b_WA])
                    SC(WA[:, kc, 688:704], stage[:, 640:656], g, rd, [b_WA])
                    SC(WA[:, kc, 704:1216], stage[:, 672:1184], g, rd, [b_WA])
                    sq = stage[:, 672:1184].rearrange("p (h d) -> p h d", d=64)
                    dq = WA[:, kc, 1216:1728].rearrange("p (h d) -> p h d", d=64)
                    SC(dq[:, :, 0:8], sq[:, :, 8:16], ng, rd, [b_WA])
                    SC(dq[:, :, 8:16], sq[:, :, 0:8], g, rd, [b_WA])
                    for (src_o, dst_o) in ((1184, 1728), (1440, 1984), (1696, 2240)):
                        SC(WA[:, kc, dst_o:dst_o + 128], stage[:, src_o:src_o + 128], g, rd, [b_WA])
                        sk = stage[:, src_o:src_o + 128].rearrange("p (h d) -> p h d", d=64)
                        dk = WA[:, kc, dst_o + 128:dst_o + 256].rearrange("p (h d) -> p h d", d=64)
                        SC(dk[:, :, 0:8], sk[:, :, 8:16], ng, rd, [b_WA])
                        SC(dk[:, :, 8:16], sk[:, :, 0:8], g, rd, [b_WA])
                    SC(WA[:, kc, 2496:2624], stage[:, 1312:1440], g, rd, [b_WA])
                    SC(WA[:, kc, 2624:2752], stage[:, 1568:1696], g, rd, [b_WA])
                    SC(WA[:, kc, 2752:2880], stage[:, 1824:1952], g, rd, [b_WA])
                    SC(WA[:, kc, 2880:2904], stage[:, 1952:1976], g, rd, [b_WA])
                for kc in range(3):
                    LD(stage[:, 0:768], W["mla_w_uq"][l][kc * 128:(kc + 1) * 128, :], [b_stage])
                    g = gq[:, kc:kc + 1]
                    ng = ngq[:, kc:kc + 1]
                    rd = [b_stage, b_gq, b_ngq]
                    s3 = stage[:, 0:768].rearrange("p (h d) -> p h d", d=96)
                    d3 = WQ[:, kc, 0:768].rearrange("p (h d) -> p h d", d=96)
                    SC(d3[:, :, 0:32], s3[:, :, 64:96], g, rd, [b_WQ])
                    SC(d3[:, :, 32:96], s3[:, :, 0:64], g, rd, [b_WQ])
                    r3 = WQ[:, kc, 768:1024].rearrange("p (h d) -> p h d", d=32)
                    SC(r3[:, :, 0:16], s3[:, :, 80:96], ng, rd, [b_WQ])
                    SC(r3[:, :, 16:32], s3[:, :, 64:80], g, rd, [b_WQ])
                for kc in range(2):
                    LD(stage[:, 0:1024], W["mla_w_ukv"][l][kc * 128:(kc + 1) * 128, :], [b_stage])
                    g = gkv[:, kc:kc + 1]
                    rd = [b_stage, b_gkv]
                    s3 = stage[:, 0:1024].rearrange("p (h d) -> p h d", d=128)
                    SC(WKV[:, kc, 0:512].rearrange("p (h d) -> p h d", d=64), s3[:, :, 0:64], g, rd, [b_WKV])
                    SC(WKV[:, kc, 512:1024].rearrange("p (h d) -> p h d", d=64), s3[:, :, 64:128], g, rd, [b_WKV])

                gf, b_gf = vec_col(st, "gf", W["norm_ffn_pre"][l], D)
                wst, b_wst = sbt(st, "wst", [128, 5632], F32)
                wsb, b_wsb = sbt(st, "wsb", [128, 5632], BF16)
                for kc in range(8):
                    LD(wst[:, 0:4608], w_in[kc * 128:(kc + 1) * 128, 0:4608], [b_wst])
                    SC(wsb[:, 0:4608], wst[:, 0:4608], gpre[:, kc:kc + 1], [b_wst, b_gpre], [b_wsb])
                    ST(wgin_b[kc * 128:(kc + 1) * 128, :], wsb[:, 0:4608], [b_wsb], [B["wgin_b"]])
                for kc in range(8):
                    LD(wst[:, :], W["ffn_w_up"][l][kc * 128:(kc + 1) * 128, :], [b_wst])
                    SC(wsb[:, :], wst[:, :], gf[:, kc:kc + 1], [b_wst, b_gf], [b_wsb])
                    ST(wup_b[kc * 128:(kc + 1) * 128, :], wsb[:, :], [b_wsb], [B["wup_b"]])
                for kc in range(22):
                    LD(wst[:, 0:1024], W["ffn_w_down"][l][kc * 128:(kc + 1) * 128, :], [b_wst])
                    CP(wsb[:, 0:1024], wst[:, 0:1024], [b_wst], [b_wsb])
                    ST(wdn_b[kc * 128:(kc + 1) * 128, :], wsb[:, 0:1024], [b_wsb], [B["wdn_b"]])

                xt4, b_xt4 = sbt(st, "xt4", [128, 4, D], F32)
                hT, b_hT = sbt(st, "hT", [128, 8, 512], BF16)
                hn, b_hn = sbt(st, "hn", [128, D], BF16)
                junk, b_junk = sbt(st, "junk", [128, D], BF16)
                rs, b_rs = sbt(st, "rs", [128, 8], F32)
                posi, b_posi = sbt(st, "posi", [128, 512], I32)
                posf, b_posf = sbt(st, "posf", [128, 512], F32)
                wk, b_wk = sbt(st, "wk", [128, 2, 512], F32)
                wki, b_wki = sbt(st, "wki", [128, 512], I32)
                Cm, b_Cm = sbt(st, "Cm", [128, 512], F32)
                Sm, b_Sm = sbt(st, "Sm", [128, 512], F32)
                Cn, b_Cn = sbt(st, "Cn", [128, 512], F32)
                Sn, b_Sn = sbt(st, "Sn", [128, 512], F32)
                cqT, b_cqT = sbt(st, "cqT", [128, 3, 512], BF16)
                ckvT, b_ckvT = sbt(st, "ckvT", [128, 2, 512], BF16)
                sqb, b_sqb = sbt(st, "sqb", [128, 3, 512], BF16)
                rq, b_rq = sbt(st, "rq", [128, 512], F32)
                rkv, b_rkv = sbt(st, "rkv", [128, 512], F32)
                rkt, b_rkt = sbt(st, "rkt", [128, 4], F32)
                CR, b_CR = sbt(st, "CR", [128, 512], F32)
                SR, b_SR = sbt(st, "SR", [128, 512], F32)
                t1s = [sbt(st, "t1_%d" % i, [128, 512], F32) for i in range(2)]
                t2s = [sbt(st, "t2_%d" % i, [128, 512], F32) for i in range(2)]
                tti = [0]

                def next_tt():
                    tti[0] += 1
                    return t1s[tti[0] % 2] + t2s[tti[0] % 2]
                ob = [sbt(st, "ob%d" % i, [128, 512], BF16) for i in range(3)]
                vt = [sbt(st, "vt%d" % i, [128, 8, 65], BF16) for i in range(2)]
                vs2 = [sbt(st, "vs2_%d" % i, [128, 2, 2, 65], BF16) for i in range(2)]
                gt = [sbt(st, "gt%d" % i, [128, 24], F32) for i in range(2)]
                for (t_, b_) in vt:
                    MS(t_[:], 1.0, [b_])
                for (t_, b_) in vs2:
                    MS(t_[:], 1.0, [b_])
                obi = [0]

                def next_ob():
                    obi[0] += 1
                    return ob[obi[0] % 3]

                for s in range(NS):
                    s0 = s * 512
                    norm_transpose(x_src, s0, xt4, b_xt4, hT, b_hT, rs, b_rs, hn, b_hn, junk, b_junk, b_xsrc,
                                   banks[7], bb[7])
                    LD(posi[:], pos_in[0:1, s0:s0 + 512].partition_broadcast(128), [b_posi])
                    CP(posf[:], posi[:], [b_posi], [b_posf])
                    sincos("cm", posf, b_posf, 0, 128, float(np.pi / 2), Cm, b_Cm, wk, b_wk, wki, b_wki)
                    sincos("sm", posf, b_posf, 0, 128, 0.0, Sm, b_Sm, wk, b_wk, wki, b_wki)
                    sincos("cn", posf, b_posf, 1, 128, float(np.pi / 2), Cn, b_Cn, wk, b_wk, wki, b_wki)
                    sincos("sn", posf, b_posf, 1, 128, 0.0, Sn, b_Sn, wk, b_wk, wki, b_wki)

                    def proj(bank_i, c0, m, rd_extra=()):
                        for kc in range(8):
                            MM(banks[bank_i][0:m, :], WA[:, kc, c0:c0 + m], hT[:, kc, :], kc == 0, kc == 7,
                               [b_WA, b_hT], [bb[bank_i]])

                    for c in range(3):
                        proj(c % 2, c * 128, 128)
                        ACT(cqT[:, c, :], banks[c % 2][:, :], AF.Copy, [bb[c % 2]], [b_cqT])
                        ACT(sqb[:, c, :], banks[c % 2][:, :], AF.Square, [bb[c % 2]], [b_sqb])
                    for c in range(3):
                        MM(banks[2][:, :], ones_b[:, :], sqb[:, c, :], c == 0, c == 2, [b_ones, b_sqb], [bb[2]])
                    rsqrt_ip(rq[:], b_rq, 1.0 / 384, src=banks[2][:, :], b_src=bb[2])
                    TT(CR[:], Cm[:], rq[:], ALU.mult, [b_Cm, b_rq], [b_CR])
                    TT(SR[:], Sm[:], rq[:], ALU.mult, [b_Sm, b_rq], [b_SR])
                    for h in range(8):
                        bq = 3 + (h % 2) * 2
                        for kc in range(3):
                            MM(banks[bq][0:96, :], WQ[:, kc, h * 96:(h + 1) * 96], cqT[:, kc, :], kc == 0, kc == 2,
                               [b_WQ, b_cqT], [bb[bq]])
                        for kc in range(3):
                            MM(banks[bq + 1][0:32, :], WQ[:, kc, 768 + h * 32:768 + (h + 1) * 32], cqT[:, kc, :],
                               kc == 0, kc == 2, [b_WQ, b_cqT], [bb[bq + 1]])
                        o_t, o_b = next_ob()
                        t1, b_t1, t2, b_t2 = next_tt()
                        TT(t1[0:96, :], banks[bq][0:96, :], CR[0:96, :], ALU.mult, [bb[bq], b_CR], [b_t1])
                        TT(t2[0:32, :], banks[bq + 1][0:32, :], SR[0:32, :], ALU.mult, [bb[bq + 1], b_SR], [b_t2])
                        TT(o_t[0:32, :], t1[0:32, :], t2[0:32, :], ALU.add, [b_t1, b_t2], [o_b])
                        ACT(o_t[32:64, :], t1[32:64, :], AF.Copy, [b_t1], [o_b])
                        ACT(o_t[64:96, :], t1[64:96, :], AF.Copy, [b_t1], [o_b])
                        ST(qmT[h, :, s0:s0 + 512], o_t[0:96, :], [o_b], [B["qmT"]])
                    for c in range(2):
                        proj(c, 384 + c * 128, 128)
                        ACT(ckvT[:, c, :], banks[c][:, :], AF.Copy, [bb[c]], [b_ckvT])
                        ACT(sqb[:, c, :], banks[c][:, :], AF.Square, [bb[c]], [b_sqb])
                    for c in range(2):
                        MM(banks[2][:, :], ones_b[:, :], sqb[:, c, :], c == 0, c == 1, [b_ones, b_sqb], [bb[2]])
                    rsqrt_ip(rkv[:], b_rkv, 1.0 / 256, src=banks[2][:, :], b_src=bb[2])
                    for t in range(4):
                        for c in range(2):
                            MM(banks[2][:, t:t + 1], sqb[:, c, t * 128:(t + 1) * 128], ones_b[:, 0:1],
                               (t == 0 and c == 0), c == 1, [b_ones, b_sqb], [bb[2]])
                    rsqrt_ip(rkt[:], b_rkt, 1.0 / 256, src=banks[2][:, 0:4], b_src=bb[2])
                    for g2 in range(4):
                        bk = 3 + (g2 % 2)
                        for kc in range(2):
                            MM(banks[bk][:, :], WKV[:, kc, g2 * 128:(g2 + 1) * 128], ckvT[:, kc, :], kc == 0, kc == 1,
                               [b_WKV, b_ckvT], [bb[bk]])
                        o_t, o_b = next_ob()
                        TT(o_t[:, :], banks[bk][:, :], rkv[:, :], ALU.mult, [bb[bk], b_rkv], [o_b])
                        ST(kmT[2 * g2, 32:96, s0:s0 + 512], o_t[0:64, :], [o_b], [B["kmT"]])
                        ST(kmT[2 * g2 + 1, 32:96, s0:s0 + 512], o_t[64:128, :], [o_b], [B["kmT"]])
                    for t in range(4):
                        bv = 5 + (t % 2)
                        for kc in range(2):
                            MM(banks[bv][:, :], ckvT[:, kc, t * 128:(t + 1) * 128], WKV[:, kc, 512:1024], kc == 0,
                               kc == 1, [b_WKV, b_ckvT], [bb[bv]])
                        v_t, v_b = vt[t % 2]
                        TS(v_t[:, :, 0:64], banks[bv][:, :].rearrange("p (h d) -> p h d", d=64), rkt[:, t:t + 1], None,
                           ALU.mult, None, [bb[bv], b_rkt], [v_b])
                        ST(vm[s0 + t * 128:s0 + (t + 1) * 128, :, :], v_t[:], [v_b], [B["vm"]])
                    proj(0, 640, 32)
                    proj(1, 672, 32)
                    o_t, o_b = next_ob()
                    t1, b_t1, t2, b_t2 = next_tt()
                    TT(t1[0:32, :], banks[0][0:32, :], Cm[0:32, :], ALU.mult, [bb[0], b_Cm], [b_t1])
                    TT(t2[0:32, :], banks[1][0:32, :], Sm[0:32, :], ALU.mult, [bb[1], b_Sm], [b_t2])
                    TT(o_t[0:32, :], t1[0:32, :], t2[0:32, :], ALU.add, [b_t1, b_t2], [o_b])
                    for h in range(8):
                        ST(kmT[h, 0:32, s0:s0 + 512], o_t[0:32, :], [o_b], [B["kmT"]])
                    rgi = [0]

                    def rope_group(c0, r0, dests):
                        rgi[0] += 1
                        bq = 3 + 2 * (rgi[0] % 2)
                        proj(bq, c0, 128)
                        proj(bq + 1, r0, 128)
                        o_t, o_b = next_ob()
                        t1, b_t1, t2, b_t2 = next_tt()
                        TT(t1[:, :], banks[bq][:, :], Cn[:, :], ALU.mult, [bb[bq], b_Cn], [b_t1])
                        TT(t2[:, :], banks[bq + 1][:, :], Sn[:, :], ALU.mult, [bb[bq + 1], b_Sn], [b_t2])
                        TT(o_t[:, :], t1[:, :], t2[:, :], ALU.add, [b_t1, b_t2], [o_b])
                        for (dst, bname, lo) in dests:
                            ST(dst, o_t[lo:lo + 64, :], [o_b], [B[bname]])
                    for pr in range(4):
                        h0 = 2 * pr
                        rope_group(704 + pr * 128, 1216 + pr * 128,
                                   [(qnT[h0 // 4, :, h0 % 4, s0:s0 + 512], "qnT", 0),
                                    (qnT[(h0 + 1) // 4, :, (h0 + 1) % 4, s0:s0 + 512], "qnT", 64)])
                    for (c0, dst, nm) in ((1728, kcmpT, "kcmpT"), (1984, kslcT, "kslcT"), (2240, kwinT, "kwinT")):
                        rope_group(c0, c0 + 128, [(dst[0, :, s0:s0 + 512], nm, 0), (dst[1, :, s0:s0 + 512], nm, 64)])
                    proj(5, 2496, 128)
                    o_t, o_b = next_ob()
                    ACT(o_t[:, :], banks[5][:, :], AF.Copy, [bb[5]], [o_b])
                    ST(vcmpT[0, :, s0:s0 + 512], o_t[0:64, :], [o_b], [B["vcmpT"]])
                    ST(vcmpT[1, :, s0:s0 + 512], o_t[64:128, :], [o_b], [B["vcmpT"]])
                    for t in range(4):
                        bv = 5 + (t % 2)
                        for kc in range(8):
                            MM(banks[bv][:, 0:280], hT[:, kc, t * 128:(t + 1) * 128], WA[:, kc, 2624:2904], kc == 0,
                               kc == 7, [b_WA, b_hT], [bb[bv]])
                        v_t, v_b = vs2[t % 2]
                        CP(v_t[:, :, :, 0:64], banks[bv][:, 0:256].rearrange("p (a g d) -> p a g d", a=2, g=2),
                           [bb[bv]], [v_b])
                        g_t, g_b = gt[t % 2]
                        ACT(g_t[:, :], banks[bv][:, 256:280], AF.Sigmoid, [bb[bv]], [g_b])
                        r0, r1 = s0 + t * 128, s0 + (t + 1) * 128
                        ST(vslc[r0:r1, :, :], v_t[:, 0, :, :], [v_b], [B["vslc"]])
                        ST(vwin[r0:r1, :, :], v_t[:, 1, :, :], [v_b], [B["vwin"]])
                        ST(gat[r0:r1, :], g_t[:, :], [g_b], [B["gat"]])
                cx.barrier()
                cx.emit()
            PH[0] += 1
            if stop_after is not None and PH[0] >= stop_after:
                break

            with ExitStack() as st:
                tri, b_tri = sbt(st, "tri", [128, 128], BF16)
                trif, b_trif = sbt(st, "trif", [128, 128], F32)
                LD(trif[:], c_tri[:, :], [b_trif])
                CP(tri[:], trif[:], [b_trif], [b_tri])
                KT, b_KT = sbt(st, "KT", [96, 4, S], BF16)
                VV, b_VV = sbt(st, "VV", [128, NT, 4, 65], BF16)
                QT = [sbt(st, "QT%d" % i, [96, 4, 128], BF16) for i in range(2)]
                PT = [sbt(st, "PT%d" % i, [128, 512], BF16) for i in range(6)]
                recs = [sbt(st, "rec%d" % i, [128, 4], F32) for i in range(2)]
                yo = [sbt(st, "yo%d" % i, [128, 4, 64], BF16) for i in range(2)]
                scale_m = float(96 ** -0.5)
                LAG = 2

                def run_pipeline(items):
                    n = len(items)
                    for k in range(n + LAG):
                        if k >= LAG:
                            items[k - LAG][2]()
                        if k < n:
                            items[k][0]()
                            items[k][1]()

                for hg in range(2):
                    for h in range(4):
                        LD(KT[:, h, :], kmT[hg * 4 + h, :, :], [b_KT], reads=[B["kmT"]])
                    for jt in range(NT):
                        LD(VV[:, jt, :, :], vm[jt * 128:(jt + 1) * 128, hg * 4:(hg + 1) * 4, :], [b_VV],
                           reads=[B["vm"]])
                    items = []
                    for i in range(NT):
                        for j in range(i + 1):
                            k = len(items)

                            def fS(i=i, j=j, k=k):
                                q_t, q_b = QT[i % 2]
                                if j == 0:
                                    for h in range(4):
                                        LD(q_t[:, h, :], qmT[hg * 4 + h, :, i * 128:(i + 1) * 128], [q_b],
                                           reads=[B["qmT"]])
                                sb_i = (0, 1, 2, 3, 6, 7)[k % 6]
                                for h in range(4):
                                    MM(banks[sb_i][:, h * 128:(h + 1) * 128], KT[:, h, j * 128:(j + 1) * 128],
                                       q_t[:, h, :], h == 0, h == 3, [b_KT, q_b], [bb[sb_i]])

                            def fE(i=i, j=j, k=k):
                                sb_i = (0, 1, 2, 3, 6, 7)[k % 6]
                                p_t, p_b = PT[k % 6]
                                ACT(p_t[:, :], banks[sb_i][:, :], AF.Exp, [bb[sb_i]], [p_b], scale=scale_m)
                                if j == i:
                                    TT(p_t[:, :].rearrange("p (h q) -> p h q", h=4),
                                       p_t[:, :].rearrange("p (h q) -> p h q", h=4), bcast_mid(tri[:, :], 4), ALU.mult,
                                       [p_b, b_tri], [p_b], eng="pool")

                            def fP(i=i, j=j, k=k):
                                p_t, p_b = PT[k % 6]
                                ab = 4 + (i % 2)
                                for h in range(4):
                                    MM(banks[ab][:, h * 65:(h + 1) * 65], p_t[:, h * 128:(h + 1) * 128],
                                       VV[:, j, h, :], (j == 0 and h == 0), j == i, [p_b, b_VV], [bb[ab]])
                                if j == i:
                                    rec, b_rec = recs[i % 2]
                                    acc = banks[ab][:, 0:260].rearrange("p (h c) -> p h c", c=65)
                                    TS(rec[:, :], acc[:, :, 64], 1e-30, None, ALU.max, None, [bb[ab]], [b_rec])
                                    cx.op("dve", lambda e, rec=rec: e.reciprocal(out=rec[:, :], in_=rec[:, :]),
                                          [b_rec], [b_rec])
                                    y_t, y_b = yo[i % 2]
                                    TT(y_t[:, :, :], acc[:, :, 0:64], bcast_last(rec[:, :], 64), ALU.mult,
                                       [bb[ab], b_rec], [y_b])
                                    ST(ymla[i * 128:(i + 1) * 128, hg * 256:(hg + 1) * 256],
                                       y_t[:, :, :].rearrange("p h d -> p (h d)"), [y_b], [B["ymla"]])

                            items.append((fS, fE, fP))
                    run_pipeline(items)
                cx.barrier()
                cx.emit()
            PH[0] += 1
            if stop_after is not None and PH[0] >= stop_after:
                break

            with ExitStack() as st:
                cst, b_cst = sbt(st, "cst", [128, 16 * 128], F32)
                tri, b_tri = sbt(st, "tri", [128, 128], BF16)
                wlow, b_wlow = sbt(st, "wlow", [128, 128], BF16)
                maskc, b_maskc = sbt(st, "maskc", [128, 16, 128], BF16)
                emat, b_emat = sbt(st, "emat", [128, NT, 128], BF16)
                LD(cst[:, 0:128], c_tri[:, :], [b_cst])
                CP(tri[:], cst[:, 0:128], [b_cst], [b_tri])
                LD(cst[:, 0:128], c_wlow[:, :], [b_cst])
                CP(wlow[:], cst[:, 0:128], [b_cst], [b_wlow])
                LD(cst[:, :].rearrange("p (o q) -> p o q", o=16), c_maskc[:, :, :], [b_cst])
                CP(maskc[:], cst[:, :].rearrange("p (o q) -> p o q", o=16), [b_cst], [b_maskc])
                for j0 in range(0, NT, 16):
                    jn = min(16, NT - j0)
                    LD(cst[:, 0:jn * 128].rearrange("p (o q) -> p o q", o=jn), c_emat[:, j0:j0 + jn, :], [b_cst])
                    CP(emat[:, j0:j0 + jn, :], cst[:, 0:jn * 128].rearrange("p (o q) -> p o q", o=jn), [b_cst],
                       [b_emat])
                wck, b_wck = sbt(st, "wck", [64, 32, 64], BF16)
                wcv, b_wcv = sbt(st, "wcv", [64, 32, 64], BF16)
                wcs, b_wcs = sbt(st, "wcs", [64, 32, 64], F32)
                LD(wcs[:], W["nsa_cmp_w_k"][l].rearrange("(r d) e -> d r e", d=64), [b_wcs])
                CP(wck[:], wcs[:], [b_wcs], [b_wck])
                LD(wcs[:], W["nsa_cmp_w_v"][l].rearrange("(r d) e -> d r e", d=64), [b_wcs])
                CP(wcv[:], wcs[:], [b_wcs], [b_wcv])
                pkv, b_pkv = sbt(st, "pkv", [32, 2, 64], F32)
                LD(pkv[:, 0, :], W["nsa_cmp_pos_k"][l], [b_pkv])
                LD(pkv[:, 1, :], W["nsa_cmp_pos_v"][l], [b_pkv])
                posT, b_posT = sbt(st, "posT", [64, 2, 32], BF16)
                for a in range(2):
                    TR(banks[0][0:64, a * 32:(a + 1) * 32], pkv[:, a, :], identf[0:32, 0:32], [b_pkv, b_identf],
                       [bb[0]])
                CP(posT[:, :, :], banks[0][0:64, 0:64].rearrange("p (a r) -> p a r", a=2), [bb[0]], [b_posT])
                biask, b_biask = sbt(st, "biask", [64, 1], F32)
                biasv, b_biasv = sbt(st, "biasv", [1, 64], BF16)
                for r in range(32):
                    MM(banks[1][0:64, 0:1], wck[:, r, :], posT[:, 0, r:r + 1], r == 0, r == 31, [b_wck, b_posT],
                       [bb[1]])
                CP(biask[:, :], banks[1][0:64, 0:1], [bb[1]], [b_biask])
                for r in range(32):
                    MM(banks[2][0:1, 0:64], posT[:, 1, r:r + 1], wcv[:, r, :], r == 0, r == 31, [b_wcv, b_posT],
                       [bb[2]])
                CP(biasv[:, :], banks[2][0:1, 0:64], [bb[2]], [b_biasv])
                KS, b_KS = sbt(st, "KS", [64, 2, S], BF16)
                KW, b_KW = sbt(st, "KW", [64, 2, S], BF16)
                kcT, b_kcT = sbt(st, "kcT", [64, 2, 512], BF16)
                VC, b_VC = sbt(st, "VC", [128, 4, 2, CW], BF16)
                MS(kcT[:], 0.0, [b_kcT])
                MS(VC[:], 0.0, [b_VC])
                for g in range(2):
                    LD(KS[:, g, :], kcmpT[g, :, :], [b_KS], reads=[B["kcmpT"]])
                    LD(KW[:, g, :], vcmpT[g, :, :], [b_KW], reads=[B["vcmpT"]])
                LD(cst[:, 0:512].rearrange("p (t m) -> p t m", t=4), c_ovl[:, :, :], [b_cst])
                for g in range(2):
                    MS(VC[:, :, g, 64:65], 1.0, [b_VC])
                    CP(VC[:, :, g, 65:CW], cst[:, 0:512].rearrange("p (t m) -> p t m", t=4), [b_cst], [b_VC])
                for g in range(2):
                    n0 = 0
                    while n0 < NCMP:
                        nn = min(512, NCMP - n0)
                        for r in range(32):
                            src = KS[:, g, r + 16 * n0:r + 16 * n0 + 16 * (nn - 1) + 1]
                            rhs = bass.AP(src.tensor, src.offset, [list(src.ap[0]), [16, nn]])
                            MM(banks[3][0:64, 0:nn], wck[:, r, :], rhs, r == 0, r == 31, [b_wck, b_KS], [bb[3]])
                        ACT(kcT[:, g, n0:n0 + nn], banks[3][0:64, 0:nn], AF.Identity, [bb[3], b_biask], [b_kcT],
                            bias=biask[:, 0:1], scale=1.0)
                        n0 += nn
                    for jt in range(NCT):
                        nn = min(128, NCMP - jt * 128)
                        for r in range(32):
                            o = r + 16 * jt * 128
                            src = KW[:, g, o:o + 16 * (nn - 1) + 1]
                            lhsT = bass.AP(src.tensor, src.offset, [list(src.ap[0]), [16, nn]])
                            MM(banks[4][0:nn, 0:64], lhsT, wcv[:, r, :], r == 0, False, [b_wcv, b_KW], [bb[4]])
                        MM(banks[4][0:nn, 0:64], ones_b[0:1, 0:nn], biasv[0:1, :], False, True, [b_ones, b_biasv],
                           [bb[4]])
                        CP(VC[0:nn, jt, g, 0:64], banks[4][0:nn, 0:64], [bb[4]], [b_VC])
                VS, b_VS = sbt(st, "VS", [128, NT, 2, 65], BF16)
                VW, b_VW = sbt(st, "VW", [128, NT, 2, 65], BF16)
                for g in range(2):
                    LD(KS[:, g, :], kslcT[g, :, :], [b_KS], reads=[B["kslcT"]])
                    LD(KW[:, g, :], kwinT[g, :, :], [b_KW], reads=[B["kwinT"]])
                for jt in range(NT):
                    LD(VS[:, jt, :, :], vslc[jt * 128:(jt + 1) * 128, :, :], [b_VS], reads=[B["vslc"]])
                    LD(VW[:, jt, :, :], vwin[jt * 128:(jt + 1) * 128, :, :], [b_VW], reads=[B["vwin"]])
                QN = [sbt(st, "QN%d" % i, [64, 2, 4, 128], BF16) for i in range(2)]
                GA = [sbt(st, "GA%d" % i, [128, 24], F32) for i in range(2)]
                AM = [sbt(st, "AM%d" % i, [128, 128], F32) for i in range(2)]
                PT = [sbt(st, "PTn%d" % i, [128, 512], BF16) for i in range(4)]
                imps = [sbt(st, "imp%d" % i, [128, 128], F32) for i in range(2)]
                impws = [sbt(st, "impw%d" % i, [128, 128], F32) for i in range(2)]
                top16s = [sbt(st, "top16_%d" % i, [128, 16], F32) for i in range(2)]
                negms = [sbt(st, "negm%d" % i, [128, 128], BF16) for i in range(2)]
                negTs = [sbt(st, "negT%d" % i, [128, 128], BF16) for i in range(2)]
                recs = [sbt(st, "recn%d" % i, [128, 4], F32) for i in range(6)]
                wgts = [sbt(st, "wgt%d" % i, [128, 4], F32) for i in range(6)]
                yaccs = [sbt(st, "yacc%d" % i, [128, 8, 64], F32) for i in range(2)]
                ytmps = [sbt(st, "ytmp%d" % i, [128, 4, 64], F32) for i in range(2)]
                yo = [sbt(st, "yon%d" % i, [128, 512], BF16) for i in range(2)]
                scale_n = 0.125
                LAG = 2
                rot = [0]

                def small():
                    rot[0] += 1
                    return recs[rot[0] % 6], wgts[rot[0] % 6]

                def exp_mask(k, masks):
                    sb_i = k % 3
                    p_t, p_b = PT[k % 4]
                    ACT(p_t[:, :], banks[sb_i][:, :], AF.Exp, [bb[sb_i]], [p_b], scale=scale_n)
                    for (m_ap, m_b) in masks:
                        TT(p_t[:, :].rearrange("p (h q) -> p h q", h=4),
                           p_t[:, :].rearrange("p (h q) -> p h q", h=4), bcast_mid(m_ap, 4), ALU.mult,
                           [p_b, m_b], [p_b], eng="pool")

                def branch_out(i, g, bank_i, br):
                    acc = banks[bank_i][:, 0:260].rearrange("p (h c) -> p h c", c=65)
                    acc_b = bb[bank_i]
                    (rec, b_rec), (wgt, b_wgt) = small()
                    g_t, g_b = GA[i % 2]
                    gv = g_t[:, :].rearrange("p (h b) -> p h b", b=3)
                    yacc, b_yacc = yaccs[i % 2]
                    ytmp, b_ytmp = ytmps[(2 * i + g + br) % 2]
                    TS(rec[:, :], acc[:, :, 64], 1e-30, None, ALU.max, None, [acc_b], [b_rec])
                    cx.op("dve", lambda e, rec=rec: e.reciprocal(out=rec[:, :], in_=rec[:, :]), [b_rec], [b_rec])
                    TT(wgt[:, :], rec[:, :], gv[:, g * 4:(g + 1) * 4, br], ALU.mult, [b_rec, g_b], [b_wgt])
                    TT(ytmp[:, :, :], acc[:, :, 0:64], bcast_last(wgt[:, :], 64), ALU.mult, [acc_b, b_wgt], [b_ytmp])
                    TT(yacc[:, g * 4:(g + 1) * 4, :], yacc[:, g * 4:(g + 1) * 4, :], ytmp[:, :, :], ALU.add,
                       [b_yacc, b_ytmp], [b_yacc], eng="pool")

                def cmp_final(i, g):
                    (rec, b_rec), (wgt, b_wgt) = small()
                    par = (2 * i + g) % 2
                    imp, b_imp = imps[par]
                    impw, b_impw = impws[par]
                    top16, b_top16 = top16s[par]
                    negm, b_negm = negms[par]
                    g_t, g_b = GA[i % 2]
                    a_t, a_b = AM[i % 2]
                    gv = g_t[:, :].rearrange("p (h b) -> p h b", b=3)
                    yacc, b_yacc = yaccs[i % 2]
                    for h in range(4):
                        bk = 3 + h // 2
                        a0 = (h % 2) * CW
                        TS(rec[:, h:h + 1], banks[bk][:, a0 + 64:a0 + 65], 1e-30, None, ALU.max, None, [bb[bk]],
                           [b_rec])
                    cx.op("dve", lambda e, rec=rec: e.reciprocal(out=rec[:, :], in_=rec[:, :]), [b_rec], [b_rec])
                    for h in range(4):
                        bk = 3 + h // 2
                        a0 = (h % 2) * CW
                        if h == 0:
                            TS(imp[:, :], banks[bk][:, a0 + 65:a0 + CW], rec[:, 0:1], None, ALU.mult, None,
                               [bb[bk], b_rec], [b_imp])
                        else:
                            STT(imp[:, :], banks[bk][:, a0 + 65:a0 + CW], rec[:, h:h + 1], imp[:, :], ALU.mult,
                                ALU.add, [bb[bk], b_rec, b_imp], [b_imp])
                    TT(wgt[:, :], rec[:, :], gv[:, g * 4:(g + 1) * 4, 0], ALU.mult, [b_rec, g_b], [b_wgt])
                    for h in range(4):
                        bk = 3 + h // 2
                        a0 = (h % 2) * CW
                        TS(yacc[:, g * 4 + h, :], banks[bk][:, a0:a0 + 64], wgt[:, h:h + 1], None, ALU.mult, None,
                           [bb[bk], b_wgt], [b_yacc])
                    TT(imp[:, :], imp[:, :], a_t[:, :], ALU.add, [b_imp, a_b], [b_imp])
                    cx.op("dve", lambda e, top16=top16, imp=imp: e.max(out=top16[:, 0:8], in_=imp[:, :]),
                          [b_imp], [b_top16])
                    cx.op("dve", lambda e, top16=top16, imp=imp, impw=impw:
                          e.match_replace(out=impw[:, :], in_to_replace=top16[:, 0:8], in_values=imp[:, :],
                                          imm_value=-3.0), [b_imp, b_top16], [b_impw])
                    cx.op("dve", lambda e, top16=top16, impw=impw: e.max(out=top16[:, 8:16], in_=impw[:, :]),
                          [b_impw], [b_top16])
                    TS(impw[:, :], imp[:, :], top16[:, 15:16], -1.0, ALU.is_ge, ALU.add, [b_imp, b_top16], [b_impw])
                    TS(negm[:, :], impw[:, :], -NEG, None, ALU.mult, None, [b_impw], [b_negm])

                def neg_transpose(i, g):
                    par = (2 * i + g) % 2
                    negm, b_negm = negms[par]
                    negT, b_negT = negTs[par]
                    pb = banks[7][:, :].bitcast(BF16)
                    TR(pb[:, 0:128], negm[:, :], identb[:, :], [b_negm, b_identb], [bb[7]])
                    CP(negT[:, :], pb[:, 0:128], [bb[7]], [b_negT])

                items = []
                for i in range(NT):
                    for g in range(2):
                        jl = i // 16
                        j_lo = max(0, i - 4)
                        seq = [("c", jt) for jt in range(jl + 1)] + [("w", j) for j in range(j_lo, i + 1)] + \
                              [("s", j) for j in range(i + 1)]
                        for (kind, j) in seq:
                            k = len(items)

                            def fS(i=i, g=g, kind=kind, j=j, k=k):
                                q_t, q_b = QN[i % 2]
                                if g == 0 and kind == "c" and j == 0:
                                    g_t, g_b = GA[i % 2]
                                    a_t, a_b = AM[i % 2]
                                    for gg in range(2):
                                        LD(q_t[:, gg, :, :], qnT[gg, :, :, i * 128:(i + 1) * 128], [q_b],
                                           reads=[B["qnT"]])
                                    LD(g_t[:, :], gat[i * 128:(i + 1) * 128, :], [g_b], reads=[B["gat"]])
                                    LD(a_t[:, :], c_amask[i, :, :], [a_b])
                                qrhs = q_t[:, g, :, :]
                                sb_i = k % 3
                                out3 = banks[sb_i][:, :].rearrange("p (h q) -> p h q", h=4)
                                if kind == "c":
                                    MM(out3, kcT[:, g, j * 128:(j + 1) * 128], qrhs, True, True, [b_kcT, q_b],
                                       [bb[sb_i]])
                                elif kind == "w":
                                    MM(out3, KW[:, g, j * 128:(j + 1) * 128], qrhs, True, True, [b_KW, q_b],
                                       [bb[sb_i]])
                                else:
                                    if j == 0:
                                        neg_transpose(i, g)
                                    negT, b_negT = negTs[(2 * i + g) % 2]
                                    MM(out3, KS[:, g, j * 128:(j + 1) * 128], qrhs, True, False, [b_KS, q_b],
                                       [bb[sb_i]])
                                    MM(out3, emat[:, j, :], bcast_mid(negT[:, :], 4), False, True,
                                       [b_emat, b_negT], [bb[sb_i]])

                            def fE(i=i, g=g, kind=kind, j=j, k=k):
                                masks = []
                                if kind == "c":
                                    if j == i // 16:
                                        masks.append((maskc[:, i % 16, :], b_maskc))
                                else:
                                    if j == i:
                                        masks.append((tri[:, :], b_tri))
                                    if kind == "w" and j == i - 4:
                                        masks.append((wlow[:, :], b_wlow))
                                exp_mask(k, masks)

                            def fP(i=i, g=g, kind=kind, j=j, k=k):
                                p_t, p_b = PT[k % 4]
                                if kind == "c":
                                    jl = i // 16
                                    for h in range(4):
                                        bk = 3 + h // 2
                                        MM(banks[bk][:, (h % 2) * CW:(h % 2 + 1) * CW], p_t[:, h * 128:(h + 1) * 128],
                                           VC[:, j, g, :], (j == 0 and h % 2 == 0), j == jl, [p_b, b_VC], [bb[bk]])
                                    if j == jl:
                                        cmp_final(i, g)
                                elif kind == "w":
                                    j_lo = max(0, i - 4)
                                    for h in range(4):
                                        MM(banks[6][:, h * 65:(h + 1) * 65], p_t[:, h * 128:(h + 1) * 128],
                                           VW[:, j, g, :], (j == j_lo and h == 0), j == i, [p_b, b_VW], [bb[6]])
                                    if j == i:
                                        branch_out(i, g, 6, 2)
                                else:
                                    for h in range(4):
                                        MM(banks[5][:, h * 65:(h + 1) * 65], p_t[:, h * 128:(h + 1) * 128],
                                           VS[:, j, g, :], (j == 0 and h == 0), j == i, [p_b, b_VS], [bb[5]])
                                    if j == i:
                                        branch_out(i, g, 5, 1)
                                        if g == 1:
                                            yacc, b_yacc = yaccs[i % 2]
                                            y_t, y_b = yo[i % 2]
                                            CP(y_t[:, :], yacc[:, :, :].rearrange("p h d -> p (h d)"), [b_yacc], [y_b])
                                            ST(ynsa[i * 128:(i + 1) * 128, :], y_t[:, :], [y_b], [B["ynsa"]])

                            items.append((fS, fE, fP))
                n_items = len(items)
                for k in range(n_items + LAG):
                    if k >= LAG:
                        items[k - LAG][2]()
                    if k < n_items:
                        items[k][0]()
                        items[k][1]()
                cx.barrier()
                cx.emit()
            PH[0] += 1
            if stop_after is not None and PH[0] >= stop_after:
                break

            with ExitStack() as st:
                WB, b_WB = sbt(st, "WB", [128, 12, D], BF16)
                WO, b_WO = sbt(st, "WO", [128, 8, D], BF16)
                wst, b_wst = sbt(st, "wstT", [128, D], F32)
                for bi, nm in enumerate(("w_branch_conv", "w_branch_mla", "w_branch_nsa")):
                    for kc in range(4):
                        LD(wst[:], W[nm][l][kc * 128:(kc + 1) * 128, :], [b_wst])
                        CP(WB[:, bi * 4 + kc, :], wst[:], [b_wst], [b_WB])
                for kc in range(8):
                    LD(wst[:], W["w_out"][l][kc * 128:(kc + 1) * 128, :], [b_wst])
                    CP(WO[:, kc, :], wst[:], [b_wst], [b_WO])
                gpost, b_gpost = sbt(st, "gpost", [128, D], F32)
                gfpost, b_gfpost = sbt(st, "gfpost", [128, D], F32)
                LD(gpost[:], W["norm_mix_post"][l:l + 1, :].partition_broadcast(128), [b_gpost])
                LD(gfpost[:], W["norm_ffn_post"][l:l + 1, :].partition_broadcast(128), [b_gfpost])
                cw, b_cw = sbt(st, "cw", [128, 4, 3], F32)
                fw, b_fw = sbt(st, "fw", [128, 22, 3], F32)
                fb, b_fb = sbt(st, "fb", [128, 22], F32)
                for k in range(3):
                    for c in range(4):
                        LD(cw[:, c, k:k + 1], W["conv_w"][l][k, c * 128:(c + 1) * 128].rearrange("(p o) -> p o", o=1),
                           [b_cw])
                    for c in range(22):
                        LD(fw[:, c, k:k + 1],
                           W["ffn_conv_w"][l][k, c * 128:(c + 1) * 128].rearrange("(p o) -> p o", o=1), [b_fw])
                for c in range(22):
                    LD(fb[:, c:c + 1], W["ffn_conv_b"][l][c * 128:(c + 1) * 128].rearrange("(p o) -> p o", o=1), [b_fb])
                xt4, b_xt4 = sbt(st, "xt4T", [128, 4, D], F32)
                hT, b_hT = sbt(st, "hTT", [128, 8, 512], BF16)
                hns = [sbt(st, "hnT%d" % i, [128, D], BF16) for i in range(2)]
                junk, b_junk = sbt(st, "junkT", [128, D], BF16)
                rs, b_rs = sbt(st, "rsT", [128, 16], F32)
                WS = [sbt(st, "WS%d" % i, [128, 8, 256], BF16) for i in range(3)]
                WD = [sbt(st, "WD%d" % i, [128, D], BF16) for i in range(2)]
                gT, b_gT = sbt(st, "gT", [128, 22, 512], BF16)
                cin, b_cin = gT, b_gT
                uu, b_uu = sbt(st, "uu", [128, 4, 514], F32)
                ycT, b_ycT = sbt(st, "ycT", [128, 4, 512], BF16)
                ymT, b_ymT = sbt(st, "ymT", [128, 4, 512], BF16)
                ynT, b_ynT = sbt(st, "ynT", [128, 4, 512], BF16)
                ytks = [sbt(st, "ytk%d" % i, [128, 512], BF16) for i in range(2)]
                gsbs = [[sbt(st, "gsb%d_%d" % (i, k), [128, 512], BF16) for k in range(3)] for i in range(2)]
                zA, b_zA = sbt(st, "zA", [128, 512], F32)
                zB, b_zB = sbt(st, "zB", [128, 512], F32)
                mT, b_mT = sbt(st, "mT", [128, 8, 512], BF16)
                aas = [sbt(st, "aa%d" % i, [128, 514], F32) for i in range(3)]
                bsbs = [sbt(st, "bsb%d" % i, [128, 512], BF16) for i in range(3)]
                halo, b_halo = sbt(st, "halo", [128, 22, 2], F32)
                z1s = [sbt(st, "z1_%d" % i, [128, 512], F32) for i in range(3)]
                z2s = [sbt(st, "z2_%d" % i, [128, 512], F32) for i in range(3)]
                xos = [sbt(st, "xo%d" % i, [128, D], F32) for i in range(2)]
                MS(uu[:], 0.0, [b_uu])
                MS(halo[:], 0.0, [b_halo])
                wsi = [0]

                def load_ws(src, c0):
                    wsi[0] += 1
                    w_t, w_b = WS[wsi[0] % 3]
                    LD(w_t[:, :, :], src[:, c0:c0 + 256].rearrange("(k p) c -> p k c", p=128), [w_b],
                       reads=[B["wgin_b"], B["wup_b"]])
                    return w_t, w_b

                pnr = [0]

                def post_norm_residual(t, gtile, b_gt, first_bank):
                    pnr[0] += 1
                    xo, b_xo = xos[pnr[0] % 2]
                    c0 = 4 + 2 * (pnr[0] % 4)
                    MS(rs[:, c0:c0 + 2], 0.0, [b_rs], eng="dve")
                    ACT(junk[:, 0:512], banks[first_bank][:, :], AF.Square, [bb[first_bank]], [b_junk, b_rs],
                        accum_out=rs[:, c0:c0 + 1])
                    ACT(junk[:, 512:1024], banks[first_bank + 1][:, :], AF.Square, [bb[first_bank + 1]],
                        [b_junk, b_rs], accum_out=rs[:, c0 + 1:c0 + 2])
                    TT(rs[:, c0:c0 + 1], rs[:, c0:c0 + 1], rs[:, c0 + 1:c0 + 2], ALU.add, [b_rs], [b_rs])
                    rsqrt_ip(rs[:, c0:c0 + 1], b_rs, 1.0 / D)
                    for hf in range(2):
                        STT(xo[:, hf * 512:(hf + 1) * 512], banks[first_bank + hf][:, :], rs[:, c0:c0 + 1],
                            gtile[:, hf * 512:(hf + 1) * 512], ALU.mult, ALU.mult,
                            [bb[first_bank + hf], b_rs, b_gt], [b_xo])
                    TT(xt4[:, t, :], xt4[:, t, :], xo[:, :], ALU.add, [b_xt4, b_xo], [b_xt4])

                def norm_tr(t, tb):
                    hn, b_hn = hns[t % 2]
                    MS(rs[:, t:t + 1], 0.0, [b_rs], eng="dve")
                    rms_tile(xt4[:, t, :], b_xt4, rs, b_rs, t, junk[:], b_junk, D)
                    TS(hn[:], xt4[:, t, :], rs[:, t:t + 1], None, ALU.mult, None, [b_xt4, b_rs], [b_hn])
                    pb = banks[tb][:, :].bitcast(BF16)
                    for k in range(8):
                        TR(pb[:, k * 128:(k + 1) * 128], hn[:, k * 128:(k + 1) * 128], identb[:],
                           [b_hn, b_identb], [bb[tb]])
                    ACT(hT[:, :, t * 128:(t + 1) * 128], pb[:, 0:1024].rearrange("p (k t) -> p k t", k=8),
                        AF.Copy, [bb[tb]], [b_hT])

                for s in range(NS):
                    s0 = s * 512
                    for t in range(4):
                        LD(xt4[:, t, :], x_src[s0 + t * 128:s0 + (t + 1) * 128, :], [b_xt4], reads=[b_xsrc])
                    for t in range(4):
                        norm_tr(t, 6 + t % 2)
                    yi = 0
                    for (ysrc, bname, yT, b_yT) in ((ymla, "ymla", ymT, b_ymT), (ynsa, "ynsa", ynT, b_ynT)):
                        for t in range(4):
                            ytk, b_ytk = ytks[yi % 2]
                            tb = 4 + yi % 2
                            yi += 1
                            LD(ytk[:, :], ysrc[s0 + t * 128:s0 + (t + 1) * 128, :], [b_ytk], reads=[B[bname]])
                            pb = banks[tb][:, :].bitcast(BF16)
                            for k in range(4):
                                TR(pb[:, k * 128:(k + 1) * 128], ytk[:, k * 128:(k + 1) * 128], identb[:],
                                   [b_ytk, b_identb], [bb[tb]])
                            ACT(yT[:, :, t * 128:(t + 1) * 128], pb[:, 0:512].rearrange("p (k t) -> p k t", k=4),
                                AF.Copy, [bb[tb]], [b_yT])
                    for c2 in range(6):
                        w_t, w_b = load_ws(wgin_b, 3072 + c2 * 256)
                        for c in range(2):
                            bk = (c2 * 2 + c) % 4
                            for kc in range(8):
                                MM(banks[bk][:, :], w_t[:, kc, c * 128:(c + 1) * 128], hT[:, kc, :], kc == 0, kc == 7,
                                   [w_b, b_hT], [bb[bk]])
                            if c == 0:
                                ACT(cin[:, c2 * 2 + c, :], banks[bk][:, :], AF.Copy, [bb[bk]], [b_cin])
                            else:
                                CP(cin[:, c2 * 2 + c, :], banks[bk][:, :], [bb[bk]], [b_cin])
                    for c in range(4):
                        CP(uu[:, c, 0:2], uu[:, c, 512:514], [b_uu], [b_uu])
                    for c in range(4):
                        TT(uu[:, c, 2:514], cin[:, 4 + c, :], cin[:, 8 + c, :], ALU.mult, [b_cin], [b_uu])
                        TS(zA[:, :], uu[:, c, 0:512], cw[:, c, 0:1], None, ALU.mult, None, [b_uu, b_cw], [b_zA])
                        STT(zA[:, :], uu[:, c, 1:513], cw[:, c, 1:2], zA[:, :], ALU.mult, ALU.add,
                            [b_uu, b_cw, b_zA], [b_zA])
                        STT(zA[:, :], uu[:, c, 2:514], cw[:, c, 2:3], zA[:, :], ALU.mult, ALU.add,
                            [b_uu, b_cw, b_zA], [b_zA])
                        TT(ycT[:, c, :], zA[:, :], cin[:, c, :], ALU.mult, [b_zA, b_cin], [b_ycT])
                    for n2 in range(4):
                        wts = [load_ws(wgin_b, bi * 1024 + n2 * 256) for bi in range(3)]
                        for nn in range(2):
                            n = n2 * 2 + nn
                            gs = gsbs[n % 2]
                            for bi in range(3):
                                w_t, w_b = wts[bi]
                                for kc in range(8):
                                    MM(banks[bi][:, :], w_t[:, kc, nn * 128:(nn + 1) * 128], hT[:, kc, :], kc == 0,
                                       kc == 7, [w_b, b_hT], [bb[bi]])
                                ACT(gs[bi][0][:, :], banks[bi][:, :], AF.Sigmoid, [bb[bi]], [gs[bi][1]])
                            for bi, (yT, b_yT) in enumerate(((ycT, b_ycT), (ymT, b_ymT), (ynT, b_ynT))):
                                for kc in range(4):
                                    MM(banks[3 + bi][:, :], WB[:, bi * 4 + kc, n * 128:(n + 1) * 128], yT[:, kc, :],
                                       kc == 0, kc == 3, [b_WB, b_yT], [bb[3 + bi]])
                            TT(zA[:, :], banks[3][:, :], gs[0][0][:, :], ALU.mult, [bb[3], gs[0][1]], [b_zA])
                            TT(zB[:, :], banks[4][:, :], gs[1][0][:, :], ALU.mult, [bb[4], gs[1][1]], [b_zB])
                            TT(zA[:, :], zA[:, :], zB[:, :], ALU.add, [b_zA, b_zB], [b_zA])
                            TT(zB[:, :], banks[5][:, :], gs[2][0][:, :], ALU.mult, [bb[5], gs[2][1]], [b_zB])
                            TT(mT[:, n, :], zA[:, :], zB[:, :], ALU.add, [b_zA, b_zB], [b_mT])
                    for t in range(4):
                        fbk = (t % 2) * 2
                        for hf in range(2):
                            for kc in range(8):
                                MM(banks[fbk + hf][:, :], mT[:, kc, t * 128:(t + 1) * 128],
                                   WO[:, kc, hf * 512:(hf + 1) * 512], kc == 0, kc == 7, [b_mT, b_WO], [bb[fbk + hf]])
                        post_norm_residual(t, gpost, b_gpost, fbk)
                    for t in range(4):
                        norm_tr(t, 6 + t % 2)
                    for c2 in range(11):
                        wa_t, wa_b = load_ws(wup_b, c2 * 256)
                        wb_t, wb_b = load_ws(wup_b, DFF + c2 * 256)
                        for cc in range(2):
                            c = c2 * 2 + cc
                            par = c % 2
                            p3 = c % 3
                            ba, bbk = par * 2, par * 2 + 1
                            aa, b_aa = aas[p3]
                            z1, b_z1 = z1s[p3]
                            z2, b_z2 = z2s[p3]
                            for kc in range(8):
                                MM(banks[ba][:, :], wa_t[:, kc, cc * 128:(cc + 1) * 128], hT[:, kc, :],
                                   kc == 0, kc == 7, [wa_b, b_hT], [bb[ba]])
                            for kc in range(8):
                                MM(banks[bbk][:, :], wb_t[:, kc, cc * 128:(cc + 1) * 128], hT[:, kc, :],
                                   kc == 0, kc == 7, [wb_b, b_hT], [bb[bbk]])
                            bsb, b_bsb = bsbs[p3]
                            CP(aa[:, 0:2], halo[:, c, :], [b_halo], [b_aa], eng="pool")
                            ACT(aa[:, 2:514], banks[ba][:, :], AF.Copy, [bb[ba]], [b_aa])
                            ACT(bsb[:, :], banks[bbk][:, :], AF.Copy, [bb[bbk]], [b_bsb])
                            CP(halo[:, c, :], aa[:, 512:514], [b_aa], [b_halo], eng="pool")
                            ACT(z1[:, :], aa[:, 0:512], AF.Identity, [b_aa, b_fw, b_fb], [b_z1],
                                scale=fw[:, c, 0:1], bias=fb[:, c:c + 1])
                            STT(z1[:, :], aa[:, 1:513], fw[:, c, 1:2], z1[:, :], ALU.mult, ALU.add,
                                [b_aa, b_fw, b_z1], [b_z1])
                            STT(z1[:, :], aa[:, 2:514], fw[:, c, 2:3], z1[:, :], ALU.mult, ALU.add,
                                [b_aa, b_fw, b_z1], [b_z1])
                            ACT(z2[:, :], z1[:, :], AF.Square, [b_z1], [b_z2], scale=float(np.sqrt(0.044715)))
                            STT(z2[:, :], z2[:, :], 1.0, z1[:, :], ALU.add, ALU.mult, [b_z2, b_z1], [b_z2])
                            ACT(z2[:, :], z2[:, :], AF.Sigmoid, [b_z2], [b_z2], scale=float(2.0 * np.sqrt(2.0 / np.pi)))
                            TT(z1[:, :], z1[:, :], bsb[:, :], ALU.mult, [b_z1, b_bsb], [b_z1])
                            TT(gT[:, c, :], z2[:, :], z1[:, :], ALU.mult, [b_z2, b_z1], [b_gT])
                    wdi = 0
                    for tp in range(2):
                        for kc in range(22):
                            w_t, w_b = WD[wdi % 2]
                            wdi += 1
                            LD(w_t[:, :], wdn_b[kc * 128:(kc + 1) * 128, :], [w_b], reads=[B["wdn_b"]])
                            for tt in range(2):
                                t = tp * 2 + tt
                                for hf in range(2):
                                    bk = 4 + tt * 2 + hf
                                    MM(banks[bk][:, :], gT[:, kc, t * 128:(t + 1) * 128],
                                       w_t[:, hf * 512:(hf + 1) * 512], kc == 0, kc == 21, [b_gT, w_b], [bb[bk]])
                        for tt in range(2):
                            t = tp * 2 + tt
                            post_norm_residual(t, gfpost, b_gfpost, 4 + tt * 2)
                            ST(x_dst[s0 + t * 128:s0 + (t + 1) * 128, :], xt4[:, t, :], [b_xt4], [b_xdst])
                cx.barrier()
                cx.emit()
            PH[0] += 1
            if stop_after is not None and PH[0] >= stop_after:
                break
    return nc


_CACHE = {}


def _in_maps(inputs):
    x = np.asarray(inputs["x"])
    Bn, S, _ = x.shape
    consts = host_consts(S)
    shared = {}
    for k, v in inputs.items():
        if k in ("x", "positions"):
            continue
        shared[k] = np.ascontiguousarray(np.asarray(v, dtype=np.float32))
    for k, v in consts.items():
        shared["c_" + k] = np.ascontiguousarray(v.astype(np.float32))
    pos = np.asarray(inputs["positions"]).astype(np.int32)
    in_maps = []
    for c in range(8):
        b = c % Bn
        m = dict(shared)
        m["x"] = np.ascontiguousarray(x[b].astype(np.float32))
        m["positions"] = np.ascontiguousarray(pos[b][None, :])
        in_maps.append(m)
    return in_maps


def kernel(_stop_after=None, **inputs):
    x = np.asarray(inputs["x"])
    Bn, S, _ = x.shape
    depth = np.asarray(inputs["w_in"]).shape[0]
    key = (S, depth, _stop_after)
    if key not in _CACHE:
        _CACHE[key] = build(S, depth, _stop_after)
    nc = _CACHE[key]
    in_maps = _in_maps(inputs)
    res = run_bass_kernel_spmd(nc, in_maps, core_ids=list(range(8)))
    out = np.stack([np.asarray(res.results[b]["y"]) for b in range(Bn)], axis=0)
    return out.astype(np.float32)
```
